# Optimizing a Trainium2 kernel written in Bass

```python
import math
import jax, jax.numpy as jnp
from jax import lax
import numpy as np


D_MODEL = 1024
BATCH = 4
SEQ = 8192
DEPTH = 1

ML_WIDTH = D_MODEL
ML_HEADS = 4
ML_HEAD_DIM = ML_WIDTH // ML_HEADS
ML_CHUNK = 128
CONV_WIDTH = 5
DA_WIDTH = D_MODEL
DA_HEADS = 8
DA_V_DIM = DA_WIDTH // DA_HEADS
DA_QK_DIM = DA_V_DIM // 2
Q_BLOCK = 128
ROPE_THETA = 10000.0
NORM_EPS = 1e-6
SEG_SIZES = (ML_WIDTH,) * 5 + (4 * ML_HEADS,) + (DA_WIDTH,) * 4 + (D_MODEL,) * 2
PROJ_WIDTH = sum(SEG_SIZES)

kernel_name = "hybrid_mlstm_diffattn_gated_block"


def _split_points():
    pts, acc = [], 0
    for s in SEG_SIZES[:-1]:
        acc += s
        pts.append(acc)
    return tuple(pts)


def rmsnorm(x, g):
    xf = x.astype(jnp.float32)
    y = xf * lax.rsqrt(jnp.mean(xf * xf, axis=-1, keepdims=True) + NORM_EPS)
    return (y * g.astype(jnp.float32)).astype(x.dtype)


def centred_depthwise_conv(x, w):
    pad = CONV_WIDTH // 2
    return lax.conv_general_dilated(
        x, w[:, None, :].astype(x.dtype), window_strides=(1,), padding=[(pad, pad)],
        dimension_numbers=('NWC', 'WIO', 'NWC'), feature_group_count=x.shape[-1])


def apply_rope(t, positions):
    d = t.shape[-1]
    inv_freq = ROPE_THETA ** (-jnp.arange(0, d, 2, dtype=jnp.float32) / d)
    ang = positions.astype(jnp.float32)[..., None] * inv_freq
    cos = jnp.cos(ang)[:, :, None, None, :]
    sin = jnp.sin(ang)[:, :, None, None, :]
    tf = t.astype(jnp.float32)
    t1, t2 = tf[..., : d // 2], tf[..., d // 2:]
    return jnp.concatenate([t1 * cos - t2 * sin, t2 * cos + t1 * sin], axis=-1)


def mlstm_one_direction(q, k, v, i_pre, log_f):
    B, H, S, d = q.shape
    L = ML_CHUNK
    nc = S // L

    def chunks(t):
        return jnp.moveaxis(t.reshape(t.shape[:2] + (nc, L) + t.shape[3:]), 2, 0)

    lower = jnp.tril(jnp.ones((L, L), dtype=bool))

    def step(carry, inp):
        c_mat, n_vec, m_st = carry
        qc, kc, vc, ic, fc = inp
        b = jnp.cumsum(fc, axis=-1)
        log_d = jnp.where(lower, b[..., :, None] - b[..., None, :] + ic[..., None, :], -jnp.inf)
        log_inter = b + m_st[..., None]
        m_t = jnp.maximum(log_inter, jnp.max(log_d, axis=-1))
        w_intra = jnp.einsum('bhtd,bhsd->bhts', qc, kc) * jnp.exp(log_d - m_t[..., None])
        s_inter = jnp.exp(log_inter - m_t)
        num = (jnp.einsum('bhts,bhsd->bhtd', w_intra, vc)
               + s_inter[..., None] * jnp.einsum('bhvk,bhtk->bhtv', c_mat, qc))
        den = jnp.sum(w_intra, axis=-1) + s_inter * jnp.einsum('bhk,bhtk->bht', n_vec, qc)
        h = num / jnp.maximum(jnp.abs(den), jnp.exp(-m_t))[..., None]
        b_last = b[..., -1]
        log_w = b_last[..., None] - b + ic
        m_new = jnp.maximum(b_last + m_st, jnp.max(log_w, axis=-1))
        w_state = jnp.exp(log_w - m_new[..., None])
        decay = jnp.exp(b_last + m_st - m_new)
        c_new = decay[..., None, None] * c_mat + jnp.einsum('bhsv,bhsk->bhvk', w_state[..., None] * vc, kc)
        n_new = decay[..., None] * n_vec + jnp.einsum('bhs,bhsk->bhk', w_state, kc)
        return (c_new, n_new, m_new), h

    init = (jnp.zeros((B, H, d, d), jnp.float32), jnp.zeros((B, H, d), jnp.float32),
            jnp.zeros((B, H), jnp.float32))
    _, h = lax.scan(step, init, (chunks(q), chunks(k), chunks(v), chunks(i_pre), chunks(log_f)))
    return jnp.moveaxis(h, 0, 2).reshape(B, H, S, d)


def diff_attention(q, k, v, lam):
    B, S, H = q.shape[:3]
    nb = S // Q_BLOCK
    q_blocks = jnp.moveaxis(q.reshape((B, nb, Q_BLOCK) + q.shape[2:]), 1, 0)
    scale = DA_QK_DIM ** -0.5

    def one_block(qb):
        scores = jnp.einsum('bqhjd,bkhjd->bhjqk', qb, k) * scale
        p = jax.nn.softmax(scores, axis=-1)
        p_diff = p[:, :, 0] - lam * p[:, :, 1]
        return jnp.einsum('bhqk,bkhd->bqhd', p_diff, v)

    out = lax.map(one_block, q_blocks)
    return jnp.moveaxis(out, 0, 1).reshape(B, S, H, v.shape[-1])


def setup_inputs(seed: int = 0) -> dict:
    key = jax.random.key(seed)
    ks = jax.random.split(key, 14)
    f32 = jnp.float32
    x = jax.random.normal(ks[0], (BATCH, SEQ, D_MODEL), f32)
    positions = (jnp.arange(SEQ, dtype=jnp.int32)[None, :]
                 + jax.random.randint(ks[1], (BATCH, 1), 0, 1024, dtype=jnp.int32))
    norm_g = 1.0 + 0.02 * jax.random.normal(ks[2], (DEPTH, D_MODEL), f32)
    w_in = jax.random.normal(ks[3], (DEPTH, D_MODEL, PROJ_WIDTH), f32) * D_MODEL ** -0.5
    zeros_h = jnp.zeros((ML_HEADS,), f32)
    f_off = jnp.linspace(3.0, 6.0, ML_HEADS, dtype=f32)
    gate_offsets = jnp.stack([zeros_h, f_off, zeros_h, f_off])
    ml_gate_b = 0.1 * jax.random.normal(ks[4], (DEPTH, 4, ML_HEADS), f32) + gate_offsets[None]
    ml_conv_w = jax.random.normal(ks[5], (DEPTH, CONV_WIDTH, 2 * ML_WIDTH), f32) * CONV_WIDTH ** -0.5
    ml_norm_g = 1.0 + 0.02 * jax.random.normal(ks[6], (DEPTH, ML_HEADS, ML_HEAD_DIM), f32)
    da_lambda = 0.1 * jax.random.normal(ks[7], (DEPTH, 4, DA_QK_DIM), f32)
    da_subln_g = 1.0 + 0.02 * jax.random.normal(ks[8], (DEPTH, DA_V_DIM), f32)
    gate_b = 0.02 * jax.random.normal(ks[9], (DEPTH, 2, D_MODEL), f32)
    w_branch_a = jax.random.normal(ks[10], (DEPTH, ML_WIDTH, D_MODEL), f32) * ML_WIDTH ** -0.5
    w_branch_b = jax.random.normal(ks[11], (DEPTH, DA_WIDTH, D_MODEL), f32) * DA_WIDTH ** -0.5
    w_out = jax.random.normal(ks[12], (DEPTH, D_MODEL, D_MODEL), f32) * D_MODEL ** -0.5
    final_g = 1.0 + 0.02 * jax.random.normal(ks[13], (D_MODEL,), f32)
    return {"x": x, "positions": positions, "norm_g": norm_g, "w_in": w_in,
            "ml_gate_b": ml_gate_b, "ml_conv_w": ml_conv_w, "ml_norm_g": ml_norm_g,
            "da_lambda": da_lambda, "da_subln_g": da_subln_g, "gate_b": gate_b,
            "w_branch_a": w_branch_a, "w_branch_b": w_branch_b, "w_out": w_out,
            "final_g": final_g}


def reference(x, positions, norm_g, w_in, ml_gate_b, ml_conv_w, ml_norm_g, da_lambda,
              da_subln_g, gate_b, w_branch_a, w_branch_b, w_out, final_g):
    B, S, _ = x.shape
    f32 = jnp.float32

    def heads(t):
        return t.reshape(B, S, ML_HEADS, ML_HEAD_DIM).transpose(0, 2, 1, 3).astype(f32)

    def flip(t):
        return jnp.flip(t, axis=2)

    for layer in range(DEPTH):
        h = rmsnorm(x, norm_g[layer])
        proj = jnp.einsum('bsd,de->bse', h, w_in[layer])
        (a_q, a_k, a_v, a_o, a_z, a_g, b_q, b_k, b_v, b_z, g_a, g_b) = jnp.split(
            proj, _split_points(), axis=-1)

        qk = jax.nn.silu(centred_depthwise_conv(jnp.concatenate([a_q, a_k], axis=-1), ml_conv_w[layer]))
        mq = heads(qk[..., :ML_WIDTH])
        mk = heads(qk[..., ML_WIDTH:]) * (ML_HEAD_DIM ** -0.5)
        mv = heads(a_v)
        gates = (a_g.astype(f32).reshape(B, S, 4, ML_HEADS)
                 + ml_gate_b[layer].astype(f32)).transpose(2, 0, 3, 1)
        i_fw, lf_fw = gates[0], jax.nn.log_sigmoid(gates[1])
        i_bw, lf_bw = gates[2], jax.nn.log_sigmoid(gates[3])
        h_fw = mlstm_one_direction(mq, mk, mv, i_fw, lf_fw)
        h_bw = flip(mlstm_one_direction(flip(mq), flip(mk), flip(mv), flip(i_bw), flip(lf_bw)))
        hm = rmsnorm((h_fw + h_bw).transpose(0, 2, 1, 3), ml_norm_g[layer])
        hm = hm.reshape(B, S, ML_WIDTH).astype(x.dtype)
        y_a = hm * jax.nn.sigmoid(a_o) * jax.nn.silu(a_z)

        lambda_init = 0.8 - 0.6 * math.exp(-0.3 * layer)
        lp = da_lambda[layer].astype(f32)
        lam = jnp.exp(jnp.sum(lp[0] * lp[1])) - jnp.exp(jnp.sum(lp[2] * lp[3])) + lambda_init
        dq = apply_rope(b_q.reshape(B, S, DA_HEADS, 2, DA_QK_DIM), positions)
        dk = apply_rope(b_k.reshape(B, S, DA_HEADS, 2, DA_QK_DIM), positions)
        da_v = b_v.reshape(B, S, DA_HEADS, DA_V_DIM).astype(f32)
        att = diff_attention(dq, dk, da_v, lam)
        att = rmsnorm(att, da_subln_g[layer]) * (1.0 - lambda_init)
        y_b = att.reshape(B, S, DA_WIDTH).astype(x.dtype) * jax.nn.silu(b_z)

        gb = gate_b[layer]
        mix = (jax.nn.sigmoid(g_a + gb[0]) * jnp.einsum('bsc,cd->bsd', y_a, w_branch_a[layer])
               + jax.nn.sigmoid(g_b + gb[1]) * jnp.einsum('bsc,cd->bsd', y_b, w_branch_b[layer]))
        x = x + jnp.einsum('bsd,de->bse', mix, w_out[layer])

    return rmsnorm(x, final_g)
```

```python
import contextlib
import math
import numpy as np
import concourse.bass as bass
import concourse.mybir as mybir
from concourse.bass_utils import run_bass_kernel_spmd

F32, BF16, I32 = mybir.dt.float32, mybir.dt.bfloat16, mybir.dt.int32
AF = mybir.ActivationFunctionType
ALU = mybir.AluOpType
AX = mybir.AxisListType

D = 1024
S = 8192
OWN = 4096
NCORES = 8
PROJ = 11280
C_AQ, C_AK, C_AV, C_AO, C_AZ, C_AG = 0, 1024, 2048, 3072, 4096, 5120
C_BQ, C_BK, C_BV, C_BZ, C_GA, C_GB = 5136, 6160, 7184, 8208, 9232, 10256
NORM_EPS = 1e-6
LAMBDA_INIT = 0.8 - 0.6 * math.exp(-0.3 * 0)
SAME_ENGINE_SYNC = True


class _Op:
    __slots__ = ("eng", "fn", "dma", "clock", "idx", "waits", "signal", "semval", "sem", "know")


class Prog:
    ENGS = ("sp", "act", "dve", "pool", "pe")

    def __init__(self, nc, stack, n_dma=8):
        self.nc = nc
        self.sem = {e: stack.enter_context(nc.semaphore("cs_" + e)) for e in self.ENGS}
        self.dsem = {q: [stack.enter_context(nc.semaphore("ds_%s_%d" % (q, i))) for i in range(n_dma)]
                     for q in ("sp", "pool", "act")}
        self.n_dma = n_dma
        self.dcount = {q: 0 for q in self.dsem}
        self.dlast = {}
        self.cnt = {e: 0 for e in self.ENGS}
        self.sigcnt = {e: 0 for e in self.ENGS}
        self.know = {e: {} for e in self.ENGS}
        self.pending = {e: [] for e in self.ENGS}
        self.last_w = {}
        self.readers = {}
        self.nops = 0

    def add(self, eng, fn, reads=(), writes=(), dma=False, extra_deps=()):
        op = _Op()
        op.eng, op.fn, op.dma, op.signal, op.semval = eng, fn, dma, False, None
        self.nops += 1
        deps = []
        seen = set()

        def push(d):
            if d is not None and id(d) not in seen:
                seen.add(id(d))
                deps.append(d)

        for k in reads:
            push(self.last_w.get(k))
        for k in writes:
            push(self.last_w.get(k))
            for r in self.readers.get(k, ()):
                push(r)
        for d in extra_deps:
            push(d)
        if dma:
            slot = self.dcount[eng] % self.n_dma
            self.dcount[eng] += 1
            op.clock = (eng, slot)
            prev = self.dlast.get(op.clock)
            op.idx = (prev.idx + 1) if prev is not None else 1
            op.sem = self.dsem[eng][slot]
            op.semval = 16 * op.idx
            push(prev)
            self.dlast[op.clock] = op
        else:
            self.cnt[eng] += 1
            op.clock = eng
            op.idx = self.cnt[eng]
            op.sem = self.sem[eng]
        know = self.know[eng]
        waits = []
        for d in deps:
            if (not d.dma) and (not dma) and d.eng == eng and (eng == "pe" or not SAME_ENGINE_SYNC):
                continue
            if know.get(d.clock, 0) >= d.idx:
                continue
            waits.append(d)
            d.signal = True
            for c, v in d.know.items():
                if know.get(c, 0) < v:
                    know[c] = v
            if know.get(d.clock, 0) < d.idx:
                know[d.clock] = d.idx
        op.waits = waits
        op.know = dict(know)
        for k in writes:
            self.last_w[k] = op
            self.readers[k] = []
        for k in reads:
            self.readers.setdefault(k, []).append(op)
        self.pending[eng].append(op)
        return op

    def pe(self, fn, r=(), w=()):
        return self.add("pe", fn, r, w)

    def act(self, fn, r=(), w=()):
        return self.add("act", fn, r, w)

    def dve(self, fn, r=(), w=()):
        return self.add("dve", fn, r, w)

    def pool(self, fn, r=(), w=()):
        return self.add("pool", fn, r, w)

    def dma(self, q, out, in_, r=(), w=()):
        return self.add(q, lambda e: e.dma_start(out=out, in_=in_), r, w, dma=True)

    def flush(self, final=False):
        outstanding = [d for d in self.dlast.values()]
        self.add("sp", None, extra_deps=outstanding)
        for e in self.ENGS:
            for op in self.pending[e]:
                if not op.dma and op.signal:
                    self.sigcnt[e] += 1
                    op.semval = self.sigcnt[e]
                elif not op.dma:
                    op.semval = None
        pend = self.pending
        sems = self.sem

        def make_body(eng):
            ops = pend[eng]

            def body(e):
                for op in ops:
                    for d in op.waits:
                        assert d.semval is not None
                        e.wait_ge(d.sem, d.semval)
                    if op.fn is None:
                        continue
                    ins = op.fn(e)
                    if op.dma:
                        ins.then_inc(op.sem, 16)
                    elif op.signal:
                        ins.then_inc(sems[eng], 1)
            return body

        with self.nc.Block() as block:
            block.sync(make_body("sp"))
            block.scalar(make_body("act"))
            block.vector(make_body("dve"))
            block.gpsimd(make_body("pool"))
            block.tensor(make_body("pe"))
        self.pending = {e: [] for e in self.ENGS}
        self.last_w = {}
        self.readers = {}
        full = {}
        for e in self.ENGS:
            full[e] = self.cnt[e]
        for c, d in self.dlast.items():
            full[c] = d.idx
        self.know = {e: dict(full) for e in self.ENGS}


def build_nc(debug=False, phases=("P", "M", "D", "O")):
    nc = bass.Bass("TRN2", target_bir_lowering=False)
    IN = lambda name, shape, dt=F32: nc.dram_tensor(name, list(shape), dt, kind="ExternalInput").ap()
    skind = "ExternalOutput" if debug else "Internal"
    SCR = lambda name, shape, dt=BF16: nc.dram_tensor(name, list(shape), dt, kind=skind).ap()

    x = IN("x", [S, D])
    posr = IN("posr", [128, S], I32)
    w_in = IN("w_in", [D, PROJ])
    normg_rep = IN("normg_rep", [128, D])
    gateb_rep = IN("gateb_rep", [128, 16])
    convw = IN("convw", [128, 16, 5])
    mlng_rep = IN("mlng_rep", [128, D])
    lam_rep = IN("lam_rep", [128, 256])
    subln_rep = IN("subln_rep", [128, 128])
    gb_fm = IN("gb_fm", [128, 2, 8])
    w_a = IN("w_a", [D, D])
    w_b = IN("w_b", [D, D])
    w_o = IN("w_o", [D, D])
    fing_rep = IN("fing_rep", [128, D])
    c_ident = IN("c_ident", [128, 128])
    c_rswap = IN("c_rswap", [128, 128])
    c_invf = IN("c_invf", [128, 1])
    c_tri = IN("c_tri", [128, 4, 128])
    out = nc.dram_tensor("out", [OWN, D], F32, kind="ExternalOutput").ap()

    mqT = SCR("mqT", [D, S])
    mkT = SCR("mkT", [D, S])
    mv = SCR("mv", [S, D])
    dqT = SCR("dqT", [D, OWN])
    dkT = SCR("dkT", [D, S])
    dv = SCR("dv", [S, D])
    gA = SCR("gA", [OWN, D])
    gB = SCR("gB", [OWN, D])
    sgT = SCR("sgT", [2, D, OWN])
    gall_d = SCR("gall_d", [128, 64 * 16], F32)
    yaT = SCR("yaT", [D, OWN])
    ybT = SCR("ybT", [D, OWN])

    with contextlib.ExitStack() as top:
        P = Prog(nc, top)
        if "P" in phases:
            phase_P(nc, P, locals())
        if "M" in phases:
            phase_M(nc, P, locals())
        if "D" in phases:
            phase_D(nc, P, locals())
        if "O" in phases:
            phase_O(nc, P, locals())
    return nc


def phase_P(nc, P, T):
    x, posr, w_in = T["x"], T["posr"], T["w_in"]
    with contextlib.ExitStack() as st:
        sb = lambda name, shape, dt=F32: st.enter_context(nc.sbuf_tensor("P_" + name, list(shape), dt))
        ps = lambda name, shape, dt=F32: st.enter_context(nc.psum_tensor("P_" + name, list(shape), dt))
        hT = sb("hT", [128, 8, 2048], BF16)
        xt = [sb("xt%d" % i, [128, D]) for i in range(2)]
        xn = [sb("xn%d" % i, [128, D], BF16) for i in range(2)]
        junk = sb("junk", [128, D], BF16)
        ss = [sb("ss%d" % i, [128, 1]) for i in range(2)]
        rstd = [sb("rstd%d" % i, [128, 1]) for i in range(2)]
        grep = sb("grep", [128, D])
        NW = 3
        wB = [sb("wB%d" % i, [128, 8, 512], BF16) for i in range(NW)]
        wG = sb("wG", [128, 8, 16], BF16)
        NWK = 5
        WK = [sb("WK%d" % i, [128, 2054]) for i in range(NWK)]
        OB = [sb("OB%d" % i, [128, 8192], BF16) for i in range(2)]
        cosT = sb("cosT", [128, 2048])
        sinT = sb("sinT", [128, 2048])
        gall = sb("gall", [128, 64 * 16])
        gbrep = sb("gbrep", [128, 16])
        cw = sb("cw", [128, 16, 5])
        carry = sb("carry", [128, 16, 4])
        identf = sb("identf", [128, 128])
        identb = sb("identb", [128, 128], BF16)
        rswap = sb("rswap", [128, 128])
        invf = sb("invf", [128, 1])
        gbfm = sb("gbfm", [128, 2, 8])
        posi = sb("posi", [128, 2048], I32)
        ptr = [ps("ptr%d" % i, [128, D], BF16) for i in range(2)]
        pp = [ps("pp%d" % i, [128, 512]) for i in range(4)]
        prot = [ps("prot%d" % i, [128, 512]) for i in range(2)]

        P.dma("sp", grep[:], T["normg_rep"], w=["grep"])
        P.dma("sp", gbrep[:], T["gateb_rep"], w=["gbrep"])
        P.dma("sp", cw[:], T["convw"], w=["cw"])
        P.dma("sp", identf[:], T["c_ident"], w=["identf"])
        P.dma("sp", rswap[:], T["c_rswap"], w=["rswap"])
        P.dma("sp", invf[:], T["c_invf"], w=["invf"])
        P.dma("sp", gbfm[:], T["gb_fm"], w=["gbfm"])
        P.dma("pool", wG[:], w_in[:, C_AG:C_AG + 16].rearrange("(kc p) c -> p kc c", p=128), w=["wG"])
        P.dve(lambda e: e.tensor_copy(out=identb[:], in_=identf[:]), r=["identf"], w=["identb"])
        P.dve(lambda e: e.memset(carry[:], 0.0), w=["carry"])
        for i in range(NWK):
            P.dve(lambda e, i=i: e.memset(WK[i][:, 2052:2054], 0.0), w=[("WKz", i)])

        wslot = [0]

        def load_w(col0, ncols):
            s = wslot[0] % NW
            wslot[0] += 1
            P.dma("pool", wB[s][:, :, 0:ncols],
                  w_in[:, col0:col0 + ncols].rearrange("(kc p) c -> p kc c", p=128), w=[("wB", s)])
            return s

        ppslot = [0]

        def next_pp():
            s = ppslot[0] % 4
            ppslot[0] += 1
            return s

        wkslot = [0]

        def next_wk():
            s = wkslot[0] % NWK
            wkslot[0] += 1
            return s

        obslot = [0]

        def next_ob():
            s = obslot[0] % 2
            obslot[0] += 1
            return s

        def fm_mm(ws, wc, tq):
            s = next_pp()
            for kc in range(8):
                P.pe(lambda e, s=s, ws=ws, wc=wc, kc=kc, tq=tq: e.matmul(
                    pp[s][:, :], wB[ws][:, kc, wc * 128:(wc + 1) * 128], hT[:, kc, tq * 512:(tq + 1) * 512],
                    start=(kc == 0), stop=(kc == 7)),
                    r=[("wB", ws), "hT"], w=[("pp", s)])
            return s

        def tm_mm(ws, ncols, tt):
            s = next_pp()
            for kc in range(8):
                P.pe(lambda e, s=s, ws=ws, kc=kc, tt=tt, ncols=ncols: e.matmul(
                    pp[s][:, 0:ncols], hT[:, kc, tt * 128:(tt + 1) * 128], wB[ws][:, kc, 0:ncols],
                    start=(kc == 0), stop=(kc == 7)),
                    r=[("wB", ws), "hT"], w=[("pp", s)])
            return s

        for stile in range(4):
            own = stile < 2
            t0 = stile * 2048
            last = stile == 3
            for tt in range(16):
                sl = tt % 2
                r0 = t0 + tt * 128
                P.dma("sp", xt[sl][:], x[r0:r0 + 128, :], w=[("xt", sl)])
                P.act(lambda e, sl=sl: e.activation(out=junk[:], in_=xt[sl][:], func=AF.Square, accum_out=ss[sl][:]),
                      r=[("xt", sl)], w=[("ss", sl)])
                P.act(lambda e, sl=sl: e.activation(out=ss[sl][:], in_=ss[sl][:], func=AF.Sqrt,
                                                    scale=1.0 / D, bias=NORM_EPS),
                      r=[("ss", sl)], w=[("ss", sl)])
                P.dve(lambda e, sl=sl: e.reciprocal(out=rstd[sl][:], in_=ss[sl][:]), r=[("ss", sl)], w=[("rstd", sl)])
                P.dve(lambda e, sl=sl: e.scalar_tensor_tensor(out=xn[sl][:], in0=xt[sl][:], scalar=rstd[sl][:],
                                                              in1=grep[:], op0=ALU.mult, op1=ALU.mult),
                      r=[("xt", sl), ("rstd", sl), "grep"], w=[("xn", sl)])
                for kc in range(8):
                    P.pe(lambda e, sl=sl, kc=kc: e.transpose(out=ptr[sl][:, kc * 128:(kc + 1) * 128],
                                                             in_=xn[sl][:, kc * 128:(kc + 1) * 128],
                                                             identity=identb[:]),
                         r=[("xn", sl), "identb"], w=[("ptr", sl)])
                P.act(lambda e, sl=sl, tt=tt: e.activation(
                    out=hT[:, :, tt * 128:(tt + 1) * 128],
                    in_=ptr[sl][:, :].rearrange("p (k t) -> p k t", k=8), func=AF.Copy),
                    r=[("ptr", sl)], w=["hT"])

            P.dma("sp", posi[:], posr[:, t0:t0 + 2048], w=["posi"])
            build_rope_tables(P, posi, invf, cosT, sinT, WK, next_wk)

            for tt in range(16):
                s = next_pp()
                for kc in range(8):
                    P.pe(lambda e, s=s, kc=kc, tt=tt: e.matmul(
                        pp[s][:, 0:16], hT[:, kc, tt * 128:(tt + 1) * 128], wG[:, kc, :],
                        start=(kc == 0), stop=(kc == 7)), r=["wG", "hT"], w=[("pp", s)])
                gt = stile * 16 + tt
                P.dve(lambda e, s=s, gt=gt: e.tensor_tensor(out=gall[:, gt * 16:(gt + 1) * 16], in0=pp[s][:, 0:16],
                                                            in1=gbrep[:], op=ALU.add),
                      r=[("pp", s), "gbrep"], w=["gall"])

            for (c0, dst) in ((C_AV, T["mv"]), (C_BV, T["dv"])):
                for cg in range(2):
                    ws = load_w(c0 + cg * 512, 512)
                    ob = next_ob()
                    for tt in range(16):
                        s = tm_mm(ws, 512, tt)
                        eng = P.act if tt % 2 == 0 else P.dve
                        if tt % 2 == 0:
                            P.act(lambda e, s=s, ob=ob, tt=tt: e.activation(
                                out=OB[ob][:, tt * 512:(tt + 1) * 512], in_=pp[s][:, :], func=AF.Copy),
                                r=[("pp", s)], w=[("OB", ob)])
                        else:
                            P.dve(lambda e, s=s, ob=ob, tt=tt: e.tensor_copy(
                                out=OB[ob][:, tt * 512:(tt + 1) * 512], in_=pp[s][:, :]),
                                r=[("pp", s)], w=[("OB", ob)])
                    P.dma("sp", dst[t0:t0 + 2048, cg * 512:(cg + 1) * 512].rearrange("(tt p) c -> p tt c", p=128),
                          OB[ob][:, :].rearrange("p (tt c) -> p tt c", c=512), r=[("OB", ob)])

            for fam, (c0, dst, ksc) in enumerate(((C_AQ, T["mqT"], 1.0), (C_AK, T["mkT"], 1.0 / 16.0))):
                for g4 in range(2):
                    ws = load_w(c0 + g4 * 512, 512)
                    for wc in range(4):
                        fc = fam * 8 + g4 * 4 + wc
                        row0 = (g4 * 4 + wc) * 128
                        stg = next_wk()
                        acc = next_wk()
                        sg = next_wk()
                        ob = next_ob()
                        P.dve(lambda e, stg=stg, fc=fc: e.tensor_copy(out=WK[stg][:, 0:4], in_=carry[:, fc, :]),
                              r=["carry"], w=[("WK", stg)])
                        for tq in range(4):
                            s = fm_mm(ws, wc, tq)
                            P.act(lambda e, s=s, stg=stg, tq=tq: e.activation(
                                out=WK[stg][:, 4 + tq * 512:4 + (tq + 1) * 512], in_=pp[s][:, :], func=AF.Copy),
                                r=[("pp", s)], w=[("WK", stg)])
                        P.dve(lambda e, stg=stg, fc=fc: e.tensor_copy(out=carry[:, fc, :], in_=WK[stg][:, 2048:2052]),
                              r=[("WK", stg)], w=["carry"])
                        nout = 2050 if last else 2048
                        P.dve(lambda e, stg=stg, acc=acc, fc=fc, nout=nout: e.tensor_scalar(
                            out=WK[acc][:, 0:nout], in0=WK[stg][:, 0:nout], scalar1=cw[:, fc, 0:1], scalar2=None,
                            op0=ALU.mult), r=[("WK", stg), "cw", ("WKz", stg)], w=[("WK", acc)])
                        for j in range(1, 5):
                            P.dve(lambda e, stg=stg, acc=acc, fc=fc, nout=nout, j=j: e.scalar_tensor_tensor(
                                out=WK[acc][:, 0:nout], in0=WK[stg][:, j:j + nout], scalar=cw[:, fc, j:j + 1],
                                in1=WK[acc][:, 0:nout], op0=ALU.mult, op1=ALU.add),
                                r=[("WK", stg), "cw"], w=[("WK", acc)])
                        P.act(lambda e, acc=acc, sg=sg, nout=nout: e.activation(
                            out=WK[sg][:, 0:nout], in_=WK[acc][:, 0:nout], func=AF.Sigmoid),
                            r=[("WK", acc)], w=[("WK", sg)])
                        P.dve(lambda e, acc=acc, sg=sg, ob=ob, nout=nout, ksc=ksc: e.scalar_tensor_tensor(
                            out=OB[ob][:, 0:nout], in0=WK[acc][:, 0:nout], scalar=ksc, in1=WK[sg][:, 0:nout],
                            op0=ALU.mult, op1=ALU.mult), r=[("WK", acc), ("WK", sg)], w=[("OB", ob)])
                        if stile == 0:
                            P.dma("sp", dst[row0:row0 + 128, 0:nout - 2], OB[ob][:, 2:nout], r=[("OB", ob)])
                        else:
                            P.dma("sp", dst[row0:row0 + 128, t0 - 2:t0 - 2 + nout], OB[ob][:, 0:nout], r=[("OB", ob)])

            fams = [(C_BK, T["dkT"], 1.0, t0)]
            if own:
                fams.append((C_BQ, T["dqT"], 0.125, t0))
            for (c0, dst, sc, tcol) in fams:
                for g4 in range(2):
                    ws = load_w(c0 + g4 * 512, 512)
                    for wc in range(4):
                        row0 = (g4 * 4 + wc) * 128
                        ob = next_ob()
                        for tq in range(4):
                            s = fm_mm(ws, wc, tq)
                            wk = next_wk()
                            pr = tq % 2
                            cs = slice(tq * 512, (tq + 1) * 512)
                            P.act(lambda e, s=s, wk=wk: e.activation(out=WK[wk][:, 0:512], in_=pp[s][:, :], func=AF.Copy),
                                  r=[("pp", s)], w=[("WK", wk)])
                            P.pe(lambda e, wk=wk, pr=pr: e.matmul(prot[pr][:, :], rswap[:, :], WK[wk][:, 0:512],
                                                                  start=True, stop=True),
                                 r=[("WK", wk), "rswap"], w=[("prot", pr)])
                            P.dve(lambda e, wk=wk, cs=cs, sc=sc: e.scalar_tensor_tensor(
                                out=WK[wk][:, 512:1024], in0=WK[wk][:, 0:512], scalar=sc, in1=cosT[:, cs],
                                op0=ALU.mult, op1=ALU.mult), r=[("WK", wk), "cosT"], w=[("WK", wk)])
                            P.dve(lambda e, wk=wk, cs=cs, sc=sc, pr=pr: e.scalar_tensor_tensor(
                                out=WK[wk][:, 1024:1536], in0=prot[pr][:, :], scalar=sc, in1=sinT[:, cs],
                                op0=ALU.mult, op1=ALU.mult), r=[("prot", pr), "sinT"], w=[("WK", wk)])
                            P.dve(lambda e, wk=wk, cs=cs, ob=ob: e.tensor_tensor(
                                out=OB[ob][:, cs], in0=WK[wk][:, 512:1024], in1=WK[wk][:, 1024:1536], op=ALU.add),
                                r=[("WK", wk)], w=[("OB", ob)])
                        P.dma("sp", dst[row0:row0 + 128, tcol:tcol + 2048], OB[ob][:, 0:2048], r=[("OB", ob)])

            if own:
                for gi, c0 in enumerate((C_GA, C_GB)):
                    for g4 in range(2):
                        ws = load_w(c0 + g4 * 512, 512)
                        for wc in range(4):
                            cc = g4 * 4 + wc
                            ob = next_ob()
                            for tq in range(4):
                                s = fm_mm(ws, wc, tq)
                                P.act(lambda e, s=s, ob=ob, tq=tq, gi=gi, cc=cc: e.activation(
                                    out=OB[ob][:, tq * 512:(tq + 1) * 512], in_=pp[s][:, :], func=AF.Sigmoid,
                                    bias=gbfm[:, gi, cc:cc + 1]), r=[("pp", s), "gbfm"], w=[("OB", ob)])
                            P.dma("sp", T["sgT"][gi, cc * 128:(cc + 1) * 128, t0:t0 + 2048], OB[ob][:, 0:2048],
                                  r=[("OB", ob)])
                for cg in range(2):
                    wso = load_w(C_AO + cg * 512, 512)
                    wsz = load_w(C_AZ + cg * 512, 512)
                    ob = next_ob()
                    for tt in range(16):
                        so = tm_mm(wso, 512, tt)
                        sz = tm_mm(wsz, 512, tt)
                        wk = next_wk()
                        P.act(lambda e, so=so, wk=wk: e.activation(out=WK[wk][:, 0:512], in_=pp[so][:, :], func=AF.Sigmoid),
                              r=[("pp", so)], w=[("WK", wk)])
                        P.act(lambda e, sz=sz, wk=wk: e.activation(out=WK[wk][:, 512:1024], in_=pp[sz][:, :], func=AF.Sigmoid),
                              r=[("pp", sz)], w=[("WK", wk)])
                        P.dve(lambda e, sz=sz, wk=wk: e.tensor_tensor(out=WK[wk][:, 512:1024], in0=pp[sz][:, :],
                                                                      in1=WK[wk][:, 512:1024], op=ALU.mult),
                              r=[("pp", sz), ("WK", wk)], w=[("WK", wk)])
                        P.dve(lambda e, wk=wk, ob=ob, tt=tt: e.tensor_tensor(
                            out=OB[ob][:, tt * 512:(tt + 1) * 512], in0=WK[wk][:, 0:512], in1=WK[wk][:, 512:1024],
                            op=ALU.mult), r=[("WK", wk)], w=[("OB", ob)])
                    P.dma("sp", T["gA"][t0:t0 + 2048, cg * 512:(cg + 1) * 512].rearrange("(tt p) c -> p tt c", p=128),
                          OB[ob][:, :].rearrange("p (tt c) -> p tt c", c=512), r=[("OB", ob)])
                for cg in range(2):
                    wsz = load_w(C_BZ + cg * 512, 512)
                    ob = next_ob()
                    for tt in range(16):
                        sz = tm_mm(wsz, 512, tt)
                        wk = next_wk()
                        P.act(lambda e, sz=sz, wk=wk: e.activation(out=WK[wk][:, 0:512], in_=pp[sz][:, :], func=AF.Sigmoid),
                              r=[("pp", sz)], w=[("WK", wk)])
                        P.dve(lambda e, sz=sz, wk=wk, ob=ob, tt=tt: e.tensor_tensor(
                            out=OB[ob][:, tt * 512:(tt + 1) * 512], in0=pp[sz][:, :], in1=WK[wk][:, 0:512],
                            op=ALU.mult), r=[("pp", sz), ("WK", wk)], w=[("OB", ob)])
                    P.dma("sp", T["gB"][t0:t0 + 2048, cg * 512:(cg + 1) * 512].rearrange("(tt p) c -> p tt c", p=128),
                          OB[ob][:, :].rearrange("p (tt c) -> p tt c", c=512), r=[("OB", ob)])

        P.dma("sp", T["gall_d"][:, :], gall[:, :], r=["gall"])
        P.flush()


def build_rope_tables(P, posi, invf, cosT, sinT, WK, next_wk):
    TWO_PI = 2.0 * math.pi
    C1 = 6.28125
    C2 = TWO_PI - C1
    a = next_wk()
    b = next_wk()
    c = next_wk()
    A, Bk, R = WK[a], WK[b], WK[c]
    N = 2048
    P.dve(lambda e: e.tensor_copy(out=A[:, 0:N], in_=posi[:, :]), r=["posi"], w=[("WK", a)])
    P.dve(lambda e: e.tensor_scalar(out=A[:, 0:N], in0=A[:, 0:N], scalar1=invf[:, 0:1], scalar2=None, op0=ALU.mult),
          r=[("WK", a), "invf"], w=[("WK", a)])
    ki = posi
    P.dve(lambda e: e.tensor_scalar(out=Bk[:, 0:N], in0=A[:, 0:N], scalar1=1.0 / TWO_PI, scalar2=None, op0=ALU.mult),
          r=[("WK", a)], w=[("WK", b)])
    P.dve(lambda e: e.tensor_copy(out=ki[:, :], in_=Bk[:, 0:N]), r=[("WK", b)], w=["posi"])
    P.dve(lambda e: e.tensor_copy(out=Bk[:, 0:N], in_=ki[:, :]), r=["posi"], w=[("WK", b)])
    P.dve(lambda e: e.scalar_tensor_tensor(out=R[:, 0:N], in0=Bk[:, 0:N], scalar=-C1, in1=A[:, 0:N],
                                           op0=ALU.mult, op1=ALU.add), r=[("WK", a), ("WK", b)], w=[("WK", c)])
    P.dve(lambda e: e.scalar_tensor_tensor(out=R[:, 0:N], in0=Bk[:, 0:N], scalar=-C2, in1=R[:, 0:N],
                                           op0=ALU.mult, op1=ALU.add), r=[("WK", b), ("WK", c)], w=[("WK", c)])

    def wrap(X, key):
        P.dve(lambda e: e.tensor_scalar(out=Bk[:, 0:N], in0=X[:, 0:N], scalar1=math.pi, scalar2=-TWO_PI,
                                        op0=ALU.is_gt, op1=ALU.mult), r=[key], w=[("WK", b)])
        P.dve(lambda e: e.tensor_tensor(out=X[:, 0:N], in0=X[:, 0:N], in1=Bk[:, 0:N], op=ALU.add),
              r=[key, ("WK", b)], w=[key])
        P.dve(lambda e: e.tensor_scalar(out=Bk[:, 0:N], in0=X[:, 0:N], scalar1=-math.pi, scalar2=TWO_PI,
                                        op0=ALU.is_lt, op1=ALU.mult), r=[key], w=[("WK", b)])
        P.dve(lambda e: e.tensor_tensor(out=X[:, 0:N], in0=X[:, 0:N], in1=Bk[:, 0:N], op=ALU.add),
              r=[key, ("WK", b)], w=[key])

    wrap(R, ("WK", c))
    P.act(lambda e: e.activation(out=sinT[:, :], in_=R[:, 0:N], func=AF.Sin), r=[("WK", c)], w=["sinT"])
    P.dve(lambda e: e.tensor_scalar(out=R[:, 0:N], in0=R[:, 0:N], scalar1=math.pi / 2, scalar2=None, op0=ALU.add),
          r=[("WK", c)], w=[("WK", c)])
    wrap(R, ("WK", c))
    P.act(lambda e: e.activation(out=cosT[:, :], in_=R[:, 0:N], func=AF.Sin), r=[("WK", c)], w=["cosT"])


def phase_M(nc, P, T):
    mqT, mkT, mv, gA, yaT = T["mqT"], T["mkT"], T["mv"], T["gA"], T["yaT"]
    with contextlib.ExitStack() as st:
        sb = lambda name, shape, dt=F32: st.enter_context(nc.sbuf_tensor("M_" + name, list(shape), dt))
        ps = lambda name, shape, dt=F32: st.enter_context(nc.psum_tensor("M_" + name, list(shape), dt))
        qT = sb("qT", [128, 2, OWN], BF16)
        kT = sb("kT", [128, 2, S], BF16)
        va = sb("va", [128, 64, 257], BF16)
        ktok = sb("ktok", [128, 64, 256], BF16)
        hacc = sb("hacc", [128, 32, 256])
        gat = [sb("gat%d" % i, [128, 256], BF16) for i in range(2)]
        gall = sb("gall", [128, 1024])
        mlng = sb("mlng", [128, D])
        tri = sb("tri", [128, 4, 128])
        identf = sb("identf", [128, 128])
        identb = sb("identb", [128, 128], BF16)
        onesf = sb("onesf", [128, 128])
        LFt = [sb("LFt%d" % d, [128, 256]) for d in range(2)]
        Bc = [sb("Bc%d" % d, [128, 256]) for d in range(2)]
        Aa = [sb("Aa%d" % d, [128, 256]) for d in range(2)]
        EB = [sb("EB%d" % d, [128, 256]) for d in range(2)]
        WST = [sb("WST%d" % d, [128, 256]) for d in range(2)]
        DEC = [sb("DEC%d" % d, [128, 256]) for d in range(2)]
        Cst = [sb("Cst%d" % d, [128, 2, 257]) for d in range(2)]
        Cb = [sb("Cb%d" % d, [128, 2, 257], BF16) for d in range(2)]
        dg = [sb("dg%d" % i, [128, 128]) for i in range(2)]
        Dm = [sb("Dm%d" % i, [128, 128]) for i in range(2)]
        Wm = [sb("Wm%d" % i, [128, 128], BF16) for i in range(2)]
        vw = [sb("vw%d" % i, [128, 257], BF16) for i in range(2)]
        tmpc = [sb("tmpc%d" % i, [128, 257]) for i in range(2)]
        tot = [sb("tot%d" % i, [128, 257]) for i in range(2)]
        sm = [sb("sm%d" % i, [128, 4]) for i in range(2)]
        hs = [sb("hs%d" % i, [128, 256]) for i in range(2)]
        sqv = [sb("sqv%d" % i, [128, 256]) for i in range(2)]
        yab = [sb("yab%d" % i, [128, 256], BF16) for i in range(2)]
        yas = [sb("yas%d" % i, [128, 2, 128], BF16) for i in range(2)]
        pD = ps("pD", [128, 512])
        pST = ps("pST", [128, 512])
        pI = ps("pI", [128, 512])
        pC = ps("pC", [128, 512])
        pU = [ps("pU%d" % i, [128, 512]) for i in range(2)]
        ptrk = ps("ptrk", [128, 1024], BF16)
        ptry = ps("ptry", [128, 1024], BF16)

        P.dma("sp", gall[:], T["gall_d"], w=["gall"])
        P.dma("sp", mlng[:], T["mlng_rep"], w=["mlng"])
        P.dma("sp", tri[:], T["c_tri"], w=["tri"])
        P.dma("sp", identf[:], T["c_ident"], w=["identf"])
        P.dve(lambda e: e.tensor_copy(out=identb[:], in_=identf[:]), r=["identf"], w=["identb"])
        P.dve(lambda e: e.memset(onesf[:], 1.0), w=["onesf"])
        P.dve(lambda e: e.memset(va[:, :, 256:257], 1.0), w=["va1"])

        g4 = gall[:, :].rearrange("p (c g h) -> p c g h", g=4, h=4)
        v3 = lambda t: t[:, :].rearrange("p (c h) -> p c h", h=4)
        for d in range(2):
            i_d = g4[:, :, 2 * d, :]
            f_d = g4[:, :, 2 * d + 1, :]
            P.act(lambda e, d=d, f_d=f_d: e.activation(out=v3(LFt[d]), in_=f_d, func=AF.Exp, scale=-1.0),
                  r=["gall"], w=[("LFt", d)])
            P.act(lambda e, d=d: e.activation(out=LFt[d][:, :], in_=LFt[d][:, :], func=AF.Ln, bias=1.0),
                  r=[("LFt", d)], w=[("LFt", d)])
            P.pe(lambda e, d=d: e.matmul(pI[:, 0:256], tri[:, d, :], LFt[d][:, :], start=True, stop=True),
                 r=["tri", ("LFt", d)], w=["pI"])
            P.pe(lambda e, d=d: e.matmul(pC[:, 0:256], onesf[:, :], LFt[d][:, :], start=True, stop=True),
                 r=["onesf", ("LFt", d)], w=["pC"])
            P.dve(lambda e, d=d: e.tensor_scalar(out=Bc[d][:, :], in0=pI[:, 0:256], scalar1=-1.0, scalar2=None, op0=ALU.mult),
                  r=["pI"], w=[("Bc", d)])
            P.dve(lambda e, d=d, i_d=i_d: e.tensor_tensor(out=v3(Aa[d]), in0=pI[:, 0:256].rearrange("p (c h) -> p c h", h=4),
                                                         in1=i_d, op=ALU.add),
                  r=["pI", "gall"], w=[("Aa", d)])
            P.act(lambda e, d=d: e.activation(out=EB[d][:, :], in_=Bc[d][:, :], func=AF.Exp),
                  r=[("Bc", d)], w=[("EB", d)])
            P.dve(lambda e, d=d: e.tensor_copy(out=DEC[d][:, :], in_=pC[:, 0:256]), r=["pC"], w=[("DEC", d)])
            P.dve(lambda e, d=d: e.tensor_tensor(out=WST[d][:, :], in0=Aa[d][:, :], in1=DEC[d][:, :], op=ALU.subtract),
                  r=[("Aa", d), ("DEC", d)], w=[("WST", d)])
            P.act(lambda e, d=d: e.activation(out=WST[d][:, :], in_=WST[d][:, :], func=AF.Exp),
                  r=[("WST", d)], w=[("WST", d)])
            P.act(lambda e, d=d: e.activation(out=DEC[d][:, :], in_=DEC[d][:, :], func=AF.Exp, scale=-1.0),
                  r=[("DEC", d), ("WST", d)], w=[("DEC", d)])

        cnt = {"o": 0, "u": 0, "g": 0, "y": 0}

        def output_step(h, d, c):
            i = cnt["o"] % 2
            cnt["o"] += 1
            col = c * 4 + h
            cs = slice(c * 128, (c + 1) * 128)
            P.dve(lambda e: e.tensor_scalar(out=dg[i][:], in0=identf[:], scalar1=Bc[d][:, col:col + 1], scalar2=None, op0=ALU.mult),
                  r=["identf", ("Bc", d)], w=[("dg", i)])
            P.pe(lambda e: e.matmul(pD[:, 0:128], onesf[:, :], dg[i][:, :], start=True, stop=False), r=["onesf", ("dg", i)], w=["pD"])
            P.pe(lambda e: e.matmul(pD[:, 0:128], identf[:, :], tri[:, 2 + d, :], start=False, stop=True), r=["identf", "tri"], w=["pD"])
            P.act(lambda e: e.activation(out=Dm[i][:], in_=pD[:, 0:128], func=AF.Exp, bias=Aa[d][:, col:col + 1]),
                  r=["pD", ("Aa", d)], w=[("Dm", i)])
            for dkc in range(2):
                P.pe(lambda e, dkc=dkc: e.matmul(pST[:, 0:128], kT[:, dkc, cs], qT[:, dkc, cs], start=(dkc == 0), stop=(dkc == 1)),
                     r=["kT", "qT"], w=["pST"])
            P.dve(lambda e: e.tensor_tensor(out=Wm[i][:], in0=pST[:, 0:128], in1=Dm[i][:], op=ALU.mult),
                  r=["pST", ("Dm", i)], w=[("Wm", i)])
            P.pe(lambda e: e.matmul(pI[:, 0:257], Wm[i][:, :], va[:, c, :], start=True, stop=True),
                 r=[("Wm", i), "va", "va1"], w=["pI"])
            for dkc in range(2):
                P.pe(lambda e, dkc=dkc: e.matmul(pC[:, 0:257], qT[:, dkc, cs], Cb[d][:, dkc, :], start=(dkc == 0), stop=(dkc == 1)),
                     r=["qT", ("Cb", d)], w=["pC"])
            P.dve(lambda e: e.tensor_scalar(out=tmpc[i][:], in0=pC[:, 0:257], scalar1=EB[d][:, col:col + 1], scalar2=None, op0=ALU.mult),
                  r=["pC", ("EB", d)], w=[("tmpc", i)])
            P.dve(lambda e: e.tensor_tensor(out=tot[i][:], in0=tmpc[i][:], in1=pI[:, 0:257], op=ALU.add),
                  r=[("tmpc", i), "pI"], w=[("tot", i)])
            P.dve(lambda e: e.tensor_scalar(out=sm[i][:, 3:4], in0=tot[i][:, 256:257], scalar1=-1.0, scalar2=1.0, op0=ALU.mult, op1=ALU.max),
                  r=[("tot", i)], w=[("sm", i)])
            P.dve(lambda e: e.tensor_tensor(out=sm[i][:, 0:1], in0=tot[i][:, 256:257], in1=sm[i][:, 3:4], op=ALU.max),
                  r=[("tot", i), ("sm", i)], w=[("sm", i)])
            P.dve(lambda e: e.reciprocal(out=sm[i][:, 1:2], in_=sm[i][:, 0:1]), r=[("sm", i)], w=[("sm", i)])
            if d == 0:
                P.dve(lambda e: e.tensor_scalar(out=hacc[:, c, :], in0=tot[i][:, 0:256], scalar1=sm[i][:, 1:2], scalar2=None, op0=ALU.mult),
                      r=[("tot", i), ("sm", i)], w=[("hacc", c)])
                return
            gi = cnt["g"] % 2
            cnt["g"] += 1
            P.dma("sp", gat[gi][:], gA[c * 128:(c + 1) * 128, h * 256:(h + 1) * 256], w=[("gat", gi)])
            P.dve(lambda e: e.scalar_tensor_tensor(out=hs[i][:], in0=tot[i][:, 0:256], scalar=sm[i][:, 1:2], in1=hacc[:, c, :],
                                                   op0=ALU.mult, op1=ALU.add),
                  r=[("tot", i), ("sm", i), ("hacc", c)], w=[("hs", i)])
            P.dve(lambda e: e.tensor_tensor(out=sqv[i][:], in0=hs[i][:], in1=hs[i][:], op=ALU.mult), r=[("hs", i)], w=[("sqv", i)])
            P.dve(lambda e: e.reduce_sum(out=sm[i][:, 2:3], in_=sqv[i][:], axis=AX.X), r=[("sqv", i)], w=[("smb", i)])
            P.act(lambda e: e.activation(out=sm[i][:, 2:3], in_=sm[i][:, 2:3], func=AF.Ln, scale=1.0 / 256.0, bias=NORM_EPS),
                  r=[("smb", i)], w=[("smb", i)])
            P.act(lambda e: e.activation(out=sm[i][:, 2:3], in_=sm[i][:, 2:3], func=AF.Exp, scale=-0.5),
                  r=[("smb", i)], w=[("smb", i)])
            P.dve(lambda e: e.scalar_tensor_tensor(out=sqv[i][:], in0=hs[i][:], scalar=sm[i][:, 2:3],
                                                   in1=mlng[:, h * 256:(h + 1) * 256], op0=ALU.mult, op1=ALU.mult),
                  r=[("hs", i), ("smb", i), "mlng"], w=[("sqv", i)])
            P.dve(lambda e: e.tensor_tensor(out=yab[i][:], in0=sqv[i][:], in1=gat[gi][:], op=ALU.mult),
                  r=[("sqv", i), ("gat", gi)], w=[("yab", i)])
            for k in range(2):
                P.pe(lambda e, k=k: e.transpose(out=ptry[:, k * 128:(k + 1) * 128], in_=yab[i][:, k * 128:(k + 1) * 128],
                                                identity=identb[:]), r=[("yab", i), "identb"], w=["ptry"])
            P.act(lambda e: e.activation(out=yas[i][:, :, :], in_=ptry[:, 0:256].rearrange("p (k t) -> p k t", k=2), func=AF.Copy),
                  r=["ptry"], w=[("yas", i)])
            P.dma("sp", yaT[h * 256:(h + 1) * 256, c * 128:(c + 1) * 128].rearrange("(k p) t -> p k t", p=128),
                  yas[i][:, :, :], r=[("yas", i)])

        def update_step(h, d, c):
            i = cnt["u"] % 2
            cnt["u"] += 1
            col = c * 4 + h
            P.pool(lambda e: e.tensor_scalar(out=vw[i][:], in0=va[:, c, :], scalar1=WST[d][:, col:col + 1], scalar2=None, op0=ALU.mult),
                   r=["va", "va1", ("WST", d)], w=[("vw", i)])
            for dkc in range(2):
                P.pe(lambda e, dkc=dkc: e.matmul(pU[dkc][:, 0:257], ktok[:, c, dkc * 128:(dkc + 1) * 128], vw[i][:, :], start=True, stop=True),
                     r=[("ktok", c), ("vw", i)], w=[("pU", dkc)])
                P.dve(lambda e, dkc=dkc: e.scalar_tensor_tensor(out=Cst[d][:, dkc, :], in0=Cst[d][:, dkc, :], scalar=DEC[d][:, col:col + 1],
                                                                in1=pU[dkc][:, 0:257], op0=ALU.mult, op1=ALU.add),
                      r=[("Cst", d, dkc), ("DEC", d), ("pU", dkc)], w=[("Cst", d, dkc)])
                P.act(lambda e, dkc=dkc: e.activation(out=Cb[d][:, dkc, :], in_=Cst[d][:, dkc, :], func=AF.Copy),
                      r=[("Cst", d, dkc)], w=[("Cb", d)])

        for h in range(4):
            for dkc in range(2):
                r0 = h * 256 + dkc * 128
                P.dma("sp", qT[:, dkc, :], mqT[r0:r0 + 128, 0:OWN], w=["qT"])
                P.dma("sp", kT[:, dkc, :], mkT[r0:r0 + 128, :], w=["kT"])
            P.dma("sp", va[:, :, 0:256], mv[:, h * 256:(h + 1) * 256].rearrange("(c p) f -> p c f", p=128), w=["va"])
            for d in range(2):
                P.dve(lambda e, d=d: e.memset(Cst[d][:], 0.0), w=[("Cst", d, 0), ("Cst", d, 1)])
                P.dve(lambda e, d=d: e.memset(Cb[d][:], 0.0), w=[("Cb", d)])
            for c in range(64):
                for dkc in range(2):
                    P.pe(lambda e, c=c, dkc=dkc: e.transpose(out=ptrk[:, dkc * 128:(dkc + 1) * 128],
                                                             in_=kT[:, dkc, c * 128:(c + 1) * 128], identity=identb[:]),
                         r=["kT", "identb"], w=["ptrk"])
                if c % 2 == 0:
                    P.act(lambda e, c=c: e.activation(out=ktok[:, c, :], in_=ptrk[:, 0:256], func=AF.Copy), r=["ptrk"], w=[("ktok", c)])
                else:
                    P.dve(lambda e, c=c: e.tensor_copy(out=ktok[:, c, :], in_=ptrk[:, 0:256]), r=["ptrk"], w=[("ktok", c)])
            for i in range(32):
                output_step(h, 0, i)
                if i < 31:
                    update_step(h, 0, i)
                update_step(h, 1, 63 - i)
            for i in range(32, 64):
                c = 63 - i
                output_step(h, 1, c)
                if c > 0:
                    update_step(h, 1, c)
        P.flush()


def phase_D(nc, P, T):
    dqT, dkT, dv, gB, ybT = T["dqT"], T["dkT"], T["dv"], T["gB"], T["ybT"]
    with contextlib.ExitStack() as st:
        sb = lambda name, shape, dt=F32: st.enter_context(nc.sbuf_tensor("D_" + name, list(shape), dt))
        ps = lambda name, shape, dt=F32: st.enter_context(nc.psum_tensor("D_" + name, list(shape), dt))
        kT = [sb("dkT%d" % i, [128, S], BF16) for i in range(2)]
        va = [sb("dva%d" % i, [128, 64, 129], BF16) for i in range(2)]
        qT = [sb("dqT%d" % i, [128, OWN], BF16) for i in range(2)]
        NE = 3
        E = [sb("dE%d" % i, [128, 1024], BF16) for i in range(NE)]
        gbt = [sb("dgb%d" % i, [128, 4, 128], BF16) for i in range(2)]
        lamr = sb("lamr", [128, 256])
        ltmp = sb("ltmp", [128, 128])
        lam = sb("lam", [128, 4])
        subg = sb("subg", [128, 128])
        identf = sb("identf", [128, 128])
        identb = sb("identb", [128, 128], BF16)
        r12 = [sb("r12_%d" % i, [128, 4]) for i in range(2)]
        o1 = [sb("o1_%d" % i, [128, 128]) for i in range(2)]
        o2 = [sb("o2_%d" % i, [128, 128]) for i in range(2)]
        sq = [sb("sq_%d" % i, [128, 128]) for i in range(2)]
        ybq = [sb("ybq%d" % i, [128, 128], BF16) for i in range(2)]
        ybs = [sb("ybs%d" % i, [128, 512], BF16) for i in range(2)]
        pS = [ps("pS%d" % i, [128, 1024]) for i in range(2)]
        pacc = [ps("pacc%d" % i, [128, 512]) for i in range(3)]
        ptr = ps("dptr", [128, 512], BF16)

        P.dma("sp", lamr[:], T["lam_rep"], w=["lamr"])
        P.dma("sp", subg[:], T["subln_rep"], w=["subg"])
        P.dma("sp", identf[:], T["c_ident"], w=["identf"])
        P.dve(lambda e: e.tensor_copy(out=identb[:], in_=identf[:]), r=["identf"], w=["identb"])
        P.dve(lambda e: e.tensor_scalar(out=subg[:], in0=subg[:], scalar1=(1.0 - LAMBDA_INIT), scalar2=None, op0=ALU.mult),
              r=["subg"], w=["subg"])
        P.dve(lambda e: e.tensor_tensor(out=ltmp[:, 0:64], in0=lamr[:, 0:64], in1=lamr[:, 64:128], op=ALU.mult),
              r=["lamr"], w=["ltmp"])
        P.dve(lambda e: e.tensor_tensor(out=ltmp[:, 64:128], in0=lamr[:, 128:192], in1=lamr[:, 192:256], op=ALU.mult),
              r=["lamr"], w=["ltmp"])
        P.dve(lambda e: e.reduce_sum(out=lam[:, 0:2], in_=ltmp[:, :].rearrange("p (a b) -> p a b", a=2), axis=AX.X),
              r=["ltmp"], w=["lam"])
        P.act(lambda e: e.activation(out=lam[:, 0:2], in_=lam[:, 0:2], func=AF.Exp), r=["lam"], w=["lam"])
        P.dve(lambda e: e.tensor_tensor(out=lam[:, 2:3], in0=lam[:, 0:1], in1=lam[:, 1:2], op=ALU.subtract),
              r=["lam"], w=["lam"])
        P.dve(lambda e: e.tensor_scalar(out=lam[:, 3:4], in0=lam[:, 2:3], scalar1=LAMBDA_INIT, scalar2=-1.0,
                                        op0=ALU.add, op1=ALU.mult), r=["lam"], w=["lam"])
        for i in range(2):
            P.dve(lambda e, i=i: e.memset(va[i][:, :, 128:129], 1.0), w=[("va1", i)])

        def acc_ap(a, lo, hi):
            return pacc[a // 3][:, (a % 3) * 129 + lo:(a % 3) * 129 + hi]

        ecount = [0]
        epi = [0]
        for h in range(8):
            hs = h % 2
            P.dma("sp", kT[hs][:], dkT[h * 128:(h + 1) * 128, :], w=[("kT", hs)])
            P.dma("sp", qT[hs][:], dqT[h * 128:(h + 1) * 128, :], w=[("qT", hs)])
            P.dma("sp", va[hs][:, :, 0:128], dv[:, h * 128:(h + 1) * 128].rearrange("(kb p) c -> p kb c", p=128),
                  w=[("va", hs)])
            for qt in range(8):
                gs = (h * 8 + qt) % 2
                P.dma("sp", gbt[gs][:], gB[qt * 512:(qt + 1) * 512, h * 128:(h + 1) * 128].rearrange("(u p) c -> p u c", p=128),
                      w=[("gbt", gs)])
                for kb in range(64):
                    sl = kb % 2
                    es = ecount[0] % NE
                    ecount[0] += 1
                    for j in range(2):
                        P.pe(lambda e, sl=sl, j=j, hs=hs, kb=kb, qt=qt: e.matmul(
                            pS[sl][:, j * 512:(j + 1) * 512], kT[hs][j * 64:(j + 1) * 64, kb * 128:(kb + 1) * 128],
                            qT[hs][j * 64:(j + 1) * 64, qt * 512:(qt + 1) * 512], start=True, stop=True),
                            r=[("kT", hs), ("qT", hs)], w=[("pS", sl)])
                    P.act(lambda e, sl=sl, es=es: e.activation(out=E[es][:, :], in_=pS[sl][:, :], func=AF.Exp),
                          r=[("pS", sl)], w=[("E", es)])
                    for a in range(8):
                        j, u = a // 4, a % 4
                        P.pe(lambda e, a=a, j=j, u=u, es=es, hs=hs, kb=kb: e.matmul(
                            acc_ap(a, 0, 129), E[es][:, j * 512 + u * 128:j * 512 + (u + 1) * 128], va[hs][:, kb, :],
                            start=(kb == 0 and a % 3 == 0), stop=(kb == 63), skip_group_check=True),
                            r=[("E", es), ("va", hs), ("va1", hs)], w=[("accb", a // 3)])
                ys = (h * 8 + qt) % 2
                for u in range(4):
                    ep = epi[0] % 2
                    epi[0] += 1
                    a0, a1 = u, 4 + u
                    P.dve(lambda e, ep=ep, a0=a0: e.reciprocal(out=r12[ep][:, 0:1], in_=acc_ap(a0, 128, 129)),
                          r=[("accb", a0 // 3)], w=[("r12", ep)])
                    P.dve(lambda e, ep=ep, a1=a1: e.reciprocal(out=r12[ep][:, 1:2], in_=acc_ap(a1, 128, 129)),
                          r=[("accb", a1 // 3)], w=[("r12", ep)])
                    P.dve(lambda e, ep=ep: e.tensor_tensor(out=r12[ep][:, 2:3], in0=r12[ep][:, 1:2], in1=lam[:, 3:4], op=ALU.mult),
                          r=[("r12", ep), "lam"], w=[("r12", ep)])
                    P.dve(lambda e, ep=ep, a0=a0: e.tensor_scalar(out=o1[ep][:], in0=acc_ap(a0, 0, 128), scalar1=r12[ep][:, 0:1],
                                                                  scalar2=None, op0=ALU.mult),
                          r=[("accb", a0 // 3), ("r12", ep)], w=[("o1", ep)])
                    P.dve(lambda e, ep=ep, a1=a1: e.scalar_tensor_tensor(out=o2[ep][:], in0=acc_ap(a1, 0, 128), scalar=r12[ep][:, 2:3],
                                                                         in1=o1[ep][:], op0=ALU.mult, op1=ALU.add),
                          r=[("accb", a1 // 3), ("r12", ep), ("o1", ep)], w=[("o2", ep)])
                    P.dve(lambda e, ep=ep: e.tensor_tensor(out=sq[ep][:], in0=o2[ep][:], in1=o2[ep][:], op=ALU.mult),
                          r=[("o2", ep)], w=[("sq", ep)])
                    P.dve(lambda e, ep=ep: e.reduce_sum(out=r12[ep][:, 3:4], in_=sq[ep][:], axis=AX.X),
                          r=[("sq", ep)], w=[("r12b", ep)])
                    P.act(lambda e, ep=ep: e.activation(out=r12[ep][:, 3:4], in_=r12[ep][:, 3:4], func=AF.Ln,
                                                        scale=1.0 / 128.0, bias=NORM_EPS),
                          r=[("r12b", ep)], w=[("r12b", ep)])
                    P.act(lambda e, ep=ep: e.activation(out=r12[ep][:, 3:4], in_=r12[ep][:, 3:4], func=AF.Exp, scale=-0.5),
                          r=[("r12b", ep)], w=[("r12b", ep)])
                    P.dve(lambda e, ep=ep: e.scalar_tensor_tensor(out=o1[ep][:], in0=o2[ep][:], scalar=r12[ep][:, 3:4],
                                                                  in1=subg[:], op0=ALU.mult, op1=ALU.mult),
                          r=[("o2", ep), ("r12b", ep), "subg"], w=[("o1", ep)])
                    P.dve(lambda e, ep=ep, gs=gs, u=u: e.tensor_tensor(out=ybq[ep][:], in0=o1[ep][:], in1=gbt[gs][:, u, :], op=ALU.mult),
                          r=[("o1", ep), ("gbt", gs)], w=[("ybq", ep)])
                    P.pe(lambda e, ep=ep, u=u: e.transpose(out=ptr[:, u * 128:(u + 1) * 128], in_=ybq[ep][:], identity=identb[:]),
                         r=[("ybq", ep), "identb"], w=["ptr"])
                    P.dve(lambda e, u=u, ys=ys: e.tensor_copy(out=ybs[ys][:, u * 128:(u + 1) * 128], in_=ptr[:, u * 128:(u + 1) * 128]),
                          r=["ptr"], w=[("ybs", ys)])
                P.dma("sp", ybT[h * 128:(h + 1) * 128, qt * 512:(qt + 1) * 512], ybs[ys][:], r=[("ybs", ys)])
        P.flush()


def phase_O(nc, P, T):
    x, out, yaT, ybT, sgT = T["x"], T["out"], T["yaT"], T["ybT"], T["sgT"]
    with contextlib.ExitStack() as st:
        sb = lambda name, shape, dt=F32: st.enter_context(nc.sbuf_tensor("O_" + name, list(shape), dt))
        ps = lambda name, shape, dt=F32: st.enter_context(nc.psum_tensor("O_" + name, list(shape), dt))
        Wa = sb("Wa", [128, 8, D], BF16)
        Wb = sb("Wb", [128, 8, D], BF16)
        Wo = sb("Wo", [128, 8, D], BF16)
        fing = sb("fing", [128, D])
        ya = [sb("ya%d" % i, [128, 8, 512], BF16) for i in range(2)]
        yb = [sb("yb%d" % i, [128, 8, 512], BF16) for i in range(2)]
        sa = [sb("sa%d" % i, [128, 8, 512], BF16) for i in range(2)]
        sbb = [sb("sb%d" % i, [128, 8, 512], BF16) for i in range(2)]
        mixT = [sb("mixT%d" % i, [128, 8, 512], BF16) for i in range(2)]
        t1 = [sb("t1_%d" % i, [128, 512]) for i in range(2)]
        t2 = [sb("t2_%d" % i, [128, 512]) for i in range(2)]
        xt = [sb("xt%d" % i, [128, D]) for i in range(2)]
        xo = [sb("xo%d" % i, [128, D]) for i in range(2)]
        junk = sb("junk", [128, D], BF16)
        ssq = [sb("ssq%d" % i, [128, 2]) for i in range(2)]
        pa = [ps("pa%d" % i, [128, 512]) for i in range(2)]
        pb = [ps("pb%d" % i, [128, 512]) for i in range(2)]
        po = [ps("po%d" % i, [128, 512]) for i in range(4)]

        for (W, src, key) in ((Wa, T["w_a"], "Wa"), (Wb, T["w_b"], "Wb"), (Wo, T["w_o"], "Wo")):
            for hh in range(2):
                P.dma("pool", W[:, :, hh * 512:(hh + 1) * 512],
                      src[:, hh * 512:(hh + 1) * 512].rearrange("(kc p) c -> p kc c", p=128), w=[(key, hh)])
        P.dma("sp", fing[:], T["fing_rep"], w=["fing"])
        wkeys = lambda k: [(k, 0), (k, 1)]
        cnt = {"p": 0, "o": 0, "x": 0}
        for tt in range(8):
            sl = tt % 2
            ts_ = slice(tt * 512, (tt + 1) * 512)
            P.dma("sp", ya[sl][:], yaT[:, ts_].rearrange("(cc p) t -> p cc t", p=128), w=[("ya", sl)])
            P.dma("sp", yb[sl][:], ybT[:, ts_].rearrange("(cc p) t -> p cc t", p=128), w=[("yb", sl)])
            P.dma("sp", sa[sl][:], sgT[0, :, ts_].rearrange("(cc p) t -> p cc t", p=128), w=[("sa", sl)])
            P.dma("sp", sbb[sl][:], sgT[1, :, ts_].rearrange("(cc p) t -> p cc t", p=128), w=[("sb", sl)])
            for dd in range(8):
                i = cnt["p"] % 2
                cnt["p"] += 1
                for cc in range(8):
                    P.pe(lambda e, i=i, cc=cc, dd=dd, sl=sl: e.matmul(pa[i][:, :], Wa[:, cc, dd * 128:(dd + 1) * 128], ya[sl][:, cc, :],
                                                                      start=(cc == 0), stop=(cc == 7)),
                         r=wkeys("Wa") + [("ya", sl)], w=[("pa", i)])
                for cc in range(8):
                    P.pe(lambda e, i=i, cc=cc, dd=dd, sl=sl: e.matmul(pb[i][:, :], Wb[:, cc, dd * 128:(dd + 1) * 128], yb[sl][:, cc, :],
                                                                      start=(cc == 0), stop=(cc == 7)),
                         r=wkeys("Wb") + [("yb", sl)], w=[("pb", i)])
                P.dve(lambda e, i=i, dd=dd, sl=sl: e.tensor_tensor(out=t1[i][:], in0=pa[i][:, :], in1=sa[sl][:, dd, :], op=ALU.mult),
                      r=[("pa", i), ("sa", sl)], w=[("t1", i)])
                P.dve(lambda e, i=i, dd=dd, sl=sl: e.tensor_tensor(out=t2[i][:], in0=pb[i][:, :], in1=sbb[sl][:, dd, :], op=ALU.mult),
                      r=[("pb", i), ("sb", sl)], w=[("t2", i)])
                P.pool(lambda e, i=i, dd=dd, sl=sl: e.tensor_tensor(out=mixT[sl][:, dd, :], in0=t1[i][:], in1=t2[i][:], op=ALU.add),
                       r=[("t1", i), ("t2", i)], w=[("mixT", sl)])
            for u in range(4):
                xs = cnt["x"] % 2
                cnt["x"] += 1
                r0 = tt * 512 + u * 128
                P.dma("sp", xt[xs][:], x[r0:r0 + 128, :], w=[("xt", xs)])
                for eg in range(2):
                    o = cnt["o"] % 4
                    cnt["o"] += 1
                    for dd in range(8):
                        P.pe(lambda e, o=o, dd=dd, sl=sl, u=u, eg=eg: e.matmul(
                            po[o][:, :], mixT[sl][:, dd, u * 128:(u + 1) * 128], Wo[:, dd, eg * 512:(eg + 1) * 512],
                            start=(dd == 0), stop=(dd == 7)), r=[("mixT", sl), ("Wo", eg)], w=[("po", o)])
                    P.dve(lambda e, o=o, xs=xs, eg=eg: e.tensor_tensor(out=xo[xs][:, eg * 512:(eg + 1) * 512], in0=po[o][:, :],
                                                                       in1=xt[xs][:, eg * 512:(eg + 1) * 512], op=ALU.add),
                          r=[("po", o), ("xt", xs)], w=[("xo", xs, eg)])
                P.act(lambda e, xs=xs: e.activation(out=junk[:], in_=xo[xs][:], func=AF.Square, accum_out=ssq[xs][:, 0:1]),
                      r=[("xo", xs, 0), ("xo", xs, 1)], w=[("ssq", xs)])
                P.act(lambda e, xs=xs: e.activation(out=ssq[xs][:, 0:1], in_=ssq[xs][:, 0:1], func=AF.Sqrt, scale=1.0 / D, bias=NORM_EPS),
                      r=[("ssq", xs)], w=[("ssq", xs)])
                P.dve(lambda e, xs=xs: e.reciprocal(out=ssq[xs][:, 1:2], in_=ssq[xs][:, 0:1]), r=[("ssq", xs)], w=[("ssq", xs)])
                P.dve(lambda e, xs=xs: e.scalar_tensor_tensor(out=xo[xs][:], in0=xo[xs][:], scalar=ssq[xs][:, 1:2], in1=fing[:],
                                                              op0=ALU.mult, op1=ALU.mult),
                      r=[("xo", xs, 0), ("xo", xs, 1), ("ssq", xs), "fing"], w=[("xo", xs, 0), ("xo", xs, 1)])
                P.dma("sp", out[r0:r0 + 128, :], xo[xs][:], r=[("xo", xs, 0), ("xo", xs, 1)])
        P.flush()


def make_in_maps(x, positions, norm_g, w_in, ml_gate_b, ml_conv_w, ml_norm_g, da_lambda,
                 da_subln_g, gate_b, w_branch_a, w_branch_b, w_out, final_g):
    f32 = np.float32
    x = np.asarray(x, f32)
    positions = np.asarray(positions, np.int32)
    w_in0 = np.ascontiguousarray(np.asarray(w_in, f32)[0])
    gb0 = np.asarray(ml_gate_b, f32)[0]
    cw0 = np.asarray(ml_conv_w, f32)[0]
    w_in1 = w_in0.copy()
    ag = w_in0[:, C_AG:C_AG + 16].reshape(D, 4, 4)
    w_in1[:, C_AG:C_AG + 16] = ag[:, [2, 3, 0, 1], :].reshape(D, 16)
    gb1 = gb0[[2, 3, 0, 1], :]
    cw1 = cw0[::-1, :]
    rep = lambda v, n=128: np.ascontiguousarray(np.broadcast_to(np.asarray(v, f32).reshape(1, -1), (n, np.asarray(v).size)))
    ident = np.eye(128, dtype=f32)
    rsw = np.zeros((128, 128), f32)
    for r in range(128):
        m = r % 64
        if m < 32:
            rsw[r + 32, r] = -1.0
        else:
            rsw[r - 32, r] = 1.0
    invf = (10000.0 ** (-np.arange(0, 64, 2, dtype=f32) / f32(64))).astype(f32)
    invf_p = np.array([invf[(p % 64) % 32] for p in range(128)], f32).reshape(128, 1)
    ii = np.arange(128)
    U = (ii[:, None] <= ii[None, :]).astype(f32)
    L = (ii[:, None] >= ii[None, :]).astype(f32)
    NEG = -30000.0
    tri = np.stack([U, L, (1 - U) * NEG, (1 - L) * NEG], axis=1).astype(f32)
    common = {
        "normg_rep": rep(np.asarray(norm_g, f32)[0]),
        "mlng_rep": rep(np.asarray(ml_norm_g, f32)[0].reshape(-1)),
        "lam_rep": rep(np.asarray(da_lambda, f32)[0].reshape(-1)),
        "subln_rep": rep(np.asarray(da_subln_g, f32)[0]),
        "gb_fm": np.ascontiguousarray(np.asarray(gate_b, f32)[0].reshape(2, 8, 128).transpose(2, 0, 1)),
        "w_a": np.ascontiguousarray(np.asarray(w_branch_a, f32)[0]),
        "w_b": np.ascontiguousarray(np.asarray(w_branch_b, f32)[0]),
        "w_o": np.ascontiguousarray(np.asarray(w_out, f32)[0]),
        "fing_rep": rep(np.asarray(final_g, f32)),
        "c_ident": ident, "c_rswap": rsw, "c_invf": invf_p, "c_tri": tri,
    }
    in_maps = []
    for core in range(NCORES):
        b, half = core // 2, core % 2
        xb = x[b]
        pb = positions[b]
        if half == 1:
            xb = xb[::-1]
            pb = pb[::-1]
        gbx = gb1 if half else gb0
        cwx = cw1 if half else cw0
        m = dict(common)
        m["x"] = np.ascontiguousarray(xb)
        m["posr"] = np.ascontiguousarray(np.broadcast_to(pb.reshape(1, S), (128, S))).astype(np.int32)
        m["w_in"] = w_in1 if half else w_in0
        m["gateb_rep"] = rep(gbx.reshape(-1))
        m["convw"] = np.ascontiguousarray(cwx.reshape(5, 16, 128).transpose(2, 1, 0))
        in_maps.append(m)
    return in_maps


_NC_CACHE = {}


def kernel(**inputs):
    in_maps = make_in_maps(**inputs)
    if "nc" not in _NC_CACHE:
        _NC_CACHE["nc"] = build_nc()
    nc = _NC_CACHE["nc"]
    res = run_bass_kernel_spmd(nc, in_maps, core_ids=list(range(NCORES)))
    B = 4
    outp = np.empty((B, S, D), np.float32)
    for core in range(NCORES):
        b, half = core // 2, core % 2
        o = np.asarray(res.results[core]["out"], np.float32)
        if half == 0:
            outp[b, :OWN] = o
        else:
            outp[b, OWN:] = o[::-1]
    return outp
```

```python
import contextlib
import math
import numpy as np
import concourse.bass as bass
import concourse.mybir as mybir
from concourse.bass_utils import run_bass_kernel_spmd

F32, BF16, I32 = mybir.dt.float32, mybir.dt.bfloat16, mybir.dt.int32
AF = mybir.ActivationFunctionType
ALU = mybir.AluOpType
AX = mybir.AxisListType

D = 1024
S = 8192
OWN = 4096
NCORES = 8
PROJ = 11280
C_AQ, C_AK, C_AV, C_AO, C_AZ, C_AG = 0, 1024, 2048, 3072, 4096, 5120
C_BQ, C_BK, C_BV, C_BZ, C_GA, C_GB = 5136, 6160, 7184, 8208, 9232, 10256
NORM_EPS = 1e-6
LAMBDA_INIT = 0.8 - 0.6 * math.exp(-0.3 * 0)
SAME_ENGINE_SYNC = True


class _Op:
    __slots__ = ("eng", "fn", "dma", "clock", "idx", "waits", "signal", "semval", "sem", "know")


class Prog:
    ENGS = ("sp", "act", "dve", "pool", "pe")

    def __init__(self, nc, stack, n_dma=8):
        self.nc = nc
        self.sem = {e: stack.enter_context(nc.semaphore("cs_" + e)) for e in self.ENGS}
        self.dsem = {q: [stack.enter_context(nc.semaphore("ds_%s_%d" % (q, i))) for i in range(n_dma)]
                     for q in ("sp", "pool", "act")}
        self.n_dma = n_dma
        self.dcount = {q: 0 for q in self.dsem}
        self.dlast = {}
        self.cnt = {e: 0 for e in self.ENGS}
        self.sigcnt = {e: 0 for e in self.ENGS}
        self.know = {e: {} for e in self.ENGS}
        self.pending = {e: [] for e in self.ENGS}
        self.last_w = {}
        self.readers = {}
        self.nops = 0

    def add(self, eng, fn, reads=(), writes=(), dma=False, extra_deps=()):
        op = _Op()
        op.eng, op.fn, op.dma, op.signal, op.semval = eng, fn, dma, False, None
        self.nops += 1
        deps = []
        seen = set()

        def push(d):
            if d is not None and id(d) not in seen:
                seen.add(id(d))
                deps.append(d)

        for k in reads:
            push(self.last_w.get(k))
        for k in writes:
            push(self.last_w.get(k))
            for r in self.readers.get(k, ()):
                push(r)
        for d in extra_deps:
            push(d)
        if dma:
            slot = self.dcount[eng] % self.n_dma
            self.dcount[eng] += 1
            op.clock = (eng, slot)
            prev = self.dlast.get(op.clock)
            op.idx = (prev.idx + 1) if prev is not None else 1
            op.sem = self.dsem[eng][slot]
            op.semval = 16 * op.idx
            push(prev)
            self.dlast[op.clock] = op
        else:
            self.cnt[eng] += 1
            op.clock = eng
            op.idx = self.cnt[eng]
            op.sem = self.sem[eng]
        know = self.know[eng]
        waits = []
        for d in deps:
            if (not d.dma) and (not dma) and d.eng == eng and (eng == "pe" or not SAME_ENGINE_SYNC):
                continue
            if know.get(d.clock, 0) >= d.idx:
                continue
            waits.append(d)
            d.signal = True
            for c, v in d.know.items():
                if know.get(c, 0) < v:
                    know[c] = v
            if know.get(d.clock, 0) < d.idx:
                know[d.clock] = d.idx
        op.waits = waits
        op.know = dict(know)
        for k in writes:
            self.last_w[k] = op
            self.readers[k] = []
        for k in reads:
            self.readers.setdefault(k, []).append(op)
        self.pending[eng].append(op)
        return op

    def pe(self, fn, r=(), w=()):
        return self.add("pe", fn, r, w)

    def act(self, fn, r=(), w=()):
        return self.add("act", fn, r, w)

    def dve(self, fn, r=(), w=()):
        return self.add("dve", fn, r, w)

    def pool(self, fn, r=(), w=()):
        return self.add("pool", fn, r, w)

    def dma(self, q, out, in_, r=(), w=()):
        return self.add(q, lambda e: e.dma_start(out=out, in_=in_), r, w, dma=True)

    def flush(self, final=False):
        outstanding = [d for d in self.dlast.values()]
        self.add("sp", None, extra_deps=outstanding)
        for e in self.ENGS:
            for op in self.pending[e]:
                if not op.dma and op.signal:
                    self.sigcnt[e] += 1
                    op.semval = self.sigcnt[e]
                elif not op.dma:
                    op.semval = None
        pend = self.pending
        sems = self.sem

        def make_body(eng):
            ops = pend[eng]

            def body(e):
                for op in ops:
                    for d in op.waits:
                        assert d.semval is not None
                        e.wait_ge(d.sem, d.semval)
                    if op.fn is None:
                        continue
                    ins = op.fn(e)
                    if op.dma:
                        ins.then_inc(op.sem, 16)
                    elif op.signal:
                        ins.then_inc(sems[eng], 1)
            return body

        with self.nc.Block() as block:
            block.sync(make_body("sp"))
            block.scalar(make_body("act"))
            block.vector(make_body("dve"))
            block.gpsimd(make_body("pool"))
            block.tensor(make_body("pe"))
        self.pending = {e: [] for e in self.ENGS}
        self.last_w = {}
        self.readers = {}
        full = {}
        for e in self.ENGS:
            full[e] = self.cnt[e]
        for c, d in self.dlast.items():
            full[c] = d.idx
        self.know = {e: dict(full) for e in self.ENGS}


def build_nc(debug=False, phases=("P", "M", "D", "O")):
    nc = bass.Bass("TRN2", target_bir_lowering=False)
    IN = lambda name, shape, dt=F32: nc.dram_tensor(name, list(shape), dt, kind="ExternalInput").ap()
    skind = "ExternalOutput" if debug else "Internal"
    SCR = lambda name, shape, dt=BF16: nc.dram_tensor(name, list(shape), dt, kind=skind).ap()

    x = IN("x", [S, D])
    posr = IN("posr", [128, S], I32)
    w_in = IN("w_in", [D, PROJ])
    normg_rep = IN("normg_rep", [128, D])
    gateb_rep = IN("gateb_rep", [128, 16])
    convw = IN("convw", [128, 16, 5])
    mlng_rep = IN("mlng_rep", [128, D])
    lam_rep = IN("lam_rep", [128, 256])
    subln_rep = IN("subln_rep", [128, 128])
    gb_fm = IN("gb_fm", [128, 2, 8])
    w_a = IN("w_a", [D, D])
    w_b = IN("w_b", [D, D])
    w_o = IN("w_o", [D, D])
    fing_rep = IN("fing_rep", [128, D])
    c_ident = IN("c_ident", [128, 128])
    c_rswap = IN("c_rswap", [128, 128])
    c_invf = IN("c_invf", [128, 1])
    c_tri = IN("c_tri", [128, 4, 128])
    out = nc.dram_tensor("out", [OWN, D], F32, kind="ExternalOutput").ap()

    mqT = SCR("mqT", [D, S])
    mkT = SCR("mkT", [D, S])
    mv = SCR("mv", [S, D])
    dqT = SCR("dqT", [D, OWN])
    dkT = SCR("dkT", [D, S])
    dv = SCR("dv", [S, D])
    gA = SCR("gA", [OWN, D])
    gB = SCR("gB", [OWN, D])
    sgT = SCR("sgT", [2, D, OWN])
    gall_d = SCR("gall_d", [128, 64 * 16], F32)
    yaT = SCR("yaT", [D, OWN])
    ybT = SCR("ybT", [D, OWN])

    with contextlib.ExitStack() as top:
        P = Prog(nc, top)
        if "P" in phases:
            phase_P(nc, P, locals())
        if "M" in phases:
            phase_M(nc, P, locals())
        if "D" in phases:
            phase_D(nc, P, locals())
        if "O" in phases:
            phase_O(nc, P, locals())
    return nc


def phase_P(nc, P, T):
    x, posr, w_in = T["x"], T["posr"], T["w_in"]
    with contextlib.ExitStack() as st:
        sb = lambda name, shape, dt=F32: st.enter_context(nc.sbuf_tensor("P_" + name, list(shape), dt))
        ps = lambda name, shape, dt=F32: st.enter_context(nc.psum_tensor("P_" + name, list(shape), dt))
        hT = sb("hT", [128, 8, 2048], BF16)
        xt = [sb("xt%d" % i, [128, D]) for i in range(2)]
        xn = [sb("xn%d" % i, [128, D], BF16) for i in range(2)]
        junk = sb("junk", [128, D], BF16)
        ss = [sb("ss%d" % i, [128, 1]) for i in range(2)]
        rstd = [sb("rstd%d" % i, [128, 1]) for i in range(2)]
        grep = sb("grep", [128, D])
        NW = 3
        wB = [sb("wB%d" % i, [128, 8, 512], BF16) for i in range(NW)]
        wG = sb("wG", [128, 8, 16], BF16)
        NWK = 5
        WK = [sb("WK%d" % i, [128, 2054]) for i in range(NWK)]
        OB = [sb("OB%d" % i, [128, 8192], BF16) for i in range(2)]
        cosT = sb("cosT", [128, 2048])
        sinT = sb("sinT", [128, 2048])
        gall = sb("gall", [128, 64 * 16])
        gbrep = sb("gbrep", [128, 16])
        cw = sb("cw", [128, 16, 5])
        carry = sb("carry", [128, 16, 4])
        identf = sb("identf", [128, 128])
        identb = sb("identb", [128, 128], BF16)
        rswap = sb("rswap", [128, 128])
        invf = sb("invf", [128, 1])
        gbfm = sb("gbfm", [128, 2, 8])
        posi = sb("posi", [128, 2048], I32)
        ptr = [ps("ptr%d" % i, [128, D], BF16) for i in range(2)]
        pp = [ps("pp%d" % i, [128, 512]) for i in range(4)]
        prot = [ps("prot%d" % i, [128, 512]) for i in range(2)]

        P.dma("sp", grep[:], T["normg_rep"], w=["grep"])
        P.dma("sp", gbrep[:], T["gateb_rep"], w=["gbrep"])
        P.dma("sp", cw[:], T["convw"], w=["cw"])
        P.dma("sp", identf[:], T["c_ident"], w=["identf"])
        P.dma("sp", rswap[:], T["c_rswap"], w=["rswap"])
        P.dma("sp", invf[:], T["c_invf"], w=["invf"])
        P.dma("sp", gbfm[:], T["gb_fm"], w=["gbfm"])
        P.dma("pool", wG[:], w_in[:, C_AG:C_AG + 16].rearrange("(kc p) c -> p kc c", p=128), w=["wG"])
        P.dve(lambda e: e.tensor_copy(out=identb[:], in_=identf[:]), r=["identf"], w=["identb"])
        P.dve(lambda e: e.memset(carry[:], 0.0), w=["carry"])
        for i in range(NWK):
            P.dve(lambda e, i=i: e.memset(WK[i][:, 2052:2054], 0.0), w=[("WKz", i)])

        wslot = [0]

        def load_w(col0, ncols):
            s = wslot[0] % NW
            wslot[0] += 1
            P.dma("pool", wB[s][:, :, 0:ncols],
                  w_in[:, col0:col0 + ncols].rearrange("(kc p) c -> p kc c", p=128), w=[("wB", s)])
            return s

        ppslot = [0]

        def next_pp():
            s = ppslot[0] % 4
            ppslot[0] += 1
            return s

        wkslot = [0]

        def next_wk():
            s = wkslot[0] % NWK
            wkslot[0] += 1
            return s

        obslot = [0]

        def next_ob():
            s = obslot[0] % 2
            obslot[0] += 1
            return s

        def fm_mm(ws, wc, tq):
            s = next_pp()
            for kc in range(8):
                P.pe(lambda e, s=s, ws=ws, wc=wc, kc=kc, tq=tq: e.matmul(
                    pp[s][:, :], wB[ws][:, kc, wc * 128:(wc + 1) * 128], hT[:, kc, tq * 512:(tq + 1) * 512],
                    start=(kc == 0), stop=(kc == 7)),
                    r=[("wB", ws), "hT"], w=[("pp", s)])
            return s

        def tm_mm(ws, ncols, tt):
            s = next_pp()
            for kc in range(8):
                P.pe(lambda e, s=s, ws=ws, kc=kc, tt=tt, ncols=ncols: e.matmul(
                    pp[s][:, 0:ncols], hT[:, kc, tt * 128:(tt + 1) * 128], wB[ws][:, kc, 0:ncols],
                    start=(kc == 0), stop=(kc == 7)),
                    r=[("wB", ws), "hT"], w=[("pp", s)])
            return s

        for stile in range(4):
            own = stile < 2
            t0 = stile * 2048
            last = stile == 3
            for tt in range(16):
                sl = tt % 2
                r0 = t0 + tt * 128
                P.dma("sp", xt[sl][:], x[r0:r0 + 128, :], w=[("xt", sl)])
                P.act(lambda e, sl=sl: e.activation(out=junk[:], in_=xt[sl][:], func=AF.Square, accum_out=ss[sl][:]),
                      r=[("xt", sl)], w=[("ss", sl)])
                P.act(lambda e, sl=sl: e.activation(out=ss[sl][:], in_=ss[sl][:], func=AF.Sqrt,
                                                    scale=1.0 / D, bias=NORM_EPS),
                      r=[("ss", sl)], w=[("ss", sl)])
                P.dve(lambda e, sl=sl: e.reciprocal(out=rstd[sl][:], in_=ss[sl][:]), r=[("ss", sl)], w=[("rstd", sl)])
                P.dve(lambda e, sl=sl: e.scalar_tensor_tensor(out=xn[sl][:], in0=xt[sl][:], scalar=rstd[sl][:],
                                                              in1=grep[:], op0=ALU.mult, op1=ALU.mult),
                      r=[("xt", sl), ("rstd", sl), "grep"], w=[("xn", sl)])
                for kc in range(8):
                    P.pe(lambda e, sl=sl, kc=kc: e.transpose(out=ptr[sl][:, kc * 128:(kc + 1) * 128],
                                                             in_=xn[sl][:, kc * 128:(kc + 1) * 128],
                                                             identity=identb[:]),
                         r=[("xn", sl), "identb"], w=[("ptr", sl)])
                P.act(lambda e, sl=sl, tt=tt: e.activation(
                    out=hT[:, :, tt * 128:(tt + 1) * 128],
                    in_=ptr[sl][:, :].rearrange("p (k t) -> p k t", k=8), func=AF.Copy),
                    r=[("ptr", sl)], w=["hT"])

            P.dma("sp", posi[:], posr[:, t0:t0 + 2048], w=["posi"])
            build_rope_tables(P, posi, invf, cosT, sinT, WK, next_wk)

            for tt in range(16):
                s = next_pp()
                for kc in range(8):
                    P.pe(lambda e, s=s, kc=kc, tt=tt: e.matmul(
                        pp[s][:, 0:16], hT[:, kc, tt * 128:(tt + 1) * 128], wG[:, kc, :],
                        start=(kc == 0), stop=(kc == 7)), r=["wG", "hT"], w=[("pp", s)])
                gt = stile * 16 + tt
                P.dve(lambda e, s=s, gt=gt: e.tensor_tensor(out=gall[:, gt * 16:(gt + 1) * 16], in0=pp[s][:, 0:16],
                                                            in1=gbrep[:], op=ALU.add),
                      r=[("pp", s), "gbrep"], w=["gall"])

            for (c0, dst) in ((C_AV, T["mv"]), (C_BV, T["dv"])):
                for cg in range(2):
                    ws = load_w(c0 + cg * 512, 512)
                    ob = next_ob()
                    for tt in range(16):
                        s = tm_mm(ws, 512, tt)
                        eng = P.act if tt % 2 == 0 else P.dve
                        if tt % 2 == 0:
                            P.act(lambda e, s=s, ob=ob, tt=tt: e.activation(
                                out=OB[ob][:, tt * 512:(tt + 1) * 512], in_=pp[s][:, :], func=AF.Copy),
                                r=[("pp", s)], w=[("OB", ob)])
                        else:
                            P.dve(lambda e, s=s, ob=ob, tt=tt: e.tensor_copy(
                                out=OB[ob][:, tt * 512:(tt + 1) * 512], in_=pp[s][:, :]),
                                r=[("pp", s)], w=[("OB", ob)])
                    P.dma("sp", dst[t0:t0 + 2048, cg * 512:(cg + 1) * 512].rearrange("(tt p) c -> p tt c", p=128),
                          OB[ob][:, :].rearrange("p (tt c) -> p tt c", c=512), r=[("OB", ob)])

            for fam, (c0, dst, ksc) in enumerate(((C_AQ, T["mqT"], 1.0), (C_AK, T["mkT"], 1.0 / 16.0))):
                for g4 in range(2):
                    ws = load_w(c0 + g4 * 512, 512)
                    for wc in range(4):
                        fc = fam * 8 + g4 * 4 + wc
                        row0 = (g4 * 4 + wc) * 128
                        stg = next_wk()
                        acc = next_wk()
                        sg = next_wk()
                        ob = next_ob()
                        P.dve(lambda e, stg=stg, fc=fc: e.tensor_copy(out=WK[stg][:, 0:4], in_=carry[:, fc, :]),
                              r=["carry"], w=[("WK", stg)])
                        for tq in range(4):
                            s = fm_mm(ws, wc, tq)
                            P.act(lambda e, s=s, stg=stg, tq=tq: e.activation(
                                out=WK[stg][:, 4 + tq * 512:4 + (tq + 1) * 512], in_=pp[s][:, :], func=AF.Copy),
                                r=[("pp", s)], w=[("WK", stg)])
                        P.dve(lambda e, stg=stg, fc=fc: e.tensor_copy(out=carry[:, fc, :], in_=WK[stg][:, 2048:2052]),
                              r=[("WK", stg)], w=["carry"])
                        nout = 2050 if last else 2048
                        P.dve(lambda e, stg=stg, acc=acc, fc=fc, nout=nout: e.tensor_scalar(
                            out=WK[acc][:, 0:nout], in0=WK[stg][:, 0:nout], scalar1=cw[:, fc, 0:1], scalar2=None,
                            op0=ALU.mult), r=[("WK", stg), "cw", ("WKz", stg)], w=[("WK", acc)])
                        for j in range(1, 5):
                            P.dve(lambda e, stg=stg, acc=acc, fc=fc, nout=nout, j=j: e.scalar_tensor_tensor(
                                out=WK[acc][:, 0:nout], in0=WK[stg][:, j:j + nout], scalar=cw[:, fc, j:j + 1],
                                in1=WK[acc][:, 0:nout], op0=ALU.mult, op1=ALU.add),
                                r=[("WK", stg), "cw"], w=[("WK", acc)])
                        P.act(lambda e, acc=acc, sg=sg, nout=nout: e.activation(
                            out=WK[sg][:, 0:nout], in_=WK[acc][:, 0:nout], func=AF.Sigmoid),
                            r=[("WK", acc)], w=[("WK", sg)])
                        P.dve(lambda e, acc=acc, sg=sg, ob=ob, nout=nout, ksc=ksc: e.scalar_tensor_tensor(
                            out=OB[ob][:, 0:nout], in0=WK[acc][:, 0:nout], scalar=ksc, in1=WK[sg][:, 0:nout],
                            op0=ALU.mult, op1=ALU.mult), r=[("WK", acc), ("WK", sg)], w=[("OB", ob)])
                        if stile == 0:
                            P.dma("sp", dst[row0:row0 + 128, 0:nout - 2], OB[ob][:, 2:nout], r=[("OB", ob)])
                        else:
                            P.dma("sp", dst[row0:row0 + 128, t0 - 2:t0 - 2 + nout], OB[ob][:, 0:nout], r=[("OB", ob)])

            fams = [(C_BK, T["dkT"], 1.0, t0)]
            if own:
                fams.append((C_BQ, T["dqT"], 0.125, t0))
            for (c0, dst, sc, tcol) in fams:
                for g4 in range(2):
                    ws = load_w(c0 + g4 * 512, 512)
                    for wc in range(4):
                        row0 = (g4 * 4 + wc) * 128
                        ob = next_ob()
                        for tq in range(4):
                            s = fm_mm(ws, wc, tq)
                            wk = next_wk()
                            pr = tq % 2
                            cs = slice(tq * 512, (tq + 1) * 512)
                            P.act(lambda e, s=s, wk=wk: e.activation(out=WK[wk][:, 0:512], in_=pp[s][:, :], func=AF.Copy),
                                  r=[("pp", s)], w=[("WK", wk)])
                            P.pe(lambda e, wk=wk, pr=pr: e.matmul(prot[pr][:, :], rswap[:, :], WK[wk][:, 0:512],
                                                                  start=True, stop=True),
                                 r=[("WK", wk), "rswap"], w=[("prot", pr)])
                            P.dve(lambda e, wk=wk, cs=cs, sc=sc: e.scalar_tensor_tensor(
                                out=WK[wk][:, 512:1024], in0=WK[wk][:, 0:512], scalar=sc, in1=cosT[:, cs],
                                op0=ALU.mult, op1=ALU.mult), r=[("WK", wk), "cosT"], w=[("WK", wk)])
                            P.dve(lambda e, wk=wk, cs=cs, sc=sc, pr=pr: e.scalar_tensor_tensor(
                                out=WK[wk][:, 1024:1536], in0=prot[pr][:, :], scalar=sc, in1=sinT[:, cs],
                                op0=ALU.mult, op1=ALU.mult), r=[("prot", pr), "sinT"], w=[("WK", wk)])
                            P.dve(lambda e, wk=wk, cs=cs, ob=ob: e.tensor_tensor(
                                out=OB[ob][:, cs], in0=WK[wk][:, 512:1024], in1=WK[wk][:, 1024:1536], op=ALU.add),
                                r=[("WK", wk)], w=[("OB", ob)])
                        P.dma("sp", dst[row0:row0 + 128, tcol:tcol + 2048], OB[ob][:, 0:2048], r=[("OB", ob)])

            if own:
                for gi, c0 in enumerate((C_GA, C_GB)):
                    for g4 in range(2):
                        ws = load_w(c0 + g4 * 512, 512)
                        for wc in range(4):
                            cc = g4 * 4 + wc
                            ob = next_ob()
                            for tq in range(4):
                                s = fm_mm(ws, wc, tq)
                                P.act(lambda e, s=s, ob=ob, tq=tq, gi=gi, cc=cc: e.activation(
                                    out=OB[ob][:, tq * 512:(tq + 1) * 512], in_=pp[s][:, :], func=AF.Sigmoid,
                                    bias=gbfm[:, gi, cc:cc + 1]), r=[("pp", s), "gbfm"], w=[("OB", ob)])
                            P.dma("sp", T["sgT"][gi, cc * 128:(cc + 1) * 128, t0:t0 + 2048], OB[ob][:, 0:2048],
                                  r=[("OB", ob)])
                for cg in range(2):
                    wso = load_w(C_AO + cg * 512, 512)
                    wsz = load_w(C_AZ + cg * 512, 512)
                    ob = next_ob()
                    for tt in range(16):
                        so = tm_mm(wso, 512, tt)
                        sz = tm_mm(wsz, 512, tt)
                        wk = next_wk()
                        P.act(lambda e, so=so, wk=wk: e.activation(out=WK[wk][:, 0:512], in_=pp[so][:, :], func=AF.Sigmoid),
                              r=[("pp", so)], w=[("WK", wk)])
                        P.act(lambda e, sz=sz, wk=wk: e.activation(out=WK[wk][:, 512:1024], in_=pp[sz][:, :], func=AF.Sigmoid),
                              r=[("pp", sz)], w=[("WK", wk)])
                        P.dve(lambda e, sz=sz, wk=wk: e.tensor_tensor(out=WK[wk][:, 512:1024], in0=pp[sz][:, :],
                                                                      in1=WK[wk][:, 512:1024], op=ALU.mult),
                              r=[("pp", sz), ("WK", wk)], w=[("WK", wk)])
                        P.dve(lambda e, wk=wk, ob=ob, tt=tt: e.tensor_tensor(
                            out=OB[ob][:, tt * 512:(tt + 1) * 512], in0=WK[wk][:, 0:512], in1=WK[wk][:, 512:1024],
                            op=ALU.mult), r=[("WK", wk)], w=[("OB", ob)])
                    P.dma("sp", T["gA"][t0:t0 + 2048, cg * 512:(cg + 1) * 512].rearrange("(tt p) c -> p tt c", p=128),
                          OB[ob][:, :].rearrange("p (tt c) -> p tt c", c=512), r=[("OB", ob)])
                for cg in range(2):
                    wsz = load_w(C_BZ + cg * 512, 512)
                    ob = next_ob()
                    for tt in range(16):
                        sz = tm_mm(wsz, 512, tt)
                        wk = next_wk()
                        P.act(lambda e, sz=sz, wk=wk: e.activation(out=WK[wk][:, 0:512], in_=pp[sz][:, :], func=AF.Sigmoid),
                              r=[("pp", sz)], w=[("WK", wk)])
                        P.dve(lambda e, sz=sz, wk=wk, ob=ob, tt=tt: e.tensor_tensor(
                            out=OB[ob][:, tt * 512:(tt + 1) * 512], in0=pp[sz][:, :], in1=WK[wk][:, 0:512],
                            op=ALU.mult), r=[("pp", sz), ("WK", wk)], w=[("OB", ob)])
                    P.dma("sp", T["gB"][t0:t0 + 2048, cg * 512:(cg + 1) * 512].rearrange("(tt p) c -> p tt c", p=128),
                          OB[ob][:, :].rearrange("p (tt c) -> p tt c", c=512), r=[("OB", ob)])

        P.dma("sp", T["gall_d"][:, :], gall[:, :], r=["gall"])
        P.flush()


def build_rope_tables(P, posi, invf, cosT, sinT, WK, next_wk):
    TWO_PI = 2.0 * math.pi
    C1 = 6.28125
    C2 = TWO_PI - C1
    a = next_wk()
    b = next_wk()
    c = next_wk()
    A, Bk, R = WK[a], WK[b], WK[c]
    N = 2048
    P.dve(lambda e: e.tensor_copy(out=A[:, 0:N], in_=posi[:, :]), r=["posi"], w=[("WK", a)])
    P.dve(lambda e: e.tensor_scalar(out=A[:, 0:N], in0=A[:, 0:N], scalar1=invf[:, 0:1], scalar2=None, op0=ALU.mult),
          r=[("WK", a), "invf"], w=[("WK", a)])
    ki = posi
    P.dve(lambda e: e.tensor_scalar(out=Bk[:, 0:N], in0=A[:, 0:N], scalar1=1.0 / TWO_PI, scalar2=None, op0=ALU.mult),
          r=[("WK", a)], w=[("WK", b)])
    P.dve(lambda e: e.tensor_copy(out=ki[:, :], in_=Bk[:, 0:N]), r=[("WK", b)], w=["posi"])
    P.dve(lambda e: e.tensor_copy(out=Bk[:, 0:N], in_=ki[:, :]), r=["posi"], w=[("WK", b)])
    P.dve(lambda e: e.scalar_tensor_tensor(out=R[:, 0:N], in0=Bk[:, 0:N], scalar=-C1, in1=A[:, 0:N],
                                           op0=ALU.mult, op1=ALU.add), r=[("WK", a), ("WK", b)], w=[("WK", c)])
    P.dve(lambda e: e.scalar_tensor_tensor(out=R[:, 0:N], in0=Bk[:, 0:N], scalar=-C2, in1=R[:, 0:N],
                                           op0=ALU.mult, op1=ALU.add), r=[("WK", b), ("WK", c)], w=[("WK", c)])

    def wrap(X, key):
        P.dve(lambda e: e.tensor_scalar(out=Bk[:, 0:N], in0=X[:, 0:N], scalar1=math.pi, scalar2=-TWO_PI,
                                        op0=ALU.is_gt, op1=ALU.mult), r=[key], w=[("WK", b)])
        P.dve(lambda e: e.tensor_tensor(out=X[:, 0:N], in0=X[:, 0:N], in1=Bk[:, 0:N], op=ALU.add),
              r=[key, ("WK", b)], w=[key])
        P.dve(lambda e: e.tensor_scalar(out=Bk[:, 0:N], in0=X[:, 0:N], scalar1=-math.pi, scalar2=TWO_PI,
                                        op0=ALU.is_lt, op1=ALU.mult), r=[key], w=[("WK", b)])
        P.dve(lambda e: e.tensor_tensor(out=X[:, 0:N], in0=X[:, 0:N], in1=Bk[:, 0:N], op=ALU.add),
              r=[key, ("WK", b)], w=[key])

    wrap(R, ("WK", c))
    P.act(lambda e: e.activation(out=sinT[:, :], in_=R[:, 0:N], func=AF.Sin), r=[("WK", c)], w=["sinT"])
    P.dve(lambda e: e.tensor_scalar(out=R[:, 0:N], in0=R[:, 0:N], scalar1=math.pi / 2, scalar2=None, op0=ALU.add),
          r=[("WK", c)], w=[("WK", c)])
    wrap(R, ("WK", c))
    P.act(lambda e: e.activation(out=cosT[:, :], in_=R[:, 0:N], func=AF.Sin), r=[("WK", c)], w=["cosT"])


def phase_M(nc, P, T):
    mqT, mkT, mv, gA, yaT = T["mqT"], T["mkT"], T["mv"], T["gA"], T["yaT"]
    with contextlib.ExitStack() as st:
        sb = lambda name, shape, dt=F32: st.enter_context(nc.sbuf_tensor("M_" + name, list(shape), dt))
        ps = lambda name, shape, dt=F32: st.enter_context(nc.psum_tensor("M_" + name, list(shape), dt))
        qT = sb("qT", [128, 2, OWN], BF16)
        kT = sb("kT", [128, 2, S], BF16)
        va = sb("va", [128, 64, 257], BF16)
        ktok = sb("ktok", [128, 64, 256], BF16)
        hacc = sb("hacc", [128, 32, 256])
        gat = [sb("gat%d" % i, [128, 256], BF16) for i in range(2)]
        gall = sb("gall", [128, 1024])
        mlng = sb("mlng", [128, D])
        tri = sb("tri", [128, 4, 128])
        identf = sb("identf", [128, 128])
        identb = sb("identb", [128, 128], BF16)
        onesf = sb("onesf", [128, 128])
        LFt = [sb("LFt%d" % d, [128, 256]) for d in range(2)]
        Bc = [sb("Bc%d" % d, [128, 256]) for d in range(2)]
        Aa = [sb("Aa%d" % d, [128, 256]) for d in range(2)]
        EB = [sb("EB%d" % d, [128, 256]) for d in range(2)]
        WST = [sb("WST%d" % d, [128, 256]) for d in range(2)]
        DEC = [sb("DEC%d" % d, [128, 256]) for d in range(2)]
        Cst = [sb("Cst%d" % d, [128, 2, 257]) for d in range(2)]
        Cb = [sb("Cb%d" % d, [128, 2, 257], BF16) for d in range(2)]
        dg = [sb("dg%d" % i, [128, 128]) for i in range(2)]
        Dm = [sb("Dm%d" % i, [128, 128]) for i in range(2)]
        Wm = [sb("Wm%d" % i, [128, 128], BF16) for i in range(2)]
        vw = [sb("vw%d" % i, [128, 257], BF16) for i in range(2)]
        tmpc = [sb("tmpc%d" % i, [128, 257]) for i in range(2)]
        tot = [sb("tot%d" % i, [128, 257]) for i in range(2)]
        sm = [sb("sm%d" % i, [128, 4]) for i in range(2)]
        hs = [sb("hs%d" % i, [128, 256]) for i in range(2)]
        sqv = [sb("sqv%d" % i, [128, 256]) for i in range(2)]
        yab = [sb("yab%d" % i, [128, 256], BF16) for i in range(2)]
        yas = [sb("yas%d" % i, [128, 2, 128], BF16) for i in range(2)]
        pD = ps("pD", [128, 512])
        pST = ps("pST", [128, 512])
        pI = ps("pI", [128, 512])
        pC = ps("pC", [128, 512])
        pU = [ps("pU%d" % i, [128, 512]) for i in range(2)]
        ptrk = ps("ptrk", [128, 1024], BF16)
        ptry = ps("ptry", [128, 1024], BF16)

        P.dma("sp", gall[:], T["gall_d"], w=["gall"])
        P.dma("sp", mlng[:], T["mlng_rep"], w=["mlng"])
        P.dma("sp", tri[:], T["c_tri"], w=["tri"])
        P.dma("sp", identf[:], T["c_ident"], w=["identf"])
        P.dve(lambda e: e.tensor_copy(out=identb[:], in_=identf[:]), r=["identf"], w=["identb"])
        P.dve(lambda e: e.memset(onesf[:], 1.0), w=["onesf"])
        P.dve(lambda e: e.memset(va[:, :, 256:257], 1.0), w=["va1"])

        g4 = gall[:, :].rearrange("p (c g h) -> p c g h", g=4, h=4)
        v3 = lambda t: t[:, :].rearrange("p (c h) -> p c h", h=4)
        for d in range(2):
            i_d = g4[:, :, 2 * d, :]
            f_d = g4[:, :, 2 * d + 1, :]
            P.act(lambda e, d=d, f_d=f_d: e.activation(out=v3(LFt[d]), in_=f_d, func=AF.Exp, scale=-1.0),
                  r=["gall"], w=[("LFt", d)])
            P.act(lambda e, d=d: e.activation(out=LFt[d][:, :], in_=LFt[d][:, :], func=AF.Ln, bias=1.0),
                  r=[("LFt", d)], w=[("LFt", d)])
            P.pe(lambda e, d=d: e.matmul(pI[:, 0:256], tri[:, d, :], LFt[d][:, :], start=True, stop=True),
                 r=["tri", ("LFt", d)], w=["pI"])
            P.pe(lambda e, d=d: e.matmul(pC[:, 0:256], onesf[:, :], LFt[d][:, :], start=True, stop=True),
                 r=["onesf", ("LFt", d)], w=["pC"])
            P.dve(lambda e, d=d: e.tensor_scalar(out=Bc[d][:, :], in0=pI[:, 0:256], scalar1=-1.0, scalar2=None, op0=ALU.mult),
                  r=["pI"], w=[("Bc", d)])
            P.dve(lambda e, d=d, i_d=i_d: e.tensor_tensor(out=v3(Aa[d]), in0=pI[:, 0:256].rearrange("p (c h) -> p c h", h=4),
                                                         in1=i_d, op=ALU.add),
                  r=["pI", "gall"], w=[("Aa", d)])
            P.act(lambda e, d=d: e.activation(out=EB[d][:, :], in_=Bc[d][:, :], func=AF.Exp),
                  r=[("Bc", d)], w=[("EB", d)])
            P.dve(lambda e, d=d: e.tensor_copy(out=DEC[d][:, :], in_=pC[:, 0:256]), r=["pC"], w=[("DEC", d)])
            P.dve(lambda e, d=d: e.tensor_tensor(out=WST[d][:, :], in0=Aa[d][:, :], in1=DEC[d][:, :], op=ALU.subtract),
                  r=[("Aa", d), ("DEC", d)], w=[("WST", d)])
            P.act(lambda e, d=d: e.activation(out=WST[d][:, :], in_=WST[d][:, :], func=AF.Exp),
                  r=[("WST", d)], w=[("WST", d)])
            P.act(lambda e, d=d: e.activation(out=DEC[d][:, :], in_=DEC[d][:, :], func=AF.Exp, scale=-1.0),
                  r=[("DEC", d), ("WST", d)], w=[("DEC", d)])

        cnt = {"o": 0, "u": 0, "g": 0, "y": 0}

        def output_step(h, d, c):
            i = cnt["o"] % 2
            cnt["o"] += 1
            col = c * 4 + h
            cs = slice(c * 128, (c + 1) * 128)
            P.dve(lambda e: e.tensor_scalar(out=dg[i][:], in0=identf[:], scalar1=Bc[d][:, col:col + 1], scalar2=None, op0=ALU.mult),
                  r=["identf", ("Bc", d)], w=[("dg", i)])
            P.pe(lambda e: e.matmul(pD[:, 0:128], onesf[:, :], dg[i][:, :], start=True, stop=False), r=["onesf", ("dg", i)], w=["pD"])
            P.pe(lambda e: e.matmul(pD[:, 0:128], identf[:, :], tri[:, 2 + d, :], start=False, stop=True), r=["identf", "tri"], w=["pD"])
            P.act(lambda e: e.activation(out=Dm[i][:], in_=pD[:, 0:128], func=AF.Exp, bias=Aa[d][:, col:col + 1]),
                  r=["pD", ("Aa", d)], w=[("Dm", i)])
            for dkc in range(2):
                P.pe(lambda e, dkc=dkc: e.matmul(pST[:, 0:128], kT[:, dkc, cs], qT[:, dkc, cs], start=(dkc == 0), stop=(dkc == 1)),
                     r=["kT", "qT"], w=["pST"])
            P.dve(lambda e: e.tensor_tensor(out=Wm[i][:], in0=pST[:, 0:128], in1=Dm[i][:], op=ALU.mult),
                  r=["pST", ("Dm", i)], w=[("Wm", i)])
            P.pe(lambda e: e.matmul(pI[:, 0:257], Wm[i][:, :], va[:, c, :], start=True, stop=True),
                 r=[("Wm", i), "va", "va1"], w=["pI"])
            for dkc in range(2):
                P.pe(lambda e, dkc=dkc: e.matmul(pC[:, 0:257], qT[:, dkc, cs], Cb[d][:, dkc, :], start=(dkc == 0), stop=(dkc == 1)),
                     r=["qT", ("Cb", d)], w=["pC"])
            P.dve(lambda e: e.tensor_scalar(out=tmpc[i][:], in0=pC[:, 0:257], scalar1=EB[d][:, col:col + 1], scalar2=None, op0=ALU.mult),
                  r=["pC", ("EB", d)], w=[("tmpc", i)])
            P.dve(lambda e: e.tensor_tensor(out=tot[i][:], in0=tmpc[i][:], in1=pI[:, 0:257], op=ALU.add),
                  r=[("tmpc", i), "pI"], w=[("tot", i)])
            P.dve(lambda e: e.tensor_scalar(out=sm[i][:, 3:4], in0=tot[i][:, 256:257], scalar1=-1.0, scalar2=1.0, op0=ALU.mult, op1=ALU.max),
                  r=[("tot", i)], w=[("sm", i)])
            P.dve(lambda e: e.tensor_tensor(out=sm[i][:, 0:1], in0=tot[i][:, 256:257], in1=sm[i][:, 3:4], op=ALU.max),
                  r=[("tot", i), ("sm", i)], w=[("sm", i)])
            P.dve(lambda e: e.reciprocal(out=sm[i][:, 1:2], in_=sm[i][:, 0:1]), r=[("sm", i)], w=[("sm", i)])
            if d == 0:
                P.dve(lambda e: e.tensor_scalar(out=hacc[:, c, :], in0=tot[i][:, 0:256], scalar1=sm[i][:, 1:2], scalar2=None, op0=ALU.mult),
                      r=[("tot", i), ("sm", i)], w=[("hacc", c)])
                return
            gi = cnt["g"] % 2
            cnt["g"] += 1
            P.dma("sp", gat[gi][:], gA[c * 128:(c + 1) * 128, h * 256:(h + 1) * 256], w=[("gat", gi)])
            P.dve(lambda e: e.scalar_tensor_tensor(out=hs[i][:], in0=tot[i][:, 0:256], scalar=sm[i][:, 1:2], in1=hacc[:, c, :],
                                                   op0=ALU.mult, op1=ALU.add),
                  r=[("tot", i), ("sm", i), ("hacc", c)], w=[("hs", i)])
            P.dve(lambda e: e.tensor_tensor(out=sqv[i][:], in0=hs[i][:], in1=hs[i][:], op=ALU.mult), r=[("hs", i)], w=[("sqv", i)])
            P.dve(lambda e: e.reduce_sum(out=sm[i][:, 2:3], in_=sqv[i][:], axis=AX.X), r=[("sqv", i)], w=[("smb", i)])
            P.act(lambda e: e.activation(out=sm[i][:, 2:3], in_=sm[i][:, 2:3], func=AF.Ln, scale=1.0 / 256.0, bias=NORM_EPS),
                  r=[("smb", i)], w=[("smb", i)])
            P.act(lambda e: e.activation(out=sm[i][:, 2:3], in_=sm[i][:, 2:3], func=AF.Exp, scale=-0.5),
                  r=[("smb", i)], w=[("smb", i)])
            P.dve(lambda e: e.scalar_tensor_tensor(out=sqv[i][:], in0=hs[i][:], scalar=sm[i][:, 2:3],
                                                   in1=mlng[:, h * 256:(h + 1) * 256], op0=ALU.mult, op1=ALU.mult),
                  r=[("hs", i), ("smb", i), "mlng"], w=[("sqv", i)])
            P.dve(lambda e: e.tensor_tensor(out=yab[i][:], in0=sqv[i][:], in1=gat[gi][:], op=ALU.mult),
                  r=[("sqv", i), ("gat", gi)], w=[("yab", i)])
            for k in range(2):
                P.pe(lambda e, k=k: e.transpose(out=ptry[:, k * 128:(k + 1) * 128], in_=yab[i][:, k * 128:(k + 1) * 128],
                                                identity=identb[:]), r=[("yab", i), "identb"], w=["ptry"])
            P.act(lambda e: e.activation(out=yas[i][:, :, :], in_=ptry[:, 0:256].rearrange("p (k t) -> p k t", k=2), func=AF.Copy),
                  r=["ptry"], w=[("yas", i)])
            P.dma("sp", yaT[h * 256:(h + 1) * 256, c * 128:(c + 1) * 128].rearrange("(k p) t -> p k t", p=128),
                  yas[i][:, :, :], r=[("yas", i)])

        def update_step(h, d, c):
            i = cnt["u"] % 2
            cnt["u"] += 1
            col = c * 4 + h
            P.pool(lambda e: e.tensor_scalar(out=vw[i][:], in0=va[:, c, :], scalar1=WST[d][:, col:col + 1], scalar2=None, op0=ALU.mult),
                   r=["va", "va1", ("WST", d)], w=[("vw", i)])
            for dkc in range(2):
                P.pe(lambda e, dkc=dkc: e.matmul(pU[dkc][:, 0:257], ktok[:, c, dkc * 128:(dkc + 1) * 128], vw[i][:, :], start=True, stop=True),
                     r=[("ktok", c), ("vw", i)], w=[("pU", dkc)])
                P.dve(lambda e, dkc=dkc: e.scalar_tensor_tensor(out=Cst[d][:, dkc, :], in0=Cst[d][:, dkc, :], scalar=DEC[d][:, col:col + 1],
                                                                in1=pU[dkc][:, 0:257], op0=ALU.mult, op1=ALU.add),
                      r=[("Cst", d, dkc), ("DEC", d), ("pU", dkc)], w=[("Cst", d, dkc)])
                P.act(lambda e, dkc=dkc: e.activation(out=Cb[d][:, dkc, :], in_=Cst[d][:, dkc, :], func=AF.Copy),
                      r=[("Cst", d, dkc)], w=[("Cb", d)])

        for h in range(4):
            for dkc in range(2):
                r0 = h * 256 + dkc * 128
                P.dma("sp", qT[:, dkc, :], mqT[r0:r0 + 128, 0:OWN], w=["qT"])
                P.dma("sp", kT[:, dkc, :], mkT[r0:r0 + 128, :], w=["kT"])
            P.dma("sp", va[:, :, 0:256], mv[:, h * 256:(h + 1) * 256].rearrange("(c p) f -> p c f", p=128), w=["va"])
            for d in range(2):
                P.dve(lambda e, d=d: e.memset(Cst[d][:], 0.0), w=[("Cst", d, 0), ("Cst", d, 1)])
                P.dve(lambda e, d=d: e.memset(Cb[d][:], 0.0), w=[("Cb", d)])
            for c in range(64):
                for dkc in range(2):
                    P.pe(lambda e, c=c, dkc=dkc: e.transpose(out=ptrk[:, dkc * 128:(dkc + 1) * 128],
                                                             in_=kT[:, dkc, c * 128:(c + 1) * 128], identity=identb[:]),
                         r=["kT", "identb"], w=["ptrk"])
                if c % 2 == 0:
                    P.act(lambda e, c=c: e.activation(out=ktok[:, c, :], in_=ptrk[:, 0:256], func=AF.Copy), r=["ptrk"], w=[("ktok", c)])
                else:
                    P.dve(lambda e, c=c: e.tensor_copy(out=ktok[:, c, :], in_=ptrk[:, 0:256]), r=["ptrk"], w=[("ktok", c)])
            for i in range(32):
                output_step(h, 0, i)
                if i < 31:
                    update_step(h, 0, i)
                update_step(h, 1, 63 - i)
            for i in range(32, 64):
                c = 63 - i
                output_step(h, 1, c)
                if c > 0:
                    update_step(h, 1, c)
        P.flush()


def phase_D(nc, P, T):
    dqT, dkT, dv, gB, ybT = T["dqT"], T["dkT"], T["dv"], T["gB"], T["ybT"]
    with contextlib.ExitStack() as st:
        sb = lambda name, shape, dt=F32: st.enter_context(nc.sbuf_tensor("D_" + name, list(shape), dt))
        ps = lambda name, shape, dt=F32: st.enter_context(nc.psum_tensor("D_" + name, list(shape), dt))
        kT = [sb("dkT%d" % i, [128, S], BF16) for i in range(2)]
        va = [sb("dva%d" % i, [128, 64, 129], BF16) for i in range(2)]
        qT = [sb("dqT%d" % i, [128, OWN], BF16) for i in range(2)]
        NE = 3
        E = [sb("dE%d" % i, [128, 1024], BF16) for i in range(NE)]
        gbt = [sb("dgb%d" % i, [128, 4, 128], BF16) for i in range(2)]
        lamr = sb("lamr", [128, 256])
        ltmp = sb("ltmp", [128, 128])
        lam = sb("lam", [128, 4])
        subg = sb("subg", [128, 128])
        identf = sb("identf", [128, 128])
        identb = sb("identb", [128, 128], BF16)
        r12 = [sb("r12_%d" % i, [128, 4]) for i in range(2)]
        o1 = [sb("o1_%d" % i, [128, 128]) for i in range(2)]
        o2 = [sb("o2_%d" % i, [128, 128]) for i in range(2)]
        sq = [sb("sq_%d" % i, [128, 128]) for i in range(2)]
        ybq = [sb("ybq%d" % i, [128, 128], BF16) for i in range(2)]
        ybs = [sb("ybs%d" % i, [128, 512], BF16) for i in range(2)]
        pS = [ps("pS%d" % i, [128, 1024]) for i in range(2)]
        pacc = [ps("pacc%d" % i, [128, 512]) for i in range(3)]
        ptr = ps("dptr", [128, 512], BF16)

        P.dma("sp", lamr[:], T["lam_rep"], w=["lamr"])
        P.dma("sp", subg[:], T["subln_rep"], w=["subg"])
        P.dma("sp", identf[:], T["c_ident"], w=["identf"])
        P.dve(lambda e: e.tensor_copy(out=identb[:], in_=identf[:]), r=["identf"], w=["identb"])
        P.dve(lambda e: e.tensor_scalar(out=subg[:], in0=subg[:], scalar1=(1.0 - LAMBDA_INIT), scalar2=None, op0=ALU.mult),
              r=["subg"], w=["subg"])
        P.dve(lambda e: e.tensor_tensor(out=ltmp[:, 0:64], in0=lamr[:, 0:64], in1=lamr[:, 64:128], op=ALU.mult),
              r=["lamr"], w=["ltmp"])
        P.dve(lambda e: e.tensor_tensor(out=ltmp[:, 64:128], in0=lamr[:, 128:192], in1=lamr[:, 192:256], op=ALU.mult),
              r=["lamr"], w=["ltmp"])
        P.dve(lambda e: e.reduce_sum(out=lam[:, 0:2], in_=ltmp[:, :].rearrange("p (a b) -> p a b", a=2), axis=AX.X),
              r=["ltmp"], w=["lam"])
        P.act(lambda e: e.activation(out=lam[:, 0:2], in_=lam[:, 0:2], func=AF.Exp), r=["lam"], w=["lam"])
        P.dve(lambda e: e.tensor_tensor(out=lam[:, 2:3], in0=lam[:, 0:1], in1=lam[:, 1:2], op=ALU.subtract),
              r=["lam"], w=["lam"])
        P.dve(lambda e: e.tensor_scalar(out=lam[:, 3:4], in0=lam[:, 2:3], scalar1=LAMBDA_INIT, scalar2=-1.0,
                                        op0=ALU.add, op1=ALU.mult), r=["lam"], w=["lam"])
        for i in range(2):
            P.dve(lambda e, i=i: e.memset(va[i][:, :, 128:129], 1.0), w=[("va1", i)])

        def acc_ap(a, lo, hi):
            return pacc[a // 3][:, (a % 3) * 129 + lo:(a % 3) * 129 + hi]

        accS = [sb("accS%d" % i, [128, 8 * 129]) for i in range(2)]
        mhalf = sb("mhalf", [128, 1])
        P.dve(lambda e: e.memset(mhalf[:], -0.5), w=["mhalf"])

        def load_head(h):
            hs = h % 2
            P.dma("sp", kT[hs][:], dkT[h * 128:(h + 1) * 128, :], w=[("kT", hs)])
            P.dma("sp", qT[hs][:], dqT[h * 128:(h + 1) * 128, :], w=[("qT", hs)])
            P.dma("sp", va[hs][:, :, 0:128], dv[:, h * 128:(h + 1) * 128].rearrange("(kb p) c -> p kb c", p=128),
                  w=[("va", hs)])

        def load_gb(h, qt):
            gs = (h * 8 + qt) % 2
            P.dma("sp", gbt[gs][:], gB[qt * 512:(qt + 1) * 512, h * 128:(h + 1) * 128].rearrange("(u p) c -> p u c", p=128),
                  w=[("gbt", gs)])

        def qk(i, h, qt, kb):
            sl, hs = i % 2, h % 2
            for j in range(2):
                P.pe(lambda e, j=j: e.matmul(
                    pS[sl][:, j * 512:(j + 1) * 512], kT[hs][j * 64:(j + 1) * 64, kb * 128:(kb + 1) * 128],
                    qT[hs][j * 64:(j + 1) * 64, qt * 512:(qt + 1) * 512], start=True, stop=True),
                    r=[("kT", hs), ("qT", hs)], w=[("pS", sl)])

        def ex(i):
            sl, es = i % 2, i % NE
            P.act(lambda e: e.activation(out=E[es][:, :], in_=pS[sl][:, :], func=AF.Exp), r=[("pS", sl)], w=[("E", es)])

        def pv(i, h, kb):
            es, hs = i % NE, h % 2
            for a in range(8):
                j, u = a // 4, a % 4
                P.pe(lambda e, a=a, j=j, u=u: e.matmul(
                    acc_ap(a, 0, 129), E[es][:, j * 512 + u * 128:j * 512 + (u + 1) * 128], va[hs][:, kb, :],
                    start=(kb == 0 and a % 3 == 0), stop=(kb == 63), skip_group_check=True),
                    r=[("E", es), ("va", hs), ("va1", hs)], w=[("accb", a // 3)])

        epi = [0]

        def epilogue(h, qt):
            gs = (h * 8 + qt) % 2
            ys = gs
            ai = gs
            A = accS[ai]
            for b in range(3):
                n = 387 if b < 2 else 258
                P.dve(lambda e, b=b, n=n: e.tensor_copy(out=A[:, b * 387:b * 387 + n], in_=pacc[b][:, 0:n]),
                      r=[("accb", b)], w=[("accS", ai)])
            sa = lambda a, lo, hi: A[:, a * 129 + lo:a * 129 + hi]
            for u in range(4):
                ep = epi[0] % 2
                epi[0] += 1
                a0, a1 = u, 4 + u
                P.dve(lambda e, ep=ep, a0=a0: e.reciprocal(out=r12[ep][:, 0:1], in_=sa(a0, 128, 129)),
                      r=[("accS", ai)], w=[("r12", ep)])
                P.dve(lambda e, ep=ep, a1=a1: e.reciprocal(out=r12[ep][:, 1:2], in_=sa(a1, 128, 129)),
                      r=[("accS", ai)], w=[("r12", ep)])
                P.dve(lambda e, ep=ep: e.tensor_tensor(out=r12[ep][:, 2:3], in0=r12[ep][:, 1:2], in1=lam[:, 3:4], op=ALU.mult),
                      r=[("r12", ep), "lam"], w=[("r12", ep)])
                P.dve(lambda e, ep=ep, a0=a0: e.tensor_scalar(out=o1[ep][:], in0=sa(a0, 0, 128), scalar1=r12[ep][:, 0:1],
                                                              scalar2=None, op0=ALU.mult),
                      r=[("accS", ai), ("r12", ep)], w=[("o1", ep)])
                P.dve(lambda e, ep=ep, a1=a1: e.scalar_tensor_tensor(out=o2[ep][:], in0=sa(a1, 0, 128), scalar=r12[ep][:, 2:3],
                                                                     in1=o1[ep][:], op0=ALU.mult, op1=ALU.add),
                      r=[("accS", ai), ("r12", ep), ("o1", ep)], w=[("o2", ep)])
                P.dve(lambda e, ep=ep: e.tensor_tensor(out=sq[ep][:], in0=o2[ep][:], in1=o2[ep][:], op=ALU.mult),
                      r=[("o2", ep)], w=[("sq", ep)])
                P.dve(lambda e, ep=ep: e.reduce_sum(out=r12[ep][:, 3:4], in_=sq[ep][:], axis=AX.X),
                      r=[("sq", ep)], w=[("r12b", ep)])
                P.dve(lambda e, ep=ep: e.tensor_scalar(out=r12[ep][:, 3:4], in0=r12[ep][:, 3:4], scalar1=1.0 / 128.0, scalar2=NORM_EPS,
                                                       op0=ALU.mult, op1=ALU.add), r=[("r12b", ep)], w=[("r12b", ep)])
                P.pool(lambda e, ep=ep: e.tensor_tensor(out=r12[ep][:, 3:4], in0=r12[ep][:, 3:4], in1=mhalf[:, 0:1], op=ALU.pow),
                       r=[("r12b", ep), "mhalf"], w=[("r12b", ep)])
                P.dve(lambda e, ep=ep: e.scalar_tensor_tensor(out=o1[ep][:], in0=o2[ep][:], scalar=r12[ep][:, 3:4],
                                                              in1=subg[:], op0=ALU.mult, op1=ALU.mult),
                      r=[("o2", ep), ("r12b", ep), "subg"], w=[("o1", ep)])
                P.dve(lambda e, ep=ep, u=u: e.tensor_tensor(out=ybq[ep][:], in0=o1[ep][:], in1=gbt[gs][:, u, :], op=ALU.mult),
                      r=[("o1", ep), ("gbt", gs)], w=[("ybq", ep)])
                P.pe(lambda e, ep=ep, u=u: e.transpose(out=ptr[:, u * 128:(u + 1) * 128], in_=ybq[ep][:], identity=identb[:]),
                     r=[("ybq", ep), "identb"], w=["ptr"])
                P.dve(lambda e, u=u: e.tensor_copy(out=ybs[ys][:, u * 128:(u + 1) * 128], in_=ptr[:, u * 128:(u + 1) * 128]),
                      r=["ptr"], w=[("ybs", ys)])
            P.dma("sp", ybT[h * 128:(h + 1) * 128, qt * 512:(qt + 1) * 512], ybs[ys][:], r=[("ybs", ys)])

        blocks = [(h, qt, kb) for h in range(8) for qt in range(8) for kb in range(64)]
        nb = len(blocks)
        load_head(0)
        load_gb(0, 0)
        qk(0, *blocks[0])
        qk(1, *blocks[1])
        for i, (h, qt, kb) in enumerate(blocks):
            if kb == 0 and qt == 0 and h + 1 < 8:
                load_head(h + 1)
            if kb == 0:
                nxt = h * 8 + qt + 1
                if nxt < 64:
                    load_gb(nxt // 8, nxt % 8)
            ex(i)
            pv(i, h, kb)
            if i + 2 < nb:
                qk(i + 2, *blocks[i + 2])
            if kb == 63:
                epilogue(h, qt)
        P.flush()


def phase_O(nc, P, T):
    x, out, yaT, ybT, sgT = T["x"], T["out"], T["yaT"], T["ybT"], T["sgT"]
    with contextlib.ExitStack() as st:
        sb = lambda name, shape, dt=F32: st.enter_context(nc.sbuf_tensor("O_" + name, list(shape), dt))
        ps = lambda name, shape, dt=F32: st.enter_context(nc.psum_tensor("O_" + name, list(shape), dt))
        Wa = sb("Wa", [128, 8, D], BF16)
        Wb = sb("Wb", [128, 8, D], BF16)
        Wo = sb("Wo", [128, 8, D], BF16)
        fing = sb("fing", [128, D])
        ya = [sb("ya%d" % i, [128, 8, 512], BF16) for i in range(2)]
        yb = [sb("yb%d" % i, [128, 8, 512], BF16) for i in range(2)]
        sa = [sb("sa%d" % i, [128, 8, 512], BF16) for i in range(2)]
        sbb = [sb("sb%d" % i, [128, 8, 512], BF16) for i in range(2)]
        mixT = [sb("mixT%d" % i, [128, 8, 512], BF16) for i in range(2)]
        t1 = [sb("t1_%d" % i, [128, 512]) for i in range(2)]
        t2 = [sb("t2_%d" % i, [128, 512]) for i in range(2)]
        xt = [sb("xt%d" % i, [128, D]) for i in range(2)]
        xo = [sb("xo%d" % i, [128, D]) for i in range(2)]
        junk = sb("junk", [128, D], BF16)
        ssq = [sb("ssq%d" % i, [128, 2]) for i in range(2)]
        pa = [ps("pa%d" % i, [128, 512]) for i in range(2)]
        pb = [ps("pb%d" % i, [128, 512]) for i in range(2)]
        po = [ps("po%d" % i, [128, 512]) for i in range(4)]

        for (W, src, key) in ((Wa, T["w_a"], "Wa"), (Wb, T["w_b"], "Wb"), (Wo, T["w_o"], "Wo")):
            for hh in range(2):
                P.dma("pool", W[:, :, hh * 512:(hh + 1) * 512],
                      src[:, hh * 512:(hh + 1) * 512].rearrange("(kc p) c -> p kc c", p=128), w=[(key, hh)])
        P.dma("sp", fing[:], T["fing_rep"], w=["fing"])
        wkeys = lambda k: [(k, 0), (k, 1)]
        cnt = {"p": 0, "o": 0, "x": 0}
        for tt in range(8):
            sl = tt % 2
            ts_ = slice(tt * 512, (tt + 1) * 512)
            P.dma("sp", ya[sl][:], yaT[:, ts_].rearrange("(cc p) t -> p cc t", p=128), w=[("ya", sl)])
            P.dma("sp", yb[sl][:], ybT[:, ts_].rearrange("(cc p) t -> p cc t", p=128), w=[("yb", sl)])
            P.dma("sp", sa[sl][:], sgT[0, :, ts_].rearrange("(cc p) t -> p cc t", p=128), w=[("sa", sl)])
            P.dma("sp", sbb[sl][:], sgT[1, :, ts_].rearrange("(cc p) t -> p cc t", p=128), w=[("sb", sl)])
            for dd in range(8):
                i = cnt["p"] % 2
                cnt["p"] += 1
                for cc in range(8):
                    P.pe(lambda e, i=i, cc=cc, dd=dd, sl=sl: e.matmul(pa[i][:, :], Wa[:, cc, dd * 128:(dd + 1) * 128], ya[sl][:, cc, :],
                                                                      start=(cc == 0), stop=(cc == 7)),
                         r=wkeys("Wa") + [("ya", sl)], w=[("pa", i)])
                for cc in range(8):
                    P.pe(lambda e, i=i, cc=cc, dd=dd, sl=sl: e.matmul(pb[i][:, :], Wb[:, cc, dd * 128:(dd + 1) * 128], yb[sl][:, cc, :],
                                                                      start=(cc == 0), stop=(cc == 7)),
                         r=wkeys("Wb") + [("yb", sl)], w=[("pb", i)])
                P.dve(lambda e, i=i, dd=dd, sl=sl: e.tensor_tensor(out=t1[i][:], in0=pa[i][:, :], in1=sa[sl][:, dd, :], op=ALU.mult),
                      r=[("pa", i), ("sa", sl)], w=[("t1", i)])
                P.dve(lambda e, i=i, dd=dd, sl=sl: e.tensor_tensor(out=t2[i][:], in0=pb[i][:, :], in1=sbb[sl][:, dd, :], op=ALU.mult),
                      r=[("pb", i), ("sb", sl)], w=[("t2", i)])
                P.pool(lambda e, i=i, dd=dd, sl=sl: e.tensor_tensor(out=mixT[sl][:, dd, :], in0=t1[i][:], in1=t2[i][:], op=ALU.add),
                       r=[("t1", i), ("t2", i)], w=[("mixT", sl)])
            for u in range(4):
                xs = cnt["x"] % 2
                cnt["x"] += 1
                r0 = tt * 512 + u * 128
                P.dma("sp", xt[xs][:], x[r0:r0 + 128, :], w=[("xt", xs)])
                for eg in range(2):
                    o = cnt["o"] % 4
                    cnt["o"] += 1
                    for dd in range(8):
                        P.pe(lambda e, o=o, dd=dd, sl=sl, u=u, eg=eg: e.matmul(
                            po[o][:, :], mixT[sl][:, dd, u * 128:(u + 1) * 128], Wo[:, dd, eg * 512:(eg + 1) * 512],
                            start=(dd == 0), stop=(dd == 7)), r=[("mixT", sl), ("Wo", eg)], w=[("po", o)])
                    P.dve(lambda e, o=o, xs=xs, eg=eg: e.tensor_tensor(out=xo[xs][:, eg * 512:(eg + 1) * 512], in0=po[o][:, :],
                                                                       in1=xt[xs][:, eg * 512:(eg + 1) * 512], op=ALU.add),
                          r=[("po", o), ("xt", xs)], w=[("xo", xs, eg)])
                P.act(lambda e, xs=xs: e.activation(out=junk[:], in_=xo[xs][:], func=AF.Square, accum_out=ssq[xs][:, 0:1]),
                      r=[("xo", xs, 0), ("xo", xs, 1)], w=[("ssq", xs)])
                P.act(lambda e, xs=xs: e.activation(out=ssq[xs][:, 0:1], in_=ssq[xs][:, 0:1], func=AF.Sqrt, scale=1.0 / D, bias=NORM_EPS),
                      r=[("ssq", xs)], w=[("ssq", xs)])
                P.dve(lambda e, xs=xs: e.reciprocal(out=ssq[xs][:, 1:2], in_=ssq[xs][:, 0:1]), r=[("ssq", xs)], w=[("ssq", xs)])
                P.dve(lambda e, xs=xs: e.scalar_tensor_tensor(out=xo[xs][:], in0=xo[xs][:], scalar=ssq[xs][:, 1:2], in1=fing[:],
                                                              op0=ALU.mult, op1=ALU.mult),
                      r=[("xo", xs, 0), ("xo", xs, 1), ("ssq", xs), "fing"], w=[("xo", xs, 0), ("xo", xs, 1)])
                P.dma("sp", out[r0:r0 + 128, :], xo[xs][:], r=[("xo", xs, 0), ("xo", xs, 1)])
        P.flush()


def make_in_maps(x, positions, norm_g, w_in, ml_gate_b, ml_conv_w, ml_norm_g, da_lambda,
                 da_subln_g, gate_b, w_branch_a, w_branch_b, w_out, final_g):
    f32 = np.float32
    x = np.asarray(x, f32)
    positions = np.asarray(positions, np.int32)
    w_in0 = np.ascontiguousarray(np.asarray(w_in, f32)[0])
    gb0 = np.asarray(ml_gate_b, f32)[0]
    cw0 = np.asarray(ml_conv_w, f32)[0]
    w_in1 = w_in0.copy()
    ag = w_in0[:, C_AG:C_AG + 16].reshape(D, 4, 4)
    w_in1[:, C_AG:C_AG + 16] = ag[:, [2, 3, 0, 1], :].reshape(D, 16)
    gb1 = gb0[[2, 3, 0, 1], :]
    cw1 = cw0[::-1, :]
    rep = lambda v, n=128: np.ascontiguousarray(np.broadcast_to(np.asarray(v, f32).reshape(1, -1), (n, np.asarray(v).size)))
    ident = np.eye(128, dtype=f32)
    rsw = np.zeros((128, 128), f32)
    for r in range(128):
        m = r % 64
        if m < 32:
            rsw[r + 32, r] = -1.0
        else:
            rsw[r - 32, r] = 1.0
    invf = (10000.0 ** (-np.arange(0, 64, 2, dtype=f32) / f32(64))).astype(f32)
    invf_p = np.array([invf[(p % 64) % 32] for p in range(128)], f32).reshape(128, 1)
    ii = np.arange(128)
    U = (ii[:, None] <= ii[None, :]).astype(f32)
    L = (ii[:, None] >= ii[None, :]).astype(f32)
    NEG = -30000.0
    tri = np.stack([U, L, (1 - U) * NEG, (1 - L) * NEG], axis=1).astype(f32)
    common = {
        "normg_rep": rep(np.asarray(norm_g, f32)[0]),
        "mlng_rep": rep(np.asarray(ml_norm_g, f32)[0].reshape(-1)),
        "lam_rep": rep(np.asarray(da_lambda, f32)[0].reshape(-1)),
        "subln_rep": rep(np.asarray(da_subln_g, f32)[0]),
        "gb_fm": np.ascontiguousarray(np.asarray(gate_b, f32)[0].reshape(2, 8, 128).transpose(2, 0, 1)),
        "w_a": np.ascontiguousarray(np.asarray(w_branch_a, f32)[0]),
        "w_b": np.ascontiguousarray(np.asarray(w_branch_b, f32)[0]),
        "w_o": np.ascontiguousarray(np.asarray(w_out, f32)[0]),
        "fing_rep": rep(np.asarray(final_g, f32)),
        "c_ident": ident, "c_rswap": rsw, "c_invf": invf_p, "c_tri": tri,
    }
    in_maps = []
    for core in range(NCORES):
        b, half = core // 2, core % 2
        xb = x[b]
        pb = positions[b]
        if half == 1:
            xb = xb[::-1]
            pb = pb[::-1]
        gbx = gb1 if half else gb0
        cwx = cw1 if half else cw0
        m = dict(common)
        m["x"] = np.ascontiguousarray(xb)
        m["posr"] = np.ascontiguousarray(np.broadcast_to(pb.reshape(1, S), (128, S))).astype(np.int32)
        m["w_in"] = w_in1 if half else w_in0
        m["gateb_rep"] = rep(gbx.reshape(-1))
        m["convw"] = np.ascontiguousarray(cwx.reshape(5, 16, 128).transpose(2, 1, 0))
        in_maps.append(m)
    return in_maps


_NC_CACHE = {}


def kernel(**inputs):
    in_maps = make_in_maps(**inputs)
    if "nc" not in _NC_CACHE:
        _NC_CACHE["nc"] = build_nc()
    nc = _NC_CACHE["nc"]
    res = run_bass_kernel_spmd(nc, in_maps, core_ids=list(range(NCORES)))
    B = 4
    outp = np.empty((B, S, D), np.float32)
    for core in range(NCORES):
        b, half = core // 2, core % 2
        o = np.asarray(res.results[core]["out"], np.float32)
        if half == 0:
            outp[b, :OWN] = o
        else:
            outp[b, OWN:] = o[::-1]
    return outp
```

```python
import contextlib
import math
import numpy as np
import concourse.bass as bass
import concourse.mybir as mybir
from concourse.bass_utils import run_bass_kernel_spmd

F32, BF16, I32 = mybir.dt.float32, mybir.dt.bfloat16, mybir.dt.int32
AF = mybir.ActivationFunctionType
ALU = mybir.AluOpType
AX = mybir.AxisListType

D = 1024
S = 8192
OWN = 4096
NCORES = 8
PROJ = 11280
C_AQ, C_AK, C_AV, C_AO, C_AZ, C_AG = 0, 1024, 2048, 3072, 4096, 5120
C_BQ, C_BK, C_BV, C_BZ, C_GA, C_GB = 5136, 6160, 7184, 8208, 9232, 10256
NORM_EPS = 1e-6
LAMBDA_INIT = 0.8 - 0.6 * math.exp(-0.3 * 0)
SAME_ENGINE_SYNC = True
SAME_ENGINE_RAW_ONLY = True


class _Op:
    __slots__ = ("eng", "fn", "dma", "clock", "idx", "waits", "signal", "semval", "sem", "know")


class Prog:
    ENGS = ("sp", "act", "dve", "pool", "pe")

    def __init__(self, nc, stack, n_dma=8):
        self.nc = nc
        self.sem = {e: stack.enter_context(nc.semaphore("cs_" + e)) for e in self.ENGS}
        self.dsem = {q: [stack.enter_context(nc.semaphore("ds_%s_%d" % (q, i))) for i in range(n_dma)]
                     for q in ("sp", "pool", "act")}
        self.n_dma = n_dma
        self.dcount = {q: 0 for q in self.dsem}
        self.dlast = {}
        self.cnt = {e: 0 for e in self.ENGS}
        self.sigcnt = {e: 0 for e in self.ENGS}
        self.know = {e: {} for e in self.ENGS}
        self.pending = {e: [] for e in self.ENGS}
        self.last_w = {}
        self.readers = {}
        self.nops = 0

    def add(self, eng, fn, reads=(), writes=(), dma=False, extra_deps=()):
        op = _Op()
        op.eng, op.fn, op.dma, op.signal, op.semval = eng, fn, dma, False, None
        self.nops += 1
        deps = []
        seen = set()

        raw = set()

        def push(d, is_raw=False):
            if d is None:
                return
            if is_raw:
                raw.add(id(d))
            if id(d) not in seen:
                seen.add(id(d))
                deps.append(d)

        for k in reads:
            push(self.last_w.get(k), True)
        for k in writes:
            push(self.last_w.get(k))
            for r in self.readers.get(k, ()):
                push(r)
        for d in extra_deps:
            push(d, True)
        if dma:
            slot = self.dcount[eng] % self.n_dma
            self.dcount[eng] += 1
            op.clock = (eng, slot)
            prev = self.dlast.get(op.clock)
            op.idx = (prev.idx + 1) if prev is not None else 1
            op.sem = self.dsem[eng][slot]
            op.semval = 16 * op.idx
            push(prev)
            self.dlast[op.clock] = op
        else:
            self.cnt[eng] += 1
            op.clock = eng
            op.idx = self.cnt[eng]
            op.sem = self.sem[eng]
        know = self.know[eng]
        waits = []
        for d in deps:
            if (not d.dma) and (not dma) and d.eng == eng:
                if eng == "pe" or not SAME_ENGINE_SYNC:
                    continue
                if SAME_ENGINE_RAW_ONLY and id(d) not in raw:
                    continue
            if know.get(d.clock, 0) >= d.idx:
                continue
            waits.append(d)
            d.signal = True
            for c, v in d.know.items():
                if know.get(c, 0) < v:
                    know[c] = v
            if know.get(d.clock, 0) < d.idx:
                know[d.clock] = d.idx
        op.waits = waits
        op.know = dict(know)
        for k in writes:
            self.last_w[k] = op
            self.readers[k] = []
        for k in reads:
            self.readers.setdefault(k, []).append(op)
        self.pending[eng].append(op)
        return op

    def pe(self, fn, r=(), w=()):
        return self.add("pe", fn, r, w)

    def act(self, fn, r=(), w=()):
        return self.add("act", fn, r, w)

    def dve(self, fn, r=(), w=()):
        return self.add("dve", fn, r, w)

    def pool(self, fn, r=(), w=()):
        return self.add("pool", fn, r, w)

    def dma(self, q, out, in_, r=(), w=()):
        return self.add(q, lambda e: e.dma_start(out=out, in_=in_), r, w, dma=True)

    def flush(self, final=False):
        outstanding = [d for d in self.dlast.values()]
        self.add("sp", None, extra_deps=outstanding)
        for e in self.ENGS:
            for op in self.pending[e]:
                if not op.dma and op.signal:
                    self.sigcnt[e] += 1
                    op.semval = self.sigcnt[e]
                elif not op.dma:
                    op.semval = None
        pend = self.pending
        sems = self.sem

        def make_body(eng):
            ops = pend[eng]

            def body(e):
                for op in ops:
                    for d in op.waits:
                        assert d.semval is not None
                        e.wait_ge(d.sem, d.semval)
                    if op.fn is None:
                        continue
                    ins = op.fn(e)
                    if op.dma:
                        ins.then_inc(op.sem, 16)
                    elif op.signal:
                        ins.then_inc(sems[eng], 1)
            return body

        with self.nc.Block() as block:
            block.sync(make_body("sp"))
            block.scalar(make_body("act"))
            block.vector(make_body("dve"))
            block.gpsimd(make_body("pool"))
            block.tensor(make_body("pe"))
        self.pending = {e: [] for e in self.ENGS}
        self.last_w = {}
        self.readers = {}
        full = {}
        for e in self.ENGS:
            full[e] = self.cnt[e]
        for c, d in self.dlast.items():
            full[c] = d.idx
        self.know = {e: dict(full) for e in self.ENGS}


def build_nc(debug=False, phases=("P", "M", "D", "O")):
    nc = bass.Bass("TRN2", target_bir_lowering=False)
    IN = lambda name, shape, dt=F32: nc.dram_tensor(name, list(shape), dt, kind="ExternalInput").ap()
    skind = "ExternalOutput" if debug else "Internal"
    SCR = lambda name, shape, dt=BF16: nc.dram_tensor(name, list(shape), dt, kind=skind).ap()

    x = IN("x", [S, D])
    posr = IN("posr", [128, S], I32)
    w_in = IN("w_in", [D, PROJ])
    normg_rep = IN("normg_rep", [128, D])
    gateb_rep = IN("gateb_rep", [128, 16])
    convw = IN("convw", [128, 16, 5])
    mlng_rep = IN("mlng_rep", [128, D])
    lam_rep = IN("lam_rep", [128, 256])
    subln_rep = IN("subln_rep", [128, 128])
    gb_fm = IN("gb_fm", [128, 2, 8])
    w_a = IN("w_a", [D, D])
    w_b = IN("w_b", [D, D])
    w_o = IN("w_o", [D, D])
    fing_rep = IN("fing_rep", [128, D])
    c_ident = IN("c_ident", [128, 128])
    c_rswap = IN("c_rswap", [128, 128])
    c_invf = IN("c_invf", [128, 1])
    c_tri = IN("c_tri", [128, 4, 128])
    out = nc.dram_tensor("out", [OWN, D], F32, kind="ExternalOutput").ap()

    mqT = SCR("mqT", [D, S])
    mkT = SCR("mkT", [D, S])
    mv = SCR("mv", [S, D])
    dqT = SCR("dqT", [D, OWN])
    dkT = SCR("dkT", [D, S])
    dv = SCR("dv", [S, D])
    gA = SCR("gA", [OWN, D])
    gB = SCR("gB", [OWN, D])
    sgT = SCR("sgT", [2, D, OWN])
    gall_d = SCR("gall_d", [128, 64 * 16], F32)
    yaT = SCR("yaT", [D, OWN])
    ybT = SCR("ybT", [D, OWN])

    with contextlib.ExitStack() as top:
        P = Prog(nc, top)
        if "P" in phases:
            phase_P(nc, P, locals())
        if "M" in phases:
            phase_M(nc, P, locals())
        if "D" in phases:
            phase_D(nc, P, locals())
        if "O" in phases:
            phase_O(nc, P, locals())
    return nc


def phase_P(nc, P, T):
    x, posr, w_in = T["x"], T["posr"], T["w_in"]
    with contextlib.ExitStack() as st:
        sb = lambda name, shape, dt=F32: st.enter_context(nc.sbuf_tensor("P_" + name, list(shape), dt))
        ps = lambda name, shape, dt=F32: st.enter_context(nc.psum_tensor("P_" + name, list(shape), dt))
        hT = sb("hT", [128, 8, 2048], BF16)
        xt = [sb("xt%d" % i, [128, D]) for i in range(2)]
        xn = [sb("xn%d" % i, [128, D], BF16) for i in range(2)]
        junk = sb("junk", [128, D], BF16)
        ss = [sb("ss%d" % i, [128, 1]) for i in range(2)]
        rstd = [sb("rstd%d" % i, [128, 1]) for i in range(2)]
        grep = sb("grep", [128, D])
        NW = 3
        wB = [sb("wB%d" % i, [128, 8, 512], BF16) for i in range(NW)]
        wG = sb("wG", [128, 8, 16], BF16)
        NWK = 5
        WK = [sb("WK%d" % i, [128, 2054]) for i in range(NWK)]
        OB = [sb("OB%d" % i, [128, 8192], BF16) for i in range(2)]
        cosT = sb("cosT", [128, 2048])
        sinT = sb("sinT", [128, 2048])
        gall = sb("gall", [128, 64 * 16])
        gbrep = sb("gbrep", [128, 16])
        cw = sb("cw", [128, 16, 5])
        carry = sb("carry", [128, 16, 4])
        identf = sb("identf", [128, 128])
        identb = sb("identb", [128, 128], BF16)
        rswap = sb("rswap", [128, 128])
        invf = sb("invf", [128, 1])
        gbfm = sb("gbfm", [128, 2, 8])
        posi = sb("posi", [128, 2048], I32)
        ptr = [ps("ptr%d" % i, [128, D], BF16) for i in range(2)]
        pp = [ps("pp%d" % i, [128, 512]) for i in range(4)]
        prot = [ps("prot%d" % i, [128, 512]) for i in range(2)]

        P.dma("sp", grep[:], T["normg_rep"], w=["grep"])
        P.dma("sp", gbrep[:], T["gateb_rep"], w=["gbrep"])
        P.dma("sp", cw[:], T["convw"], w=["cw"])
        P.dma("sp", identf[:], T["c_ident"], w=["identf"])
        P.dma("sp", rswap[:], T["c_rswap"], w=["rswap"])
        P.dma("sp", invf[:], T["c_invf"], w=["invf"])
        P.dma("sp", gbfm[:], T["gb_fm"], w=["gbfm"])
        P.dma("pool", wG[:], w_in[:, C_AG:C_AG + 16].rearrange("(kc p) c -> p kc c", p=128), w=["wG"])
        P.dve(lambda e: e.tensor_copy(out=identb[:], in_=identf[:]), r=["identf"], w=["identb"])
        P.dve(lambda e: e.memset(carry[:], 0.0), w=["carry"])
        for i in range(NWK):
            P.dve(lambda e, i=i: e.memset(WK[i][:, 2052:2054], 0.0), w=[("WKz", i)])

        wslot = [0]

        def load_w(col0, ncols):
            s = wslot[0] % NW
            wslot[0] += 1
            P.dma("pool", wB[s][:, :, 0:ncols],
                  w_in[:, col0:col0 + ncols].rearrange("(kc p) c -> p kc c", p=128), w=[("wB", s)])
            return s

        ppslot = [0]

        def next_pp():
            s = ppslot[0] % 4
            ppslot[0] += 1
            return s

        wkslot = [0]

        def next_wk():
            s = wkslot[0] % NWK
            wkslot[0] += 1
            return s

        obslot = [0]

        def next_ob():
            s = obslot[0] % 2
            obslot[0] += 1
            return s

        def fm_mm(ws, wc, tq):
            s = next_pp()
            for kc in range(8):
                P.pe(lambda e, s=s, ws=ws, wc=wc, kc=kc, tq=tq: e.matmul(
                    pp[s][:, :], wB[ws][:, kc, wc * 128:(wc + 1) * 128], hT[:, kc, tq * 512:(tq + 1) * 512],
                    start=(kc == 0), stop=(kc == 7)),
                    r=[("wB", ws), "hT"], w=[("pp", s)])
            return s

        def tm_mm(ws, ncols, tt):
            s = next_pp()
            for kc in range(8):
                P.pe(lambda e, s=s, ws=ws, kc=kc, tt=tt, ncols=ncols: e.matmul(
                    pp[s][:, 0:ncols], hT[:, kc, tt * 128:(tt + 1) * 128], wB[ws][:, kc, 0:ncols],
                    start=(kc == 0), stop=(kc == 7)),
                    r=[("wB", ws), "hT"], w=[("pp", s)])
            return s

        for stile in range(4):
            own = stile < 2
            t0 = stile * 2048
            last = stile == 3
            for tt in range(16):
                sl = tt % 2
                r0 = t0 + tt * 128
                P.dma("sp", xt[sl][:], x[r0:r0 + 128, :], w=[("xt", sl)])
                P.act(lambda e, sl=sl: e.activation(out=junk[:], in_=xt[sl][:], func=AF.Square, accum_out=ss[sl][:]),
                      r=[("xt", sl)], w=[("ss", sl)])
                P.act(lambda e, sl=sl: e.activation(out=ss[sl][:], in_=ss[sl][:], func=AF.Sqrt,
                                                    scale=1.0 / D, bias=NORM_EPS),
                      r=[("ss", sl)], w=[("ss", sl)])
                P.dve(lambda e, sl=sl: e.reciprocal(out=rstd[sl][:], in_=ss[sl][:]), r=[("ss", sl)], w=[("rstd", sl)])
                P.dve(lambda e, sl=sl: e.scalar_tensor_tensor(out=xn[sl][:], in0=xt[sl][:], scalar=rstd[sl][:],
                                                              in1=grep[:], op0=ALU.mult, op1=ALU.mult),
                      r=[("xt", sl), ("rstd", sl), "grep"], w=[("xn", sl)])
                for kc in range(8):
                    P.pe(lambda e, sl=sl, kc=kc: e.transpose(out=ptr[sl][:, kc * 128:(kc + 1) * 128],
                                                             in_=xn[sl][:, kc * 128:(kc + 1) * 128],
                                                             identity=identb[:]),
                         r=[("xn", sl), "identb"], w=[("ptr", sl)])
                P.act(lambda e, sl=sl, tt=tt: e.activation(
                    out=hT[:, :, tt * 128:(tt + 1) * 128],
                    in_=ptr[sl][:, :].rearrange("p (k t) -> p k t", k=8), func=AF.Copy),
                    r=[("ptr", sl)], w=["hT"])

            P.dma("sp", posi[:], posr[:, t0:t0 + 2048], w=["posi"])
            build_rope_tables(P, posi, invf, cosT, sinT, WK, next_wk)

            for tt in range(16):
                s = next_pp()
                for kc in range(8):
                    P.pe(lambda e, s=s, kc=kc, tt=tt: e.matmul(
                        pp[s][:, 0:16], hT[:, kc, tt * 128:(tt + 1) * 128], wG[:, kc, :],
                        start=(kc == 0), stop=(kc == 7)), r=["wG", "hT"], w=[("pp", s)])
                gt = stile * 16 + tt
                P.dve(lambda e, s=s, gt=gt: e.tensor_tensor(out=gall[:, gt * 16:(gt + 1) * 16], in0=pp[s][:, 0:16],
                                                            in1=gbrep[:], op=ALU.add),
                      r=[("pp", s), "gbrep"], w=["gall"])

            for (c0, dst) in ((C_AV, T["mv"]), (C_BV, T["dv"])):
                for cg in range(2):
                    ws = load_w(c0 + cg * 512, 512)
                    ob = next_ob()
                    for tt in range(16):
                        s = tm_mm(ws, 512, tt)
                        eng = P.act if tt % 2 == 0 else P.dve
                        if tt % 2 == 0:
                            P.act(lambda e, s=s, ob=ob, tt=tt: e.activation(
                                out=OB[ob][:, tt * 512:(tt + 1) * 512], in_=pp[s][:, :], func=AF.Copy),
                                r=[("pp", s)], w=[("OB", ob)])
                        else:
                            P.dve(lambda e, s=s, ob=ob, tt=tt: e.tensor_copy(
                                out=OB[ob][:, tt * 512:(tt + 1) * 512], in_=pp[s][:, :]),
                                r=[("pp", s)], w=[("OB", ob)])
                    P.dma("sp", dst[t0:t0 + 2048, cg * 512:(cg + 1) * 512].rearrange("(tt p) c -> p tt c", p=128),
                          OB[ob][:, :].rearrange("p (tt c) -> p tt c", c=512), r=[("OB", ob)])

            for fam, (c0, dst, ksc) in enumerate(((C_AQ, T["mqT"], 1.0), (C_AK, T["mkT"], 1.0 / 16.0))):
                for g4 in range(2):
                    ws = load_w(c0 + g4 * 512, 512)
                    for wc in range(4):
                        fc = fam * 8 + g4 * 4 + wc
                        row0 = (g4 * 4 + wc) * 128
                        stg = next_wk()
                        acc = next_wk()
                        sg = next_wk()
                        ob = next_ob()
                        P.dve(lambda e, stg=stg, fc=fc: e.tensor_copy(out=WK[stg][:, 0:4], in_=carry[:, fc, :]),
                              r=["carry"], w=[("WK", stg)])
                        for tq in range(4):
                            s = fm_mm(ws, wc, tq)
                            P.act(lambda e, s=s, stg=stg, tq=tq: e.activation(
                                out=WK[stg][:, 4 + tq * 512:4 + (tq + 1) * 512], in_=pp[s][:, :], func=AF.Copy),
                                r=[("pp", s)], w=[("WK", stg)])
                        P.dve(lambda e, stg=stg, fc=fc: e.tensor_copy(out=carry[:, fc, :], in_=WK[stg][:, 2048:2052]),
                              r=[("WK", stg)], w=["carry"])
                        nout = 2050 if last else 2048
                        P.dve(lambda e, stg=stg, acc=acc, fc=fc, nout=nout: e.tensor_scalar(
                            out=WK[acc][:, 0:nout], in0=WK[stg][:, 0:nout], scalar1=cw[:, fc, 0:1], scalar2=None,
                            op0=ALU.mult), r=[("WK", stg), "cw", ("WKz", stg)], w=[("WK", acc)])
                        for j in range(1, 5):
                            P.dve(lambda e, stg=stg, acc=acc, fc=fc, nout=nout, j=j: e.scalar_tensor_tensor(
                                out=WK[acc][:, 0:nout], in0=WK[stg][:, j:j + nout], scalar=cw[:, fc, j:j + 1],
                                in1=WK[acc][:, 0:nout], op0=ALU.mult, op1=ALU.add),
                                r=[("WK", stg), "cw"], w=[("WK", acc)])
                        P.act(lambda e, acc=acc, sg=sg, nout=nout: e.activation(
                            out=WK[sg][:, 0:nout], in_=WK[acc][:, 0:nout], func=AF.Sigmoid),
                            r=[("WK", acc)], w=[("WK", sg)])
                        P.dve(lambda e, acc=acc, sg=sg, ob=ob, nout=nout, ksc=ksc: e.scalar_tensor_tensor(
                            out=OB[ob][:, 0:nout], in0=WK[acc][:, 0:nout], scalar=ksc, in1=WK[sg][:, 0:nout],
                            op0=ALU.mult, op1=ALU.mult), r=[("WK", acc), ("WK", sg)], w=[("OB", ob)])
                        if stile == 0:
                            P.dma("sp", dst[row0:row0 + 128, 0:nout - 2], OB[ob][:, 2:nout], r=[("OB", ob)])
                        else:
                            P.dma("sp", dst[row0:row0 + 128, t0 - 2:t0 - 2 + nout], OB[ob][:, 0:nout], r=[("OB", ob)])

            fams = [(C_BK, T["dkT"], 1.0, t0)]
            if own:
                fams.append((C_BQ, T["dqT"], 0.125, t0))
            for (c0, dst, sc, tcol) in fams:
                for g4 in range(2):
                    ws = load_w(c0 + g4 * 512, 512)
                    for wc in range(4):
                        row0 = (g4 * 4 + wc) * 128
                        ob = next_ob()
                        for tq in range(4):
                            s = fm_mm(ws, wc, tq)
                            wk = next_wk()
                            pr = tq % 2
                            cs = slice(tq * 512, (tq + 1) * 512)
                            P.act(lambda e, s=s, wk=wk: e.activation(out=WK[wk][:, 0:512], in_=pp[s][:, :], func=AF.Copy),
                                  r=[("pp", s)], w=[("WK", wk)])
                            P.pe(lambda e, wk=wk, pr=pr: e.matmul(prot[pr][:, :], rswap[:, :], WK[wk][:, 0:512],
                                                                  start=True, stop=True),
                                 r=[("WK", wk), "rswap"], w=[("prot", pr)])
                            P.dve(lambda e, wk=wk, cs=cs, sc=sc: e.scalar_tensor_tensor(
                                out=WK[wk][:, 512:1024], in0=WK[wk][:, 0:512], scalar=sc, in1=cosT[:, cs],
                                op0=ALU.mult, op1=ALU.mult), r=[("WK", wk), "cosT"], w=[("WK", wk)])
                            P.dve(lambda e, wk=wk, cs=cs, sc=sc, pr=pr: e.scalar_tensor_tensor(
                                out=WK[wk][:, 1024:1536], in0=prot[pr][:, :], scalar=sc, in1=sinT[:, cs],
                                op0=ALU.mult, op1=ALU.mult), r=[("prot", pr), "sinT"], w=[("WK", wk)])
                            P.dve(lambda e, wk=wk, cs=cs, ob=ob: e.tensor_tensor(
                                out=OB[ob][:, cs], in0=WK[wk][:, 512:1024], in1=WK[wk][:, 1024:1536], op=ALU.add),
                                r=[("WK", wk)], w=[("OB", ob)])
                        P.dma("sp", dst[row0:row0 + 128, tcol:tcol + 2048], OB[ob][:, 0:2048], r=[("OB", ob)])

            if own:
                for gi, c0 in enumerate((C_GA, C_GB)):
                    for g4 in range(2):
                        ws = load_w(c0 + g4 * 512, 512)
                        for wc in range(4):
                            cc = g4 * 4 + wc
                            ob = next_ob()
                            for tq in range(4):
                                s = fm_mm(ws, wc, tq)
                                P.act(lambda e, s=s, ob=ob, tq=tq, gi=gi, cc=cc: e.activation(
                                    out=OB[ob][:, tq * 512:(tq + 1) * 512], in_=pp[s][:, :], func=AF.Sigmoid,
                                    bias=gbfm[:, gi, cc:cc + 1]), r=[("pp", s), "gbfm"], w=[("OB", ob)])
                            P.dma("sp", T["sgT"][gi, cc * 128:(cc + 1) * 128, t0:t0 + 2048], OB[ob][:, 0:2048],
                                  r=[("OB", ob)])
                for cg in range(2):
                    wso = load_w(C_AO + cg * 512, 512)
                    wsz = load_w(C_AZ + cg * 512, 512)
                    ob = next_ob()
                    for tt in range(16):
                        so = tm_mm(wso, 512, tt)
                        sz = tm_mm(wsz, 512, tt)
                        wk = next_wk()
                        P.act(lambda e, so=so, wk=wk: e.activation(out=WK[wk][:, 0:512], in_=pp[so][:, :], func=AF.Sigmoid),
                              r=[("pp", so)], w=[("WK", wk)])
                        P.act(lambda e, sz=sz, wk=wk: e.activation(out=WK[wk][:, 512:1024], in_=pp[sz][:, :], func=AF.Sigmoid),
                              r=[("pp", sz)], w=[("WK", wk)])
                        P.dve(lambda e, sz=sz, wk=wk: e.tensor_tensor(out=WK[wk][:, 512:1024], in0=pp[sz][:, :],
                                                                      in1=WK[wk][:, 512:1024], op=ALU.mult),
                              r=[("pp", sz), ("WK", wk)], w=[("WK", wk)])
                        P.dve(lambda e, wk=wk, ob=ob, tt=tt: e.tensor_tensor(
                            out=OB[ob][:, tt * 512:(tt + 1) * 512], in0=WK[wk][:, 0:512], in1=WK[wk][:, 512:1024],
                            op=ALU.mult), r=[("WK", wk)], w=[("OB", ob)])
                    P.dma("sp", T["gA"][t0:t0 + 2048, cg * 512:(cg + 1) * 512].rearrange("(tt p) c -> p tt c", p=128),
                          OB[ob][:, :].rearrange("p (tt c) -> p tt c", c=512), r=[("OB", ob)])
                for cg in range(2):
                    wsz = load_w(C_BZ + cg * 512, 512)
                    ob = next_ob()
                    for tt in range(16):
                        sz = tm_mm(wsz, 512, tt)
                        wk = next_wk()
                        P.act(lambda e, sz=sz, wk=wk: e.activation(out=WK[wk][:, 0:512], in_=pp[sz][:, :], func=AF.Sigmoid),
                              r=[("pp", sz)], w=[("WK", wk)])
                        P.dve(lambda e, sz=sz, wk=wk, ob=ob, tt=tt: e.tensor_tensor(
                            out=OB[ob][:, tt * 512:(tt + 1) * 512], in0=pp[sz][:, :], in1=WK[wk][:, 0:512],
                            op=ALU.mult), r=[("pp", sz), ("WK", wk)], w=[("OB", ob)])
                    P.dma("sp", T["gB"][t0:t0 + 2048, cg * 512:(cg + 1) * 512].rearrange("(tt p) c -> p tt c", p=128),
                          OB[ob][:, :].rearrange("p (tt c) -> p tt c", c=512), r=[("OB", ob)])

        P.dma("sp", T["gall_d"][:, :], gall[:, :], r=["gall"])
        P.flush()


def build_rope_tables(P, posi, invf, cosT, sinT, WK, next_wk):
    TWO_PI = 2.0 * math.pi
    C1 = 6.28125
    C2 = TWO_PI - C1
    a = next_wk()
    b = next_wk()
    c = next_wk()
    A, Bk, R = WK[a], WK[b], WK[c]
    N = 2048
    P.dve(lambda e: e.tensor_copy(out=A[:, 0:N], in_=posi[:, :]), r=["posi"], w=[("WK", a)])
    P.dve(lambda e: e.tensor_scalar(out=A[:, 0:N], in0=A[:, 0:N], scalar1=invf[:, 0:1], scalar2=None, op0=ALU.mult),
          r=[("WK", a), "invf"], w=[("WK", a)])
    ki = posi
    P.dve(lambda e: e.tensor_scalar(out=Bk[:, 0:N], in0=A[:, 0:N], scalar1=1.0 / TWO_PI, scalar2=None, op0=ALU.mult),
          r=[("WK", a)], w=[("WK", b)])
    P.dve(lambda e: e.tensor_copy(out=ki[:, :], in_=Bk[:, 0:N]), r=[("WK", b)], w=["posi"])
    P.dve(lambda e: e.tensor_copy(out=Bk[:, 0:N], in_=ki[:, :]), r=["posi"], w=[("WK", b)])
    P.dve(lambda e: e.scalar_tensor_tensor(out=R[:, 0:N], in0=Bk[:, 0:N], scalar=-C1, in1=A[:, 0:N],
                                           op0=ALU.mult, op1=ALU.add), r=[("WK", a), ("WK", b)], w=[("WK", c)])
    P.dve(lambda e: e.scalar_tensor_tensor(out=R[:, 0:N], in0=Bk[:, 0:N], scalar=-C2, in1=R[:, 0:N],
                                           op0=ALU.mult, op1=ALU.add), r=[("WK", b), ("WK", c)], w=[("WK", c)])

    def wrap(X, key):
        P.dve(lambda e: e.tensor_scalar(out=Bk[:, 0:N], in0=X[:, 0:N], scalar1=math.pi, scalar2=-TWO_PI,
                                        op0=ALU.is_gt, op1=ALU.mult), r=[key], w=[("WK", b)])
        P.dve(lambda e: e.tensor_tensor(out=X[:, 0:N], in0=X[:, 0:N], in1=Bk[:, 0:N], op=ALU.add),
              r=[key, ("WK", b)], w=[key])
        P.dve(lambda e: e.tensor_scalar(out=Bk[:, 0:N], in0=X[:, 0:N], scalar1=-math.pi, scalar2=TWO_PI,
                                        op0=ALU.is_lt, op1=ALU.mult), r=[key], w=[("WK", b)])
        P.dve(lambda e: e.tensor_tensor(out=X[:, 0:N], in0=X[:, 0:N], in1=Bk[:, 0:N], op=ALU.add),
              r=[key, ("WK", b)], w=[key])

    wrap(R, ("WK", c))
    P.act(lambda e: e.activation(out=sinT[:, :], in_=R[:, 0:N], func=AF.Sin), r=[("WK", c)], w=["sinT"])
    P.dve(lambda e: e.tensor_scalar(out=R[:, 0:N], in0=R[:, 0:N], scalar1=math.pi / 2, scalar2=None, op0=ALU.add),
          r=[("WK", c)], w=[("WK", c)])
    wrap(R, ("WK", c))
    P.act(lambda e: e.activation(out=cosT[:, :], in_=R[:, 0:N], func=AF.Sin), r=[("WK", c)], w=["cosT"])


def phase_M(nc, P, T):
    mqT, mkT, mv, gA, yaT = T["mqT"], T["mkT"], T["mv"], T["gA"], T["yaT"]
    with contextlib.ExitStack() as st:
        sb = lambda name, shape, dt=F32: st.enter_context(nc.sbuf_tensor("M_" + name, list(shape), dt))
        ps = lambda name, shape, dt=F32: st.enter_context(nc.psum_tensor("M_" + name, list(shape), dt))
        qT = sb("qT", [128, 2, OWN], BF16)
        kT = sb("kT", [128, 2, S], BF16)
        va = sb("va", [128, 64, 257], BF16)
        ktok = sb("ktok", [128, 64, 256], BF16)
        hacc = sb("hacc", [128, 32, 256])
        gat = [sb("gat%d" % i, [128, 256], BF16) for i in range(2)]
        gall = sb("gall", [128, 1024])
        mlng = sb("mlng", [128, D])
        tri = sb("tri", [128, 4, 128])
        identf = sb("identf", [128, 128])
        identb = sb("identb", [128, 128], BF16)
        onesf = sb("onesf", [128, 128])
        LFt = [sb("LFt%d" % d, [128, 256]) for d in range(2)]
        Bc = [sb("Bc%d" % d, [128, 256]) for d in range(2)]
        Aa = [sb("Aa%d" % d, [128, 256]) for d in range(2)]
        EB = [sb("EB%d" % d, [128, 256]) for d in range(2)]
        WST = [sb("WST%d" % d, [128, 256]) for d in range(2)]
        DEC = [sb("DEC%d" % d, [128, 256]) for d in range(2)]
        Cst = [sb("Cst%d" % d, [128, 2, 257]) for d in range(2)]
        Cb = [sb("Cb%d" % d, [128, 2, 257], BF16) for d in range(2)]
        dg = [sb("dg%d" % i, [128, 128]) for i in range(2)]
        Dm = [sb("Dm%d" % i, [128, 128]) for i in range(2)]
        Wm = [sb("Wm%d" % i, [128, 128], BF16) for i in range(2)]
        vw = [sb("vw%d" % i, [128, 257], BF16) for i in range(2)]
        tmpc = [sb("tmpc%d" % i, [128, 257]) for i in range(2)]
        tot = [sb("tot%d" % i, [128, 257]) for i in range(2)]
        sm = [sb("sm%d" % i, [128, 4]) for i in range(2)]
        hs = [sb("hs%d" % i, [128, 256]) for i in range(2)]
        sqv = [sb("sqv%d" % i, [128, 256]) for i in range(2)]
        yab = [sb("yab%d" % i, [128, 256], BF16) for i in range(2)]
        yas = [sb("yas%d" % i, [128, 2, 128], BF16) for i in range(2)]
        pD = ps("pD", [128, 512])
        pST = ps("pST", [128, 512])
        pI = ps("pI", [128, 512])
        pC = ps("pC", [128, 512])
        pU = [ps("pU%d" % i, [128, 512]) for i in range(2)]
        ptrk = ps("ptrk", [128, 1024], BF16)
        ptry = ps("ptry", [128, 1024], BF16)

        P.dma("sp", gall[:], T["gall_d"], w=["gall"])
        P.dma("sp", mlng[:], T["mlng_rep"], w=["mlng"])
        P.dma("sp", tri[:], T["c_tri"], w=["tri"])
        P.dma("sp", identf[:], T["c_ident"], w=["identf"])
        P.dve(lambda e: e.tensor_copy(out=identb[:], in_=identf[:]), r=["identf"], w=["identb"])
        P.dve(lambda e: e.memset(onesf[:], 1.0), w=["onesf"])
        P.dve(lambda e: e.memset(va[:, :, 256:257], 1.0), w=["va1"])

        g4 = gall[:, :].rearrange("p (c g h) -> p c g h", g=4, h=4)
        v3 = lambda t: t[:, :].rearrange("p (c h) -> p c h", h=4)
        for d in range(2):
            i_d = g4[:, :, 2 * d, :]
            f_d = g4[:, :, 2 * d + 1, :]
            P.act(lambda e, d=d, f_d=f_d: e.activation(out=v3(LFt[d]), in_=f_d, func=AF.Exp, scale=-1.0),
                  r=["gall"], w=[("LFt", d)])
            P.act(lambda e, d=d: e.activation(out=LFt[d][:, :], in_=LFt[d][:, :], func=AF.Ln, bias=1.0),
                  r=[("LFt", d)], w=[("LFt", d)])
            P.pe(lambda e, d=d: e.matmul(pI[:, 0:256], tri[:, d, :], LFt[d][:, :], start=True, stop=True),
                 r=["tri", ("LFt", d)], w=["pI"])
            P.pe(lambda e, d=d: e.matmul(pC[:, 0:256], onesf[:, :], LFt[d][:, :], start=True, stop=True),
                 r=["onesf", ("LFt", d)], w=["pC"])
            P.dve(lambda e, d=d: e.tensor_scalar(out=Bc[d][:, :], in0=pI[:, 0:256], scalar1=-1.0, scalar2=None, op0=ALU.mult),
                  r=["pI"], w=[("Bc", d)])
            P.dve(lambda e, d=d, i_d=i_d: e.tensor_tensor(out=v3(Aa[d]), in0=pI[:, 0:256].rearrange("p (c h) -> p c h", h=4),
                                                         in1=i_d, op=ALU.add),
                  r=["pI", "gall"], w=[("Aa", d)])
            P.act(lambda e, d=d: e.activation(out=EB[d][:, :], in_=Bc[d][:, :], func=AF.Exp),
                  r=[("Bc", d)], w=[("EB", d)])
            P.dve(lambda e, d=d: e.tensor_copy(out=DEC[d][:, :], in_=pC[:, 0:256]), r=["pC"], w=[("DEC", d)])
            P.dve(lambda e, d=d: e.tensor_tensor(out=WST[d][:, :], in0=Aa[d][:, :], in1=DEC[d][:, :], op=ALU.subtract),
                  r=[("Aa", d), ("DEC", d)], w=[("WST", d)])
            P.act(lambda e, d=d: e.activation(out=WST[d][:, :], in_=WST[d][:, :], func=AF.Exp),
                  r=[("WST", d)], w=[("WST", d)])
            P.act(lambda e, d=d: e.activation(out=DEC[d][:, :], in_=DEC[d][:, :], func=AF.Exp, scale=-1.0),
                  r=[("DEC", d), ("WST", d)], w=[("DEC", d)])

        cnt = {"o": 0, "u": 0, "g": 0, "y": 0}

        def output_step(h, d, c):
            i = cnt["o"] % 2
            cnt["o"] += 1
            col = c * 4 + h
            cs = slice(c * 128, (c + 1) * 128)
            P.dve(lambda e: e.tensor_scalar(out=dg[i][:], in0=identf[:], scalar1=Bc[d][:, col:col + 1], scalar2=None, op0=ALU.mult),
                  r=["identf", ("Bc", d)], w=[("dg", i)])
            P.pe(lambda e: e.matmul(pD[:, 0:128], onesf[:, :], dg[i][:, :], start=True, stop=False), r=["onesf", ("dg", i)], w=["pD"])
            P.pe(lambda e: e.matmul(pD[:, 0:128], identf[:, :], tri[:, 2 + d, :], start=False, stop=True), r=["identf", "tri"], w=["pD"])
            P.act(lambda e: e.activation(out=Dm[i][:], in_=pD[:, 0:128], func=AF.Exp, bias=Aa[d][:, col:col + 1]),
                  r=["pD", ("Aa", d)], w=[("Dm", i)])
            for dkc in range(2):
                P.pe(lambda e, dkc=dkc: e.matmul(pST[:, 0:128], kT[:, dkc, cs], qT[:, dkc, cs], start=(dkc == 0), stop=(dkc == 1)),
                     r=["kT", "qT"], w=["pST"])
            P.dve(lambda e: e.tensor_tensor(out=Wm[i][:], in0=pST[:, 0:128], in1=Dm[i][:], op=ALU.mult),
                  r=["pST", ("Dm", i)], w=[("Wm", i)])
            P.pe(lambda e: e.matmul(pI[:, 0:257], Wm[i][:, :], va[:, c, :], start=True, stop=True),
                 r=[("Wm", i), "va", "va1"], w=["pI"])
            for dkc in range(2):
                P.pe(lambda e, dkc=dkc: e.matmul(pC[:, 0:257], qT[:, dkc, cs], Cb[d][:, dkc, :], start=(dkc == 0), stop=(dkc == 1)),
                     r=["qT", ("Cb", d)], w=["pC"])
            P.dve(lambda e: e.tensor_scalar(out=tmpc[i][:], in0=pC[:, 0:257], scalar1=EB[d][:, col:col + 1], scalar2=None, op0=ALU.mult),
                  r=["pC", ("EB", d)], w=[("tmpc", i)])
            P.dve(lambda e: e.tensor_tensor(out=tot[i][:], in0=tmpc[i][:], in1=pI[:, 0:257], op=ALU.add),
                  r=[("tmpc", i), "pI"], w=[("tot", i)])
            P.dve(lambda e: e.tensor_scalar(out=sm[i][:, 3:4], in0=tot[i][:, 256:257], scalar1=-1.0, scalar2=1.0, op0=ALU.mult, op1=ALU.max),
                  r=[("tot", i)], w=[("sm", i)])
            P.dve(lambda e: e.tensor_tensor(out=sm[i][:, 0:1], in0=tot[i][:, 256:257], in1=sm[i][:, 3:4], op=ALU.max),
                  r=[("tot", i), ("sm", i)], w=[("sm", i)])
            P.dve(lambda e: e.reciprocal(out=sm[i][:, 1:2], in_=sm[i][:, 0:1]), r=[("sm", i)], w=[("sm", i)])
            if d == 0:
                P.dve(lambda e: e.tensor_scalar(out=hacc[:, c, :], in0=tot[i][:, 0:256], scalar1=sm[i][:, 1:2], scalar2=None, op0=ALU.mult),
                      r=[("tot", i), ("sm", i)], w=[("hacc", c)])
                return
            gi = cnt["g"] % 2
            cnt["g"] += 1
            P.dma("sp", gat[gi][:], gA[c * 128:(c + 1) * 128, h * 256:(h + 1) * 256], w=[("gat", gi)])
            P.dve(lambda e: e.scalar_tensor_tensor(out=hs[i][:], in0=tot[i][:, 0:256], scalar=sm[i][:, 1:2], in1=hacc[:, c, :],
                                                   op0=ALU.mult, op1=ALU.add),
                  r=[("tot", i), ("sm", i), ("hacc", c)], w=[("hs", i)])
            P.dve(lambda e: e.tensor_tensor(out=sqv[i][:], in0=hs[i][:], in1=hs[i][:], op=ALU.mult), r=[("hs", i)], w=[("sqv", i)])
            P.dve(lambda e: e.reduce_sum(out=sm[i][:, 2:3], in_=sqv[i][:], axis=AX.X), r=[("sqv", i)], w=[("smb", i)])
            P.act(lambda e: e.activation(out=sm[i][:, 2:3], in_=sm[i][:, 2:3], func=AF.Ln, scale=1.0 / 256.0, bias=NORM_EPS),
                  r=[("smb", i)], w=[("smb", i)])
            P.act(lambda e: e.activation(out=sm[i][:, 2:3], in_=sm[i][:, 2:3], func=AF.Exp, scale=-0.5),
                  r=[("smb", i)], w=[("smb", i)])
            P.dve(lambda e: e.scalar_tensor_tensor(out=sqv[i][:], in0=hs[i][:], scalar=sm[i][:, 2:3],
                                                   in1=mlng[:, h * 256:(h + 1) * 256], op0=ALU.mult, op1=ALU.mult),
                  r=[("hs", i), ("smb", i), "mlng"], w=[("sqv", i)])
            P.dve(lambda e: e.tensor_tensor(out=yab[i][:], in0=sqv[i][:], in1=gat[gi][:], op=ALU.mult),
                  r=[("sqv", i), ("gat", gi)], w=[("yab", i)])
            for k in range(2):
                P.pe(lambda e, k=k: e.transpose(out=ptry[:, k * 128:(k + 1) * 128], in_=yab[i][:, k * 128:(k + 1) * 128],
                                                identity=identb[:]), r=[("yab", i), "identb"], w=["ptry"])
            P.act(lambda e: e.activation(out=yas[i][:, :, :], in_=ptry[:, 0:256].rearrange("p (k t) -> p k t", k=2), func=AF.Copy),
                  r=["ptry"], w=[("yas", i)])
            P.dma("sp", yaT[h * 256:(h + 1) * 256, c * 128:(c + 1) * 128].rearrange("(k p) t -> p k t", p=128),
                  yas[i][:, :, :], r=[("yas", i)])

        def update_step(h, d, c):
            i = cnt["u"] % 2
            cnt["u"] += 1
            col = c * 4 + h
            P.pool(lambda e: e.tensor_scalar(out=vw[i][:], in0=va[:, c, :], scalar1=WST[d][:, col:col + 1], scalar2=None, op0=ALU.mult),
                   r=["va", "va1", ("WST", d)], w=[("vw", i)])
            for dkc in range(2):
                P.pe(lambda e, dkc=dkc: e.matmul(pU[dkc][:, 0:257], ktok[:, c, dkc * 128:(dkc + 1) * 128], vw[i][:, :], start=True, stop=True),
                     r=[("ktok", c), ("vw", i)], w=[("pU", dkc)])
                P.dve(lambda e, dkc=dkc: e.scalar_tensor_tensor(out=Cst[d][:, dkc, :], in0=Cst[d][:, dkc, :], scalar=DEC[d][:, col:col + 1],
                                                                in1=pU[dkc][:, 0:257], op0=ALU.mult, op1=ALU.add),
                      r=[("Cst", d, dkc), ("DEC", d), ("pU", dkc)], w=[("Cst", d, dkc)])
                P.act(lambda e, dkc=dkc: e.activation(out=Cb[d][:, dkc, :], in_=Cst[d][:, dkc, :], func=AF.Copy),
                      r=[("Cst", d, dkc)], w=[("Cb", d)])

        for h in range(4):
            for dkc in range(2):
                r0 = h * 256 + dkc * 128
                P.dma("sp", qT[:, dkc, :], mqT[r0:r0 + 128, 0:OWN], w=["qT"])
                P.dma("sp", kT[:, dkc, :], mkT[r0:r0 + 128, :], w=["kT"])
            P.dma("sp", va[:, :, 0:256], mv[:, h * 256:(h + 1) * 256].rearrange("(c p) f -> p c f", p=128), w=["va"])
            for d in range(2):
                P.dve(lambda e, d=d: e.memset(Cst[d][:], 0.0), w=[("Cst", d, 0), ("Cst", d, 1)])
                P.dve(lambda e, d=d: e.memset(Cb[d][:], 0.0), w=[("Cb", d)])
            for c in range(64):
                for dkc in range(2):
                    P.pe(lambda e, c=c, dkc=dkc: e.transpose(out=ptrk[:, dkc * 128:(dkc + 1) * 128],
                                                             in_=kT[:, dkc, c * 128:(c + 1) * 128], identity=identb[:]),
                         r=["kT", "identb"], w=["ptrk"])
                if c % 2 == 0:
                    P.act(lambda e, c=c: e.activation(out=ktok[:, c, :], in_=ptrk[:, 0:256], func=AF.Copy), r=["ptrk"], w=[("ktok", c)])
                else:
                    P.dve(lambda e, c=c: e.tensor_copy(out=ktok[:, c, :], in_=ptrk[:, 0:256]), r=["ptrk"], w=[("ktok", c)])
            for i in range(32):
                output_step(h, 0, i)
                if i < 31:
                    update_step(h, 0, i)
                update_step(h, 1, 63 - i)
            for i in range(32, 64):
                c = 63 - i
                output_step(h, 1, c)
                if c > 0:
                    update_step(h, 1, c)
        P.flush()


def phase_D(nc, P, T):
    dqT, dkT, dv, gB, ybT = T["dqT"], T["dkT"], T["dv"], T["gB"], T["ybT"]
    with contextlib.ExitStack() as st:
        sb = lambda name, shape, dt=F32: st.enter_context(nc.sbuf_tensor("D_" + name, list(shape), dt))
        ps = lambda name, shape, dt=F32: st.enter_context(nc.psum_tensor("D_" + name, list(shape), dt))
        kT = [sb("dkT%d" % i, [128, S], BF16) for i in range(2)]
        va = [sb("dva%d" % i, [128, 64, 129], BF16) for i in range(2)]
        qT = [sb("dqT%d" % i, [128, OWN], BF16) for i in range(2)]
        NE = 3
        E = [sb("dE%d" % i, [128, 1024], BF16) for i in range(NE)]
        gbt = [sb("dgb%d" % i, [128, 4, 128], BF16) for i in range(2)]
        lamr = sb("lamr", [128, 256])
        ltmp = sb("ltmp", [128, 128])
        lam = sb("lam", [128, 4])
        subg = sb("subg", [128, 128])
        identf = sb("identf", [128, 128])
        identb = sb("identb", [128, 128], BF16)
        r12 = [sb("r12_%d" % i, [128, 4]) for i in range(2)]
        o1 = [sb("o1_%d" % i, [128, 128]) for i in range(2)]
        o2 = [sb("o2_%d" % i, [128, 128]) for i in range(2)]
        sq = [sb("sq_%d" % i, [128, 128]) for i in range(2)]
        ybq = [sb("ybq%d" % i, [128, 128], BF16) for i in range(2)]
        ybs = [sb("ybs%d" % i, [128, 512], BF16) for i in range(2)]
        pS = [ps("pS%d" % i, [128, 1024]) for i in range(2)]
        pacc = [ps("pacc%d" % i, [128, 512]) for i in range(3)]
        ptr = ps("dptr", [128, 512], BF16)

        P.dma("sp", lamr[:], T["lam_rep"], w=["lamr"])
        P.dma("sp", subg[:], T["subln_rep"], w=["subg"])
        P.dma("sp", identf[:], T["c_ident"], w=["identf"])
        P.dve(lambda e: e.tensor_copy(out=identb[:], in_=identf[:]), r=["identf"], w=["identb"])
        P.dve(lambda e: e.tensor_scalar(out=subg[:], in0=subg[:], scalar1=(1.0 - LAMBDA_INIT), scalar2=None, op0=ALU.mult),
              r=["subg"], w=["subg"])
        P.dve(lambda e: e.tensor_tensor(out=ltmp[:, 0:64], in0=lamr[:, 0:64], in1=lamr[:, 64:128], op=ALU.mult),
              r=["lamr"], w=["ltmp"])
        P.dve(lambda e: e.tensor_tensor(out=ltmp[:, 64:128], in0=lamr[:, 128:192], in1=lamr[:, 192:256], op=ALU.mult),
              r=["lamr"], w=["ltmp"])
        P.dve(lambda e: e.reduce_sum(out=lam[:, 0:2], in_=ltmp[:, :].rearrange("p (a b) -> p a b", a=2), axis=AX.X),
              r=["ltmp"], w=["lam"])
        P.act(lambda e: e.activation(out=lam[:, 0:2], in_=lam[:, 0:2], func=AF.Exp), r=["lam"], w=["lam"])
        P.dve(lambda e: e.tensor_tensor(out=lam[:, 2:3], in0=lam[:, 0:1], in1=lam[:, 1:2], op=ALU.subtract),
              r=["lam"], w=["lam"])
        P.dve(lambda e: e.tensor_scalar(out=lam[:, 3:4], in0=lam[:, 2:3], scalar1=LAMBDA_INIT, scalar2=-1.0,
                                        op0=ALU.add, op1=ALU.mult), r=["lam"], w=["lam"])
        for i in range(2):
            P.dve(lambda e, i=i: e.memset(va[i][:, :, 128:129], 1.0), w=[("va1", i)])

        def acc_ap(a, lo, hi):
            return pacc[a // 3][:, (a % 3) * 129 + lo:(a % 3) * 129 + hi]

        accS = [sb("accS%d" % i, [128, 8 * 129]) for i in range(2)]
        mhalf = sb("mhalf", [128, 1])
        P.dve(lambda e: e.memset(mhalf[:], -0.5), w=["mhalf"])

        def load_head(h):
            hs = h % 2
            P.dma("sp", kT[hs][:], dkT[h * 128:(h + 1) * 128, :], w=[("kT", hs)])
            P.dma("sp", qT[hs][:], dqT[h * 128:(h + 1) * 128, :], w=[("qT", hs)])
            P.dma("sp", va[hs][:, :, 0:128], dv[:, h * 128:(h + 1) * 128].rearrange("(kb p) c -> p kb c", p=128),
                  w=[("va", hs)])

        def load_gb(h, qt):
            gs = (h * 8 + qt) % 2
            P.dma("sp", gbt[gs][:], gB[qt * 512:(qt + 1) * 512, h * 128:(h + 1) * 128].rearrange("(u p) c -> p u c", p=128),
                  w=[("gbt", gs)])

        def qk(i, h, qt, kb):
            sl, hs = i % 2, h % 2
            for j in range(2):
                P.pe(lambda e, j=j: e.matmul(
                    pS[sl][:, j * 512:(j + 1) * 512], kT[hs][j * 64:(j + 1) * 64, kb * 128:(kb + 1) * 128],
                    qT[hs][j * 64:(j + 1) * 64, qt * 512:(qt + 1) * 512], start=True, stop=True),
                    r=[("kT", hs), ("qT", hs)], w=[("pS", sl)])

        def ex(i):
            sl, es = i % 2, i % NE
            P.act(lambda e: e.activation(out=E[es][:, :], in_=pS[sl][:, :], func=AF.Exp), r=[("pS", sl)], w=[("E", es)])

        def pv(i, h, kb):
            es, hs = i % NE, h % 2
            for a in range(8):
                j, u = a // 4, a % 4
                P.pe(lambda e, a=a, j=j, u=u: e.matmul(
                    acc_ap(a, 0, 129), E[es][:, j * 512 + u * 128:j * 512 + (u + 1) * 128], va[hs][:, kb, :],
                    start=(kb == 0 and a % 3 == 0), stop=(kb == 63), skip_group_check=True),
                    r=[("E", es), ("va", hs), ("va1", hs)], w=[("accb", a // 3)])

        epi = [0]

        def epilogue(h, qt):
            gs = (h * 8 + qt) % 2
            ys = gs
            ai = gs
            A = accS[ai]
            for b in range(3):
                n = 387 if b < 2 else 258
                P.dve(lambda e, b=b, n=n: e.tensor_copy(out=A[:, b * 387:b * 387 + n], in_=pacc[b][:, 0:n]),
                      r=[("accb", b)], w=[("accS", ai)])
            sa = lambda a, lo, hi: A[:, a * 129 + lo:a * 129 + hi]
            for u in range(4):
                ep = epi[0] % 2
                epi[0] += 1
                a0, a1 = u, 4 + u
                P.dve(lambda e, ep=ep, a0=a0: e.reciprocal(out=r12[ep][:, 0:1], in_=sa(a0, 128, 129)),
                      r=[("accS", ai)], w=[("r12", ep)])
                P.dve(lambda e, ep=ep, a1=a1: e.reciprocal(out=r12[ep][:, 1:2], in_=sa(a1, 128, 129)),
                      r=[("accS", ai)], w=[("r12", ep)])
                P.dve(lambda e, ep=ep: e.tensor_tensor(out=r12[ep][:, 2:3], in0=r12[ep][:, 1:2], in1=lam[:, 3:4], op=ALU.mult),
                      r=[("r12", ep), "lam"], w=[("r12", ep)])
                P.dve(lambda e, ep=ep, a0=a0: e.tensor_scalar(out=o1[ep][:], in0=sa(a0, 0, 128), scalar1=r12[ep][:, 0:1],
                                                              scalar2=None, op0=ALU.mult),
                      r=[("accS", ai), ("r12", ep)], w=[("o1", ep)])
                P.dve(lambda e, ep=ep, a1=a1: e.scalar_tensor_tensor(out=o2[ep][:], in0=sa(a1, 0, 128), scalar=r12[ep][:, 2:3],
                                                                     in1=o1[ep][:], op0=ALU.mult, op1=ALU.add),
                      r=[("accS", ai), ("r12", ep), ("o1", ep)], w=[("o2", ep)])
                P.dve(lambda e, ep=ep: e.tensor_tensor(out=sq[ep][:], in0=o2[ep][:], in1=o2[ep][:], op=ALU.mult),
                      r=[("o2", ep)], w=[("sq", ep)])
                P.dve(lambda e, ep=ep: e.reduce_sum(out=r12[ep][:, 3:4], in_=sq[ep][:], axis=AX.X),
                      r=[("sq", ep)], w=[("r12b", ep)])
                P.dve(lambda e, ep=ep: e.tensor_scalar(out=r12[ep][:, 3:4], in0=r12[ep][:, 3:4], scalar1=1.0 / 128.0, scalar2=NORM_EPS,
                                                       op0=ALU.mult, op1=ALU.add), r=[("r12b", ep)], w=[("r12b", ep)])
                P.pool(lambda e, ep=ep: e.tensor_tensor(out=r12[ep][:, 3:4], in0=r12[ep][:, 3:4], in1=mhalf[:, 0:1], op=ALU.pow),
                       r=[("r12b", ep), "mhalf"], w=[("r12b", ep)])
                P.dve(lambda e, ep=ep: e.scalar_tensor_tensor(out=o1[ep][:], in0=o2[ep][:], scalar=r12[ep][:, 3:4],
                                                              in1=subg[:], op0=ALU.mult, op1=ALU.mult),
                      r=[("o2", ep), ("r12b", ep), "subg"], w=[("o1", ep)])
                P.dve(lambda e, ep=ep, u=u: e.tensor_tensor(out=ybq[ep][:], in0=o1[ep][:], in1=gbt[gs][:, u, :], op=ALU.mult),
                      r=[("o1", ep), ("gbt", gs)], w=[("ybq", ep)])
                P.pe(lambda e, ep=ep, u=u: e.transpose(out=ptr[:, u * 128:(u + 1) * 128], in_=ybq[ep][:], identity=identb[:]),
                     r=[("ybq", ep), "identb"], w=["ptr"])
                P.dve(lambda e, u=u: e.tensor_copy(out=ybs[ys][:, u * 128:(u + 1) * 128], in_=ptr[:, u * 128:(u + 1) * 128]),
                      r=["ptr"], w=[("ybs", ys)])
            P.dma("sp", ybT[h * 128:(h + 1) * 128, qt * 512:(qt + 1) * 512], ybs[ys][:], r=[("ybs", ys)])

        blocks = [(h, qt, kb) for h in range(8) for qt in range(8) for kb in range(64)]
        nb = len(blocks)
        load_head(0)
        load_gb(0, 0)
        qk(0, *blocks[0])
        qk(1, *blocks[1])
        for i, (h, qt, kb) in enumerate(blocks):
            if kb == 0 and qt == 0 and h + 1 < 8:
                load_head(h + 1)
            if kb == 0:
                nxt = h * 8 + qt + 1
                if nxt < 64:
                    load_gb(nxt // 8, nxt % 8)
            ex(i)
            pv(i, h, kb)
            if i + 2 < nb:
                qk(i + 2, *blocks[i + 2])
            if kb == 63:
                epilogue(h, qt)
        P.flush()


def phase_O(nc, P, T):
    x, out, yaT, ybT, sgT = T["x"], T["out"], T["yaT"], T["ybT"], T["sgT"]
    with contextlib.ExitStack() as st:
        sb = lambda name, shape, dt=F32: st.enter_context(nc.sbuf_tensor("O_" + name, list(shape), dt))
        ps = lambda name, shape, dt=F32: st.enter_context(nc.psum_tensor("O_" + name, list(shape), dt))
        Wa = sb("Wa", [128, 8, D], BF16)
        Wb = sb("Wb", [128, 8, D], BF16)
        Wo = sb("Wo", [128, 8, D], BF16)
        fing = sb("fing", [128, D])
        ya = [sb("ya%d" % i, [128, 8, 512], BF16) for i in range(2)]
        yb = [sb("yb%d" % i, [128, 8, 512], BF16) for i in range(2)]
        sa = [sb("sa%d" % i, [128, 8, 512], BF16) for i in range(2)]
        sbb = [sb("sb%d" % i, [128, 8, 512], BF16) for i in range(2)]
        mixT = [sb("mixT%d" % i, [128, 8, 512], BF16) for i in range(2)]
        t1 = [sb("t1_%d" % i, [128, 512]) for i in range(2)]
        t2 = [sb("t2_%d" % i, [128, 512]) for i in range(2)]
        xt = [sb("xt%d" % i, [128, D]) for i in range(2)]
        xo = [sb("xo%d" % i, [128, D]) for i in range(2)]
        junk = sb("junk", [128, D], BF16)
        ssq = [sb("ssq%d" % i, [128, 2]) for i in range(2)]
        pa = [ps("pa%d" % i, [128, 512]) for i in range(2)]
        pb = [ps("pb%d" % i, [128, 512]) for i in range(2)]
        po = [ps("po%d" % i, [128, 512]) for i in range(4)]

        for (W, src, key) in ((Wa, T["w_a"], "Wa"), (Wb, T["w_b"], "Wb"), (Wo, T["w_o"], "Wo")):
            for hh in range(2):
                P.dma("pool", W[:, :, hh * 512:(hh + 1) * 512],
                      src[:, hh * 512:(hh + 1) * 512].rearrange("(kc p) c -> p kc c", p=128), w=[(key, hh)])
        P.dma("sp", fing[:], T["fing_rep"], w=["fing"])
        wkeys = lambda k: [(k, 0), (k, 1)]
        cnt = {"p": 0, "o": 0, "x": 0}
        for tt in range(8):
            sl = tt % 2
            ts_ = slice(tt * 512, (tt + 1) * 512)
            P.dma("sp", ya[sl][:], yaT[:, ts_].rearrange("(cc p) t -> p cc t", p=128), w=[("ya", sl)])
            P.dma("sp", yb[sl][:], ybT[:, ts_].rearrange("(cc p) t -> p cc t", p=128), w=[("yb", sl)])
            P.dma("sp", sa[sl][:], sgT[0, :, ts_].rearrange("(cc p) t -> p cc t", p=128), w=[("sa", sl)])
            P.dma("sp", sbb[sl][:], sgT[1, :, ts_].rearrange("(cc p) t -> p cc t", p=128), w=[("sb", sl)])
            for dd in range(8):
                i = cnt["p"] % 2
                cnt["p"] += 1
                for cc in range(8):
                    P.pe(lambda e, i=i, cc=cc, dd=dd, sl=sl: e.matmul(pa[i][:, :], Wa[:, cc, dd * 128:(dd + 1) * 128], ya[sl][:, cc, :],
                                                                      start=(cc == 0), stop=(cc == 7)),
                         r=wkeys("Wa") + [("ya", sl)], w=[("pa", i)])
                for cc in range(8):
                    P.pe(lambda e, i=i, cc=cc, dd=dd, sl=sl: e.matmul(pb[i][:, :], Wb[:, cc, dd * 128:(dd + 1) * 128], yb[sl][:, cc, :],
                                                                      start=(cc == 0), stop=(cc == 7)),
                         r=wkeys("Wb") + [("yb", sl)], w=[("pb", i)])
                P.dve(lambda e, i=i, dd=dd, sl=sl: e.tensor_tensor(out=t1[i][:], in0=pa[i][:, :], in1=sa[sl][:, dd, :], op=ALU.mult),
                      r=[("pa", i), ("sa", sl)], w=[("t1", i)])
                P.dve(lambda e, i=i, dd=dd, sl=sl: e.tensor_tensor(out=t2[i][:], in0=pb[i][:, :], in1=sbb[sl][:, dd, :], op=ALU.mult),
                      r=[("pb", i), ("sb", sl)], w=[("t2", i)])
                P.pool(lambda e, i=i, dd=dd, sl=sl: e.tensor_tensor(out=mixT[sl][:, dd, :], in0=t1[i][:], in1=t2[i][:], op=ALU.add),
                       r=[("t1", i), ("t2", i)], w=[("mixT", sl)])
            for u in range(4):
                xs = cnt["x"] % 2
                cnt["x"] += 1
                r0 = tt * 512 + u * 128
                P.dma("sp", xt[xs][:], x[r0:r0 + 128, :], w=[("xt", xs)])
                for eg in range(2):
                    o = cnt["o"] % 4
                    cnt["o"] += 1
                    for dd in range(8):
                        P.pe(lambda e, o=o, dd=dd, sl=sl, u=u, eg=eg: e.matmul(
                            po[o][:, :], mixT[sl][:, dd, u * 128:(u + 1) * 128], Wo[:, dd, eg * 512:(eg + 1) * 512],
                            start=(dd == 0), stop=(dd == 7)), r=[("mixT", sl), ("Wo", eg)], w=[("po", o)])
                    P.dve(lambda e, o=o, xs=xs, eg=eg: e.tensor_tensor(out=xo[xs][:, eg * 512:(eg + 1) * 512], in0=po[o][:, :],
                                                                       in1=xt[xs][:, eg * 512:(eg + 1) * 512], op=ALU.add),
                          r=[("po", o), ("xt", xs)], w=[("xo", xs, eg)])
                P.act(lambda e, xs=xs: e.activation(out=junk[:], in_=xo[xs][:], func=AF.Square, accum_out=ssq[xs][:, 0:1]),
                      r=[("xo", xs, 0), ("xo", xs, 1)], w=[("ssq", xs)])
                P.act(lambda e, xs=xs: e.activation(out=ssq[xs][:, 0:1], in_=ssq[xs][:, 0:1], func=AF.Sqrt, scale=1.0 / D, bias=NORM_EPS),
                      r=[("ssq", xs)], w=[("ssq", xs)])
                P.dve(lambda e, xs=xs: e.reciprocal(out=ssq[xs][:, 1:2], in_=ssq[xs][:, 0:1]), r=[("ssq", xs)], w=[("ssq", xs)])
                P.dve(lambda e, xs=xs: e.scalar_tensor_tensor(out=xo[xs][:], in0=xo[xs][:], scalar=ssq[xs][:, 1:2], in1=fing[:],
                                                              op0=ALU.mult, op1=ALU.mult),
                      r=[("xo", xs, 0), ("xo", xs, 1), ("ssq", xs), "fing"], w=[("xo", xs, 0), ("xo", xs, 1)])
                P.dma("sp", out[r0:r0 + 128, :], xo[xs][:], r=[("xo", xs, 0), ("xo", xs, 1)])
        P.flush()


def make_in_maps(x, positions, norm_g, w_in, ml_gate_b, ml_conv_w, ml_norm_g, da_lambda,
                 da_subln_g, gate_b, w_branch_a, w_branch_b, w_out, final_g):
    f32 = np.float32
    x = np.asarray(x, f32)
    positions = np.asarray(positions, np.int32)
    w_in0 = np.ascontiguousarray(np.asarray(w_in, f32)[0])
    gb0 = np.asarray(ml_gate_b, f32)[0]
    cw0 = np.asarray(ml_conv_w, f32)[0]
    w_in1 = w_in0.copy()
    ag = w_in0[:, C_AG:C_AG + 16].reshape(D, 4, 4)
    w_in1[:, C_AG:C_AG + 16] = ag[:, [2, 3, 0, 1], :].reshape(D, 16)
    gb1 = gb0[[2, 3, 0, 1], :]
    cw1 = cw0[::-1, :]
    rep = lambda v, n=128: np.ascontiguousarray(np.broadcast_to(np.asarray(v, f32).reshape(1, -1), (n, np.asarray(v).size)))
    ident = np.eye(128, dtype=f32)
    rsw = np.zeros((128, 128), f32)
    for r in range(128):
        m = r % 64
        if m < 32:
            rsw[r + 32, r] = -1.0
        else:
            rsw[r - 32, r] = 1.0
    invf = (10000.0 ** (-np.arange(0, 64, 2, dtype=f32) / f32(64))).astype(f32)
    invf_p = np.array([invf[(p % 64) % 32] for p in range(128)], f32).reshape(128, 1)
    ii = np.arange(128)
    U = (ii[:, None] <= ii[None, :]).astype(f32)
    L = (ii[:, None] >= ii[None, :]).astype(f32)
    NEG = -30000.0
    tri = np.stack([U, L, (1 - U) * NEG, (1 - L) * NEG], axis=1).astype(f32)
    common = {
        "normg_rep": rep(np.asarray(norm_g, f32)[0]),
        "mlng_rep": rep(np.asarray(ml_norm_g, f32)[0].reshape(-1)),
        "lam_rep": rep(np.asarray(da_lambda, f32)[0].reshape(-1)),
        "subln_rep": rep(np.asarray(da_subln_g, f32)[0]),
        "gb_fm": np.ascontiguousarray(np.asarray(gate_b, f32)[0].reshape(2, 8, 128).transpose(2, 0, 1)),
        "w_a": np.ascontiguousarray(np.asarray(w_branch_a, f32)[0]),
        "w_b": np.ascontiguousarray(np.asarray(w_branch_b, f32)[0]),
        "w_o": np.ascontiguousarray(np.asarray(w_out, f32)[0]),
        "fing_rep": rep(np.asarray(final_g, f32)),
        "c_ident": ident, "c_rswap": rsw, "c_invf": invf_p, "c_tri": tri,
    }
    in_maps = []
    for core in range(NCORES):
        b, half = core // 2, core % 2
        xb = x[b]
        pb = positions[b]
        if half == 1:
            xb = xb[::-1]
            pb = pb[::-1]
        gbx = gb1 if half else gb0
        cwx = cw1 if half else cw0
        m = dict(common)
        m["x"] = np.ascontiguousarray(xb)
        m["posr"] = np.ascontiguousarray(np.broadcast_to(pb.reshape(1, S), (128, S))).astype(np.int32)
        m["w_in"] = w_in1 if half else w_in0
        m["gateb_rep"] = rep(gbx.reshape(-1))
        m["convw"] = np.ascontiguousarray(cwx.reshape(5, 16, 128).transpose(2, 1, 0))
        in_maps.append(m)
    return in_maps


_NC_CACHE = {}


def kernel(**inputs):
    in_maps = make_in_maps(**inputs)
    if "nc" not in _NC_CACHE:
        _NC_CACHE["nc"] = build_nc()
    nc = _NC_CACHE["nc"]
    res = run_bass_kernel_spmd(nc, in_maps, core_ids=list(range(NCORES)))
    B = 4
    outp = np.empty((B, S, D), np.float32)
    for core in range(NCORES):
        b, half = core // 2, core % 2
        o = np.asarray(res.results[core]["out"], np.float32)
        if half == 0:
            outp[b, :OWN] = o
        else:
            outp[b, OWN:] = o[::-1]
    return outp
```

```python
import contextlib
import math
import numpy as np
import concourse.bass as bass
import concourse.mybir as mybir
from concourse.bass_utils import run_bass_kernel_spmd

F32, BF16, I32 = mybir.dt.float32, mybir.dt.bfloat16, mybir.dt.int32
AF = mybir.ActivationFunctionType
ALU = mybir.AluOpType
AX = mybir.AxisListType

D = 1024
S = 8192
OWN = 4096
NCORES = 8
PROJ = 11280
C_AQ, C_AK, C_AV, C_AO, C_AZ, C_AG = 0, 1024, 2048, 3072, 4096, 5120
C_BQ, C_BK, C_BV, C_BZ, C_GA, C_GB = 5136, 6160, 7184, 8208, 9232, 10256
NORM_EPS = 1e-6
LAMBDA_INIT = 0.8 - 0.6 * math.exp(-0.3 * 0)
SAME_ENGINE_SYNC = True
SAME_ENGINE_RAW_ONLY = True


class _Op:
    __slots__ = ("eng", "fn", "dma", "clock", "idx", "waits", "signal", "semval", "sem", "know")


class Prog:
    ENGS = ("sp", "act", "dve", "pool", "pe")

    def __init__(self, nc, stack, n_dma=8):
        self.nc = nc
        self.sem = {e: stack.enter_context(nc.semaphore("cs_" + e)) for e in self.ENGS}
        self.dsem = {q: [stack.enter_context(nc.semaphore("ds_%s_%d" % (q, i))) for i in range(n_dma)]
                     for q in ("sp", "pool", "act")}
        self.n_dma = n_dma
        self.dcount = {q: 0 for q in self.dsem}
        self.dlast = {}
        self.cnt = {e: 0 for e in self.ENGS}
        self.sigcnt = {e: 0 for e in self.ENGS}
        self.know = {e: {} for e in self.ENGS}
        self.pending = {e: [] for e in self.ENGS}
        self.last_w = {}
        self.readers = {}
        self.nops = 0

    def add(self, eng, fn, reads=(), writes=(), dma=False, extra_deps=()):
        op = _Op()
        op.eng, op.fn, op.dma, op.signal, op.semval = eng, fn, dma, False, None
        self.nops += 1
        deps = []
        seen = set()

        raw = set()

        def push(d, is_raw=False):
            if d is None:
                return
            if is_raw:
                raw.add(id(d))
            if id(d) not in seen:
                seen.add(id(d))
                deps.append(d)

        for k in reads:
            push(self.last_w.get(k), True)
        for k in writes:
            push(self.last_w.get(k))
            for r in self.readers.get(k, ()):
                push(r)
        for d in extra_deps:
            push(d, True)
        if dma:
            slot = self.dcount[eng] % self.n_dma
            self.dcount[eng] += 1
            op.clock = (eng, slot)
            prev = self.dlast.get(op.clock)
            op.idx = (prev.idx + 1) if prev is not None else 1
            op.sem = self.dsem[eng][slot]
            op.semval = 16 * op.idx
            push(prev)
            self.dlast[op.clock] = op
        else:
            self.cnt[eng] += 1
            op.clock = eng
            op.idx = self.cnt[eng]
            op.sem = self.sem[eng]
        know = self.know[eng]
        waits = []
        for d in deps:
            if (not d.dma) and (not dma) and d.eng == eng:
                if eng == "pe" or not SAME_ENGINE_SYNC:
                    continue
                if SAME_ENGINE_RAW_ONLY and id(d) not in raw:
                    continue
            if know.get(d.clock, 0) >= d.idx:
                continue
            waits.append(d)
            d.signal = True
            for c, v in d.know.items():
                if know.get(c, 0) < v:
                    know[c] = v
            if know.get(d.clock, 0) < d.idx:
                know[d.clock] = d.idx
        op.waits = waits
        op.know = dict(know)
        for k in writes:
            self.last_w[k] = op
            self.readers[k] = []
        for k in reads:
            self.readers.setdefault(k, []).append(op)
        self.pending[eng].append(op)
        return op

    def pe(self, fn, r=(), w=()):
        return self.add("pe", fn, r, w)

    def act(self, fn, r=(), w=()):
        return self.add("act", fn, r, w)

    def dve(self, fn, r=(), w=()):
        return self.add("dve", fn, r, w)

    def pool(self, fn, r=(), w=()):
        return self.add("pool", fn, r, w)

    def dma(self, q, out, in_, r=(), w=()):
        return self.add(q, lambda e: e.dma_start(out=out, in_=in_), r, w, dma=True)

    def flush(self, final=False):
        outstanding = [d for d in self.dlast.values()]
        self.add("sp", None, extra_deps=outstanding)
        for e in self.ENGS:
            for op in self.pending[e]:
                if not op.dma and op.signal:
                    self.sigcnt[e] += 1
                    op.semval = self.sigcnt[e]
                elif not op.dma:
                    op.semval = None
        pend = self.pending
        sems = self.sem

        def make_body(eng):
            ops = pend[eng]

            def body(e):
                for op in ops:
                    for d in op.waits:
                        assert d.semval is not None
                        e.wait_ge(d.sem, d.semval)
                    if op.fn is None:
                        continue
                    ins = op.fn(e)
                    if op.dma:
                        ins.then_inc(op.sem, 16)
                    elif op.signal:
                        ins.then_inc(sems[eng], 1)
            return body

        with self.nc.Block() as block:
            block.sync(make_body("sp"))
            block.scalar(make_body("act"))
            block.vector(make_body("dve"))
            block.gpsimd(make_body("pool"))
            block.tensor(make_body("pe"))
        self.pending = {e: [] for e in self.ENGS}
        self.last_w = {}
        self.readers = {}
        full = {}
        for e in self.ENGS:
            full[e] = self.cnt[e]
        for c, d in self.dlast.items():
            full[c] = d.idx
        self.know = {e: dict(full) for e in self.ENGS}


def build_nc(debug=False, phases=("P", "M", "D", "O")):
    nc = bass.Bass("TRN2", target_bir_lowering=False)
    IN = lambda name, shape, dt=F32: nc.dram_tensor(name, list(shape), dt, kind="ExternalInput").ap()
    skind = "ExternalOutput" if debug else "Internal"
    SCR = lambda name, shape, dt=BF16: nc.dram_tensor(name, list(shape), dt, kind=skind).ap()

    x = IN("x", [S, D])
    posr = IN("posr", [128, S], I32)
    w_in = IN("w_in", [D, PROJ])
    normg_rep = IN("normg_rep", [128, D])
    gateb_rep = IN("gateb_rep", [128, 16])
    convw = IN("convw", [128, 16, 5])
    mlng_rep = IN("mlng_rep", [128, D])
    lam_rep = IN("lam_rep", [128, 256])
    subln_rep = IN("subln_rep", [128, 128])
    gb_fm = IN("gb_fm", [128, 2, 8])
    w_a = IN("w_a", [D, D])
    w_b = IN("w_b", [D, D])
    w_o = IN("w_o", [D, D])
    fing_rep = IN("fing_rep", [128, D])
    c_ident = IN("c_ident", [128, 128])
    c_rswap = IN("c_rswap", [128, 128])
    c_invf = IN("c_invf", [128, 1])
    c_tri = IN("c_tri", [128, 4, 128])
    out = nc.dram_tensor("out", [OWN, D], F32, kind="ExternalOutput").ap()

    mqT = SCR("mqT", [D, S])
    mkT = SCR("mkT", [D, S])
    mv = SCR("mv", [S, D])
    dqT = SCR("dqT", [D, OWN])
    dkT = SCR("dkT", [D, S])
    dv = SCR("dv", [S, D])
    gA = SCR("gA", [OWN, D])
    gB = SCR("gB", [OWN, D])
    sgT = SCR("sgT", [2, D, OWN])
    gall_d = SCR("gall_d", [128, 64 * 16], F32)
    yaT = SCR("yaT", [D, OWN])
    ybT = SCR("ybT", [D, OWN])

    with contextlib.ExitStack() as top:
        P = Prog(nc, top)
        if "P" in phases:
            phase_P(nc, P, locals())
        if "M" in phases:
            phase_M(nc, P, locals())
        if "D" in phases:
            phase_D(nc, P, locals())
        if "O" in phases:
            phase_O(nc, P, locals())
    return nc


def phase_P(nc, P, T):
    x, posr, w_in = T["x"], T["posr"], T["w_in"]
    with contextlib.ExitStack() as st:
        sb = lambda name, shape, dt=F32: st.enter_context(nc.sbuf_tensor("P_" + name, list(shape), dt))
        ps = lambda name, shape, dt=F32: st.enter_context(nc.psum_tensor("P_" + name, list(shape), dt))
        hT = sb("hT", [128, 8, 2048], BF16)
        xt = [sb("xt%d" % i, [128, D]) for i in range(2)]
        xn = [sb("xn%d" % i, [128, D], BF16) for i in range(2)]
        junk = sb("junk", [128, D], BF16)
        ss = [sb("ss%d" % i, [128, 1]) for i in range(2)]
        rstd = [sb("rstd%d" % i, [128, 1]) for i in range(2)]
        grep = sb("grep", [128, D])
        NW = 3
        wB = [sb("wB%d" % i, [128, 8, 512], BF16) for i in range(NW)]
        wG = sb("wG", [128, 8, 16], BF16)
        NWK = 5
        WK = [sb("WK%d" % i, [128, 2054]) for i in range(NWK)]
        OB = [sb("OB%d" % i, [128, 8192], BF16) for i in range(2)]
        cosT = sb("cosT", [128, 2048])
        sinT = sb("sinT", [128, 2048])
        gall = sb("gall", [128, 64 * 16])
        gbrep = sb("gbrep", [128, 16])
        cw = sb("cw", [128, 16, 5])
        carry = sb("carry", [128, 16, 4])
        identf = sb("identf", [128, 128])
        identb = sb("identb", [128, 128], BF16)
        rswap = sb("rswap", [128, 128])
        invf = sb("invf", [128, 1])
        gbfm = sb("gbfm", [128, 2, 8])
        posi = sb("posi", [128, 2048], I32)
        ptr = [ps("ptr%d" % i, [128, D], BF16) for i in range(2)]
        pp = [ps("pp%d" % i, [128, 512]) for i in range(4)]
        prot = [ps("prot%d" % i, [128, 512]) for i in range(2)]

        P.dma("sp", grep[:], T["normg_rep"], w=["grep"])
        P.dma("sp", gbrep[:], T["gateb_rep"], w=["gbrep"])
        P.dma("sp", cw[:], T["convw"], w=["cw"])
        P.dma("sp", identf[:], T["c_ident"], w=["identf"])
        P.dma("sp", rswap[:], T["c_rswap"], w=["rswap"])
        P.dma("sp", invf[:], T["c_invf"], w=["invf"])
        P.dma("sp", gbfm[:], T["gb_fm"], w=["gbfm"])
        P.dma("pool", wG[:], w_in[:, C_AG:C_AG + 16].rearrange("(kc p) c -> p kc c", p=128), w=["wG"])
        P.dve(lambda e: e.tensor_copy(out=identb[:], in_=identf[:]), r=["identf"], w=["identb"])
        P.dve(lambda e: e.memset(carry[:], 0.0), w=["carry"])
        for i in range(NWK):
            P.dve(lambda e, i=i: e.memset(WK[i][:, 2052:2054], 0.0), w=[("WKz", i)])

        wslot = [0]

        def load_w(col0, ncols):
            s = wslot[0] % NW
            wslot[0] += 1
            P.dma("pool", wB[s][:, :, 0:ncols],
                  w_in[:, col0:col0 + ncols].rearrange("(kc p) c -> p kc c", p=128), w=[("wB", s)])
            return s

        ppslot = [0]

        def next_pp():
            s = ppslot[0] % 4
            ppslot[0] += 1
            return s

        wkslot = [0]

        def next_wk():
            s = wkslot[0] % NWK
            wkslot[0] += 1
            return s

        obslot = [0]

        def next_ob():
            s = obslot[0] % 2
            obslot[0] += 1
            return s

        def fm_mm(ws, wc, tq):
            s = next_pp()
            for kc in range(8):
                P.pe(lambda e, s=s, ws=ws, wc=wc, kc=kc, tq=tq: e.matmul(
                    pp[s][:, :], wB[ws][:, kc, wc * 128:(wc + 1) * 128], hT[:, kc, tq * 512:(tq + 1) * 512],
                    start=(kc == 0), stop=(kc == 7)),
                    r=[("wB", ws), "hT"], w=[("pp", s)])
            return s

        def tm_mm(ws, ncols, tt):
            s = next_pp()
            for kc in range(8):
                P.pe(lambda e, s=s, ws=ws, kc=kc, tt=tt, ncols=ncols: e.matmul(
                    pp[s][:, 0:ncols], hT[:, kc, tt * 128:(tt + 1) * 128], wB[ws][:, kc, 0:ncols],
                    start=(kc == 0), stop=(kc == 7)),
                    r=[("wB", ws), "hT"], w=[("pp", s)])
            return s

        for stile in range(4):
            own = stile < 2
            t0 = stile * 2048
            last = stile == 3
            for tt in range(16):
                sl = tt % 2
                r0 = t0 + tt * 128
                P.dma("sp", xt[sl][:], x[r0:r0 + 128, :], w=[("xt", sl)])
                P.act(lambda e, sl=sl: e.activation(out=junk[:], in_=xt[sl][:], func=AF.Square, accum_out=ss[sl][:]),
                      r=[("xt", sl)], w=[("ss", sl)])
                P.act(lambda e, sl=sl: e.activation(out=ss[sl][:], in_=ss[sl][:], func=AF.Sqrt,
                                                    scale=1.0 / D, bias=NORM_EPS),
                      r=[("ss", sl)], w=[("ss", sl)])
                P.dve(lambda e, sl=sl: e.reciprocal(out=rstd[sl][:], in_=ss[sl][:]), r=[("ss", sl)], w=[("rstd", sl)])
                P.dve(lambda e, sl=sl: e.scalar_tensor_tensor(out=xn[sl][:], in0=xt[sl][:], scalar=rstd[sl][:],
                                                              in1=grep[:], op0=ALU.mult, op1=ALU.mult),
                      r=[("xt", sl), ("rstd", sl), "grep"], w=[("xn", sl)])
                for kc in range(8):
                    P.pe(lambda e, sl=sl, kc=kc: e.transpose(out=ptr[sl][:, kc * 128:(kc + 1) * 128],
                                                             in_=xn[sl][:, kc * 128:(kc + 1) * 128],
                                                             identity=identb[:]),
                         r=[("xn", sl), "identb"], w=[("ptr", sl)])
                P.act(lambda e, sl=sl, tt=tt: e.activation(
                    out=hT[:, :, tt * 128:(tt + 1) * 128],
                    in_=ptr[sl][:, :].rearrange("p (k t) -> p k t", k=8), func=AF.Copy),
                    r=[("ptr", sl)], w=["hT"])

            P.dma("sp", posi[:], posr[:, t0:t0 + 2048], w=["posi"])
            build_rope_tables(P, posi, invf, cosT, sinT, WK, next_wk)

            for tt in range(16):
                s = next_pp()
                for kc in range(8):
                    P.pe(lambda e, s=s, kc=kc, tt=tt: e.matmul(
                        pp[s][:, 0:16], hT[:, kc, tt * 128:(tt + 1) * 128], wG[:, kc, :],
                        start=(kc == 0), stop=(kc == 7)), r=["wG", "hT"], w=[("pp", s)])
                gt = stile * 16 + tt
                P.dve(lambda e, s=s, gt=gt: e.tensor_tensor(out=gall[:, gt * 16:(gt + 1) * 16], in0=pp[s][:, 0:16],
                                                            in1=gbrep[:], op=ALU.add),
                      r=[("pp", s), "gbrep"], w=["gall"])

            for (c0, dst) in ((C_AV, T["mv"]), (C_BV, T["dv"])):
                for cg in range(2):
                    ws = load_w(c0 + cg * 512, 512)
                    ob = next_ob()
                    for tt in range(16):
                        s = tm_mm(ws, 512, tt)
                        eng = P.act if tt % 2 == 0 else P.dve
                        if tt % 2 == 0:
                            P.act(lambda e, s=s, ob=ob, tt=tt: e.activation(
                                out=OB[ob][:, tt * 512:(tt + 1) * 512], in_=pp[s][:, :], func=AF.Copy),
                                r=[("pp", s)], w=[("OB", ob)])
                        else:
                            P.dve(lambda e, s=s, ob=ob, tt=tt: e.tensor_copy(
                                out=OB[ob][:, tt * 512:(tt + 1) * 512], in_=pp[s][:, :]),
                                r=[("pp", s)], w=[("OB", ob)])
                    P.dma("sp", dst[t0:t0 + 2048, cg * 512:(cg + 1) * 512].rearrange("(tt p) c -> p tt c", p=128),
                          OB[ob][:, :].rearrange("p (tt c) -> p tt c", c=512), r=[("OB", ob)])

            for fam, (c0, dst, ksc) in enumerate(((C_AQ, T["mqT"], 1.0), (C_AK, T["mkT"], 1.0 / 16.0))):
                for g4 in range(2):
                    ws = load_w(c0 + g4 * 512, 512)
                    for wc in range(4):
                        fc = fam * 8 + g4 * 4 + wc
                        row0 = (g4 * 4 + wc) * 128
                        stg = next_wk()
                        acc = next_wk()
                        sg = next_wk()
                        ob = next_ob()
                        P.dve(lambda e, stg=stg, fc=fc: e.tensor_copy(out=WK[stg][:, 0:4], in_=carry[:, fc, :]),
                              r=["carry"], w=[("WK", stg)])
                        for tq in range(4):
                            s = fm_mm(ws, wc, tq)
                            P.act(lambda e, s=s, stg=stg, tq=tq: e.activation(
                                out=WK[stg][:, 4 + tq * 512:4 + (tq + 1) * 512], in_=pp[s][:, :], func=AF.Copy),
                                r=[("pp", s)], w=[("WK", stg)])
                        P.dve(lambda e, stg=stg, fc=fc: e.tensor_copy(out=carry[:, fc, :], in_=WK[stg][:, 2048:2052]),
                              r=[("WK", stg)], w=["carry"])
                        nout = 2050 if last else 2048
                        P.dve(lambda e, stg=stg, acc=acc, fc=fc, nout=nout: e.tensor_scalar(
                            out=WK[acc][:, 0:nout], in0=WK[stg][:, 0:nout], scalar1=cw[:, fc, 0:1], scalar2=None,
                            op0=ALU.mult), r=[("WK", stg), "cw", ("WKz", stg)], w=[("WK", acc)])
                        for j in range(1, 5):
                            P.dve(lambda e, stg=stg, acc=acc, fc=fc, nout=nout, j=j: e.scalar_tensor_tensor(
                                out=WK[acc][:, 0:nout], in0=WK[stg][:, j:j + nout], scalar=cw[:, fc, j:j + 1],
                                in1=WK[acc][:, 0:nout], op0=ALU.mult, op1=ALU.add),
                                r=[("WK", stg), "cw"], w=[("WK", acc)])
                        P.act(lambda e, acc=acc, sg=sg, nout=nout: e.activation(
                            out=WK[sg][:, 0:nout], in_=WK[acc][:, 0:nout], func=AF.Sigmoid),
                            r=[("WK", acc)], w=[("WK", sg)])
                        P.dve(lambda e, acc=acc, sg=sg, ob=ob, nout=nout, ksc=ksc: e.scalar_tensor_tensor(
                            out=OB[ob][:, 0:nout], in0=WK[acc][:, 0:nout], scalar=ksc, in1=WK[sg][:, 0:nout],
                            op0=ALU.mult, op1=ALU.mult), r=[("WK", acc), ("WK", sg)], w=[("OB", ob)])
                        if stile == 0:
                            P.dma("sp", dst[row0:row0 + 128, 0:nout - 2], OB[ob][:, 2:nout], r=[("OB", ob)])
                        else:
                            P.dma("sp", dst[row0:row0 + 128, t0 - 2:t0 - 2 + nout], OB[ob][:, 0:nout], r=[("OB", ob)])

            fams = [(C_BK, T["dkT"], 1.0, t0)]
            if own:
                fams.append((C_BQ, T["dqT"], 0.125, t0))
            for (c0, dst, sc, tcol) in fams:
                for g4 in range(2):
                    ws = load_w(c0 + g4 * 512, 512)
                    for wc in range(4):
                        row0 = (g4 * 4 + wc) * 128
                        ob = next_ob()
                        for tq in range(4):
                            s = fm_mm(ws, wc, tq)
                            wk = next_wk()
                            pr = tq % 2
                            cs = slice(tq * 512, (tq + 1) * 512)
                            P.act(lambda e, s=s, wk=wk: e.activation(out=WK[wk][:, 0:512], in_=pp[s][:, :], func=AF.Copy),
                                  r=[("pp", s)], w=[("WK", wk)])
                            P.pe(lambda e, wk=wk, pr=pr: e.matmul(prot[pr][:, :], rswap[:, :], WK[wk][:, 0:512],
                                                                  start=True, stop=True),
                                 r=[("WK", wk), "rswap"], w=[("prot", pr)])
                            P.dve(lambda e, wk=wk, cs=cs, sc=sc: e.scalar_tensor_tensor(
                                out=WK[wk][:, 512:1024], in0=WK[wk][:, 0:512], scalar=sc, in1=cosT[:, cs],
                                op0=ALU.mult, op1=ALU.mult), r=[("WK", wk), "cosT"], w=[("WK", wk)])
                            P.dve(lambda e, wk=wk, cs=cs, sc=sc, pr=pr: e.scalar_tensor_tensor(
                                out=WK[wk][:, 1024:1536], in0=prot[pr][:, :], scalar=sc, in1=sinT[:, cs],
                                op0=ALU.mult, op1=ALU.mult), r=[("prot", pr), "sinT"], w=[("WK", wk)])
                            P.dve(lambda e, wk=wk, cs=cs, ob=ob: e.tensor_tensor(
                                out=OB[ob][:, cs], in0=WK[wk][:, 512:1024], in1=WK[wk][:, 1024:1536], op=ALU.add),
                                r=[("WK", wk)], w=[("OB", ob)])
                        P.dma("sp", dst[row0:row0 + 128, tcol:tcol + 2048], OB[ob][:, 0:2048], r=[("OB", ob)])

            if own:
                for gi, c0 in enumerate((C_GA, C_GB)):
                    for g4 in range(2):
                        ws = load_w(c0 + g4 * 512, 512)
                        for wc in range(4):
                            cc = g4 * 4 + wc
                            ob = next_ob()
                            for tq in range(4):
                                s = fm_mm(ws, wc, tq)
                                P.act(lambda e, s=s, ob=ob, tq=tq, gi=gi, cc=cc: e.activation(
                                    out=OB[ob][:, tq * 512:(tq + 1) * 512], in_=pp[s][:, :], func=AF.Sigmoid,
                                    bias=gbfm[:, gi, cc:cc + 1]), r=[("pp", s), "gbfm"], w=[("OB", ob)])
                            P.dma("sp", T["sgT"][gi, cc * 128:(cc + 1) * 128, t0:t0 + 2048], OB[ob][:, 0:2048],
                                  r=[("OB", ob)])
                for cg in range(2):
                    wso = load_w(C_AO + cg * 512, 512)
                    wsz = load_w(C_AZ + cg * 512, 512)
                    ob = next_ob()
                    for tt in range(16):
                        so = tm_mm(wso, 512, tt)
                        sz = tm_mm(wsz, 512, tt)
                        wk = next_wk()
                        P.act(lambda e, so=so, wk=wk: e.activation(out=WK[wk][:, 0:512], in_=pp[so][:, :], func=AF.Sigmoid),
                              r=[("pp", so)], w=[("WK", wk)])
                        P.act(lambda e, sz=sz, wk=wk: e.activation(out=WK[wk][:, 512:1024], in_=pp[sz][:, :], func=AF.Sigmoid),
                              r=[("pp", sz)], w=[("WK", wk)])
                        P.dve(lambda e, sz=sz, wk=wk: e.tensor_tensor(out=WK[wk][:, 512:1024], in0=pp[sz][:, :],
                                                                      in1=WK[wk][:, 512:1024], op=ALU.mult),
                              r=[("pp", sz), ("WK", wk)], w=[("WK", wk)])
                        P.dve(lambda e, wk=wk, ob=ob, tt=tt: e.tensor_tensor(
                            out=OB[ob][:, tt * 512:(tt + 1) * 512], in0=WK[wk][:, 0:512], in1=WK[wk][:, 512:1024],
                            op=ALU.mult), r=[("WK", wk)], w=[("OB", ob)])
                    P.dma("sp", T["gA"][t0:t0 + 2048, cg * 512:(cg + 1) * 512].rearrange("(tt p) c -> p tt c", p=128),
                          OB[ob][:, :].rearrange("p (tt c) -> p tt c", c=512), r=[("OB", ob)])
                for cg in range(2):
                    wsz = load_w(C_BZ + cg * 512, 512)
                    ob = next_ob()
                    for tt in range(16):
                        sz = tm_mm(wsz, 512, tt)
                        wk = next_wk()
                        P.act(lambda e, sz=sz, wk=wk: e.activation(out=WK[wk][:, 0:512], in_=pp[sz][:, :], func=AF.Sigmoid),
                              r=[("pp", sz)], w=[("WK", wk)])
                        P.dve(lambda e, sz=sz, wk=wk, ob=ob, tt=tt: e.tensor_tensor(
                            out=OB[ob][:, tt * 512:(tt + 1) * 512], in0=pp[sz][:, :], in1=WK[wk][:, 0:512],
                            op=ALU.mult), r=[("pp", sz), ("WK", wk)], w=[("OB", ob)])
                    P.dma("sp", T["gB"][t0:t0 + 2048, cg * 512:(cg + 1) * 512].rearrange("(tt p) c -> p tt c", p=128),
                          OB[ob][:, :].rearrange("p (tt c) -> p tt c", c=512), r=[("OB", ob)])

        P.dma("sp", T["gall_d"][:, :], gall[:, :], r=["gall"])
        P.flush()


def build_rope_tables(P, posi, invf, cosT, sinT, WK, next_wk):
    TWO_PI = 2.0 * math.pi
    C1 = 6.28125
    C2 = TWO_PI - C1
    a = next_wk()
    b = next_wk()
    c = next_wk()
    A, Bk, R = WK[a], WK[b], WK[c]
    N = 2048
    P.dve(lambda e: e.tensor_copy(out=A[:, 0:N], in_=posi[:, :]), r=["posi"], w=[("WK", a)])
    P.dve(lambda e: e.tensor_scalar(out=A[:, 0:N], in0=A[:, 0:N], scalar1=invf[:, 0:1], scalar2=None, op0=ALU.mult),
          r=[("WK", a), "invf"], w=[("WK", a)])
    ki = posi
    P.dve(lambda e: e.tensor_scalar(out=Bk[:, 0:N], in0=A[:, 0:N], scalar1=1.0 / TWO_PI, scalar2=None, op0=ALU.mult),
          r=[("WK", a)], w=[("WK", b)])
    P.dve(lambda e: e.tensor_copy(out=ki[:, :], in_=Bk[:, 0:N]), r=[("WK", b)], w=["posi"])
    P.dve(lambda e: e.tensor_copy(out=Bk[:, 0:N], in_=ki[:, :]), r=["posi"], w=[("WK", b)])
    P.dve(lambda e: e.scalar_tensor_tensor(out=R[:, 0:N], in0=Bk[:, 0:N], scalar=-C1, in1=A[:, 0:N],
                                           op0=ALU.mult, op1=ALU.add), r=[("WK", a), ("WK", b)], w=[("WK", c)])
    P.dve(lambda e: e.scalar_tensor_tensor(out=R[:, 0:N], in0=Bk[:, 0:N], scalar=-C2, in1=R[:, 0:N],
                                           op0=ALU.mult, op1=ALU.add), r=[("WK", b), ("WK", c)], w=[("WK", c)])

    def wrap(X, key):
        P.dve(lambda e: e.tensor_scalar(out=Bk[:, 0:N], in0=X[:, 0:N], scalar1=math.pi, scalar2=-TWO_PI,
                                        op0=ALU.is_gt, op1=ALU.mult), r=[key], w=[("WK", b)])
        P.dve(lambda e: e.tensor_tensor(out=X[:, 0:N], in0=X[:, 0:N], in1=Bk[:, 0:N], op=ALU.add),
              r=[key, ("WK", b)], w=[key])
        P.dve(lambda e: e.tensor_scalar(out=Bk[:, 0:N], in0=X[:, 0:N], scalar1=-math.pi, scalar2=TWO_PI,
                                        op0=ALU.is_lt, op1=ALU.mult), r=[key], w=[("WK", b)])
        P.dve(lambda e: e.tensor_tensor(out=X[:, 0:N], in0=X[:, 0:N], in1=Bk[:, 0:N], op=ALU.add),
              r=[key, ("WK", b)], w=[key])

    wrap(R, ("WK", c))
    P.act(lambda e: e.activation(out=sinT[:, :], in_=R[:, 0:N], func=AF.Sin), r=[("WK", c)], w=["sinT"])
    P.dve(lambda e: e.tensor_scalar(out=R[:, 0:N], in0=R[:, 0:N], scalar1=math.pi / 2, scalar2=None, op0=ALU.add),
          r=[("WK", c)], w=[("WK", c)])
    wrap(R, ("WK", c))
    P.act(lambda e: e.activation(out=cosT[:, :], in_=R[:, 0:N], func=AF.Sin), r=[("WK", c)], w=["cosT"])


def phase_M(nc, P, T):
    mqT, mkT, mv, gA, yaT = T["mqT"], T["mkT"], T["mv"], T["gA"], T["yaT"]
    with contextlib.ExitStack() as st:
        sb = lambda name, shape, dt=F32: st.enter_context(nc.sbuf_tensor("M_" + name, list(shape), dt))
        ps = lambda name, shape, dt=F32: st.enter_context(nc.psum_tensor("M_" + name, list(shape), dt))
        qT = sb("qT", [128, 2, OWN], BF16)
        kT = sb("kT", [128, 2, S], BF16)
        va = sb("va", [128, 64, 257], BF16)
        ktok = sb("ktok", [128, 64, 256], BF16)
        hacc = sb("hacc", [128, 32, 256])
        gat = [sb("gat%d" % i, [128, 256], BF16) for i in range(2)]
        gall = sb("gall", [128, 1024])
        mlng = sb("mlng", [128, D])
        tri = sb("tri", [128, 4, 128])
        identf = sb("identf", [128, 128])
        identb = sb("identb", [128, 128], BF16)
        onesf = sb("onesf", [128, 128])
        LFt = [sb("LFt%d" % d, [128, 256]) for d in range(2)]
        Bc = [sb("Bc%d" % d, [128, 256]) for d in range(2)]
        Aa = [sb("Aa%d" % d, [128, 256]) for d in range(2)]
        EB = [sb("EB%d" % d, [128, 256]) for d in range(2)]
        WST = [sb("WST%d" % d, [128, 256]) for d in range(2)]
        DEC = [sb("DEC%d" % d, [128, 256]) for d in range(2)]
        Cst = [sb("Cst%d" % d, [128, 2, 257]) for d in range(2)]
        Cb = [sb("Cb%d" % d, [128, 2, 257], BF16) for d in range(2)]
        dg = [sb("dg%d" % i, [128, 128]) for i in range(2)]
        Dm = [sb("Dm%d" % i, [128, 128]) for i in range(2)]
        Wm = [sb("Wm%d" % i, [128, 128], BF16) for i in range(2)]
        vw = [sb("vw%d" % i, [128, 257], BF16) for i in range(2)]
        tmpc = [sb("tmpc%d" % i, [128, 257]) for i in range(2)]
        tot = [sb("tot%d" % i, [128, 257]) for i in range(2)]
        sm = [sb("sm%d" % i, [128, 4]) for i in range(2)]
        hs = [sb("hs%d" % i, [128, 256]) for i in range(2)]
        sqv = [sb("sqv%d" % i, [128, 256]) for i in range(2)]
        yab = [sb("yab%d" % i, [128, 256], BF16) for i in range(2)]
        yas = [sb("yas%d" % i, [128, 2, 128], BF16) for i in range(2)]
        pD = ps("pD", [128, 512])
        pST = ps("pST", [128, 512])
        pI = ps("pI", [128, 512])
        pC = ps("pC", [128, 512])
        pU = [ps("pU%d" % i, [128, 512]) for i in range(2)]
        ptrk = ps("ptrk", [128, 1024], BF16)
        ptry = ps("ptry", [128, 1024], BF16)

        P.dma("sp", gall[:], T["gall_d"], w=["gall"])
        P.dma("sp", mlng[:], T["mlng_rep"], w=["mlng"])
        P.dma("sp", tri[:], T["c_tri"], w=["tri"])
        P.dma("sp", identf[:], T["c_ident"], w=["identf"])
        P.dve(lambda e: e.tensor_copy(out=identb[:], in_=identf[:]), r=["identf"], w=["identb"])
        P.dve(lambda e: e.memset(onesf[:], 1.0), w=["onesf"])
        P.dve(lambda e: e.memset(va[:, :, 256:257], 1.0), w=["va1"])

        g4 = gall[:, :].rearrange("p (c g h) -> p c g h", g=4, h=4)
        v3 = lambda t: t[:, :].rearrange("p (c h) -> p c h", h=4)
        for d in range(2):
            i_d = g4[:, :, 2 * d, :]
            f_d = g4[:, :, 2 * d + 1, :]
            P.act(lambda e, d=d, f_d=f_d: e.activation(out=v3(LFt[d]), in_=f_d, func=AF.Exp, scale=-1.0),
                  r=["gall"], w=[("LFt", d)])
            P.act(lambda e, d=d: e.activation(out=LFt[d][:, :], in_=LFt[d][:, :], func=AF.Ln, bias=1.0),
                  r=[("LFt", d)], w=[("LFt", d)])
            P.pe(lambda e, d=d: e.matmul(pI[:, 0:256], tri[:, d, :], LFt[d][:, :], start=True, stop=True),
                 r=["tri", ("LFt", d)], w=["pI"])
            P.pe(lambda e, d=d: e.matmul(pC[:, 0:256], onesf[:, :], LFt[d][:, :], start=True, stop=True),
                 r=["onesf", ("LFt", d)], w=["pC"])
            P.dve(lambda e, d=d: e.tensor_scalar(out=Bc[d][:, :], in0=pI[:, 0:256], scalar1=-1.0, scalar2=None, op0=ALU.mult),
                  r=["pI"], w=[("Bc", d)])
            P.dve(lambda e, d=d, i_d=i_d: e.tensor_tensor(out=v3(Aa[d]), in0=pI[:, 0:256].rearrange("p (c h) -> p c h", h=4),
                                                         in1=i_d, op=ALU.add),
                  r=["pI", "gall"], w=[("Aa", d)])
            P.act(lambda e, d=d: e.activation(out=EB[d][:, :], in_=Bc[d][:, :], func=AF.Exp),
                  r=[("Bc", d)], w=[("EB", d)])
            P.dve(lambda e, d=d: e.tensor_copy(out=DEC[d][:, :], in_=pC[:, 0:256]), r=["pC"], w=[("DEC", d)])
            P.dve(lambda e, d=d: e.tensor_tensor(out=WST[d][:, :], in0=Aa[d][:, :], in1=DEC[d][:, :], op=ALU.subtract),
                  r=[("Aa", d), ("DEC", d)], w=[("WST", d)])
            P.act(lambda e, d=d: e.activation(out=WST[d][:, :], in_=WST[d][:, :], func=AF.Exp),
                  r=[("WST", d)], w=[("WST", d)])
            P.act(lambda e, d=d: e.activation(out=DEC[d][:, :], in_=DEC[d][:, :], func=AF.Exp, scale=-1.0),
                  r=[("DEC", d), ("WST", d)], w=[("DEC", d)])

        cnt = {"o": 0, "u": 0, "g": 0, "y": 0}

        def output_step(h, d, c):
            i = cnt["o"] % 2
            cnt["o"] += 1
            col = c * 4 + h
            cs = slice(c * 128, (c + 1) * 128)
            P.dve(lambda e: e.tensor_scalar(out=dg[i][:], in0=identf[:], scalar1=Bc[d][:, col:col + 1], scalar2=None, op0=ALU.mult),
                  r=["identf", ("Bc", d)], w=[("dg", i)])
            P.pe(lambda e: e.matmul(pD[:, 0:128], onesf[:, :], dg[i][:, :], start=True, stop=False), r=["onesf", ("dg", i)], w=["pD"])
            P.pe(lambda e: e.matmul(pD[:, 0:128], identf[:, :], tri[:, 2 + d, :], start=False, stop=True), r=["identf", "tri"], w=["pD"])
            P.act(lambda e: e.activation(out=Dm[i][:], in_=pD[:, 0:128], func=AF.Exp, bias=Aa[d][:, col:col + 1]),
                  r=["pD", ("Aa", d)], w=[("Dm", i)])
            for dkc in range(2):
                P.pe(lambda e, dkc=dkc: e.matmul(pST[:, 0:128], kT[:, dkc, cs], qT[:, dkc, cs], start=(dkc == 0), stop=(dkc == 1)),
                     r=["kT", "qT"], w=["pST"])
            P.dve(lambda e: e.tensor_tensor(out=Wm[i][:], in0=pST[:, 0:128], in1=Dm[i][:], op=ALU.mult),
                  r=["pST", ("Dm", i)], w=[("Wm", i)])
            P.pe(lambda e: e.matmul(pI[:, 0:257], Wm[i][:, :], va[:, c, :], start=True, stop=True),
                 r=[("Wm", i), "va", "va1"], w=["pI"])
            for dkc in range(2):
                P.pe(lambda e, dkc=dkc: e.matmul(pC[:, 0:257], qT[:, dkc, cs], Cb[d][:, dkc, :], start=(dkc == 0), stop=(dkc == 1)),
                     r=["qT", ("Cb", d)], w=["pC"])
            P.dve(lambda e: e.tensor_scalar(out=tmpc[i][:], in0=pC[:, 0:257], scalar1=EB[d][:, col:col + 1], scalar2=None, op0=ALU.mult),
                  r=["pC", ("EB", d)], w=[("tmpc", i)])
            P.dve(lambda e: e.tensor_tensor(out=tot[i][:], in0=tmpc[i][:], in1=pI[:, 0:257], op=ALU.add),
                  r=[("tmpc", i), "pI"], w=[("tot", i)])
            P.dve(lambda e: e.tensor_scalar(out=sm[i][:, 3:4], in0=tot[i][:, 256:257], scalar1=-1.0, scalar2=1.0, op0=ALU.mult, op1=ALU.max),
                  r=[("tot", i)], w=[("sm", i)])
            P.dve(lambda e: e.tensor_tensor(out=sm[i][:, 0:1], in0=tot[i][:, 256:257], in1=sm[i][:, 3:4], op=ALU.max),
                  r=[("tot", i), ("sm", i)], w=[("sm", i)])
            P.dve(lambda e: e.reciprocal(out=sm[i][:, 1:2], in_=sm[i][:, 0:1]), r=[("sm", i)], w=[("sm", i)])
            if d == 0:
                P.dve(lambda e: e.tensor_scalar(out=hacc[:, c, :], in0=tot[i][:, 0:256], scalar1=sm[i][:, 1:2], scalar2=None, op0=ALU.mult),
                      r=[("tot", i), ("sm", i)], w=[("hacc", c)])
                return
            gi = cnt["g"] % 2
            cnt["g"] += 1
            P.dma("sp", gat[gi][:], gA[c * 128:(c + 1) * 128, h * 256:(h + 1) * 256], w=[("gat", gi)])
            P.dve(lambda e: e.scalar_tensor_tensor(out=hs[i][:], in0=tot[i][:, 0:256], scalar=sm[i][:, 1:2], in1=hacc[:, c, :],
                                                   op0=ALU.mult, op1=ALU.add),
                  r=[("tot", i), ("sm", i), ("hacc", c)], w=[("hs", i)])
            P.dve(lambda e: e.tensor_tensor(out=sqv[i][:], in0=hs[i][:], in1=hs[i][:], op=ALU.mult), r=[("hs", i)], w=[("sqv", i)])
            P.dve(lambda e: e.reduce_sum(out=sm[i][:, 2:3], in_=sqv[i][:], axis=AX.X), r=[("sqv", i)], w=[("smb", i)])
            P.act(lambda e: e.activation(out=sm[i][:, 2:3], in_=sm[i][:, 2:3], func=AF.Ln, scale=1.0 / 256.0, bias=NORM_EPS),
                  r=[("smb", i)], w=[("smb", i)])
            P.act(lambda e: e.activation(out=sm[i][:, 2:3], in_=sm[i][:, 2:3], func=AF.Exp, scale=-0.5),
                  r=[("smb", i)], w=[("smb", i)])
            P.dve(lambda e: e.scalar_tensor_tensor(out=sqv[i][:], in0=hs[i][:], scalar=sm[i][:, 2:3],
                                                   in1=mlng[:, h * 256:(h + 1) * 256], op0=ALU.mult, op1=ALU.mult),
                  r=[("hs", i), ("smb", i), "mlng"], w=[("sqv", i)])
            P.dve(lambda e: e.tensor_tensor(out=yab[i][:], in0=sqv[i][:], in1=gat[gi][:], op=ALU.mult),
                  r=[("sqv", i), ("gat", gi)], w=[("yab", i)])
            for k in range(2):
                P.pe(lambda e, k=k: e.transpose(out=ptry[:, k * 128:(k + 1) * 128], in_=yab[i][:, k * 128:(k + 1) * 128],
                                                identity=identb[:]), r=[("yab", i), "identb"], w=["ptry"])
            P.act(lambda e: e.activation(out=yas[i][:, :, :], in_=ptry[:, 0:256].rearrange("p (k t) -> p k t", k=2), func=AF.Copy),
                  r=["ptry"], w=[("yas", i)])
            P.dma("sp", yaT[h * 256:(h + 1) * 256, c * 128:(c + 1) * 128].rearrange("(k p) t -> p k t", p=128),
                  yas[i][:, :, :], r=[("yas", i)])

        def update_step(h, d, c):
            i = cnt["u"] % 2
            cnt["u"] += 1
            col = c * 4 + h
            P.pool(lambda e: e.tensor_scalar(out=vw[i][:], in0=va[:, c, :], scalar1=WST[d][:, col:col + 1], scalar2=None, op0=ALU.mult),
                   r=["va", "va1", ("WST", d)], w=[("vw", i)])
            for dkc in range(2):
                P.pe(lambda e, dkc=dkc: e.matmul(pU[dkc][:, 0:257], ktok[:, c, dkc * 128:(dkc + 1) * 128], vw[i][:, :], start=True, stop=True),
                     r=[("ktok", c), ("vw", i)], w=[("pU", dkc)])
                P.dve(lambda e, dkc=dkc: e.scalar_tensor_tensor(out=Cst[d][:, dkc, :], in0=Cst[d][:, dkc, :], scalar=DEC[d][:, col:col + 1],
                                                                in1=pU[dkc][:, 0:257], op0=ALU.mult, op1=ALU.add),
                      r=[("Cst", d, dkc), ("DEC", d), ("pU", dkc)], w=[("Cst", d, dkc)])
                P.act(lambda e, dkc=dkc: e.activation(out=Cb[d][:, dkc, :], in_=Cst[d][:, dkc, :], func=AF.Copy),
                      r=[("Cst", d, dkc)], w=[("Cb", d)])

        for h in range(4):
            for dkc in range(2):
                r0 = h * 256 + dkc * 128
                P.dma("sp", qT[:, dkc, :], mqT[r0:r0 + 128, 0:OWN], w=["qT"])
                P.dma("sp", kT[:, dkc, :], mkT[r0:r0 + 128, :], w=["kT"])
            P.dma("sp", va[:, :, 0:256], mv[:, h * 256:(h + 1) * 256].rearrange("(c p) f -> p c f", p=128), w=["va"])
            for d in range(2):
                P.dve(lambda e, d=d: e.memset(Cst[d][:], 0.0), w=[("Cst", d, 0), ("Cst", d, 1)])
                P.dve(lambda e, d=d: e.memset(Cb[d][:], 0.0), w=[("Cb", d)])
            for c in range(64):
                for dkc in range(2):
                    P.pe(lambda e, c=c, dkc=dkc: e.transpose(out=ptrk[:, dkc * 128:(dkc + 1) * 128],
                                                             in_=kT[:, dkc, c * 128:(c + 1) * 128], identity=identb[:]),
                         r=["kT", "identb"], w=["ptrk"])
                if c % 2 == 0:
                    P.act(lambda e, c=c: e.activation(out=ktok[:, c, :], in_=ptrk[:, 0:256], func=AF.Copy), r=["ptrk"], w=[("ktok", c)])
                else:
                    P.dve(lambda e, c=c: e.tensor_copy(out=ktok[:, c, :], in_=ptrk[:, 0:256]), r=["ptrk"], w=[("ktok", c)])
            for i in range(32):
                output_step(h, 0, i)
                if i < 31:
                    update_step(h, 0, i)
                update_step(h, 1, 63 - i)
            for i in range(32, 64):
                c = 63 - i
                output_step(h, 1, c)
                if c > 0:
                    update_step(h, 1, c)
        P.flush()


def phase_D(nc, P, T):
    dqT, dkT, dv, gB, ybT = T["dqT"], T["dkT"], T["dv"], T["gB"], T["ybT"]
    with contextlib.ExitStack() as st:
        sb = lambda name, shape, dt=F32: st.enter_context(nc.sbuf_tensor("D_" + name, list(shape), dt))
        ps = lambda name, shape, dt=F32: st.enter_context(nc.psum_tensor("D_" + name, list(shape), dt))
        kT = [sb("dkT%d" % i, [128, S], BF16) for i in range(2)]
        va = [sb("dva%d" % i, [128, 64, 129], BF16) for i in range(2)]
        qT = [sb("dqT%d" % i, [128, OWN], BF16) for i in range(2)]
        NE = 3
        E = [sb("dE%d" % i, [128, 1024], BF16) for i in range(NE)]
        gbt = [sb("dgb%d" % i, [128, 4, 128], BF16) for i in range(2)]
        lamr = sb("lamr", [128, 256])
        ltmp = sb("ltmp", [128, 128])
        lam = sb("lam", [128, 4])
        subg = sb("subg", [128, 128])
        identf = sb("identf", [128, 128])
        identb = sb("identb", [128, 128], BF16)
        r12 = [sb("r12_%d" % i, [128, 4]) for i in range(2)]
        o1 = [sb("o1_%d" % i, [128, 128]) for i in range(2)]
        o2 = [sb("o2_%d" % i, [128, 128]) for i in range(2)]
        sq = [sb("sq_%d" % i, [128, 128]) for i in range(2)]
        ybq = [sb("ybq%d" % i, [128, 128], BF16) for i in range(8)]
        ybs = [sb("ybs%d" % i, [128, 512], BF16) for i in range(2)]
        pS = [ps("pS%d" % i, [128, 1024]) for i in range(2)]
        pacc = [ps("pacc%d" % i, [128, 512]) for i in range(3)]
        ptr = ps("dptr", [128, 512], BF16)

        P.dma("sp", lamr[:], T["lam_rep"], w=["lamr"])
        P.dma("sp", subg[:], T["subln_rep"], w=["subg"])
        P.dma("sp", identf[:], T["c_ident"], w=["identf"])
        P.dve(lambda e: e.tensor_copy(out=identb[:], in_=identf[:]), r=["identf"], w=["identb"])
        P.dve(lambda e: e.tensor_scalar(out=subg[:], in0=subg[:], scalar1=(1.0 - LAMBDA_INIT), scalar2=None, op0=ALU.mult),
              r=["subg"], w=["subg"])
        P.dve(lambda e: e.tensor_tensor(out=ltmp[:, 0:64], in0=lamr[:, 0:64], in1=lamr[:, 64:128], op=ALU.mult),
              r=["lamr"], w=["ltmp"])
        P.dve(lambda e: e.tensor_tensor(out=ltmp[:, 64:128], in0=lamr[:, 128:192], in1=lamr[:, 192:256], op=ALU.mult),
              r=["lamr"], w=["ltmp"])
        P.dve(lambda e: e.reduce_sum(out=lam[:, 0:2], in_=ltmp[:, :].rearrange("p (a b) -> p a b", a=2), axis=AX.X),
              r=["ltmp"], w=["lam"])
        P.act(lambda e: e.activation(out=lam[:, 0:2], in_=lam[:, 0:2], func=AF.Exp), r=["lam"], w=["lam"])
        P.dve(lambda e: e.tensor_tensor(out=lam[:, 2:3], in0=lam[:, 0:1], in1=lam[:, 1:2], op=ALU.subtract),
              r=["lam"], w=["lam"])
        P.dve(lambda e: e.tensor_scalar(out=lam[:, 3:4], in0=lam[:, 2:3], scalar1=LAMBDA_INIT, scalar2=-1.0,
                                        op0=ALU.add, op1=ALU.mult), r=["lam"], w=["lam"])
        for i in range(2):
            P.dve(lambda e, i=i: e.memset(va[i][:, :, 128:129], 1.0), w=[("va1", i)])

        def acc_ap(a, lo, hi):
            return pacc[a // 3][:, (a % 3) * 129 + lo:(a % 3) * 129 + hi]

        accS = [sb("accS%d" % i, [128, 8 * 129]) for i in range(2)]
        mhalf = sb("mhalf", [128, 1])
        P.dve(lambda e: e.memset(mhalf[:], -0.5), w=["mhalf"])

        def load_head(h):
            hs = h % 2
            P.dma("sp", kT[hs][:], dkT[h * 128:(h + 1) * 128, :], w=[("kT", hs)])
            P.dma("sp", qT[hs][:], dqT[h * 128:(h + 1) * 128, :], w=[("qT", hs)])
            P.dma("sp", va[hs][:, :, 0:128], dv[:, h * 128:(h + 1) * 128].rearrange("(kb p) c -> p kb c", p=128),
                  w=[("va", hs)])

        def load_gb(h, qt):
            gs = (h * 8 + qt) % 2
            P.dma("sp", gbt[gs][:], gB[qt * 512:(qt + 1) * 512, h * 128:(h + 1) * 128].rearrange("(u p) c -> p u c", p=128),
                  w=[("gbt", gs)])

        def qk(i, h, qt, kb):
            sl, hs = i % 2, h % 2
            for j in range(2):
                P.pe(lambda e, j=j: e.matmul(
                    pS[sl][:, j * 512:(j + 1) * 512], kT[hs][j * 64:(j + 1) * 64, kb * 128:(kb + 1) * 128],
                    qT[hs][j * 64:(j + 1) * 64, qt * 512:(qt + 1) * 512], start=True, stop=True),
                    r=[("kT", hs), ("qT", hs)], w=[("pS", sl)])

        def ex(i):
            sl, es = i % 2, i % NE
            P.act(lambda e: e.activation(out=E[es][:, :], in_=pS[sl][:, :], func=AF.Exp), r=[("pS", sl)], w=[("E", es)])

        def pv(i, h, kb):
            es, hs = i % NE, h % 2
            for a in range(8):
                j, u = a // 4, a % 4
                P.pe(lambda e, a=a, j=j, u=u: e.matmul(
                    acc_ap(a, 0, 129), E[es][:, j * 512 + u * 128:j * 512 + (u + 1) * 128], va[hs][:, kb, :],
                    start=(kb == 0 and a % 3 == 0), stop=(kb == 63), skip_group_check=True),
                    r=[("E", es), ("va", hs), ("va1", hs)], w=[("accb", a // 3)])

        epi = [0]

        def epilogue(h, qt):
            gs = (h * 8 + qt) % 2
            ys = gs
            ai = gs
            A = accS[ai]
            for b in range(3):
                n = 387 if b < 2 else 258
                P.dve(lambda e, b=b, n=n: e.tensor_copy(out=A[:, b * 387:b * 387 + n], in_=pacc[b][:, 0:n]),
                      r=[("accb", b)], w=[("accS", ai)])
            sa = lambda a, lo, hi: A[:, a * 129 + lo:a * 129 + hi]
            for u in range(4):
                ep = epi[0] % 2
                epi[0] += 1
                a0, a1 = u, 4 + u
                P.dve(lambda e, ep=ep, a0=a0: e.reciprocal(out=r12[ep][:, 0:1], in_=sa(a0, 128, 129)),
                      r=[("accS", ai)], w=[("r12", ep)])
                P.dve(lambda e, ep=ep, a1=a1: e.reciprocal(out=r12[ep][:, 1:2], in_=sa(a1, 128, 129)),
                      r=[("accS", ai)], w=[("r12", ep)])
                P.dve(lambda e, ep=ep: e.tensor_tensor(out=r12[ep][:, 2:3], in0=r12[ep][:, 1:2], in1=lam[:, 3:4], op=ALU.mult),
                      r=[("r12", ep), "lam"], w=[("r12", ep)])
                P.dve(lambda e, ep=ep, a0=a0: e.tensor_scalar(out=o1[ep][:], in0=sa(a0, 0, 128), scalar1=r12[ep][:, 0:1],
                                                              scalar2=None, op0=ALU.mult),
                      r=[("accS", ai), ("r12", ep)], w=[("o1", ep)])
                P.dve(lambda e, ep=ep, a1=a1: e.scalar_tensor_tensor(out=o2[ep][:], in0=sa(a1, 0, 128), scalar=r12[ep][:, 2:3],
                                                                     in1=o1[ep][:], op0=ALU.mult, op1=ALU.add),
                      r=[("accS", ai), ("r12", ep), ("o1", ep)], w=[("o2", ep)])
                P.dve(lambda e, ep=ep: e.tensor_tensor(out=sq[ep][:], in0=o2[ep][:], in1=o2[ep][:], op=ALU.mult),
                      r=[("o2", ep)], w=[("sq", ep)])
                P.dve(lambda e, ep=ep: e.reduce_sum(out=r12[ep][:, 3:4], in_=sq[ep][:], axis=AX.X),
                      r=[("sq", ep)], w=[("r12b", ep)])
                P.dve(lambda e, ep=ep: e.tensor_scalar(out=r12[ep][:, 3:4], in0=r12[ep][:, 3:4], scalar1=1.0 / 128.0, scalar2=NORM_EPS,
                                                       op0=ALU.mult, op1=ALU.add), r=[("r12b", ep)], w=[("r12b", ep)])
                P.pool(lambda e, ep=ep: e.tensor_tensor(out=r12[ep][:, 3:4], in0=r12[ep][:, 3:4], in1=mhalf[:, 0:1], op=ALU.pow),
                       r=[("r12b", ep), "mhalf"], w=[("r12b", ep)])
                P.dve(lambda e, ep=ep: e.scalar_tensor_tensor(out=o1[ep][:], in0=o2[ep][:], scalar=r12[ep][:, 3:4],
                                                              in1=subg[:], op0=ALU.mult, op1=ALU.mult),
                      r=[("o2", ep), ("r12b", ep), "subg"], w=[("o1", ep)])
                yq = gs * 4 + u
                P.dve(lambda e, ep=ep, u=u, yq=yq: e.tensor_tensor(out=ybq[yq][:], in0=o1[ep][:], in1=gbt[gs][:, u, :], op=ALU.mult),
                      r=[("o1", ep), ("gbt", gs)], w=[("ybq", yq)])

        def epilogue2(h, qt):
            gs = (h * 8 + qt) % 2
            ys = gs
            for u in range(4):
                yq = gs * 4 + u
                P.pe(lambda e, u=u, yq=yq: e.transpose(out=ptr[:, u * 128:(u + 1) * 128], in_=ybq[yq][:], identity=identb[:]),
                     r=[("ybq", yq), "identb"], w=["ptr"])
            P.dve(lambda e: e.tensor_copy(out=ybs[ys][:, :], in_=ptr[:, :]), r=["ptr"], w=[("ybs", ys)])
            P.dma("sp", ybT[h * 128:(h + 1) * 128, qt * 512:(qt + 1) * 512], ybs[ys][:], r=[("ybs", ys)])

        blocks = [(h, qt, kb) for h in range(8) for qt in range(8) for kb in range(64)]
        nb = len(blocks)
        load_head(0)
        load_gb(0, 0)
        qk(0, *blocks[0])
        qk(1, *blocks[1])
        for i, (h, qt, kb) in enumerate(blocks):
            if kb == 0 and qt == 0 and h + 1 < 8:
                load_head(h + 1)
            if kb == 0:
                nxt = h * 8 + qt + 1
                if nxt < 64:
                    load_gb(nxt // 8, nxt % 8)
            ex(i)
            pv(i, h, kb)
            if i + 2 < nb:
                qk(i + 2, *blocks[i + 2])
            if kb == 63:
                epilogue(h, qt)
            if kb == 40 and (h, qt) != (0, 0):
                pq = h * 8 + qt - 1
                epilogue2(pq // 8, pq % 8)
        epilogue2(7, 7)
        P.flush()


def phase_O(nc, P, T):
    x, out, yaT, ybT, sgT = T["x"], T["out"], T["yaT"], T["ybT"], T["sgT"]
    with contextlib.ExitStack() as st:
        sb = lambda name, shape, dt=F32: st.enter_context(nc.sbuf_tensor("O_" + name, list(shape), dt))
        ps = lambda name, shape, dt=F32: st.enter_context(nc.psum_tensor("O_" + name, list(shape), dt))
        Wa = sb("Wa", [128, 8, D], BF16)
        Wb = sb("Wb", [128, 8, D], BF16)
        Wo = sb("Wo", [128, 8, D], BF16)
        fing = sb("fing", [128, D])
        ya = [sb("ya%d" % i, [128, 8, 512], BF16) for i in range(2)]
        yb = [sb("yb%d" % i, [128, 8, 512], BF16) for i in range(2)]
        sa = [sb("sa%d" % i, [128, 8, 512], BF16) for i in range(2)]
        sbb = [sb("sb%d" % i, [128, 8, 512], BF16) for i in range(2)]
        mixT = [sb("mixT%d" % i, [128, 8, 512], BF16) for i in range(2)]
        t1 = [sb("t1_%d" % i, [128, 512]) for i in range(2)]
        t2 = [sb("t2_%d" % i, [128, 512]) for i in range(2)]
        xt = [sb("xt%d" % i, [128, D]) for i in range(2)]
        xo = [sb("xo%d" % i, [128, D]) for i in range(2)]
        junk = sb("junk", [128, D], BF16)
        ssq = [sb("ssq%d" % i, [128, 2]) for i in range(2)]
        pa = [ps("pa%d" % i, [128, 512]) for i in range(2)]
        pb = [ps("pb%d" % i, [128, 512]) for i in range(2)]
        po = [ps("po%d" % i, [128, 512]) for i in range(4)]

        for (W, src, key) in ((Wa, T["w_a"], "Wa"), (Wb, T["w_b"], "Wb"), (Wo, T["w_o"], "Wo")):
            for hh in range(2):
                P.dma("pool", W[:, :, hh * 512:(hh + 1) * 512],
                      src[:, hh * 512:(hh + 1) * 512].rearrange("(kc p) c -> p kc c", p=128), w=[(key, hh)])
        P.dma("sp", fing[:], T["fing_rep"], w=["fing"])
        wkeys = lambda k: [(k, 0), (k, 1)]
        cnt = {"p": 0, "o": 0, "x": 0}
        for tt in range(8):
            sl = tt % 2
            ts_ = slice(tt * 512, (tt + 1) * 512)
            P.dma("sp", ya[sl][:], yaT[:, ts_].rearrange("(cc p) t -> p cc t", p=128), w=[("ya", sl)])
            P.dma("sp", yb[sl][:], ybT[:, ts_].rearrange("(cc p) t -> p cc t", p=128), w=[("yb", sl)])
            P.dma("sp", sa[sl][:], sgT[0, :, ts_].rearrange("(cc p) t -> p cc t", p=128), w=[("sa", sl)])
            P.dma("sp", sbb[sl][:], sgT[1, :, ts_].rearrange("(cc p) t -> p cc t", p=128), w=[("sb", sl)])
            for dd in range(8):
                i = cnt["p"] % 2
                cnt["p"] += 1
                for cc in range(8):
                    P.pe(lambda e, i=i, cc=cc, dd=dd, sl=sl: e.matmul(pa[i][:, :], Wa[:, cc, dd * 128:(dd + 1) * 128], ya[sl][:, cc, :],
                                                                      start=(cc == 0), stop=(cc == 7)),
                         r=wkeys("Wa") + [("ya", sl)], w=[("pa", i)])
                for cc in range(8):
                    P.pe(lambda e, i=i, cc=cc, dd=dd, sl=sl: e.matmul(pb[i][:, :], Wb[:, cc, dd * 128:(dd + 1) * 128], yb[sl][:, cc, :],
                                                                      start=(cc == 0), stop=(cc == 7)),
                         r=wkeys("Wb") + [("yb", sl)], w=[("pb", i)])
                P.dve(lambda e, i=i, dd=dd, sl=sl: e.tensor_tensor(out=t1[i][:], in0=pa[i][:, :], in1=sa[sl][:, dd, :], op=ALU.mult),
                      r=[("pa", i), ("sa", sl)], w=[("t1", i)])
                P.dve(lambda e, i=i, dd=dd, sl=sl: e.tensor_tensor(out=t2[i][:], in0=pb[i][:, :], in1=sbb[sl][:, dd, :], op=ALU.mult),
                      r=[("pb", i), ("sb", sl)], w=[("t2", i)])
                P.pool(lambda e, i=i, dd=dd, sl=sl: e.tensor_tensor(out=mixT[sl][:, dd, :], in0=t1[i][:], in1=t2[i][:], op=ALU.add),
                       r=[("t1", i), ("t2", i)], w=[("mixT", sl)])
            for u in range(4):
                xs = cnt["x"] % 2
                cnt["x"] += 1
                r0 = tt * 512 + u * 128
                P.dma("sp", xt[xs][:], x[r0:r0 + 128, :], w=[("xt", xs)])
                for eg in range(2):
                    o = cnt["o"] % 4
                    cnt["o"] += 1
                    for dd in range(8):
                        P.pe(lambda e, o=o, dd=dd, sl=sl, u=u, eg=eg: e.matmul(
                            po[o][:, :], mixT[sl][:, dd, u * 128:(u + 1) * 128], Wo[:, dd, eg * 512:(eg + 1) * 512],
                            start=(dd == 0), stop=(dd == 7)), r=[("mixT", sl), ("Wo", eg)], w=[("po", o)])
                    P.dve(lambda e, o=o, xs=xs, eg=eg: e.tensor_tensor(out=xo[xs][:, eg * 512:(eg + 1) * 512], in0=po[o][:, :],
                                                                       in1=xt[xs][:, eg * 512:(eg + 1) * 512], op=ALU.add),
                          r=[("po", o), ("xt", xs)], w=[("xo", xs, eg)])
                P.act(lambda e, xs=xs: e.activation(out=junk[:], in_=xo[xs][:], func=AF.Square, accum_out=ssq[xs][:, 0:1]),
                      r=[("xo", xs, 0), ("xo", xs, 1)], w=[("ssq", xs)])
                P.act(lambda e, xs=xs: e.activation(out=ssq[xs][:, 0:1], in_=ssq[xs][:, 0:1], func=AF.Sqrt, scale=1.0 / D, bias=NORM_EPS),
                      r=[("ssq", xs)], w=[("ssq", xs)])
                P.dve(lambda e, xs=xs: e.reciprocal(out=ssq[xs][:, 1:2], in_=ssq[xs][:, 0:1]), r=[("ssq", xs)], w=[("ssq", xs)])
                P.dve(lambda e, xs=xs: e.scalar_tensor_tensor(out=xo[xs][:], in0=xo[xs][:], scalar=ssq[xs][:, 1:2], in1=fing[:],
                                                              op0=ALU.mult, op1=ALU.mult),
                      r=[("xo", xs, 0), ("xo", xs, 1), ("ssq", xs), "fing"], w=[("xo", xs, 0), ("xo", xs, 1)])
                P.dma("sp", out[r0:r0 + 128, :], xo[xs][:], r=[("xo", xs, 0), ("xo", xs, 1)])
        P.flush()


def make_in_maps(x, positions, norm_g, w_in, ml_gate_b, ml_conv_w, ml_norm_g, da_lambda,
                 da_subln_g, gate_b, w_branch_a, w_branch_b, w_out, final_g):
    f32 = np.float32
    x = np.asarray(x, f32)
    positions = np.asarray(positions, np.int32)
    w_in0 = np.ascontiguousarray(np.asarray(w_in, f32)[0])
    gb0 = np.asarray(ml_gate_b, f32)[0]
    cw0 = np.asarray(ml_conv_w, f32)[0]
    w_in1 = w_in0.copy()
    ag = w_in0[:, C_AG:C_AG + 16].reshape(D, 4, 4)
    w_in1[:, C_AG:C_AG + 16] = ag[:, [2, 3, 0, 1], :].reshape(D, 16)
    gb1 = gb0[[2, 3, 0, 1], :]
    cw1 = cw0[::-1, :]
    rep = lambda v, n=128: np.ascontiguousarray(np.broadcast_to(np.asarray(v, f32).reshape(1, -1), (n, np.asarray(v).size)))
    ident = np.eye(128, dtype=f32)
    rsw = np.zeros((128, 128), f32)
    for r in range(128):
        m = r % 64
        if m < 32:
            rsw[r + 32, r] = -1.0
        else:
            rsw[r - 32, r] = 1.0
    invf = (10000.0 ** (-np.arange(0, 64, 2, dtype=f32) / f32(64))).astype(f32)
    invf_p = np.array([invf[(p % 64) % 32] for p in range(128)], f32).reshape(128, 1)
    ii = np.arange(128)
    U = (ii[:, None] <= ii[None, :]).astype(f32)
    L = (ii[:, None] >= ii[None, :]).astype(f32)
    NEG = -30000.0
    tri = np.stack([U, L, (1 - U) * NEG, (1 - L) * NEG], axis=1).astype(f32)
    common = {
        "normg_rep": rep(np.asarray(norm_g, f32)[0]),
        "mlng_rep": rep(np.asarray(ml_norm_g, f32)[0].reshape(-1)),
        "lam_rep": rep(np.asarray(da_lambda, f32)[0].reshape(-1)),
        "subln_rep": rep(np.asarray(da_subln_g, f32)[0]),
        "gb_fm": np.ascontiguousarray(np.asarray(gate_b, f32)[0].reshape(2, 8, 128).transpose(2, 0, 1)),
        "w_a": np.ascontiguousarray(np.asarray(w_branch_a, f32)[0]),
        "w_b": np.ascontiguousarray(np.asarray(w_branch_b, f32)[0]),
        "w_o": np.ascontiguousarray(np.asarray(w_out, f32)[0]),
        "fing_rep": rep(np.asarray(final_g, f32)),
        "c_ident": ident, "c_rswap": rsw, "c_invf": invf_p, "c_tri": tri,
    }
    in_maps = []
    for core in range(NCORES):
        b, half = core // 2, core % 2
        xb = x[b]
        pb = positions[b]
        if half == 1:
            xb = xb[::-1]
            pb = pb[::-1]
        gbx = gb1 if half else gb0
        cwx = cw1 if half else cw0
        m = dict(common)
        m["x"] = np.ascontiguousarray(xb)
        m["posr"] = np.ascontiguousarray(np.broadcast_to(pb.reshape(1, S), (128, S))).astype(np.int32)
        m["w_in"] = w_in1 if half else w_in0
        m["gateb_rep"] = rep(gbx.reshape(-1))
        m["convw"] = np.ascontiguousarray(cwx.reshape(5, 16, 128).transpose(2, 1, 0))
        in_maps.append(m)
    return in_maps


_NC_CACHE = {}


def kernel(**inputs):
    in_maps = make_in_maps(**inputs)
    if "nc" not in _NC_CACHE:
        _NC_CACHE["nc"] = build_nc()
    nc = _NC_CACHE["nc"]
    res = run_bass_kernel_spmd(nc, in_maps, core_ids=list(range(NCORES)))
    B = 4
    outp = np.empty((B, S, D), np.float32)
    for core in range(NCORES):
        b, half = core // 2, core % 2
        o = np.asarray(res.results[core]["out"], np.float32)
        if half == 0:
            outp[b, :OWN] = o
        else:
            outp[b, OWN:] = o[::-1]
    return outp
```

```python
import contextlib
import math
import numpy as np
import concourse.bass as bass
import concourse.mybir as mybir
from concourse.bass_utils import run_bass_kernel_spmd

F32, BF16, I32 = mybir.dt.float32, mybir.dt.bfloat16, mybir.dt.int32
AF = mybir.ActivationFunctionType
ALU = mybir.AluOpType
AX = mybir.AxisListType

D = 1024
S = 8192
OWN = 4096
NCORES = 8
PROJ = 11280
C_AQ, C_AK, C_AV, C_AO, C_AZ, C_AG = 0, 1024, 2048, 3072, 4096, 5120
C_BQ, C_BK, C_BV, C_BZ, C_GA, C_GB = 5136, 6160, 7184, 8208, 9232, 10256
NORM_EPS = 1e-6
LAMBDA_INIT = 0.8 - 0.6 * math.exp(-0.3 * 0)
SAME_ENGINE_SYNC = True
SAME_ENGINE_RAW_ONLY = True


class _Op:
    __slots__ = ("eng", "fn", "dma", "clock", "idx", "waits", "signal", "semval", "sem", "know")


class Prog:
    ENGS = ("sp", "act", "dve", "pool", "pe")

    def __init__(self, nc, stack, n_dma=8):
        self.nc = nc
        self.sem = {e: stack.enter_context(nc.semaphore("cs_" + e)) for e in self.ENGS}
        self.dsem = {q: [stack.enter_context(nc.semaphore("ds_%s_%d" % (q, i))) for i in range(n_dma)]
                     for q in ("sp", "pool", "act")}
        self.n_dma = n_dma
        self.dcount = {q: 0 for q in self.dsem}
        self.dlast = {}
        self.cnt = {e: 0 for e in self.ENGS}
        self.sigcnt = {e: 0 for e in self.ENGS}
        self.know = {e: {} for e in self.ENGS}
        self.pending = {e: [] for e in self.ENGS}
        self.last_w = {}
        self.readers = {}
        self.nops = 0

    def add(self, eng, fn, reads=(), writes=(), dma=False, extra_deps=()):
        op = _Op()
        op.eng, op.fn, op.dma, op.signal, op.semval = eng, fn, dma, False, None
        self.nops += 1
        deps = []
        seen = set()

        raw = set()

        def push(d, is_raw=False):
            if d is None:
                return
            if is_raw:
                raw.add(id(d))
            if id(d) not in seen:
                seen.add(id(d))
                deps.append(d)

        for k in reads:
            push(self.last_w.get(k), True)
        for k in writes:
            push(self.last_w.get(k))
            for r in self.readers.get(k, ()):
                push(r)
        for d in extra_deps:
            push(d, True)
        if dma:
            slot = self.dcount[eng] % self.n_dma
            self.dcount[eng] += 1
            op.clock = (eng, slot)
            prev = self.dlast.get(op.clock)
            op.idx = (prev.idx + 1) if prev is not None else 1
            op.sem = self.dsem[eng][slot]
            op.semval = 16 * op.idx
            push(prev)
            self.dlast[op.clock] = op
        else:
            self.cnt[eng] += 1
            op.clock = eng
            op.idx = self.cnt[eng]
            op.sem = self.sem[eng]
        know = self.know[eng]
        waits = []
        for d in deps:
            if (not d.dma) and (not dma) and d.eng == eng:
                if eng == "pe" or not SAME_ENGINE_SYNC:
                    continue
                if SAME_ENGINE_RAW_ONLY and id(d) not in raw:
                    continue
            if know.get(d.clock, 0) >= d.idx:
                continue
            waits.append(d)
            d.signal = True
            for c, v in d.know.items():
                if know.get(c, 0) < v:
                    know[c] = v
            if know.get(d.clock, 0) < d.idx:
                know[d.clock] = d.idx
        op.waits = waits
        op.know = dict(know)
        for k in writes:
            self.last_w[k] = op
            self.readers[k] = []
        for k in reads:
            self.readers.setdefault(k, []).append(op)
        self.pending[eng].append(op)
        return op

    def pe(self, fn, r=(), w=()):
        return self.add("pe", fn, r, w)

    def act(self, fn, r=(), w=()):
        return self.add("act", fn, r, w)

    def dve(self, fn, r=(), w=()):
        return self.add("dve", fn, r, w)

    def pool(self, fn, r=(), w=()):
        return self.add("pool", fn, r, w)

    def dma(self, q, out, in_, r=(), w=()):
        return self.add(q, lambda e: e.dma_start(out=out, in_=in_), r, w, dma=True)

    def flush(self, final=False):
        outstanding = [d for d in self.dlast.values()]
        self.add("sp", None, extra_deps=outstanding)
        for e in self.ENGS:
            for op in self.pending[e]:
                if not op.dma and op.signal:
                    self.sigcnt[e] += 1
                    op.semval = self.sigcnt[e]
                elif not op.dma:
                    op.semval = None
        pend = self.pending
        sems = self.sem

        def make_body(eng):
            ops = pend[eng]

            def body(e):
                for op in ops:
                    for d in op.waits:
                        assert d.semval is not None
                        e.wait_ge(d.sem, d.semval)
                    if op.fn is None:
                        continue
                    ins = op.fn(e)
                    if op.dma:
                        ins.then_inc(op.sem, 16)
                    elif op.signal:
                        ins.then_inc(sems[eng], 1)
            return body

        with self.nc.Block() as block:
            block.sync(make_body("sp"))
            block.scalar(make_body("act"))
            block.vector(make_body("dve"))
            block.gpsimd(make_body("pool"))
            block.tensor(make_body("pe"))
        self.pending = {e: [] for e in self.ENGS}
        self.last_w = {}
        self.readers = {}
        full = {}
        for e in self.ENGS:
            full[e] = self.cnt[e]
        for c, d in self.dlast.items():
            full[c] = d.idx
        self.know = {e: dict(full) for e in self.ENGS}


def build_nc(debug=False, phases=("P", "M", "D", "O")):
    nc = bass.Bass("TRN2", target_bir_lowering=False)
    IN = lambda name, shape, dt=F32: nc.dram_tensor(name, list(shape), dt, kind="ExternalInput").ap()
    skind = "ExternalOutput" if debug else "Internal"
    SCR = lambda name, shape, dt=BF16: nc.dram_tensor(name, list(shape), dt, kind=skind).ap()

    x = IN("x", [S, D])
    posr = IN("posr", [128, S], I32)
    w_in = IN("w_in", [D, PROJ])
    normg_rep = IN("normg_rep", [128, D])
    gateb_rep = IN("gateb_rep", [128, 16])
    convw = IN("convw", [128, 16, 5])
    mlng_rep = IN("mlng_rep", [128, D])
    lam_rep = IN("lam_rep", [128, 256])
    subln_rep = IN("subln_rep", [128, 128])
    gb_fm = IN("gb_fm", [128, 2, 8])
    w_a = IN("w_a", [D, D])
    w_b = IN("w_b", [D, D])
    w_o = IN("w_o", [D, D])
    fing_rep = IN("fing_rep", [128, D])
    c_ident = IN("c_ident", [128, 128])
    c_rswap = IN("c_rswap", [128, 128])
    c_invf = IN("c_invf", [128, 1])
    c_tri = IN("c_tri", [128, 4, 128])
    out = nc.dram_tensor("out", [OWN, D], F32, kind="ExternalOutput").ap()

    mqT = SCR("mqT", [D, S])
    mkT = SCR("mkT", [D, S])
    mv = SCR("mv", [S, D])
    dqT = SCR("dqT", [D, OWN])
    dkT = SCR("dkT", [D, S])
    dv = SCR("dv", [S, D])
    gA = SCR("gA", [OWN, D])
    gB = SCR("gB", [OWN, D])
    sgT = SCR("sgT", [2, D, OWN])
    gall_d = SCR("gall_d", [128, 64 * 16], F32)
    yaT = SCR("yaT", [D, OWN])
    ybT = SCR("ybT", [D, OWN])
    hfwd = SCR("hfwd", [OWN, D], F32)

    with contextlib.ExitStack() as top:
        P = Prog(nc, top)
        if "P" in phases:
            phase_P(nc, P, locals())
        if "M" in phases:
            phase_M(nc, P, locals())
        if "M2" in phases:
            phase_M2(nc, P, locals())
        if "D" in phases:
            phase_D(nc, P, locals())
        if "O" in phases:
            phase_O(nc, P, locals())
    return nc


def phase_P(nc, P, T):
    x, posr, w_in = T["x"], T["posr"], T["w_in"]
    with contextlib.ExitStack() as st:
        sb = lambda name, shape, dt=F32: st.enter_context(nc.sbuf_tensor("P_" + name, list(shape), dt))
        ps = lambda name, shape, dt=F32: st.enter_context(nc.psum_tensor("P_" + name, list(shape), dt))
        hT = sb("hT", [128, 8, 2048], BF16)
        NX = 4
        xt = [sb("xt%d" % i, [128, D]) for i in range(NX)]
        xn = [sb("xn%d" % i, [128, D], BF16) for i in range(NX)]
        junk = sb("junk", [128, D], BF16)
        ss = [sb("ss%d" % i, [128, 1]) for i in range(NX)]
        rstd = [sb("rstd%d" % i, [128, 1]) for i in range(NX)]
        grep = sb("grep", [128, D])
        NW = 3
        wB = [sb("wB%d" % i, [128, 8, 512], BF16) for i in range(NW)]
        wG = sb("wG", [128, 8, 16], BF16)
        NWK = 5
        WK = [sb("WK%d" % i, [128, 2054]) for i in range(NWK)]
        OB = [sb("OB%d" % i, [128, 2050], BF16) for i in range(2)]
        OBT = [sb("OBT%d" % i, [128, 8192], BF16) for i in range(2)]
        obt_slot = [0]
        cosT = sb("cosT", [128, 2048])
        sinT = sb("sinT", [128, 2048])
        gall = sb("gall", [128, 64 * 16])
        gbrep = sb("gbrep", [128, 16])
        cw = sb("cw", [128, 16, 5])
        carry = sb("carry", [128, 16, 4])
        identf = sb("identf", [128, 128])
        identb = sb("identb", [128, 128], BF16)
        rswap = sb("rswap", [128, 128])
        invf = sb("invf", [128, 1])
        gbfm = sb("gbfm", [128, 2, 8])
        posi = sb("posi", [128, 2048], I32)
        ptr = [ps("ptr%d" % i, [128, D], BF16) for i in range(2)]
        pp = [ps("pp%d" % i, [128, 512]) for i in range(4)]
        prot = [ps("prot%d" % i, [128, 512]) for i in range(2)]

        P.dma("sp", grep[:], T["normg_rep"], w=["grep"])
        P.dma("sp", gbrep[:], T["gateb_rep"], w=["gbrep"])
        P.dma("sp", cw[:], T["convw"], w=["cw"])
        P.dma("sp", identf[:], T["c_ident"], w=["identf"])
        P.dma("sp", rswap[:], T["c_rswap"], w=["rswap"])
        P.dma("sp", invf[:], T["c_invf"], w=["invf"])
        P.dma("sp", gbfm[:], T["gb_fm"], w=["gbfm"])
        P.dma("pool", wG[:], w_in[:, C_AG:C_AG + 16].rearrange("(kc p) c -> p kc c", p=128), w=["wG"])
        P.dve(lambda e: e.tensor_copy(out=identb[:], in_=identf[:]), r=["identf"], w=["identb"])
        P.dve(lambda e: e.memset(carry[:], 0.0), w=["carry"])
        for i in range(NWK):
            P.dve(lambda e, i=i: e.memset(WK[i][:, 2052:2054], 0.0), w=[("WKz", i), ("WK", i)])

        wslot = [0]

        def load_w(col0, ncols):
            s = wslot[0] % NW
            wslot[0] += 1
            P.dma("pool", wB[s][:, :, 0:ncols],
                  w_in[:, col0:col0 + ncols].rearrange("(kc p) c -> p kc c", p=128), w=[("wB", s)])
            return s

        ppslot = [0]

        def next_pp():
            s = ppslot[0] % 4
            ppslot[0] += 1
            return s

        wkslot = [0]

        def next_wk():
            s = wkslot[0] % NWK
            wkslot[0] += 1
            return s

        obslot = [0]

        def next_ob():
            s = obslot[0] % 2
            obslot[0] += 1
            return s

        def fm_mm(ws, wc, tq):
            s = next_pp()
            for kc in range(8):
                P.pe(lambda e, s=s, ws=ws, wc=wc, kc=kc, tq=tq: e.matmul(
                    pp[s][:, :], wB[ws][:, kc, wc * 128:(wc + 1) * 128], hT[:, kc, tq * 512:(tq + 1) * 512],
                    start=(kc == 0), stop=(kc == 7)),
                    r=[("wB", ws), "hT"], w=[("pp", s)])
            return s

        def tm_mm(ws, ncols, tt):
            s = next_pp()
            for kc in range(8):
                P.pe(lambda e, s=s, ws=ws, kc=kc, tt=tt, ncols=ncols: e.matmul(
                    pp[s][:, 0:ncols], hT[:, kc, tt * 128:(tt + 1) * 128], wB[ws][:, kc, 0:ncols],
                    start=(kc == 0), stop=(kc == 7)),
                    r=[("wB", ws), "hT"], w=[("pp", s)])
            return s

        def load_x(stile_, tt_):
            r0_ = stile_ * 2048 + tt_ * 128
            P.dma("sp", xt[tt_ % NX][:], x[r0_:r0_ + 128, :], w=[("xt", tt_ % NX)])

        for stile in range(4):
            own = stile < 2
            t0 = stile * 2048
            last = stile == 3
            if stile == 0:
                for tt in range(NX):
                    load_x(0, tt)
            def stage_a(tt):
                sl = tt % NX
                pl = tt % 2
                P.act(lambda e, sl=sl: e.activation(out=junk[:], in_=xt[sl][:], func=AF.Square, accum_out=ss[sl][:]),
                      r=[("xt", sl)], w=[("ss", sl)])
                P.act(lambda e, sl=sl: e.activation(out=ss[sl][:], in_=ss[sl][:], func=AF.Sqrt,
                                                    scale=1.0 / D, bias=NORM_EPS),
                      r=[("ss", sl)], w=[("ss", sl)])
                P.dve(lambda e, sl=sl: e.reciprocal(out=rstd[sl][:], in_=ss[sl][:]), r=[("ss", sl)], w=[("rstd", sl)])
                P.dve(lambda e, sl=sl: e.scalar_tensor_tensor(out=xn[sl][:], in0=xt[sl][:], scalar=rstd[sl][:],
                                                              in1=grep[:], op0=ALU.mult, op1=ALU.mult),
                      r=[("xt", sl), ("rstd", sl), "grep"], w=[("xn", sl)])
                for kc in range(8):
                    P.pe(lambda e, sl=sl, kc=kc, pl=pl: e.transpose(out=ptr[pl][:, kc * 128:(kc + 1) * 128],
                                                                    in_=xn[sl][:, kc * 128:(kc + 1) * 128],
                                                                    identity=identb[:]),
                         r=[("xn", sl), "identb"], w=[("ptr", pl)])
                if tt + NX < 16:
                    load_x(stile, tt + NX)

            def stage_b(tt):
                pl = tt % 2
                if tt % 2 == 0:
                    P.act(lambda e, pl=pl, tt=tt: e.activation(
                        out=hT[:, :, tt * 128:(tt + 1) * 128],
                        in_=ptr[pl][:, :].rearrange("p (k t) -> p k t", k=8), func=AF.Copy),
                        r=[("ptr", pl)], w=["hT"])
                else:
                    P.dve(lambda e, pl=pl, tt=tt: e.tensor_copy(
                        out=hT[:, :, tt * 128:(tt + 1) * 128],
                        in_=ptr[pl][:, :].rearrange("p (k t) -> p k t", k=8)),
                        r=[("ptr", pl)], w=["hT"])

            stage_a(0)
            for tt in range(16):
                if tt + 1 < 16:
                    stage_a(tt + 1)
                stage_b(tt)
            if stile < 3:
                for tt in range(NX):
                    load_x(stile + 1, tt)

            P.dma("sp", posi[:], posr[:, t0:t0 + 2048], w=["posi"])
            build_rope_tables(P, posi, invf, cosT, sinT, WK, next_wk)

            for tt in range(16):
                s = next_pp()
                for kc in range(8):
                    P.pe(lambda e, s=s, kc=kc, tt=tt: e.matmul(
                        pp[s][:, 0:16], hT[:, kc, tt * 128:(tt + 1) * 128], wG[:, kc, :],
                        start=(kc == 0), stop=(kc == 7)), r=["wG", "hT"], w=[("pp", s)])
                gt = stile * 16 + tt
                P.dve(lambda e, s=s, gt=gt: e.tensor_tensor(out=gall[:, gt * 16:(gt + 1) * 16], in0=pp[s][:, 0:16],
                                                            in1=gbrep[:], op=ALU.add),
                      r=[("pp", s), "gbrep"], w=["gall"])

            def gen_tm_v():
                for (c0, dst) in ((C_AV, T["mv"]), (C_BV, T["dv"])):
                    for cg in range(2):
                        ws = load_w(c0 + cg * 512, 512)
                        ob = obt_slot[0] % 2
                        obt_slot[0] += 1
                        for tt in range(16):
                            s = tm_mm(ws, 512, tt)
                            if True:
                                P.act(lambda e, s=s, ob=ob, tt=tt: e.activation(
                                    out=OBT[ob][:, tt * 512:(tt + 1) * 512], in_=pp[s][:, :], func=AF.Copy),
                                    r=[("pp", s)], w=[("OBT", ob)])
                            else:
                                P.dve(lambda e, s=s, ob=ob, tt=tt: e.tensor_copy(
                                    out=OBT[ob][:, tt * 512:(tt + 1) * 512], in_=pp[s][:, :]),
                                    r=[("pp", s)], w=[("OBT", ob)])
                            if tt == 15:
                                P.dma("sp", dst[t0:t0 + 2048, cg * 512:(cg + 1) * 512].rearrange("(tt p) c -> p tt c", p=128),
                                      OBT[ob][:, :].rearrange("p (tt c) -> p tt c", c=512), r=[("OBT", ob)])
                            yield

            def gen_conv():
                prev2 = [None]
                for fam, (c0, dst, ksc) in enumerate(((C_AQ, T["mqT"], 1.0), (C_AK, T["mkT"], 1.0 / 16.0))):
                    for g4 in range(2):
                        ws = load_w(c0 + g4 * 512, 512)
                        for wc in range(4):
                            fc = fam * 8 + g4 * 4 + wc
                            row0 = (g4 * 4 + wc) * 128
                            stg = next_wk()
                            acc = next_wk()
                            sg = next_wk()
                            P.dve(lambda e, stg=stg, fc=fc: e.tensor_copy(out=WK[stg][:, 0:4], in_=carry[:, fc, :]),
                                  r=["carry"], w=[("WK", stg)])
                            for tq in range(4):
                                s = fm_mm(ws, wc, tq)
                                P.act(lambda e, s=s, stg=stg, tq=tq: e.activation(
                                    out=WK[stg][:, 4 + tq * 512:4 + (tq + 1) * 512], in_=pp[s][:, :], func=AF.Copy),
                                    r=[("pp", s)], w=[("WK", stg)])
                            P.dve(lambda e, stg=stg, fc=fc: e.tensor_copy(out=carry[:, fc, :], in_=WK[stg][:, 2048:2052]),
                                  r=[("WK", stg)], w=["carry"])
                            nout = 2050 if last else 2048
                            P.act(lambda e, stg=stg, acc=acc, fc=fc, nout=nout: e.activation(
                                out=WK[acc][:, 0:nout], in_=WK[stg][:, 0:nout], func=AF.Copy, scale=cw[:, fc, 0:1]),
                                r=[("WK", stg), "cw", ("WKz", stg)], w=[("WK", acc)])
                            for j in range(1, 5):
                                P.dve(lambda e, stg=stg, acc=acc, fc=fc, nout=nout, j=j: e.scalar_tensor_tensor(
                                    out=WK[acc][:, 0:nout], in0=WK[stg][:, j:j + nout], scalar=cw[:, fc, j:j + 1],
                                    in1=WK[acc][:, 0:nout], op0=ALU.mult, op1=ALU.add),
                                    r=[("WK", stg), "cw", ("WK", acc)], w=[("WK", acc)])

                            def part2(acc=acc, sg=sg, nout=nout, ksc=ksc, dst=dst, row0=row0):
                                ob = next_ob()
                                P.act(lambda e: e.activation(
                                    out=WK[sg][:, 0:nout], in_=WK[acc][:, 0:nout], func=AF.Sigmoid),
                                    r=[("WK", acc)], w=[("WK", sg)])
                                P.dve(lambda e: e.scalar_tensor_tensor(
                                    out=OB[ob][:, 0:nout], in0=WK[acc][:, 0:nout], scalar=ksc, in1=WK[sg][:, 0:nout],
                                    op0=ALU.mult, op1=ALU.mult), r=[("WK", acc), ("WK", sg)], w=[("OB", ob)])
                                if stile == 0:
                                    P.dma("sp", dst[row0:row0 + 128, 0:nout - 2], OB[ob][:, 2:nout], r=[("OB", ob)])
                                else:
                                    P.dma("sp", dst[row0:row0 + 128, t0 - 2:t0 - 2 + nout], OB[ob][:, 0:nout], r=[("OB", ob)])

                            if prev2[0] is not None:
                                prev2[0]()
                            prev2[0] = part2
                            yield
                if prev2[0] is not None:
                    prev2[0]()

            g_tm = gen_tm_v()
            for _ in gen_conv():
                for _k in range(4):
                    next(g_tm, None)
            for _ in g_tm:
                pass

            fams = [(C_BK, T["dkT"], 1.0, t0)]
            if own:
                fams.append((C_BQ, T["dqT"], 0.125, t0))
            for (c0, dst, sc, tcol) in fams:
                for g4 in range(2):
                    ws = load_w(c0 + g4 * 512, 512)
                    for wc in range(4):
                        row0 = (g4 * 4 + wc) * 128
                        ob = next_ob()
                        for tq in range(4):
                            s = fm_mm(ws, wc, tq)
                            wk = next_wk()
                            pr = tq % 2
                            cs = slice(tq * 512, (tq + 1) * 512)
                            P.act(lambda e, s=s, wk=wk: e.activation(out=WK[wk][:, 0:512], in_=pp[s][:, :], func=AF.Copy),
                                  r=[("pp", s)], w=[("WK", wk)])
                            P.pe(lambda e, wk=wk, pr=pr: e.matmul(prot[pr][:, :], rswap[:, :], WK[wk][:, 0:512],
                                                                  start=True, stop=True),
                                 r=[("WK", wk), "rswap"], w=[("prot", pr)])
                            P.dve(lambda e, wk=wk, cs=cs, sc=sc: e.scalar_tensor_tensor(
                                out=WK[wk][:, 512:1024], in0=WK[wk][:, 0:512], scalar=sc, in1=cosT[:, cs],
                                op0=ALU.mult, op1=ALU.mult), r=[("WK", wk), "cosT"], w=[("WK", wk)])
                            P.dve(lambda e, wk=wk, cs=cs, sc=sc, pr=pr: e.scalar_tensor_tensor(
                                out=WK[wk][:, 1024:1536], in0=prot[pr][:, :], scalar=sc, in1=sinT[:, cs],
                                op0=ALU.mult, op1=ALU.mult), r=[("prot", pr), "sinT"], w=[("WK", wk)])
                            P.dve(lambda e, wk=wk, cs=cs, ob=ob: e.tensor_tensor(
                                out=OB[ob][:, cs], in0=WK[wk][:, 512:1024], in1=WK[wk][:, 1024:1536], op=ALU.add),
                                r=[("WK", wk)], w=[("OB", ob)])
                        P.dma("sp", dst[row0:row0 + 128, tcol:tcol + 2048], OB[ob][:, 0:2048], r=[("OB", ob)])

            if own:
                for gi, c0 in enumerate((C_GA, C_GB)):
                    for g4 in range(2):
                        ws = load_w(c0 + g4 * 512, 512)
                        for wc in range(4):
                            cc = g4 * 4 + wc
                            ob = next_ob()
                            for tq in range(4):
                                s = fm_mm(ws, wc, tq)
                                P.act(lambda e, s=s, ob=ob, tq=tq, gi=gi, cc=cc: e.activation(
                                    out=OB[ob][:, tq * 512:(tq + 1) * 512], in_=pp[s][:, :], func=AF.Sigmoid,
                                    bias=gbfm[:, gi, cc:cc + 1]), r=[("pp", s), "gbfm"], w=[("OB", ob)])
                            P.dma("sp", T["sgT"][gi, cc * 128:(cc + 1) * 128, t0:t0 + 2048], OB[ob][:, 0:2048],
                                  r=[("OB", ob)])
                for cg in range(2):
                    wso = load_w(C_AO + cg * 512, 512)
                    wsz = load_w(C_AZ + cg * 512, 512)
                    ob = obt_slot[0] % 2
                    obt_slot[0] += 1
                    for tt in range(16):
                        so = tm_mm(wso, 512, tt)
                        sz = tm_mm(wsz, 512, tt)
                        wk = next_wk()
                        P.act(lambda e, so=so, wk=wk: e.activation(out=WK[wk][:, 0:512], in_=pp[so][:, :], func=AF.Sigmoid),
                              r=[("pp", so)], w=[("WK", wk)])
                        P.act(lambda e, sz=sz, wk=wk: e.activation(out=WK[wk][:, 512:1024], in_=pp[sz][:, :], func=AF.Sigmoid),
                              r=[("pp", sz)], w=[("WK", wk)])
                        P.dve(lambda e, sz=sz, wk=wk: e.tensor_tensor(out=WK[wk][:, 512:1024], in0=pp[sz][:, :],
                                                                      in1=WK[wk][:, 512:1024], op=ALU.mult),
                              r=[("pp", sz), ("WK", wk)], w=[("WK", wk)])
                        P.dve(lambda e, wk=wk, ob=ob, tt=tt: e.tensor_tensor(
                            out=OBT[ob][:, tt * 512:(tt + 1) * 512], in0=WK[wk][:, 0:512], in1=WK[wk][:, 512:1024],
                            op=ALU.mult), r=[("WK", wk)], w=[("OBT", ob)])
                    P.dma("sp", T["gA"][t0:t0 + 2048, cg * 512:(cg + 1) * 512].rearrange("(tt p) c -> p tt c", p=128),
                          OBT[ob][:, :].rearrange("p (tt c) -> p tt c", c=512), r=[("OBT", ob)])
                for cg in range(2):
                    wsz = load_w(C_BZ + cg * 512, 512)
                    ob = obt_slot[0] % 2
                    obt_slot[0] += 1
                    for tt in range(16):
                        sz = tm_mm(wsz, 512, tt)
                        wk = next_wk()
                        P.act(lambda e, sz=sz, wk=wk: e.activation(out=WK[wk][:, 0:512], in_=pp[sz][:, :], func=AF.Sigmoid),
                              r=[("pp", sz)], w=[("WK", wk)])
                        P.dve(lambda e, sz=sz, wk=wk, ob=ob, tt=tt: e.tensor_tensor(
                            out=OBT[ob][:, tt * 512:(tt + 1) * 512], in0=pp[sz][:, :], in1=WK[wk][:, 0:512],
                            op=ALU.mult), r=[("pp", sz), ("WK", wk)], w=[("OBT", ob)])
                    P.dma("sp", T["gB"][t0:t0 + 2048, cg * 512:(cg + 1) * 512].rearrange("(tt p) c -> p tt c", p=128),
                          OBT[ob][:, :].rearrange("p (tt c) -> p tt c", c=512), r=[("OBT", ob)])

        P.dma("sp", T["gall_d"][:, :], gall[:, :], r=["gall"])
        P.flush()


def build_rope_tables(P, posi, invf, cosT, sinT, WK, next_wk):
    TWO_PI = 2.0 * math.pi
    C1 = 6.28125
    C2 = TWO_PI - C1
    a = next_wk()
    b = next_wk()
    c = next_wk()
    A, Bk, R = WK[a], WK[b], WK[c]
    N = 2048
    P.dve(lambda e: e.tensor_copy(out=A[:, 0:N], in_=posi[:, :]), r=["posi"], w=[("WK", a)])
    P.dve(lambda e: e.tensor_scalar(out=A[:, 0:N], in0=A[:, 0:N], scalar1=invf[:, 0:1], scalar2=None, op0=ALU.mult),
          r=[("WK", a), "invf"], w=[("WK", a)])
    ki = posi
    P.dve(lambda e: e.tensor_scalar(out=Bk[:, 0:N], in0=A[:, 0:N], scalar1=1.0 / TWO_PI, scalar2=None, op0=ALU.mult),
          r=[("WK", a)], w=[("WK", b)])
    P.dve(lambda e: e.tensor_copy(out=ki[:, :], in_=Bk[:, 0:N]), r=[("WK", b)], w=["posi"])
    P.dve(lambda e: e.tensor_copy(out=Bk[:, 0:N], in_=ki[:, :]), r=["posi"], w=[("WK", b)])
    P.dve(lambda e: e.scalar_tensor_tensor(out=R[:, 0:N], in0=Bk[:, 0:N], scalar=-C1, in1=A[:, 0:N],
                                           op0=ALU.mult, op1=ALU.add), r=[("WK", a), ("WK", b)], w=[("WK", c)])
    P.dve(lambda e: e.scalar_tensor_tensor(out=R[:, 0:N], in0=Bk[:, 0:N], scalar=-C2, in1=R[:, 0:N],
                                           op0=ALU.mult, op1=ALU.add), r=[("WK", b), ("WK", c)], w=[("WK", c)])

    def wrap(X, key):
        P.dve(lambda e: e.tensor_scalar(out=Bk[:, 0:N], in0=X[:, 0:N], scalar1=math.pi, scalar2=-TWO_PI,
                                        op0=ALU.is_gt, op1=ALU.mult), r=[key], w=[("WK", b)])
        P.dve(lambda e: e.tensor_tensor(out=X[:, 0:N], in0=X[:, 0:N], in1=Bk[:, 0:N], op=ALU.add),
              r=[key, ("WK", b)], w=[key])
        P.dve(lambda e: e.tensor_scalar(out=Bk[:, 0:N], in0=X[:, 0:N], scalar1=-math.pi, scalar2=TWO_PI,
                                        op0=ALU.is_lt, op1=ALU.mult), r=[key], w=[("WK", b)])
        P.dve(lambda e: e.tensor_tensor(out=X[:, 0:N], in0=X[:, 0:N], in1=Bk[:, 0:N], op=ALU.add),
              r=[key, ("WK", b)], w=[key])

    wrap(R, ("WK", c))
    P.act(lambda e: e.activation(out=sinT[:, :], in_=R[:, 0:N], func=AF.Sin), r=[("WK", c)], w=["sinT"])
    P.dve(lambda e: e.tensor_scalar(out=R[:, 0:N], in0=R[:, 0:N], scalar1=math.pi / 2, scalar2=None, op0=ALU.add),
          r=[("WK", c)], w=[("WK", c)])
    wrap(R, ("WK", c))
    P.act(lambda e: e.activation(out=cosT[:, :], in_=R[:, 0:N], func=AF.Sin), r=[("WK", c)], w=["cosT"])


def phase_M(nc, P, T):
    mqT, mkT, mv, gA, yaT = T["mqT"], T["mkT"], T["mv"], T["gA"], T["yaT"]
    with contextlib.ExitStack() as st:
        sb = lambda name, shape, dt=F32: st.enter_context(nc.sbuf_tensor("M_" + name, list(shape), dt))
        ps = lambda name, shape, dt=F32: st.enter_context(nc.psum_tensor("M_" + name, list(shape), dt))
        qT = sb("qT", [128, 2, OWN], BF16)
        kT = sb("kT", [128, 2, S], BF16)
        va = sb("va", [128, 64, 257], BF16)
        ktok = sb("ktok", [128, 64, 256], BF16)
        hacc = sb("hacc", [128, 32, 256])
        gat = [sb("gat%d" % i, [128, 256], BF16) for i in range(2)]
        gall = sb("gall", [128, 1024])
        mlng = sb("mlng", [128, D])
        tri = sb("tri", [128, 4, 128])
        identf = sb("identf", [128, 128])
        identb = sb("identb", [128, 128], BF16)
        onesf = sb("onesf", [128, 128])
        LFt = [sb("LFt%d" % d, [128, 256]) for d in range(2)]
        Bc = [sb("Bc%d" % d, [128, 256]) for d in range(2)]
        Aa = [sb("Aa%d" % d, [128, 256]) for d in range(2)]
        EB = [sb("EB%d" % d, [128, 256]) for d in range(2)]
        WST = [sb("WST%d" % d, [128, 256]) for d in range(2)]
        DEC = [sb("DEC%d" % d, [128, 256]) for d in range(2)]
        Cst = [sb("Cst%d" % d, [128, 2, 257]) for d in range(2)]
        Cb = [[sb("Cb%d_%d" % (d, v), [128, 2, 257], BF16) for v in range(2)] for d in range(2)]
        cbver = [0, 0]
        dg = [sb("dg%d" % i, [128, 128]) for i in range(2)]
        Dm = [sb("Dm%d" % i, [128, 128]) for i in range(2)]
        Wm = [sb("Wm%d" % i, [128, 128], BF16) for i in range(2)]
        vw = [sb("vw%d" % i, [128, 257], BF16) for i in range(2)]
        tmpc = [sb("tmpc%d" % i, [128, 257]) for i in range(2)]
        tot = [sb("tot%d" % i, [128, 257]) for i in range(2)]
        sm = [sb("sm%d" % i, [128, 4]) for i in range(2)]
        hs = [sb("hs%d" % i, [128, 256]) for i in range(2)]
        sqv = [sb("sqv%d" % i, [128, 256]) for i in range(2)]
        yab = [sb("yab%d" % i, [128, 256], BF16) for i in range(2)]
        yas = [sb("yas%d" % i, [128, 2, 128], BF16) for i in range(2)]
        pD = ps("pD", [128, 512])
        pST = ps("pST", [128, 512])
        pI = ps("pI", [128, 512])
        pCs = [ps("pC%d" % i, [128, 512]) for i in range(2)]
        pC = pCs[0]
        pU = [ps("pU%d" % i, [128, 512]) for i in range(2)]
        ptrk = ps("ptrk", [128, 1024], BF16)
        ptry = ptrk

        P.dma("sp", gall[:], T["gall_d"], w=["gall"])
        P.dma("sp", mlng[:], T["mlng_rep"], w=["mlng"])
        P.dma("sp", tri[:], T["c_tri"], w=["tri"])
        P.dma("sp", identf[:], T["c_ident"], w=["identf"])
        P.dve(lambda e: e.tensor_copy(out=identb[:], in_=identf[:]), r=["identf"], w=["identb"])
        P.dve(lambda e: e.memset(onesf[:], 1.0), w=["onesf"])
        P.dve(lambda e: e.memset(va[:, :, 256:257], 1.0), w=["va1"])

        g4 = gall[:, :].rearrange("p (c g h) -> p c g h", g=4, h=4)
        v3 = lambda t: t[:, :].rearrange("p (c h) -> p c h", h=4)
        for d in range(2):
            i_d = g4[:, :, 2 * d, :]
            f_d = g4[:, :, 2 * d + 1, :]
            P.act(lambda e, d=d, f_d=f_d: e.activation(out=v3(LFt[d]), in_=f_d, func=AF.Exp, scale=-1.0),
                  r=["gall"], w=[("LFt", d)])
            P.act(lambda e, d=d: e.activation(out=LFt[d][:, :], in_=LFt[d][:, :], func=AF.Ln, bias=1.0),
                  r=[("LFt", d)], w=[("LFt", d)])
            P.pe(lambda e, d=d: e.matmul(pI[:, 0:256], tri[:, d, :], LFt[d][:, :], start=True, stop=True),
                 r=["tri", ("LFt", d)], w=["pI"])
            P.pe(lambda e, d=d: e.matmul(pC[:, 0:256], onesf[:, :], LFt[d][:, :], start=True, stop=True),
                 r=["onesf", ("LFt", d)], w=[("pC", 0)])
            P.dve(lambda e, d=d: e.tensor_scalar(out=Bc[d][:, :], in0=pI[:, 0:256], scalar1=-1.0, scalar2=None, op0=ALU.mult),
                  r=["pI"], w=[("Bc", d)])
            P.dve(lambda e, d=d, i_d=i_d: e.tensor_tensor(out=v3(Aa[d]), in0=pI[:, 0:256].rearrange("p (c h) -> p c h", h=4),
                                                         in1=i_d, op=ALU.add),
                  r=["pI", "gall"], w=[("Aa", d)])
            P.act(lambda e, d=d: e.activation(out=EB[d][:, :], in_=Bc[d][:, :], func=AF.Exp),
                  r=[("Bc", d)], w=[("EB", d)])
            P.dve(lambda e, d=d: e.tensor_copy(out=DEC[d][:, :], in_=pC[:, 0:256]), r=[("pC", 0)], w=[("DEC", d)])
            P.dve(lambda e, d=d: e.tensor_tensor(out=WST[d][:, :], in0=Aa[d][:, :], in1=DEC[d][:, :], op=ALU.subtract),
                  r=[("Aa", d), ("DEC", d)], w=[("WST", d)])
            P.act(lambda e, d=d: e.activation(out=WST[d][:, :], in_=WST[d][:, :], func=AF.Exp),
                  r=[("WST", d)], w=[("WST", d)])
            P.act(lambda e, d=d: e.activation(out=DEC[d][:, :], in_=DEC[d][:, :], func=AF.Exp, scale=-1.0),
                  r=[("DEC", d), ("WST", d)], w=[("DEC", d)])

        cnt = {"o": 0, "u": 0, "g": 0, "y": 0}

        def output_A(h, d, c):
            i = cnt["o"] % 2
            cnt["o"] += 1
            col = c * 4 + h
            cs = slice(c * 128, (c + 1) * 128)
            pCx = pCs[i]
            cbv = Cb[d][cbver[d]]
            cbk = ("Cb", d, cbver[d])
            P.dve(lambda e: e.tensor_scalar(out=dg[i][:], in0=identf[:], scalar1=Bc[d][:, col:col + 1], scalar2=None, op0=ALU.mult),
                  r=["identf", ("Bc", d)], w=[("dg", i)])
            P.pe(lambda e: e.matmul(pD[:, 0:128], onesf[:, :], dg[i][:, :], start=True, stop=False), r=["onesf", ("dg", i)], w=["pD"])
            P.pe(lambda e: e.matmul(pD[:, 0:128], identf[:, :], tri[:, 2 + d, :], start=False, stop=True), r=["identf", "tri"], w=["pD"])
            P.act(lambda e: e.activation(out=Dm[i][:], in_=pD[:, 0:128], func=AF.Exp, bias=Aa[d][:, col:col + 1]),
                  r=["pD", ("Aa", d)], w=[("Dm", i)])
            for dkc in range(2):
                P.pe(lambda e, dkc=dkc: e.matmul(pST[:, 0:128], kT[:, dkc, cs], qT[:, dkc, cs], start=(dkc == 0), stop=(dkc == 1)),
                     r=["kT", "qT"], w=["pST"])
            for dkc in range(2):
                P.pe(lambda e, dkc=dkc: e.matmul(pCx[:, 0:257], qT[:, dkc, cs], cbv[:, dkc, :], start=(dkc == 0), stop=(dkc == 1)),
                     r=["qT", cbk], w=[("pC", i)])
            P.dve(lambda e: e.tensor_tensor(out=Wm[i][:], in0=pST[:, 0:128], in1=Dm[i][:], op=ALU.mult),
                  r=["pST", ("Dm", i)], w=[("Wm", i)])
            P.pe(lambda e: e.matmul(pI[:, 0:257], Wm[i][:, :], va[:, c, :], start=True, stop=True),
                 r=[("Wm", i), "va", "va1"], w=["pI"])
            return (h, d, c, i)

        def output_B(ctx):
            h, d, c, i = ctx
            col = c * 4 + h
            pCx = pCs[i]
            P.act(lambda e: e.activation(out=tmpc[i][:], in_=pCx[:, 0:257], func=AF.Copy, scale=EB[d][:, col:col + 1]),
                  r=[("pC", i), ("EB", d)], w=[("tmpc", i)])
            P.dve(lambda e: e.tensor_tensor(out=tot[i][:], in0=tmpc[i][:], in1=pI[:, 0:257], op=ALU.add),
                  r=[("tmpc", i), "pI"], w=[("tot", i)])
            P.dve(lambda e: e.tensor_scalar(out=sm[i][:, 3:4], in0=tot[i][:, 256:257], scalar1=-1.0, scalar2=1.0, op0=ALU.mult, op1=ALU.max),
                  r=[("tot", i)], w=[("sm", i)])
            P.dve(lambda e: e.tensor_tensor(out=sm[i][:, 0:1], in0=tot[i][:, 256:257], in1=sm[i][:, 3:4], op=ALU.max),
                  r=[("tot", i), ("sm", i)], w=[("sm", i)])
            P.dve(lambda e: e.reciprocal(out=sm[i][:, 1:2], in_=sm[i][:, 0:1]), r=[("sm", i)], w=[("sm", i)])
            if d == 0:
                P.dve(lambda e: e.tensor_scalar(out=hacc[:, c, :], in0=tot[i][:, 0:256], scalar1=sm[i][:, 1:2], scalar2=None, op0=ALU.mult),
                      r=[("tot", i), ("sm", i)], w=[("hacc", c)])
                return
            gi = cnt["g"] % 2
            cnt["g"] += 1
            P.dma("sp", gat[gi][:], gA[c * 128:(c + 1) * 128, h * 256:(h + 1) * 256], w=[("gat", gi)])
            P.dve(lambda e: e.scalar_tensor_tensor(out=hs[i][:], in0=tot[i][:, 0:256], scalar=sm[i][:, 1:2], in1=hacc[:, c, :],
                                                   op0=ALU.mult, op1=ALU.add),
                  r=[("tot", i), ("sm", i), ("hacc", c)], w=[("hs", i)])
            P.dve(lambda e: e.tensor_tensor(out=sqv[i][:], in0=hs[i][:], in1=hs[i][:], op=ALU.mult), r=[("hs", i)], w=[("sqv", i)])
            P.dve(lambda e: e.reduce_sum(out=sm[i][:, 2:3], in_=sqv[i][:], axis=AX.X), r=[("sqv", i)], w=[("smb", i)])
            P.act(lambda e: e.activation(out=sm[i][:, 2:3], in_=sm[i][:, 2:3], func=AF.Ln, scale=1.0 / 256.0, bias=NORM_EPS),
                  r=[("smb", i)], w=[("smb", i)])
            P.act(lambda e: e.activation(out=sm[i][:, 2:3], in_=sm[i][:, 2:3], func=AF.Exp, scale=-0.5),
                  r=[("smb", i)], w=[("smb", i)])
            P.dve(lambda e: e.scalar_tensor_tensor(out=sqv[i][:], in0=hs[i][:], scalar=sm[i][:, 2:3],
                                                   in1=mlng[:, h * 256:(h + 1) * 256], op0=ALU.mult, op1=ALU.mult),
                  r=[("hs", i), ("smb", i), "mlng"], w=[("sqv", i)])
            P.dve(lambda e: e.tensor_tensor(out=yab[i][:], in0=sqv[i][:], in1=gat[gi][:], op=ALU.mult),
                  r=[("sqv", i), ("gat", gi)], w=[("yab", i)])
            for k in range(2):
                P.pe(lambda e, k=k: e.transpose(out=ptry[:, k * 128:(k + 1) * 128], in_=yab[i][:, k * 128:(k + 1) * 128],
                                                identity=identb[:]), r=[("yab", i), "identb"], w=["ptrk"])
            P.act(lambda e: e.activation(out=yas[i][:, :, :], in_=ptry[:, 0:256].rearrange("p (k t) -> p k t", k=2), func=AF.Copy),
                  r=["ptrk"], w=[("yas", i)])
            P.dma("sp", yaT[h * 256:(h + 1) * 256, c * 128:(c + 1) * 128].rearrange("(k p) t -> p k t", p=128),
                  yas[i][:, :, :], r=[("yas", i)])

        def update_step(h, d, c):
            i = cnt["u"] % 2
            cnt["u"] += 1
            col = c * 4 + h
            nv = 1 - cbver[d]
            nb_ = Cb[d][nv]
            nbk = ("Cb", d, nv)
            P.act(lambda e: e.activation(out=vw[i][:], in_=va[:, c, :], func=AF.Copy, scale=WST[d][:, col:col + 1]),
                  r=["va", "va1", ("WST", d)], w=[("vw", i)])
            for dkc in range(2):
                P.pe(lambda e, dkc=dkc: e.matmul(pU[dkc][:, 0:257], ktok[:, c, dkc * 128:(dkc + 1) * 128], vw[i][:, :], start=True, stop=True),
                     r=[("ktok", c), ("vw", i)], w=[("pU", dkc)])
                P.dve(lambda e, dkc=dkc: e.scalar_tensor_tensor(out=Cst[d][:, dkc, :], in0=Cst[d][:, dkc, :], scalar=DEC[d][:, col:col + 1],
                                                                in1=pU[dkc][:, 0:257], op0=ALU.mult, op1=ALU.add),
                      r=[("Cst", d, dkc), ("DEC", d), ("pU", dkc)], w=[("Cst", d, dkc)])
                P.act(lambda e, dkc=dkc, nb_=nb_: e.activation(out=nb_[:, dkc, :], in_=Cst[d][:, dkc, :], func=AF.Copy),
                      r=[("Cst", d, dkc)], w=[nbk])
            cbver[d] = nv

        for h in range(4):
            for dkc in range(2):
                r0 = h * 256 + dkc * 128
                P.dma("sp", qT[:, dkc, :], mqT[r0:r0 + 128, 0:OWN], w=["qT"])
                P.dma("sp", kT[:, dkc, :], mkT[r0:r0 + 128, :], w=["kT"])
            P.dma("sp", va[:, :, 0:256], mv[:, h * 256:(h + 1) * 256].rearrange("(c p) f -> p c f", p=128), w=["va"])
            for d in range(2):
                P.dve(lambda e, d=d: e.memset(Cst[d][:], 0.0), w=[("Cst", d, 0), ("Cst", d, 1)])
                P.dve(lambda e, d=d: e.memset(Cb[d][0][:], 0.0), w=[("Cb", d, 0)])
                cbver[d] = 0
            for c in range(64):
                for dkc in range(2):
                    P.pe(lambda e, c=c, dkc=dkc: e.transpose(out=ptrk[:, dkc * 128:(dkc + 1) * 128],
                                                             in_=kT[:, dkc, c * 128:(c + 1) * 128], identity=identb[:]),
                         r=["kT", "identb"], w=["ptrk"])
                if c % 2 == 0:
                    P.act(lambda e, c=c: e.activation(out=ktok[:, c, :], in_=ptrk[:, 0:256], func=AF.Copy), r=["ptrk"], w=[("ktok", c)])
                else:
                    P.dve(lambda e, c=c: e.tensor_copy(out=ktok[:, c, :], in_=ptrk[:, 0:256]), r=["ptrk"], w=[("ktok", c)])
            for i in range(32):
                ctx = output_A(h, 0, i)
                if i < 31:
                    update_step(h, 0, i)
                update_step(h, 1, 63 - i)
                output_B(ctx)
            for i in range(32, 64):
                c = 63 - i
                ctx = output_A(h, 1, c)
                if c > 0:
                    update_step(h, 1, c)
                output_B(ctx)
        P.flush()


def phase_M2(nc, P, T):
    mqT, mkT, mv, gA, yaT, hfwd = T["mqT"], T["mkT"], T["mv"], T["gA"], T["yaT"], T["hfwd"]
    with contextlib.ExitStack() as st:
        sb = lambda name, shape, dt=F32: st.enter_context(nc.sbuf_tensor("M2_" + name, list(shape), dt))
        ps = lambda name, shape, dt=F32: st.enter_context(nc.psum_tensor("M2_" + name, list(shape), dt))
        gall = sb("gall", [128, 1024])
        mlng = sb("mlng", [128, D])
        tri = sb("tri", [128, 4, 128])
        identf = sb("identf", [128, 128])
        identb = sb("identb", [128, 128], BF16)
        onesf = sb("onesf", [128, 128])
        LFt = [sb("LFt%d" % d, [128, 256]) for d in range(2)]
        Bc = [sb("Bc%d" % d, [128, 256]) for d in range(2)]
        Aa = [sb("Aa%d" % d, [128, 256]) for d in range(2)]
        EB = [sb("EB%d" % d, [128, 256]) for d in range(2)]
        WST = [sb("WST%d" % d, [128, 256]) for d in range(2)]
        DEC = [sb("DEC%d" % d, [128, 256]) for d in range(2)]
        Cst = [sb("Cst%d" % d, [128, 4, 2, 257]) for d in range(2)]
        Cb = [[sb("Cb%d_%d" % (d, v), [128, 4, 2, 257], BF16) for v in range(2)] for d in range(2)]
        cbver = [0, 0]
        NQ, NK = 2, 4
        qc = [sb("qc%d" % i, [128, 8, 128], BF16) for i in range(NQ)]
        kc = [sb("kc%d" % i, [128, 8, 128], BF16) for i in range(NK)]
        vc = [sb("vc%d" % i, [128, 4, 257], BF16) for i in range(NK)]
        ktc = [sb("ktc%d" % i, [128, 1024], BF16) for i in range(NK)]
        gac = [sb("gac%d" % i, [128, 1024], BF16) for i in range(2)]
        hfr = [sb("hfr%d" % i, [128, 1024]) for i in range(2)]
        hfw = [sb("hfw%d" % i, [128, 1024]) for i in range(2)]
        yasc = [sb("yasc%d" % i, [128, 8, 128], BF16) for i in range(2)]
        NWB = 4
        dg = [sb("dg%d" % i, [128, 128]) for i in range(NWB)]
        Dm = [sb("Dm%d" % i, [128, 128]) for i in range(NWB)]
        Wm = [sb("Wm%d" % i, [128, 128], BF16) for i in range(NWB)]
        vw = [sb("vw%d" % i, [128, 257], BF16) for i in range(NWB)]
        tmpc = [sb("tmpc%d" % i, [128, 257]) for i in range(NWB)]
        tot = [sb("tot%d" % i, [128, 257]) for i in range(NWB)]
        sm = [sb("sm%d" % i, [128, 4]) for i in range(NWB)]
        hs = [sb("hs%d" % i, [128, 256]) for i in range(NWB)]
        sqv = [sb("sqv%d" % i, [128, 256]) for i in range(NWB)]
        yab = [sb("yab%d" % i, [128, 256], BF16) for i in range(NWB)]
        pD = ps("pD", [128, 512])
        pST = ps("pST", [128, 512])
        pI = ps("pI", [128, 512])
        pC = ps("pC", [128, 512])
        pU = [ps("pU%d" % i, [128, 512]) for i in range(2)]
        ptrk = ps("ptrk", [128, 1024], BF16)
        ptry = ps("ptry", [128, 1024], BF16)

        P.dma("sp", gall[:], T["gall_d"], w=["gall"])
        P.dma("sp", mlng[:], T["mlng_rep"], w=["mlng"])
        P.dma("sp", tri[:], T["c_tri"], w=["tri"])
        P.dma("sp", identf[:], T["c_ident"], w=["identf"])
        P.dve(lambda e: e.tensor_copy(out=identb[:], in_=identf[:]), r=["identf"], w=["identb"])
        P.dve(lambda e: e.memset(onesf[:], 1.0), w=["onesf"])
        for i in range(NK):
            P.dve(lambda e, i=i: e.memset(vc[i][:, :, 256:257], 1.0), w=[("vc1", i)])
        for d in range(2):
            P.dve(lambda e, d=d: e.memset(Cst[d][:], 0.0), w=[("Cst", d, h, k) for h in range(4) for k in range(2)])
            P.dve(lambda e, d=d: e.memset(Cb[d][0][:], 0.0), w=[("Cb", d, 0, h) for h in range(4)])

        g4 = gall[:, :].rearrange("p (c g h) -> p c g h", g=4, h=4)
        v3 = lambda t: t[:, :].rearrange("p (c h) -> p c h", h=4)
        for d in range(2):
            i_d = g4[:, :, 2 * d, :]
            f_d = g4[:, :, 2 * d + 1, :]
            P.act(lambda e, d=d, f_d=f_d: e.activation(out=v3(LFt[d]), in_=f_d, func=AF.Exp, scale=-1.0),
                  r=["gall"], w=[("LFt", d)])
            P.act(lambda e, d=d: e.activation(out=LFt[d][:, :], in_=LFt[d][:, :], func=AF.Ln, bias=1.0),
                  r=[("LFt", d)], w=[("LFt", d)])
            P.pe(lambda e, d=d: e.matmul(pI[:, 0:256], tri[:, d, :], LFt[d][:, :], start=True, stop=True),
                 r=["tri", ("LFt", d)], w=["pI"])
            P.pe(lambda e, d=d: e.matmul(pC[:, 0:256], onesf[:, :], LFt[d][:, :], start=True, stop=True),
                 r=["onesf", ("LFt", d)], w=["pC"])
            P.dve(lambda e, d=d: e.tensor_scalar(out=Bc[d][:, :], in0=pI[:, 0:256], scalar1=-1.0, scalar2=None, op0=ALU.mult),
                  r=["pI"], w=[("Bc", d)])
            P.dve(lambda e, d=d, i_d=i_d: e.tensor_tensor(out=v3(Aa[d]), in0=pI[:, 0:256].rearrange("p (c h) -> p c h", h=4),
                                                         in1=i_d, op=ALU.add),
                  r=["pI", "gall"], w=[("Aa", d)])
            P.act(lambda e, d=d: e.activation(out=EB[d][:, :], in_=Bc[d][:, :], func=AF.Exp),
                  r=[("Bc", d)], w=[("EB", d)])
            P.dve(lambda e, d=d: e.tensor_copy(out=DEC[d][:, :], in_=pC[:, 0:256]), r=["pC"], w=[("DEC", d)])
            P.dve(lambda e, d=d: e.tensor_tensor(out=WST[d][:, :], in0=Aa[d][:, :], in1=DEC[d][:, :], op=ALU.subtract),
                  r=[("Aa", d), ("DEC", d)], w=[("WST", d)])
            P.act(lambda e, d=d: e.activation(out=WST[d][:, :], in_=WST[d][:, :], func=AF.Exp),
                  r=[("WST", d)], w=[("WST", d)])
            P.act(lambda e, d=d: e.activation(out=DEC[d][:, :], in_=DEC[d][:, :], func=AF.Exp, scale=-1.0),
                  r=[("DEC", d), ("WST", d)], w=[("DEC", d)])

        cnt = {"w": 0, "k": 0, "q": 0, "g": 0, "cp": 0}

        def load_chunk(c, need_q, need_bw_out):
            ks = cnt["k"] % NK
            cnt["k"] += 1
            cs = slice(c * 128, (c + 1) * 128)
            P.dma("sp", kc[ks][:], mkT[:, cs].rearrange("(f p) t -> p f t", p=128), w=[("kc", ks)])
            P.dma("sp", vc[ks][:, :, 0:256], mv[cs, :].rearrange("p (h f) -> p h f", h=4), w=[("vc", ks)])
            for f in range(8):
                P.pe(lambda e, f=f: e.transpose(out=ptrk[:, f * 128:(f + 1) * 128], in_=kc[ks][:, f, :], identity=identb[:]),
                     r=[("kc", ks), "identb"], w=["ptrk"])
            if cnt["cp"] % 2 == 0:
                P.act(lambda e: e.activation(out=ktc[ks][:, :], in_=ptrk[:, :], func=AF.Copy), r=["ptrk"], w=[("ktc", ks)])
            else:
                P.dve(lambda e: e.tensor_copy(out=ktc[ks][:, :], in_=ptrk[:, :]), r=["ptrk"], w=[("ktc", ks)])
            cnt["cp"] += 1
            ctx = {"c": c, "ks": ks}
            if need_q:
                qs = cnt["q"] % NQ
                cnt["q"] += 1
                P.dma("sp", qc[qs][:], mqT[:, cs].rearrange("(f p) t -> p f t", p=128), w=[("qc", qs)])
                ctx["qs"] = qs
            return ctx

        def load_bw_extra(ctx):
            c = ctx["c"]
            cs = slice(c * 128, (c + 1) * 128)
            gs = cnt["g"] % 2
            cnt["g"] += 1
            P.dma("sp", gac[gs][:], gA[cs, :], w=[("gac", gs)])
            P.dma("sp", hfr[gs][:], hfwd[cs, :], r=[("hfwd", c)], w=[("hfr", gs)])
            ctx["gs"] = gs

        def output_A(ck, h, d):
            i = cnt["w"] % NWB
            cnt["w"] += 1
            c, ks, qs = ck["c"], ck["ks"], ck["qs"]
            col = c * 4 + h
            cbv = Cb[d][cbver[d]]
            cbk = ("Cb", d, cbver[d], h)
            P.dve(lambda e: e.tensor_scalar(out=dg[i][:], in0=identf[:], scalar1=Bc[d][:, col:col + 1], scalar2=None, op0=ALU.mult),
                  r=["identf", ("Bc", d)], w=[("dg", i)])
            P.pe(lambda e: e.matmul(pD[:, 0:128], onesf[:, :], dg[i][:, :], start=True, stop=False), r=["onesf", ("dg", i)], w=["pD"])
            P.pe(lambda e: e.matmul(pD[:, 0:128], identf[:, :], tri[:, 2 + d, :], start=False, stop=True), r=["identf", "tri"], w=["pD"])
            P.act(lambda e: e.activation(out=Dm[i][:], in_=pD[:, 0:128], func=AF.Exp, bias=Aa[d][:, col:col + 1]),
                  r=["pD", ("Aa", d)], w=[("Dm", i)])
            for dkc in range(2):
                P.pe(lambda e, dkc=dkc: e.matmul(pST[:, 0:128], kc[ks][:, h * 2 + dkc, :], qc[qs][:, h * 2 + dkc, :],
                                                 start=(dkc == 0), stop=(dkc == 1)),
                     r=[("kc", ks), ("qc", qs)], w=["pST"])
            for dkc in range(2):
                P.pe(lambda e, dkc=dkc: e.matmul(pC[:, 0:257], qc[qs][:, h * 2 + dkc, :], cbv[:, h, dkc, :],
                                                 start=(dkc == 0), stop=(dkc == 1)),
                     r=[("qc", qs), cbk], w=["pC"])
            P.act(lambda e: e.activation(out=tmpc[i][:], in_=pC[:, 0:257], func=AF.Copy, scale=EB[d][:, col:col + 1]),
                  r=["pC", ("EB", d)], w=[("tmpc", i)])
            P.dve(lambda e: e.tensor_tensor(out=Wm[i][:], in0=pST[:, 0:128], in1=Dm[i][:], op=ALU.mult),
                  r=["pST", ("Dm", i)], w=[("Wm", i)])
            P.pe(lambda e: e.matmul(pI[:, 0:257], Wm[i][:, :], vc[ks][:, h, :], start=True, stop=True),
                 r=[("Wm", i), ("vc", ks), ("vc1", ks)], w=["pI"])
            P.dve(lambda e: e.tensor_tensor(out=tot[i][:], in0=tmpc[i][:], in1=pI[:, 0:257], op=ALU.add),
                  r=[("tmpc", i), "pI"], w=[("tot", i)])
            return (ck, h, d, i)

        def output_B(ctx, ws):
            ck, h, d, i = ctx
            c = ck["c"]
            hsl = slice(h * 256, (h + 1) * 256)
            P.dve(lambda e: e.tensor_scalar(out=sm[i][:, 3:4], in0=tot[i][:, 256:257], scalar1=-1.0, scalar2=1.0, op0=ALU.mult, op1=ALU.max),
                  r=[("tot", i)], w=[("sm", i)])
            P.dve(lambda e: e.tensor_tensor(out=sm[i][:, 0:1], in0=tot[i][:, 256:257], in1=sm[i][:, 3:4], op=ALU.max),
                  r=[("tot", i), ("sm", i)], w=[("sm", i)])
            P.dve(lambda e: e.reciprocal(out=sm[i][:, 1:2], in_=sm[i][:, 0:1]), r=[("sm", i)], w=[("sm", i)])
            if d == 0:
                P.act(lambda e: e.activation(out=hfw[ws][:, hsl], in_=tot[i][:, 0:256], func=AF.Copy, scale=sm[i][:, 1:2]),
                      r=[("tot", i), ("sm", i)], w=[("hfw", ws, h)])
                return
            gs = ck["gs"]
            P.dve(lambda e: e.scalar_tensor_tensor(out=hs[i][:], in0=tot[i][:, 0:256], scalar=sm[i][:, 1:2], in1=hfr[gs][:, hsl],
                                                   op0=ALU.mult, op1=ALU.add),
                  r=[("tot", i), ("sm", i), ("hfr", gs)], w=[("hs", i)])
            P.dve(lambda e: e.tensor_tensor(out=sqv[i][:], in0=hs[i][:], in1=hs[i][:], op=ALU.mult), r=[("hs", i)], w=[("sqv", i)])
            P.dve(lambda e: e.reduce_sum(out=sm[i][:, 2:3], in_=sqv[i][:], axis=AX.X), r=[("sqv", i)], w=[("smb", i)])
            P.act(lambda e: e.activation(out=sm[i][:, 2:3], in_=sm[i][:, 2:3], func=AF.Ln, scale=1.0 / 256.0, bias=NORM_EPS),
                  r=[("smb", i)], w=[("smb", i)])
            P.act(lambda e: e.activation(out=sm[i][:, 2:3], in_=sm[i][:, 2:3], func=AF.Exp, scale=-0.5),
                  r=[("smb", i)], w=[("smb", i)])
            P.dve(lambda e: e.scalar_tensor_tensor(out=sqv[i][:], in0=hs[i][:], scalar=sm[i][:, 2:3],
                                                   in1=mlng[:, hsl], op0=ALU.mult, op1=ALU.mult),
                  r=[("hs", i), ("smb", i), "mlng"], w=[("sqv", i)])
            P.dve(lambda e: e.tensor_tensor(out=yab[i][:], in0=sqv[i][:], in1=gac[gs][:, hsl], op=ALU.mult),
                  r=[("sqv", i), ("gac", gs)], w=[("yab", i)])
            for k in range(2):
                f = h * 2 + k
                P.pe(lambda e, k=k, f=f: e.transpose(out=ptry[:, f * 128:(f + 1) * 128], in_=yab[i][:, k * 128:(k + 1) * 128],
                                                     identity=identb[:]), r=[("yab", i), "identb"], w=["ptry"])

        def update_step(ck, h, d):
            i = cnt["w"] % NWB
            cnt["w"] += 1
            c, ks = ck["c"], ck["ks"]
            col = c * 4 + h
            nv = 1 - cbver[d]
            nb_ = Cb[d][nv]
            nbk = ("Cb", d, nv, h)
            P.act(lambda e: e.activation(out=vw[i][:], in_=vc[ks][:, h, :], func=AF.Copy, scale=WST[d][:, col:col + 1]),
                  r=[("vc", ks), ("vc1", ks), ("WST", d)], w=[("vw", i)])
            for dkc in range(2):
                f = h * 2 + dkc
                P.pe(lambda e, dkc=dkc, f=f: e.matmul(pU[dkc][:, 0:257], ktc[ks][:, f * 128:(f + 1) * 128], vw[i][:, :], start=True, stop=True),
                     r=[("ktc", ks), ("vw", i)], w=[("pU", dkc)])
                P.dve(lambda e, dkc=dkc: e.scalar_tensor_tensor(out=Cst[d][:, h, dkc, :], in0=Cst[d][:, h, dkc, :], scalar=DEC[d][:, col:col + 1],
                                                                in1=pU[dkc][:, 0:257], op0=ALU.mult, op1=ALU.add),
                      r=[("Cst", d, h, dkc), ("DEC", d), ("pU", dkc)], w=[("Cst", d, h, dkc)])
                P.act(lambda e, dkc=dkc: e.activation(out=nb_[:, h, dkc, :], in_=Cst[d][:, h, dkc, :], func=AF.Copy),
                      r=[("Cst", d, h, dkc)], w=[nbk])

        def loads_for(step):
            if step < 32:
                return [load_chunk(step, True, False), load_chunk(63 - step, False, False)]
            return [load_chunk(63 - step, True, True)]

        nxt = loads_for(0)
        for step in range(64):
            cur = nxt
            if step + 1 < 64:
                nxt = loads_for(step + 1)
            ws = step % 2
            if step < 32:
                ca, cbk_ = cur
                ctxs = [output_A(ca, h, 0) for h in range(4)]
                if step < 31:
                    for h in range(4):
                        update_step(ca, h, 0)
                    cbver[0] = 1 - cbver[0]
                for h in range(4):
                    update_step(cbk_, h, 1)
                cbver[1] = 1 - cbver[1]
                for cx in ctxs:
                    output_B(cx, ws)
                c = ca["c"]
                P.dma("sp", hfwd[c * 128:(c + 1) * 128, :], hfw[ws][:, :], r=[("hfw", ws, h) for h in range(4)], w=[("hfwd", c)])
                if step == 31:
                    load_bw_extra(nxt[0])
            else:
                ca = cur[0]
                c = ca["c"]
                ctxs = [output_A(ca, h, 1) for h in range(4)]
                if c > 0:
                    for h in range(4):
                        update_step(ca, h, 1)
                    cbver[1] = 1 - cbver[1]
                for cx in ctxs:
                    output_B(cx, ws)
                P.act(lambda e, ws=ws: e.activation(out=yasc[ws][:, :, :], in_=ptry[:, :].rearrange("p (f t) -> p f t", f=8), func=AF.Copy),
                      r=["ptry"], w=[("yasc", ws)])
                P.dma("sp", yaT[:, c * 128:(c + 1) * 128].rearrange("(f p) t -> p f t", p=128), yasc[ws][:, :, :], r=[("yasc", ws)])
                if step + 1 < 64:
                    load_bw_extra(nxt[0])
        P.flush()


def phase_D(nc, P, T):
    dqT, dkT, dv, gB, ybT = T["dqT"], T["dkT"], T["dv"], T["gB"], T["ybT"]
    with contextlib.ExitStack() as st:
        sb = lambda name, shape, dt=F32: st.enter_context(nc.sbuf_tensor("D_" + name, list(shape), dt))
        ps = lambda name, shape, dt=F32: st.enter_context(nc.psum_tensor("D_" + name, list(shape), dt))
        kT = [sb("dkT%d" % i, [128, S], BF16) for i in range(2)]
        va = [sb("dva%d" % i, [128, 64, 129], BF16) for i in range(2)]
        qT = [sb("dqT%d" % i, [128, OWN], BF16) for i in range(2)]
        NE = 3
        E = [sb("dE%d" % i, [128, 1024], BF16) for i in range(NE)]
        gbt = [sb("dgb%d" % i, [128, 4, 128], BF16) for i in range(2)]
        lamr = sb("lamr", [128, 256])
        ltmp = sb("ltmp", [128, 128])
        lam = sb("lam", [128, 4])
        subg = sb("subg", [128, 128])
        identf = sb("identf", [128, 128])
        identb = sb("identb", [128, 128], BF16)
        r12 = [sb("r12_%d" % i, [128, 4]) for i in range(2)]
        o1 = [sb("o1_%d" % i, [128, 128]) for i in range(2)]
        o2 = [sb("o2_%d" % i, [128, 128]) for i in range(2)]
        sq = [sb("sq_%d" % i, [128, 128]) for i in range(2)]
        ybq = [sb("ybq%d" % i, [128, 128], BF16) for i in range(8)]
        ybs = [sb("ybs%d" % i, [128, 512], BF16) for i in range(2)]
        pS = [ps("pS%d" % i, [128, 1024]) for i in range(2)]
        pacc = [ps("pacc%d" % i, [128, 512]) for i in range(3)]
        ptr = ps("dptr", [128, 512], BF16)

        P.dma("sp", lamr[:], T["lam_rep"], w=["lamr"])
        P.dma("sp", subg[:], T["subln_rep"], w=["subg"])
        P.dma("sp", identf[:], T["c_ident"], w=["identf"])
        P.dve(lambda e: e.tensor_copy(out=identb[:], in_=identf[:]), r=["identf"], w=["identb"])
        P.dve(lambda e: e.tensor_scalar(out=subg[:], in0=subg[:], scalar1=(1.0 - LAMBDA_INIT), scalar2=None, op0=ALU.mult),
              r=["subg"], w=["subg"])
        P.dve(lambda e: e.tensor_tensor(out=ltmp[:, 0:64], in0=lamr[:, 0:64], in1=lamr[:, 64:128], op=ALU.mult),
              r=["lamr"], w=["ltmp"])
        P.dve(lambda e: e.tensor_tensor(out=ltmp[:, 64:128], in0=lamr[:, 128:192], in1=lamr[:, 192:256], op=ALU.mult),
              r=["lamr"], w=["ltmp"])
        P.dve(lambda e: e.reduce_sum(out=lam[:, 0:2], in_=ltmp[:, :].rearrange("p (a b) -> p a b", a=2), axis=AX.X),
              r=["ltmp"], w=["lam"])
        P.act(lambda e: e.activation(out=lam[:, 0:2], in_=lam[:, 0:2], func=AF.Exp), r=["lam"], w=["lam"])
        P.dve(lambda e: e.tensor_tensor(out=lam[:, 2:3], in0=lam[:, 0:1], in1=lam[:, 1:2], op=ALU.subtract),
              r=["lam"], w=["lam"])
        P.dve(lambda e: e.tensor_scalar(out=lam[:, 3:4], in0=lam[:, 2:3], scalar1=LAMBDA_INIT, scalar2=-1.0,
                                        op0=ALU.add, op1=ALU.mult), r=["lam"], w=["lam"])
        for i in range(2):
            P.dve(lambda e, i=i: e.memset(va[i][:, :, 128:129], 1.0), w=[("va1", i)])

        def acc_ap(a, lo, hi):
            return pacc[a // 3][:, (a % 3) * 129 + lo:(a % 3) * 129 + hi]

        accS = [sb("accS%d" % i, [128, 8 * 129]) for i in range(2)]
        mhalf = sb("mhalf", [128, 1])
        P.dve(lambda e: e.memset(mhalf[:], -0.5), w=["mhalf"])

        def load_head(h):
            hs = h % 2
            P.dma("sp", kT[hs][:], dkT[h * 128:(h + 1) * 128, :], w=[("kT", hs)])
            P.dma("sp", qT[hs][:], dqT[h * 128:(h + 1) * 128, :], w=[("qT", hs)])
            P.dma("sp", va[hs][:, :, 0:128], dv[:, h * 128:(h + 1) * 128].rearrange("(kb p) c -> p kb c", p=128),
                  w=[("va", hs)])

        def load_gb(h, qt):
            gs = (h * 8 + qt) % 2
            P.dma("sp", gbt[gs][:], gB[qt * 512:(qt + 1) * 512, h * 128:(h + 1) * 128].rearrange("(u p) c -> p u c", p=128),
                  w=[("gbt", gs)])

        def qk(i, h, qt, kb):
            sl, hs = i % 2, h % 2
            for j in range(2):
                P.pe(lambda e, j=j: e.matmul(
                    pS[sl][:, j * 512:(j + 1) * 512], kT[hs][j * 64:(j + 1) * 64, kb * 128:(kb + 1) * 128],
                    qT[hs][j * 64:(j + 1) * 64, qt * 512:(qt + 1) * 512], start=True, stop=True),
                    r=[("kT", hs), ("qT", hs)], w=[("pS", sl)])

        def ex(i):
            sl, es = i % 2, i % NE
            P.act(lambda e: e.activation(out=E[es][:, :], in_=pS[sl][:, :], func=AF.Exp), r=[("pS", sl)], w=[("E", es)])

        def pv(i, h, kb):
            es, hs = i % NE, h % 2
            for a in range(8):
                j, u = a // 4, a % 4
                P.pe(lambda e, a=a, j=j, u=u: e.matmul(
                    acc_ap(a, 0, 129), E[es][:, j * 512 + u * 128:j * 512 + (u + 1) * 128], va[hs][:, kb, :],
                    start=(kb == 0 and a % 3 == 0), stop=(kb == 63), skip_group_check=True),
                    r=[("E", es), ("va", hs), ("va1", hs)], w=[("accb", a // 3)])

        epi = [0]

        def epilogue(h, qt):
            gs = (h * 8 + qt) % 2
            ys = gs
            ai = gs
            A = accS[ai]
            for b in range(3):
                n = 387 if b < 2 else 258
                P.dve(lambda e, b=b, n=n: e.tensor_copy(out=A[:, b * 387:b * 387 + n], in_=pacc[b][:, 0:n]),
                      r=[("accb", b)], w=[("accS", ai)])
            sa = lambda a, lo, hi: A[:, a * 129 + lo:a * 129 + hi]
            for u in range(4):
                ep = epi[0] % 2
                epi[0] += 1
                a0, a1 = u, 4 + u
                P.dve(lambda e, ep=ep, a0=a0: e.reciprocal(out=r12[ep][:, 0:1], in_=sa(a0, 128, 129)),
                      r=[("accS", ai)], w=[("r12", ep)])
                P.dve(lambda e, ep=ep, a1=a1: e.reciprocal(out=r12[ep][:, 1:2], in_=sa(a1, 128, 129)),
                      r=[("accS", ai)], w=[("r12", ep)])
                P.dve(lambda e, ep=ep: e.tensor_tensor(out=r12[ep][:, 2:3], in0=r12[ep][:, 1:2], in1=lam[:, 3:4], op=ALU.mult),
                      r=[("r12", ep), "lam"], w=[("r12", ep)])
                P.dve(lambda e, ep=ep, a0=a0: e.tensor_scalar(out=o1[ep][:], in0=sa(a0, 0, 128), scalar1=r12[ep][:, 0:1],
                                                              scalar2=None, op0=ALU.mult),
                      r=[("accS", ai), ("r12", ep)], w=[("o1", ep)])
                P.dve(lambda e, ep=ep, a1=a1: e.scalar_tensor_tensor(out=o2[ep][:], in0=sa(a1, 0, 128), scalar=r12[ep][:, 2:3],
                                                                     in1=o1[ep][:], op0=ALU.mult, op1=ALU.add),
                      r=[("accS", ai), ("r12", ep), ("o1", ep)], w=[("o2", ep)])
                P.dve(lambda e, ep=ep: e.tensor_tensor(out=sq[ep][:], in0=o2[ep][:], in1=o2[ep][:], op=ALU.mult),
                      r=[("o2", ep)], w=[("sq", ep)])
                P.dve(lambda e, ep=ep: e.reduce_sum(out=r12[ep][:, 3:4], in_=sq[ep][:], axis=AX.X),
                      r=[("sq", ep)], w=[("r12b", ep)])
                P.dve(lambda e, ep=ep: e.tensor_scalar(out=r12[ep][:, 3:4], in0=r12[ep][:, 3:4], scalar1=1.0 / 128.0, scalar2=NORM_EPS,
                                                       op0=ALU.mult, op1=ALU.add), r=[("r12b", ep)], w=[("r12b", ep)])
                P.pool(lambda e, ep=ep: e.tensor_tensor(out=r12[ep][:, 3:4], in0=r12[ep][:, 3:4], in1=mhalf[:, 0:1], op=ALU.pow),
                       r=[("r12b", ep), "mhalf"], w=[("r12b", ep)])
                P.dve(lambda e, ep=ep: e.scalar_tensor_tensor(out=o1[ep][:], in0=o2[ep][:], scalar=r12[ep][:, 3:4],
                                                              in1=subg[:], op0=ALU.mult, op1=ALU.mult),
                      r=[("o2", ep), ("r12b", ep), "subg"], w=[("o1", ep)])
                yq = gs * 4 + u
                P.dve(lambda e, ep=ep, u=u, yq=yq: e.tensor_tensor(out=ybq[yq][:], in0=o1[ep][:], in1=gbt[gs][:, u, :], op=ALU.mult),
                      r=[("o1", ep), ("gbt", gs)], w=[("ybq", yq)])

        def epilogue2(h, qt):
            gs = (h * 8 + qt) % 2
            ys = gs
            for u in range(4):
                yq = gs * 4 + u
                P.pe(lambda e, u=u, yq=yq: e.transpose(out=ptr[:, u * 128:(u + 1) * 128], in_=ybq[yq][:], identity=identb[:]),
                     r=[("ybq", yq), "identb"], w=["ptr"])
            P.dve(lambda e: e.tensor_copy(out=ybs[ys][:, :], in_=ptr[:, :]), r=["ptr"], w=[("ybs", ys)])
            P.dma("sp", ybT[h * 128:(h + 1) * 128, qt * 512:(qt + 1) * 512], ybs[ys][:], r=[("ybs", ys)])

        blocks = [(h, qt, kb) for h in range(8) for qt in range(8) for kb in range(64)]
        nb = len(blocks)
        load_head(0)
        load_gb(0, 0)
        qk(0, *blocks[0])
        qk(1, *blocks[1])
        for i, (h, qt, kb) in enumerate(blocks):
            if kb == 0 and qt == 0 and h + 1 < 8:
                load_head(h + 1)
            if kb == 0:
                nxt = h * 8 + qt + 1
                if nxt < 64:
                    load_gb(nxt // 8, nxt % 8)
            ex(i)
            pv(i, h, kb)
            if i + 2 < nb:
                qk(i + 2, *blocks[i + 2])
            if kb == 63:
                epilogue(h, qt)
            if kb == 40 and (h, qt) != (0, 0):
                pq = h * 8 + qt - 1
                epilogue2(pq // 8, pq % 8)
        epilogue2(7, 7)
        P.flush()


def phase_O(nc, P, T):
    x, out, yaT, ybT, sgT = T["x"], T["out"], T["yaT"], T["ybT"], T["sgT"]
    with contextlib.ExitStack() as st:
        sb = lambda name, shape, dt=F32: st.enter_context(nc.sbuf_tensor("O_" + name, list(shape), dt))
        ps = lambda name, shape, dt=F32: st.enter_context(nc.psum_tensor("O_" + name, list(shape), dt))
        Wa = sb("Wa", [128, 8, D], BF16)
        Wb = sb("Wb", [128, 8, D], BF16)
        Wo = sb("Wo", [128, 8, D], BF16)
        fing = sb("fing", [128, D])
        ya = [sb("ya%d" % i, [128, 8, 512], BF16) for i in range(2)]
        yb = [sb("yb%d" % i, [128, 8, 512], BF16) for i in range(2)]
        sa = [sb("sa%d" % i, [128, 8, 512], BF16) for i in range(2)]
        sbb = [sb("sb%d" % i, [128, 8, 512], BF16) for i in range(2)]
        mixT = [sb("mixT%d" % i, [128, 8, 512], BF16) for i in range(2)]
        t1 = [sb("t1_%d" % i, [128, 512]) for i in range(2)]
        t2 = [sb("t2_%d" % i, [128, 512]) for i in range(2)]
        xt = [sb("xt%d" % i, [128, D]) for i in range(2)]
        xo = [sb("xo%d" % i, [128, D]) for i in range(2)]
        junk = sb("junk", [128, D], BF16)
        ssq = [sb("ssq%d" % i, [128, 2]) for i in range(2)]
        pa = [ps("pa%d" % i, [128, 512]) for i in range(2)]
        pb = [ps("pb%d" % i, [128, 512]) for i in range(2)]
        po = [ps("po%d" % i, [128, 512]) for i in range(4)]

        for (W, src, key) in ((Wa, T["w_a"], "Wa"), (Wb, T["w_b"], "Wb"), (Wo, T["w_o"], "Wo")):
            for hh in range(2):
                P.dma("pool", W[:, :, hh * 512:(hh + 1) * 512],
                      src[:, hh * 512:(hh + 1) * 512].rearrange("(kc p) c -> p kc c", p=128), w=[(key, hh)])
        P.dma("sp", fing[:], T["fing_rep"], w=["fing"])
        wkeys = lambda k: [(k, 0), (k, 1)]
        cnt = {"p": 0, "o": 0, "x": 0}
        for tt in range(8):
            sl = tt % 2
            ts_ = slice(tt * 512, (tt + 1) * 512)
            P.dma("sp", ya[sl][:], yaT[:, ts_].rearrange("(cc p) t -> p cc t", p=128), w=[("ya", sl)])
            P.dma("sp", yb[sl][:], ybT[:, ts_].rearrange("(cc p) t -> p cc t", p=128), w=[("yb", sl)])
            P.dma("sp", sa[sl][:], sgT[0, :, ts_].rearrange("(cc p) t -> p cc t", p=128), w=[("sa", sl)])
            P.dma("sp", sbb[sl][:], sgT[1, :, ts_].rearrange("(cc p) t -> p cc t", p=128), w=[("sb", sl)])
            for dd in range(8):
                i = cnt["p"] % 2
                cnt["p"] += 1
                for cc in range(8):
                    P.pe(lambda e, i=i, cc=cc, dd=dd, sl=sl: e.matmul(pa[i][:, :], Wa[:, cc, dd * 128:(dd + 1) * 128], ya[sl][:, cc, :],
                                                                      start=(cc == 0), stop=(cc == 7)),
                         r=wkeys("Wa") + [("ya", sl)], w=[("pa", i)])
                for cc in range(8):
                    P.pe(lambda e, i=i, cc=cc, dd=dd, sl=sl: e.matmul(pb[i][:, :], Wb[:, cc, dd * 128:(dd + 1) * 128], yb[sl][:, cc, :],
                                                                      start=(cc == 0), stop=(cc == 7)),
                         r=wkeys("Wb") + [("yb", sl)], w=[("pb", i)])
                P.dve(lambda e, i=i, dd=dd, sl=sl: e.tensor_tensor(out=t1[i][:], in0=pa[i][:, :], in1=sa[sl][:, dd, :], op=ALU.mult),
                      r=[("pa", i), ("sa", sl)], w=[("t1", i)])
                P.dve(lambda e, i=i, dd=dd, sl=sl: e.tensor_tensor(out=t2[i][:], in0=pb[i][:, :], in1=sbb[sl][:, dd, :], op=ALU.mult),
                      r=[("pb", i), ("sb", sl)], w=[("t2", i)])
                P.pool(lambda e, i=i, dd=dd, sl=sl: e.tensor_tensor(out=mixT[sl][:, dd, :], in0=t1[i][:], in1=t2[i][:], op=ALU.add),
                       r=[("t1", i), ("t2", i)], w=[("mixT", sl)])
            for u in range(4):
                xs = cnt["x"] % 2
                cnt["x"] += 1
                r0 = tt * 512 + u * 128
                P.dma("sp", xt[xs][:], x[r0:r0 + 128, :], w=[("xt", xs)])
                for eg in range(2):
                    o = cnt["o"] % 4
                    cnt["o"] += 1
                    for dd in range(8):
                        P.pe(lambda e, o=o, dd=dd, sl=sl, u=u, eg=eg: e.matmul(
                            po[o][:, :], mixT[sl][:, dd, u * 128:(u + 1) * 128], Wo[:, dd, eg * 512:(eg + 1) * 512],
                            start=(dd == 0), stop=(dd == 7)), r=[("mixT", sl), ("Wo", eg)], w=[("po", o)])
                    P.dve(lambda e, o=o, xs=xs, eg=eg: e.tensor_tensor(out=xo[xs][:, eg * 512:(eg + 1) * 512], in0=po[o][:, :],
                                                                       in1=xt[xs][:, eg * 512:(eg + 1) * 512], op=ALU.add),
                          r=[("po", o), ("xt", xs)], w=[("xo", xs, eg)])
                P.act(lambda e, xs=xs: e.activation(out=junk[:], in_=xo[xs][:], func=AF.Square, accum_out=ssq[xs][:, 0:1]),
                      r=[("xo", xs, 0), ("xo", xs, 1)], w=[("ssq", xs)])
                P.act(lambda e, xs=xs: e.activation(out=ssq[xs][:, 0:1], in_=ssq[xs][:, 0:1], func=AF.Sqrt, scale=1.0 / D, bias=NORM_EPS),
                      r=[("ssq", xs)], w=[("ssq", xs)])
                P.dve(lambda e, xs=xs: e.reciprocal(out=ssq[xs][:, 1:2], in_=ssq[xs][:, 0:1]), r=[("ssq", xs)], w=[("ssq", xs)])
                P.dve(lambda e, xs=xs: e.scalar_tensor_tensor(out=xo[xs][:], in0=xo[xs][:], scalar=ssq[xs][:, 1:2], in1=fing[:],
                                                              op0=ALU.mult, op1=ALU.mult),
                      r=[("xo", xs, 0), ("xo", xs, 1), ("ssq", xs), "fing"], w=[("xo", xs, 0), ("xo", xs, 1)])
                P.dma("sp", out[r0:r0 + 128, :], xo[xs][:], r=[("xo", xs, 0), ("xo", xs, 1)])
        P.flush()


def make_in_maps(x, positions, norm_g, w_in, ml_gate_b, ml_conv_w, ml_norm_g, da_lambda,
                 da_subln_g, gate_b, w_branch_a, w_branch_b, w_out, final_g):
    f32 = np.float32
    x = np.asarray(x, f32)
    positions = np.asarray(positions, np.int32)
    w_in0 = np.ascontiguousarray(np.asarray(w_in, f32)[0])
    gb0 = np.asarray(ml_gate_b, f32)[0]
    cw0 = np.asarray(ml_conv_w, f32)[0]
    w_in1 = w_in0.copy()
    ag = w_in0[:, C_AG:C_AG + 16].reshape(D, 4, 4)
    w_in1[:, C_AG:C_AG + 16] = ag[:, [2, 3, 0, 1], :].reshape(D, 16)
    gb1 = gb0[[2, 3, 0, 1], :]
    cw1 = cw0[::-1, :]
    rep = lambda v, n=128: np.ascontiguousarray(np.broadcast_to(np.asarray(v, f32).reshape(1, -1), (n, np.asarray(v).size)))
    ident = np.eye(128, dtype=f32)
    rsw = np.zeros((128, 128), f32)
    for r in range(128):
        m = r % 64
        if m < 32:
            rsw[r + 32, r] = -1.0
        else:
            rsw[r - 32, r] = 1.0
    invf = (10000.0 ** (-np.arange(0, 64, 2, dtype=f32) / f32(64))).astype(f32)
    invf_p = np.array([invf[(p % 64) % 32] for p in range(128)], f32).reshape(128, 1)
    ii = np.arange(128)
    U = (ii[:, None] <= ii[None, :]).astype(f32)
    L = (ii[:, None] >= ii[None, :]).astype(f32)
    NEG = -30000.0
    tri = np.stack([U, L, (1 - U) * NEG, (1 - L) * NEG], axis=1).astype(f32)
    common = {
        "normg_rep": rep(np.asarray(norm_g, f32)[0]),
        "mlng_rep": rep(np.asarray(ml_norm_g, f32)[0].reshape(-1)),
        "lam_rep": rep(np.asarray(da_lambda, f32)[0].reshape(-1)),
        "subln_rep": rep(np.asarray(da_subln_g, f32)[0]),
        "gb_fm": np.ascontiguousarray(np.asarray(gate_b, f32)[0].reshape(2, 8, 128).transpose(2, 0, 1)),
        "w_a": np.ascontiguousarray(np.asarray(w_branch_a, f32)[0]),
        "w_b": np.ascontiguousarray(np.asarray(w_branch_b, f32)[0]),
        "w_o": np.ascontiguousarray(np.asarray(w_out, f32)[0]),
        "fing_rep": rep(np.asarray(final_g, f32)),
        "c_ident": ident, "c_rswap": rsw, "c_invf": invf_p, "c_tri": tri,
    }
    in_maps = []
    for core in range(NCORES):
        b, half = core // 2, core % 2
        xb = x[b]
        pb = positions[b]
        if half == 1:
            xb = xb[::-1]
            pb = pb[::-1]
        gbx = gb1 if half else gb0
        cwx = cw1 if half else cw0
        m = dict(common)
        m["x"] = np.ascontiguousarray(xb)
        m["posr"] = np.ascontiguousarray(np.broadcast_to(pb.reshape(1, S), (128, S))).astype(np.int32)
        m["w_in"] = w_in1 if half else w_in0
        m["gateb_rep"] = rep(gbx.reshape(-1))
        m["convw"] = np.ascontiguousarray(cwx.reshape(5, 16, 128).transpose(2, 1, 0))
        in_maps.append(m)
    return in_maps


_NC_CACHE = {}


def kernel(**inputs):
    in_maps = make_in_maps(**inputs)
    if "nc" not in _NC_CACHE:
        _NC_CACHE["nc"] = build_nc()
    nc = _NC_CACHE["nc"]
    res = run_bass_kernel_spmd(nc, in_maps, core_ids=list(range(NCORES)))
    B = 4
    outp = np.empty((B, S, D), np.float32)
    for core in range(NCORES):
        b, half = core // 2, core % 2
        o = np.asarray(res.results[core]["out"], np.float32)
        if half == 0:
            outp[b, :OWN] = o
        else:
            outp[b, OWN:] = o[::-1]
    return outp
```

```python
import contextlib
import math
import numpy as np
import concourse.bass as bass
import concourse.mybir as mybir
from concourse.bass_utils import run_bass_kernel_spmd

F32, BF16, I32 = mybir.dt.float32, mybir.dt.bfloat16, mybir.dt.int32
AF = mybir.ActivationFunctionType
ALU = mybir.AluOpType
AX = mybir.AxisListType

D = 1024
S = 8192
OWN = 4096
NCORES = 8
PROJ = 11280
C_AQ, C_AK, C_AV, C_AO, C_AZ, C_AG = 0, 1024, 2048, 3072, 4096, 5120
C_BQ, C_BK, C_BV, C_BZ, C_GA, C_GB = 5136, 6160, 7184, 8208, 9232, 10256
NORM_EPS = 1e-6
LAMBDA_INIT = 0.8 - 0.6 * math.exp(-0.3 * 0)
SAME_ENGINE_SYNC = True
SAME_ENGINE_RAW_ONLY = True


class _Op:
    __slots__ = ("eng", "fn", "dma", "clock", "idx", "waits", "signal", "semval", "sem", "know")


class Prog:
    ENGS = ("sp", "act", "dve", "pool", "pe")

    def __init__(self, nc, stack, n_dma=8):
        self.nc = nc
        self.sem = {e: stack.enter_context(nc.semaphore("cs_" + e)) for e in self.ENGS}
        self.dsem = {q: [stack.enter_context(nc.semaphore("ds_%s_%d" % (q, i))) for i in range(n_dma)]
                     for q in ("sp", "pool", "act")}
        self.n_dma = n_dma
        self.dcount = {q: 0 for q in self.dsem}
        self.dlast = {}
        self.cnt = {e: 0 for e in self.ENGS}
        self.sigcnt = {e: 0 for e in self.ENGS}
        self.know = {e: {} for e in self.ENGS}
        self.pending = {e: [] for e in self.ENGS}
        self.last_w = {}
        self.readers = {}
        self.nops = 0

    def add(self, eng, fn, reads=(), writes=(), dma=False, extra_deps=()):
        op = _Op()
        op.eng, op.fn, op.dma, op.signal, op.semval = eng, fn, dma, False, None
        self.nops += 1
        deps = []
        seen = set()

        raw = set()

        def push(d, is_raw=False):
            if d is None:
                return
            if is_raw:
                raw.add(id(d))
            if id(d) not in seen:
                seen.add(id(d))
                deps.append(d)

        for k in reads:
            push(self.last_w.get(k), True)
        for k in writes:
            push(self.last_w.get(k))
            for r in self.readers.get(k, ()):
                push(r)
        for d in extra_deps:
            push(d, True)
        if dma:
            slot = self.dcount[eng] % self.n_dma
            self.dcount[eng] += 1
            op.clock = (eng, slot)
            prev = self.dlast.get(op.clock)
            op.idx = (prev.idx + 1) if prev is not None else 1
            op.sem = self.dsem[eng][slot]
            op.semval = 16 * op.idx
            push(prev)
            self.dlast[op.clock] = op
        else:
            self.cnt[eng] += 1
            op.clock = eng
            op.idx = self.cnt[eng]
            op.sem = self.sem[eng]
        know = self.know[eng]
        waits = []
        for d in deps:
            if (not d.dma) and (not dma) and d.eng == eng:
                if eng == "pe" or not SAME_ENGINE_SYNC:
                    continue
                if SAME_ENGINE_RAW_ONLY and id(d) not in raw:
                    continue
            if know.get(d.clock, 0) >= d.idx:
                continue
            waits.append(d)
            d.signal = True
            for c, v in d.know.items():
                if know.get(c, 0) < v:
                    know[c] = v
            if know.get(d.clock, 0) < d.idx:
                know[d.clock] = d.idx
        op.waits = waits
        op.know = dict(know)
        for k in writes:
            self.last_w[k] = op
            self.readers[k] = []
        for k in reads:
            self.readers.setdefault(k, []).append(op)
        self.pending[eng].append(op)
        return op

    def pe(self, fn, r=(), w=()):
        return self.add("pe", fn, r, w)

    def act(self, fn, r=(), w=()):
        return self.add("act", fn, r, w)

    def dve(self, fn, r=(), w=()):
        return self.add("dve", fn, r, w)

    def pool(self, fn, r=(), w=()):
        return self.add("pool", fn, r, w)

    def dma(self, q, out, in_, r=(), w=()):
        return self.add(q, lambda e: e.dma_start(out=out, in_=in_), r, w, dma=True)

    def flush(self, final=False):
        outstanding = [d for d in self.dlast.values()]
        self.add("sp", None, extra_deps=outstanding)
        for e in self.ENGS:
            for op in self.pending[e]:
                if not op.dma and op.signal:
                    self.sigcnt[e] += 1
                    op.semval = self.sigcnt[e]
                elif not op.dma:
                    op.semval = None
        pend = self.pending
        sems = self.sem

        def make_body(eng):
            ops = pend[eng]

            def body(e):
                for op in ops:
                    for d in op.waits:
                        assert d.semval is not None
                        e.wait_ge(d.sem, d.semval)
                    if op.fn is None:
                        continue
                    ins = op.fn(e)
                    if op.dma:
                        ins.then_inc(op.sem, 16)
                    elif op.signal:
                        ins.then_inc(sems[eng], 1)
            return body

        with self.nc.Block() as block:
            block.sync(make_body("sp"))
            block.scalar(make_body("act"))
            block.vector(make_body("dve"))
            block.gpsimd(make_body("pool"))
            block.tensor(make_body("pe"))
        self.pending = {e: [] for e in self.ENGS}
        self.last_w = {}
        self.readers = {}
        full = {}
        for e in self.ENGS:
            full[e] = self.cnt[e]
        for c, d in self.dlast.items():
            full[c] = d.idx
        self.know = {e: dict(full) for e in self.ENGS}


def build_nc(debug=False, phases=("P", "M", "D", "O")):
    nc = bass.Bass("TRN2", target_bir_lowering=False)
    IN = lambda name, shape, dt=F32: nc.dram_tensor(name, list(shape), dt, kind="ExternalInput").ap()
    skind = "ExternalOutput" if debug else "Internal"
    SCR = lambda name, shape, dt=BF16: nc.dram_tensor(name, list(shape), dt, kind=skind).ap()

    x = IN("x", [S, D])
    posr = IN("posr", [128, S], I32)
    w_in = IN("w_in", [D, PROJ])
    normg_rep = IN("normg_rep", [128, D])
    gateb_rep = IN("gateb_rep", [128, 16])
    convw = IN("convw", [128, 16, 5])
    mlng_rep = IN("mlng_rep", [128, D])
    lam_rep = IN("lam_rep", [128, 256])
    subln_rep = IN("subln_rep", [128, 128])
    gb_fm = IN("gb_fm", [128, 2, 8])
    w_a = IN("w_a", [D, D])
    w_b = IN("w_b", [D, D])
    w_o = IN("w_o", [D, D])
    fing_rep = IN("fing_rep", [128, D])
    c_ident = IN("c_ident", [128, 128])
    c_rswap = IN("c_rswap", [128, 128])
    c_invf = IN("c_invf", [128, 1])
    c_tri = IN("c_tri", [128, 4, 128])
    out = nc.dram_tensor("out", [OWN, D], F32, kind="ExternalOutput").ap()

    mqT = SCR("mqT", [D, S])
    mkT = SCR("mkT", [D, S])
    mv = SCR("mv", [S, D])
    dqT = SCR("dqT", [D, OWN])
    dkT = SCR("dkT", [D, S])
    dv = SCR("dv", [S, D])
    gA = SCR("gA", [OWN, D])
    gB = SCR("gB", [OWN, D])
    sgT = SCR("sgT", [2, D, OWN])
    gall_d = SCR("gall_d", [128, 64 * 16], F32)
    yaT = SCR("yaT", [D, OWN])
    ybT = SCR("ybT", [D, OWN])
    hfwd = SCR("hfwd", [OWN, D], F32)

    with contextlib.ExitStack() as top:
        P = Prog(nc, top)
        if "P" in phases:
            phase_P(nc, P, locals())
        if "M" in phases:
            phase_M(nc, P, locals())
        if "M2" in phases:
            phase_M2(nc, P, locals())
        if "D" in phases:
            phase_D(nc, P, locals())
        if "O" in phases:
            phase_O(nc, P, locals())
    return nc


def phase_P(nc, P, T):
    x, posr, w_in = T["x"], T["posr"], T["w_in"]
    with contextlib.ExitStack() as st:
        sb = lambda name, shape, dt=F32: st.enter_context(nc.sbuf_tensor("P_" + name, list(shape), dt))
        ps = lambda name, shape, dt=F32: st.enter_context(nc.psum_tensor("P_" + name, list(shape), dt))
        hT = sb("hT", [128, 8, 2048], BF16)
        NX = 4
        xt = [sb("xt%d" % i, [128, D]) for i in range(NX)]
        xn = [sb("xn%d" % i, [128, D], BF16) for i in range(NX)]
        junk = sb("junk", [128, D], BF16)
        ss = [sb("ss%d" % i, [128, 1]) for i in range(NX)]
        rstd = [sb("rstd%d" % i, [128, 1]) for i in range(NX)]
        grep = sb("grep", [128, D])
        NW = 3
        wB = [sb("wB%d" % i, [128, 8, 512], BF16) for i in range(NW)]
        wG = sb("wG", [128, 8, 16], BF16)
        NWK = 5
        WK = [sb("WK%d" % i, [128, 2054]) for i in range(NWK)]
        OB = [sb("OB%d" % i, [128, 2050], BF16) for i in range(2)]
        OBT = [sb("OBT%d" % i, [128, 8192], BF16) for i in range(2)]
        obt_slot = [0]
        cosT = sb("cosT", [128, 2048])
        sinT = sb("sinT", [128, 2048])
        gall = sb("gall", [128, 64 * 16])
        gbrep = sb("gbrep", [128, 16])
        cw = sb("cw", [128, 16, 5])
        carry = sb("carry", [128, 16, 4])
        identf = sb("identf", [128, 128])
        identb = sb("identb", [128, 128], BF16)
        rswap = sb("rswap", [128, 128])
        invf = sb("invf", [128, 1])
        gbfm = sb("gbfm", [128, 2, 8])
        posi = sb("posi", [128, 2048], I32)
        ptr = [ps("ptr%d" % i, [128, D], BF16) for i in range(2)]
        pp = [ps("pp%d" % i, [128, 512]) for i in range(4)]
        prot = [ps("prot%d" % i, [128, 512]) for i in range(2)]

        P.dma("sp", grep[:], T["normg_rep"], w=["grep"])
        P.dma("sp", gbrep[:], T["gateb_rep"], w=["gbrep"])
        P.dma("sp", cw[:], T["convw"], w=["cw"])
        P.dma("sp", identf[:], T["c_ident"], w=["identf"])
        P.dma("sp", rswap[:], T["c_rswap"], w=["rswap"])
        P.dma("sp", invf[:], T["c_invf"], w=["invf"])
        P.dma("sp", gbfm[:], T["gb_fm"], w=["gbfm"])
        P.dma("pool", wG[:], w_in[:, C_AG:C_AG + 16].rearrange("(kc p) c -> p kc c", p=128), w=["wG"])
        P.dve(lambda e: e.tensor_copy(out=identb[:], in_=identf[:]), r=["identf"], w=["identb"])
        P.dve(lambda e: e.memset(carry[:], 0.0), w=["carry"])
        for i in range(NWK):
            P.dve(lambda e, i=i: e.memset(WK[i][:, 2052:2054], 0.0), w=[("WKz", i), ("WK", i)])

        wslot = [0]

        def load_w(col0, ncols):
            s = wslot[0] % NW
            wslot[0] += 1
            P.dma("pool", wB[s][:, :, 0:ncols],
                  w_in[:, col0:col0 + ncols].rearrange("(kc p) c -> p kc c", p=128), w=[("wB", s)])
            return s

        ppslot = [0]

        def next_pp():
            s = ppslot[0] % 4
            ppslot[0] += 1
            return s

        wkslot = [0]

        def next_wk():
            s = wkslot[0] % NWK
            wkslot[0] += 1
            return s

        obslot = [0]

        def next_ob():
            s = obslot[0] % 2
            obslot[0] += 1
            return s

        def fm_mm(ws, wc, tq):
            s = next_pp()
            for kc in range(8):
                P.pe(lambda e, s=s, ws=ws, wc=wc, kc=kc, tq=tq: e.matmul(
                    pp[s][:, :], wB[ws][:, kc, wc * 128:(wc + 1) * 128], hT[:, kc, tq * 512:(tq + 1) * 512],
                    start=(kc == 0), stop=(kc == 7)),
                    r=[("wB", ws), "hT"], w=[("pp", s)])
            return s

        def tm_mm(ws, ncols, tt):
            s = next_pp()
            for kc in range(8):
                P.pe(lambda e, s=s, ws=ws, kc=kc, tt=tt, ncols=ncols: e.matmul(
                    pp[s][:, 0:ncols], hT[:, kc, tt * 128:(tt + 1) * 128], wB[ws][:, kc, 0:ncols],
                    start=(kc == 0), stop=(kc == 7)),
                    r=[("wB", ws), "hT"], w=[("pp", s)])
            return s

        def load_x(stile_, tt_):
            r0_ = stile_ * 2048 + tt_ * 128
            P.dma("sp", xt[tt_ % NX][:], x[r0_:r0_ + 128, :], w=[("xt", tt_ % NX)])

        for stile in range(4):
            own = stile < 2
            t0 = stile * 2048
            last = stile == 3
            if stile == 0:
                for tt in range(NX):
                    load_x(0, tt)
            def stage_a(tt):
                sl = tt % NX
                pl = tt % 2
                P.act(lambda e, sl=sl: e.activation(out=junk[:], in_=xt[sl][:], func=AF.Square, accum_out=ss[sl][:]),
                      r=[("xt", sl)], w=[("ss", sl)])
                P.act(lambda e, sl=sl: e.activation(out=ss[sl][:], in_=ss[sl][:], func=AF.Sqrt,
                                                    scale=1.0 / D, bias=NORM_EPS),
                      r=[("ss", sl)], w=[("ss", sl)])
                P.dve(lambda e, sl=sl: e.reciprocal(out=rstd[sl][:], in_=ss[sl][:]), r=[("ss", sl)], w=[("rstd", sl)])
                P.dve(lambda e, sl=sl: e.scalar_tensor_tensor(out=xn[sl][:], in0=xt[sl][:], scalar=rstd[sl][:],
                                                              in1=grep[:], op0=ALU.mult, op1=ALU.mult),
                      r=[("xt", sl), ("rstd", sl), "grep"], w=[("xn", sl)])
                for kc in range(8):
                    P.pe(lambda e, sl=sl, kc=kc, pl=pl: e.transpose(out=ptr[pl][:, kc * 128:(kc + 1) * 128],
                                                                    in_=xn[sl][:, kc * 128:(kc + 1) * 128],
                                                                    identity=identb[:]),
                         r=[("xn", sl), "identb"], w=[("ptr", pl)])
                if tt + NX < 16:
                    load_x(stile, tt + NX)

            def stage_b(tt):
                pl = tt % 2
                if tt % 2 == 0:
                    P.act(lambda e, pl=pl, tt=tt: e.activation(
                        out=hT[:, :, tt * 128:(tt + 1) * 128],
                        in_=ptr[pl][:, :].rearrange("p (k t) -> p k t", k=8), func=AF.Copy),
                        r=[("ptr", pl)], w=["hT"])
                else:
                    P.dve(lambda e, pl=pl, tt=tt: e.tensor_copy(
                        out=hT[:, :, tt * 128:(tt + 1) * 128],
                        in_=ptr[pl][:, :].rearrange("p (k t) -> p k t", k=8)),
                        r=[("ptr", pl)], w=["hT"])

            stage_a(0)
            for tt in range(16):
                if tt + 1 < 16:
                    stage_a(tt + 1)
                stage_b(tt)
            if stile < 3:
                for tt in range(NX):
                    load_x(stile + 1, tt)

            P.dma("sp", posi[:], posr[:, t0:t0 + 2048], w=["posi"])
            build_rope_tables(P, posi, invf, cosT, sinT, WK, next_wk)

            for tt in range(16):
                s = next_pp()
                for kc in range(8):
                    P.pe(lambda e, s=s, kc=kc, tt=tt: e.matmul(
                        pp[s][:, 0:16], hT[:, kc, tt * 128:(tt + 1) * 128], wG[:, kc, :],
                        start=(kc == 0), stop=(kc == 7)), r=["wG", "hT"], w=[("pp", s)])
                gt = stile * 16 + tt
                P.dve(lambda e, s=s, gt=gt: e.tensor_tensor(out=gall[:, gt * 16:(gt + 1) * 16], in0=pp[s][:, 0:16],
                                                            in1=gbrep[:], op=ALU.add),
                      r=[("pp", s), "gbrep"], w=["gall"])

            def gen_tm_v():
                for (c0, dst) in ((C_AV, T["mv"]), (C_BV, T["dv"])):
                    for cg in range(2):
                        ws = load_w(c0 + cg * 512, 512)
                        ob = obt_slot[0] % 2
                        obt_slot[0] += 1
                        for tt in range(16):
                            s = tm_mm(ws, 512, tt)
                            if True:
                                P.act(lambda e, s=s, ob=ob, tt=tt: e.activation(
                                    out=OBT[ob][:, tt * 512:(tt + 1) * 512], in_=pp[s][:, :], func=AF.Copy),
                                    r=[("pp", s)], w=[("OBT", ob)])
                            else:
                                P.dve(lambda e, s=s, ob=ob, tt=tt: e.tensor_copy(
                                    out=OBT[ob][:, tt * 512:(tt + 1) * 512], in_=pp[s][:, :]),
                                    r=[("pp", s)], w=[("OBT", ob)])
                            if tt == 15:
                                P.dma("sp", dst[t0:t0 + 2048, cg * 512:(cg + 1) * 512].rearrange("(tt p) c -> p tt c", p=128),
                                      OBT[ob][:, :].rearrange("p (tt c) -> p tt c", c=512), r=[("OBT", ob)])
                            yield

            def gen_conv():
                prev2 = [None]
                for fam, (c0, dst, ksc) in enumerate(((C_AQ, T["mqT"], 1.0), (C_AK, T["mkT"], 1.0 / 16.0))):
                    for g4 in range(2):
                        ws = load_w(c0 + g4 * 512, 512)
                        for wc in range(4):
                            fc = fam * 8 + g4 * 4 + wc
                            row0 = (g4 * 4 + wc) * 128
                            stg = next_wk()
                            acc = next_wk()
                            sg = next_wk()
                            P.dve(lambda e, stg=stg, fc=fc: e.tensor_copy(out=WK[stg][:, 0:4], in_=carry[:, fc, :]),
                                  r=["carry"], w=[("WK", stg)])
                            for tq in range(4):
                                s = fm_mm(ws, wc, tq)
                                P.act(lambda e, s=s, stg=stg, tq=tq: e.activation(
                                    out=WK[stg][:, 4 + tq * 512:4 + (tq + 1) * 512], in_=pp[s][:, :], func=AF.Copy),
                                    r=[("pp", s)], w=[("WK", stg)])
                            P.dve(lambda e, stg=stg, fc=fc: e.tensor_copy(out=carry[:, fc, :], in_=WK[stg][:, 2048:2052]),
                                  r=[("WK", stg)], w=["carry"])
                            nout = 2050 if last else 2048
                            P.act(lambda e, stg=stg, acc=acc, fc=fc, nout=nout: e.activation(
                                out=WK[acc][:, 0:nout], in_=WK[stg][:, 0:nout], func=AF.Copy, scale=cw[:, fc, 0:1]),
                                r=[("WK", stg), "cw", ("WKz", stg)], w=[("WK", acc)])
                            for j in range(1, 5):
                                P.dve(lambda e, stg=stg, acc=acc, fc=fc, nout=nout, j=j: e.scalar_tensor_tensor(
                                    out=WK[acc][:, 0:nout], in0=WK[stg][:, j:j + nout], scalar=cw[:, fc, j:j + 1],
                                    in1=WK[acc][:, 0:nout], op0=ALU.mult, op1=ALU.add),
                                    r=[("WK", stg), "cw", ("WK", acc)], w=[("WK", acc)])

                            def part2(acc=acc, sg=sg, nout=nout, ksc=ksc, dst=dst, row0=row0):
                                ob = next_ob()
                                P.act(lambda e: e.activation(
                                    out=WK[sg][:, 0:nout], in_=WK[acc][:, 0:nout], func=AF.Sigmoid),
                                    r=[("WK", acc)], w=[("WK", sg)])
                                P.dve(lambda e: e.scalar_tensor_tensor(
                                    out=OB[ob][:, 0:nout], in0=WK[acc][:, 0:nout], scalar=ksc, in1=WK[sg][:, 0:nout],
                                    op0=ALU.mult, op1=ALU.mult), r=[("WK", acc), ("WK", sg)], w=[("OB", ob)])
                                if stile == 0:
                                    P.dma("sp", dst[row0:row0 + 128, 0:nout - 2], OB[ob][:, 2:nout], r=[("OB", ob)])
                                else:
                                    P.dma("sp", dst[row0:row0 + 128, t0 - 2:t0 - 2 + nout], OB[ob][:, 0:nout], r=[("OB", ob)])

                            if prev2[0] is not None:
                                prev2[0]()
                            prev2[0] = part2
                            yield
                if prev2[0] is not None:
                    prev2[0]()

            g_tm = gen_tm_v()
            for _ in gen_conv():
                for _k in range(4):
                    next(g_tm, None)
            for _ in g_tm:
                pass

            fams = [(C_BK, T["dkT"], 1.0, t0)]
            if own:
                fams.append((C_BQ, T["dqT"], 0.125, t0))
            pend = [None]
            for (c0, dst, sc, tcol) in fams:
                for g4 in range(2):
                    ws = load_w(c0 + g4 * 512, 512)
                    for wc in range(4):
                        row0 = (g4 * 4 + wc) * 128
                        ob = next_ob()
                        for tq in range(4):
                            s = fm_mm(ws, wc, tq)
                            wk = next_wk()
                            P.act(lambda e, s=s, wk=wk: e.activation(out=WK[wk][:, 0:512], in_=pp[s][:, :], func=AF.Copy),
                                  r=[("pp", s)], w=[("WK", wk)])

                            def tail(wk=wk, tq=tq, ob=ob, sc=sc, dst=dst, row0=row0, tcol=tcol):
                                pr = tq % 2
                                cs = slice(tq * 512, (tq + 1) * 512)
                                P.pe(lambda e: e.matmul(prot[pr][:, :], rswap[:, :], WK[wk][:, 0:512], start=True, stop=True),
                                     r=[("WK", wk), "rswap"], w=[("prot", pr)])
                                P.dve(lambda e: e.scalar_tensor_tensor(
                                    out=WK[wk][:, 512:1024], in0=WK[wk][:, 0:512], scalar=sc, in1=cosT[:, cs],
                                    op0=ALU.mult, op1=ALU.mult), r=[("WK", wk), "cosT"], w=[("WK", wk)])
                                P.dve(lambda e: e.scalar_tensor_tensor(
                                    out=WK[wk][:, 1024:1536], in0=prot[pr][:, :], scalar=sc, in1=sinT[:, cs],
                                    op0=ALU.mult, op1=ALU.mult), r=[("prot", pr), "sinT"], w=[("WK", wk)])
                                P.dve(lambda e: e.tensor_tensor(
                                    out=OB[ob][:, cs], in0=WK[wk][:, 512:1024], in1=WK[wk][:, 1024:1536], op=ALU.add),
                                    r=[("WK", wk)], w=[("OB", ob)])
                                if tq == 3:
                                    P.dma("sp", dst[row0:row0 + 128, tcol:tcol + 2048], OB[ob][:, 0:2048], r=[("OB", ob)])

                            if pend[0] is not None:
                                pend[0]()
                            pend[0] = tail
            if pend[0] is not None:
                pend[0]()

            if own:
                for gi, c0 in enumerate((C_GA, C_GB)):
                    for g4 in range(2):
                        ws = load_w(c0 + g4 * 512, 512)
                        for wc in range(4):
                            cc = g4 * 4 + wc
                            ob = next_ob()
                            for tq in range(4):
                                s = fm_mm(ws, wc, tq)
                                P.act(lambda e, s=s, ob=ob, tq=tq, gi=gi, cc=cc: e.activation(
                                    out=OB[ob][:, tq * 512:(tq + 1) * 512], in_=pp[s][:, :], func=AF.Sigmoid,
                                    bias=gbfm[:, gi, cc:cc + 1]), r=[("pp", s), "gbfm"], w=[("OB", ob)])
                            P.dma("sp", T["sgT"][gi, cc * 128:(cc + 1) * 128, t0:t0 + 2048], OB[ob][:, 0:2048],
                                  r=[("OB", ob)])
                for cg in range(2):
                    wso = load_w(C_AO + cg * 512, 512)
                    wsz = load_w(C_AZ + cg * 512, 512)
                    ob = obt_slot[0] % 2
                    obt_slot[0] += 1
                    for tt in range(16):
                        so = tm_mm(wso, 512, tt)
                        sz = tm_mm(wsz, 512, tt)
                        wk = next_wk()
                        P.act(lambda e, so=so, wk=wk: e.activation(out=WK[wk][:, 0:512], in_=pp[so][:, :], func=AF.Sigmoid),
                              r=[("pp", so)], w=[("WK", wk)])
                        P.act(lambda e, sz=sz, wk=wk: e.activation(out=WK[wk][:, 512:1024], in_=pp[sz][:, :], func=AF.Sigmoid),
                              r=[("pp", sz)], w=[("WK", wk)])
                        P.dve(lambda e, sz=sz, wk=wk: e.tensor_tensor(out=WK[wk][:, 512:1024], in0=pp[sz][:, :],
                                                                      in1=WK[wk][:, 512:1024], op=ALU.mult),
                              r=[("pp", sz), ("WK", wk)], w=[("WK", wk)])
                        P.dve(lambda e, wk=wk, ob=ob, tt=tt: e.tensor_tensor(
                            out=OBT[ob][:, tt * 512:(tt + 1) * 512], in0=WK[wk][:, 0:512], in1=WK[wk][:, 512:1024],
                            op=ALU.mult), r=[("WK", wk)], w=[("OBT", ob)])
                    P.dma("sp", T["gA"][t0:t0 + 2048, cg * 512:(cg + 1) * 512].rearrange("(tt p) c -> p tt c", p=128),
                          OBT[ob][:, :].rearrange("p (tt c) -> p tt c", c=512), r=[("OBT", ob)])
                for cg in range(2):
                    wsz = load_w(C_BZ + cg * 512, 512)
                    ob = obt_slot[0] % 2
                    obt_slot[0] += 1
                    for tt in range(16):
                        sz = tm_mm(wsz, 512, tt)
                        wk = next_wk()
                        P.act(lambda e, sz=sz, wk=wk: e.activation(out=WK[wk][:, 0:512], in_=pp[sz][:, :], func=AF.Sigmoid),
                              r=[("pp", sz)], w=[("WK", wk)])
                        P.dve(lambda e, sz=sz, wk=wk, ob=ob, tt=tt: e.tensor_tensor(
                            out=OBT[ob][:, tt * 512:(tt + 1) * 512], in0=pp[sz][:, :], in1=WK[wk][:, 0:512],
                            op=ALU.mult), r=[("pp", sz), ("WK", wk)], w=[("OBT", ob)])
                    P.dma("sp", T["gB"][t0:t0 + 2048, cg * 512:(cg + 1) * 512].rearrange("(tt p) c -> p tt c", p=128),
                          OBT[ob][:, :].rearrange("p (tt c) -> p tt c", c=512), r=[("OBT", ob)])

        P.dma("sp", T["gall_d"][:, :], gall[:, :], r=["gall"])
        P.flush()


def build_rope_tables(P, posi, invf, cosT, sinT, WK, next_wk):
    TWO_PI = 2.0 * math.pi
    C1 = 6.28125
    C2 = TWO_PI - C1
    a = next_wk()
    b = next_wk()
    c = next_wk()
    A, Bk, R = WK[a], WK[b], WK[c]
    N = 2048
    P.dve(lambda e: e.tensor_copy(out=A[:, 0:N], in_=posi[:, :]), r=["posi"], w=[("WK", a)])
    P.dve(lambda e: e.tensor_scalar(out=A[:, 0:N], in0=A[:, 0:N], scalar1=invf[:, 0:1], scalar2=None, op0=ALU.mult),
          r=[("WK", a), "invf"], w=[("WK", a)])
    ki = posi
    P.dve(lambda e: e.tensor_scalar(out=Bk[:, 0:N], in0=A[:, 0:N], scalar1=1.0 / TWO_PI, scalar2=None, op0=ALU.mult),
          r=[("WK", a)], w=[("WK", b)])
    P.dve(lambda e: e.tensor_copy(out=ki[:, :], in_=Bk[:, 0:N]), r=[("WK", b)], w=["posi"])
    P.dve(lambda e: e.tensor_copy(out=Bk[:, 0:N], in_=ki[:, :]), r=["posi"], w=[("WK", b)])
    P.dve(lambda e: e.scalar_tensor_tensor(out=R[:, 0:N], in0=Bk[:, 0:N], scalar=-C1, in1=A[:, 0:N],
                                           op0=ALU.mult, op1=ALU.add), r=[("WK", a), ("WK", b)], w=[("WK", c)])
    P.dve(lambda e: e.scalar_tensor_tensor(out=R[:, 0:N], in0=Bk[:, 0:N], scalar=-C2, in1=R[:, 0:N],
                                           op0=ALU.mult, op1=ALU.add), r=[("WK", b), ("WK", c)], w=[("WK", c)])

    def wrap(X, key):
        P.dve(lambda e: e.tensor_scalar(out=Bk[:, 0:N], in0=X[:, 0:N], scalar1=math.pi, scalar2=-TWO_PI,
                                        op0=ALU.is_gt, op1=ALU.mult), r=[key], w=[("WK", b)])
        P.dve(lambda e: e.tensor_tensor(out=X[:, 0:N], in0=X[:, 0:N], in1=Bk[:, 0:N], op=ALU.add),
              r=[key, ("WK", b)], w=[key])
        P.dve(lambda e: e.tensor_scalar(out=Bk[:, 0:N], in0=X[:, 0:N], scalar1=-math.pi, scalar2=TWO_PI,
                                        op0=ALU.is_lt, op1=ALU.mult), r=[key], w=[("WK", b)])
        P.dve(lambda e: e.tensor_tensor(out=X[:, 0:N], in0=X[:, 0:N], in1=Bk[:, 0:N], op=ALU.add),
              r=[key, ("WK", b)], w=[key])

    wrap(R, ("WK", c))
    P.act(lambda e: e.activation(out=sinT[:, :], in_=R[:, 0:N], func=AF.Sin), r=[("WK", c)], w=["sinT"])
    P.dve(lambda e: e.tensor_scalar(out=R[:, 0:N], in0=R[:, 0:N], scalar1=math.pi / 2, scalar2=None, op0=ALU.add),
          r=[("WK", c)], w=[("WK", c)])
    wrap(R, ("WK", c))
    P.act(lambda e: e.activation(out=cosT[:, :], in_=R[:, 0:N], func=AF.Sin), r=[("WK", c)], w=["cosT"])


def phase_M(nc, P, T):
    mqT, mkT, mv, gA, yaT = T["mqT"], T["mkT"], T["mv"], T["gA"], T["yaT"]
    with contextlib.ExitStack() as st:
        sb = lambda name, shape, dt=F32: st.enter_context(nc.sbuf_tensor("M_" + name, list(shape), dt))
        ps = lambda name, shape, dt=F32: st.enter_context(nc.psum_tensor("M_" + name, list(shape), dt))
        qT = sb("qT", [128, 2, OWN], BF16)
        kT = sb("kT", [128, 2, S], BF16)
        va = sb("va", [128, 64, 257], BF16)
        ktok = sb("ktok", [128, 64, 256], BF16)
        hacc = sb("hacc", [128, 32, 256])
        gat = [sb("gat%d" % i, [128, 256], BF16) for i in range(2)]
        gall = sb("gall", [128, 1024])
        mlng = sb("mlng", [128, D])
        tri = sb("tri", [128, 4, 128])
        identf = sb("identf", [128, 128])
        identb = sb("identb", [128, 128], BF16)
        onesf = sb("onesf", [128, 128])
        LFt = [sb("LFt%d" % d, [128, 256]) for d in range(2)]
        Bc = [sb("Bc%d" % d, [128, 256]) for d in range(2)]
        Aa = [sb("Aa%d" % d, [128, 256]) for d in range(2)]
        EB = [sb("EB%d" % d, [128, 256]) for d in range(2)]
        WST = [sb("WST%d" % d, [128, 256]) for d in range(2)]
        DEC = [sb("DEC%d" % d, [128, 256]) for d in range(2)]
        Cst = [sb("Cst%d" % d, [128, 2, 257]) for d in range(2)]
        Cb = [[sb("Cb%d_%d" % (d, v), [128, 2, 257], BF16) for v in range(2)] for d in range(2)]
        cbver = [0, 0]
        dg = [sb("dg%d" % i, [128, 128]) for i in range(2)]
        Dm = [sb("Dm%d" % i, [128, 128]) for i in range(2)]
        Wm = [sb("Wm%d" % i, [128, 128], BF16) for i in range(2)]
        vw = [sb("vw%d" % i, [128, 257], BF16) for i in range(2)]
        tmpc = [sb("tmpc%d" % i, [128, 257]) for i in range(2)]
        tot = [sb("tot%d" % i, [128, 257]) for i in range(2)]
        sm = [sb("sm%d" % i, [128, 4]) for i in range(2)]
        hs = [sb("hs%d" % i, [128, 256]) for i in range(2)]
        sqv = [sb("sqv%d" % i, [128, 256]) for i in range(2)]
        yab = [sb("yab%d" % i, [128, 256], BF16) for i in range(2)]
        yas = [sb("yas%d" % i, [128, 2, 128], BF16) for i in range(2)]
        pD = ps("pD", [128, 512])
        pST = ps("pST", [128, 512])
        pI = ps("pI", [128, 512])
        pCs = [ps("pC%d" % i, [128, 512]) for i in range(2)]
        pC = pCs[0]
        pU = [ps("pU%d" % i, [128, 512]) for i in range(2)]
        ptrk = ps("ptrk", [128, 1024], BF16)
        ptry = ptrk

        P.dma("sp", gall[:], T["gall_d"], w=["gall"])
        P.dma("sp", mlng[:], T["mlng_rep"], w=["mlng"])
        P.dma("sp", tri[:], T["c_tri"], w=["tri"])
        P.dma("sp", identf[:], T["c_ident"], w=["identf"])
        P.dve(lambda e: e.tensor_copy(out=identb[:], in_=identf[:]), r=["identf"], w=["identb"])
        P.dve(lambda e: e.memset(onesf[:], 1.0), w=["onesf"])
        P.dve(lambda e: e.memset(va[:, :, 256:257], 1.0), w=["va1"])

        g4 = gall[:, :].rearrange("p (c g h) -> p c g h", g=4, h=4)
        v3 = lambda t: t[:, :].rearrange("p (c h) -> p c h", h=4)
        for d in range(2):
            i_d = g4[:, :, 2 * d, :]
            f_d = g4[:, :, 2 * d + 1, :]
            P.act(lambda e, d=d, f_d=f_d: e.activation(out=v3(LFt[d]), in_=f_d, func=AF.Exp, scale=-1.0),
                  r=["gall"], w=[("LFt", d)])
            P.act(lambda e, d=d: e.activation(out=LFt[d][:, :], in_=LFt[d][:, :], func=AF.Ln, bias=1.0),
                  r=[("LFt", d)], w=[("LFt", d)])
            P.pe(lambda e, d=d: e.matmul(pI[:, 0:256], tri[:, d, :], LFt[d][:, :], start=True, stop=True),
                 r=["tri", ("LFt", d)], w=["pI"])
            P.pe(lambda e, d=d: e.matmul(pC[:, 0:256], onesf[:, :], LFt[d][:, :], start=True, stop=True),
                 r=["onesf", ("LFt", d)], w=[("pC", 0)])
            P.dve(lambda e, d=d: e.tensor_scalar(out=Bc[d][:, :], in0=pI[:, 0:256], scalar1=-1.0, scalar2=None, op0=ALU.mult),
                  r=["pI"], w=[("Bc", d)])
            P.dve(lambda e, d=d, i_d=i_d: e.tensor_tensor(out=v3(Aa[d]), in0=pI[:, 0:256].rearrange("p (c h) -> p c h", h=4),
                                                         in1=i_d, op=ALU.add),
                  r=["pI", "gall"], w=[("Aa", d)])
            P.act(lambda e, d=d: e.activation(out=EB[d][:, :], in_=Bc[d][:, :], func=AF.Exp),
                  r=[("Bc", d)], w=[("EB", d)])
            P.dve(lambda e, d=d: e.tensor_copy(out=DEC[d][:, :], in_=pC[:, 0:256]), r=[("pC", 0)], w=[("DEC", d)])
            P.dve(lambda e, d=d: e.tensor_tensor(out=WST[d][:, :], in0=Aa[d][:, :], in1=DEC[d][:, :], op=ALU.subtract),
                  r=[("Aa", d), ("DEC", d)], w=[("WST", d)])
            P.act(lambda e, d=d: e.activation(out=WST[d][:, :], in_=WST[d][:, :], func=AF.Exp),
                  r=[("WST", d)], w=[("WST", d)])
            P.act(lambda e, d=d: e.activation(out=DEC[d][:, :], in_=DEC[d][:, :], func=AF.Exp, scale=-1.0),
                  r=[("DEC", d), ("WST", d)], w=[("DEC", d)])

        cnt = {"o": 0, "u": 0, "g": 0, "y": 0}

        def output_A(h, d, c):
            i = cnt["o"] % 2
            cnt["o"] += 1
            col = c * 4 + h
            cs = slice(c * 128, (c + 1) * 128)
            pCx = pCs[i]
            cbv = Cb[d][cbver[d]]
            cbk = ("Cb", d, cbver[d])
            P.dve(lambda e: e.tensor_scalar(out=dg[i][:], in0=identf[:], scalar1=Bc[d][:, col:col + 1], scalar2=None, op0=ALU.mult),
                  r=["identf", ("Bc", d)], w=[("dg", i)])
            P.pe(lambda e: e.matmul(pD[:, 0:128], onesf[:, :], dg[i][:, :], start=True, stop=False), r=["onesf", ("dg", i)], w=["pD"])
            P.pe(lambda e: e.matmul(pD[:, 0:128], identf[:, :], tri[:, 2 + d, :], start=False, stop=True), r=["identf", "tri"], w=["pD"])
            P.act(lambda e: e.activation(out=Dm[i][:], in_=pD[:, 0:128], func=AF.Exp, bias=Aa[d][:, col:col + 1]),
                  r=["pD", ("Aa", d)], w=[("Dm", i)])
            for dkc in range(2):
                P.pe(lambda e, dkc=dkc: e.matmul(pST[:, 0:128], kT[:, dkc, cs], qT[:, dkc, cs], start=(dkc == 0), stop=(dkc == 1)),
                     r=["kT", "qT"], w=["pST"])
            for dkc in range(2):
                P.pe(lambda e, dkc=dkc: e.matmul(pCx[:, 0:257], qT[:, dkc, cs], cbv[:, dkc, :], start=(dkc == 0), stop=(dkc == 1)),
                     r=["qT", cbk], w=[("pC", i)])
            P.dve(lambda e: e.tensor_tensor(out=Wm[i][:], in0=pST[:, 0:128], in1=Dm[i][:], op=ALU.mult),
                  r=["pST", ("Dm", i)], w=[("Wm", i)])
            P.pe(lambda e: e.matmul(pI[:, 0:257], Wm[i][:, :], va[:, c, :], start=True, stop=True),
                 r=[("Wm", i), "va", "va1"], w=["pI"])
            return (h, d, c, i)

        def output_B(ctx):
            h, d, c, i = ctx
            col = c * 4 + h
            pCx = pCs[i]
            P.act(lambda e: e.activation(out=tmpc[i][:], in_=pCx[:, 0:257], func=AF.Copy, scale=EB[d][:, col:col + 1]),
                  r=[("pC", i), ("EB", d)], w=[("tmpc", i)])
            P.dve(lambda e: e.tensor_tensor(out=tot[i][:], in0=tmpc[i][:], in1=pI[:, 0:257], op=ALU.add),
                  r=[("tmpc", i), "pI"], w=[("tot", i)])
            P.dve(lambda e: e.tensor_scalar(out=sm[i][:, 3:4], in0=tot[i][:, 256:257], scalar1=-1.0, scalar2=1.0, op0=ALU.mult, op1=ALU.max),
                  r=[("tot", i)], w=[("sm", i)])
            P.dve(lambda e: e.tensor_tensor(out=sm[i][:, 0:1], in0=tot[i][:, 256:257], in1=sm[i][:, 3:4], op=ALU.max),
                  r=[("tot", i), ("sm", i)], w=[("sm", i)])
            P.dve(lambda e: e.reciprocal(out=sm[i][:, 1:2], in_=sm[i][:, 0:1]), r=[("sm", i)], w=[("sm", i)])
            if d == 0:
                P.dve(lambda e: e.tensor_scalar(out=hacc[:, c, :], in0=tot[i][:, 0:256], scalar1=sm[i][:, 1:2], scalar2=None, op0=ALU.mult),
                      r=[("tot", i), ("sm", i)], w=[("hacc", c)])
                return
            gi = cnt["g"] % 2
            cnt["g"] += 1
            P.dma("sp", gat[gi][:], gA[c * 128:(c + 1) * 128, h * 256:(h + 1) * 256], w=[("gat", gi)])
            P.dve(lambda e: e.scalar_tensor_tensor(out=hs[i][:], in0=tot[i][:, 0:256], scalar=sm[i][:, 1:2], in1=hacc[:, c, :],
                                                   op0=ALU.mult, op1=ALU.add),
                  r=[("tot", i), ("sm", i), ("hacc", c)], w=[("hs", i)])
            P.dve(lambda e: e.tensor_tensor(out=sqv[i][:], in0=hs[i][:], in1=hs[i][:], op=ALU.mult), r=[("hs", i)], w=[("sqv", i)])
            P.dve(lambda e: e.reduce_sum(out=sm[i][:, 2:3], in_=sqv[i][:], axis=AX.X), r=[("sqv", i)], w=[("smb", i)])
            P.act(lambda e: e.activation(out=sm[i][:, 2:3], in_=sm[i][:, 2:3], func=AF.Ln, scale=1.0 / 256.0, bias=NORM_EPS),
                  r=[("smb", i)], w=[("smb", i)])
            P.act(lambda e: e.activation(out=sm[i][:, 2:3], in_=sm[i][:, 2:3], func=AF.Exp, scale=-0.5),
                  r=[("smb", i)], w=[("smb", i)])
            P.dve(lambda e: e.scalar_tensor_tensor(out=sqv[i][:], in0=hs[i][:], scalar=sm[i][:, 2:3],
                                                   in1=mlng[:, h * 256:(h + 1) * 256], op0=ALU.mult, op1=ALU.mult),
                  r=[("hs", i), ("smb", i), "mlng"], w=[("sqv", i)])
            P.dve(lambda e: e.tensor_tensor(out=yab[i][:], in0=sqv[i][:], in1=gat[gi][:], op=ALU.mult),
                  r=[("sqv", i), ("gat", gi)], w=[("yab", i)])
            for k in range(2):
                P.pe(lambda e, k=k: e.transpose(out=ptry[:, k * 128:(k + 1) * 128], in_=yab[i][:, k * 128:(k + 1) * 128],
                                                identity=identb[:]), r=[("yab", i), "identb"], w=["ptrk"])
            P.act(lambda e: e.activation(out=yas[i][:, :, :], in_=ptry[:, 0:256].rearrange("p (k t) -> p k t", k=2), func=AF.Copy),
                  r=["ptrk"], w=[("yas", i)])
            P.dma("sp", yaT[h * 256:(h + 1) * 256, c * 128:(c + 1) * 128].rearrange("(k p) t -> p k t", p=128),
                  yas[i][:, :, :], r=[("yas", i)])

        def update_step(h, d, c):
            i = cnt["u"] % 2
            cnt["u"] += 1
            col = c * 4 + h
            nv = 1 - cbver[d]
            nb_ = Cb[d][nv]
            nbk = ("Cb", d, nv)
            P.act(lambda e: e.activation(out=vw[i][:], in_=va[:, c, :], func=AF.Copy, scale=WST[d][:, col:col + 1]),
                  r=["va", "va1", ("WST", d)], w=[("vw", i)])
            for dkc in range(2):
                P.pe(lambda e, dkc=dkc: e.matmul(pU[dkc][:, 0:257], ktok[:, c, dkc * 128:(dkc + 1) * 128], vw[i][:, :], start=True, stop=True),
                     r=[("ktok", c), ("vw", i)], w=[("pU", dkc)])
                P.dve(lambda e, dkc=dkc: e.scalar_tensor_tensor(out=Cst[d][:, dkc, :], in0=Cst[d][:, dkc, :], scalar=DEC[d][:, col:col + 1],
                                                                in1=pU[dkc][:, 0:257], op0=ALU.mult, op1=ALU.add),
                      r=[("Cst", d, dkc), ("DEC", d), ("pU", dkc)], w=[("Cst", d, dkc)])
                P.act(lambda e, dkc=dkc, nb_=nb_: e.activation(out=nb_[:, dkc, :], in_=Cst[d][:, dkc, :], func=AF.Copy),
                      r=[("Cst", d, dkc)], w=[nbk])
            cbver[d] = nv

        for h in range(4):
            for dkc in range(2):
                r0 = h * 256 + dkc * 128
                P.dma("sp", qT[:, dkc, :], mqT[r0:r0 + 128, 0:OWN], w=["qT"])
                P.dma("sp", kT[:, dkc, :], mkT[r0:r0 + 128, :], w=["kT"])
            P.dma("sp", va[:, :, 0:256], mv[:, h * 256:(h + 1) * 256].rearrange("(c p) f -> p c f", p=128), w=["va"])
            for d in range(2):
                P.dve(lambda e, d=d: e.memset(Cst[d][:], 0.0), w=[("Cst", d, 0), ("Cst", d, 1)])
                P.dve(lambda e, d=d: e.memset(Cb[d][0][:], 0.0), w=[("Cb", d, 0)])
                cbver[d] = 0
            for c in range(64):
                for dkc in range(2):
                    P.pe(lambda e, c=c, dkc=dkc: e.transpose(out=ptrk[:, dkc * 128:(dkc + 1) * 128],
                                                             in_=kT[:, dkc, c * 128:(c + 1) * 128], identity=identb[:]),
                         r=["kT", "identb"], w=["ptrk"])
                if c % 2 == 0:
                    P.act(lambda e, c=c: e.activation(out=ktok[:, c, :], in_=ptrk[:, 0:256], func=AF.Copy), r=["ptrk"], w=[("ktok", c)])
                else:
                    P.dve(lambda e, c=c: e.tensor_copy(out=ktok[:, c, :], in_=ptrk[:, 0:256]), r=["ptrk"], w=[("ktok", c)])
            for i in range(32):
                ctx = output_A(h, 0, i)
                if i < 31:
                    update_step(h, 0, i)
                update_step(h, 1, 63 - i)
                output_B(ctx)
            for i in range(32, 64):
                c = 63 - i
                ctx = output_A(h, 1, c)
                if c > 0:
                    update_step(h, 1, c)
                output_B(ctx)
        P.flush()


def phase_M2(nc, P, T):
    mqT, mkT, mv, gA, yaT, hfwd = T["mqT"], T["mkT"], T["mv"], T["gA"], T["yaT"], T["hfwd"]
    with contextlib.ExitStack() as st:
        sb = lambda name, shape, dt=F32: st.enter_context(nc.sbuf_tensor("M2_" + name, list(shape), dt))
        ps = lambda name, shape, dt=F32: st.enter_context(nc.psum_tensor("M2_" + name, list(shape), dt))
        gall = sb("gall", [128, 1024])
        mlng = sb("mlng", [128, D])
        tri = sb("tri", [128, 4, 128])
        identf = sb("identf", [128, 128])
        identb = sb("identb", [128, 128], BF16)
        onesf = sb("onesf", [128, 128])
        LFt = [sb("LFt%d" % d, [128, 256]) for d in range(2)]
        Bc = [sb("Bc%d" % d, [128, 256]) for d in range(2)]
        Aa = [sb("Aa%d" % d, [128, 256]) for d in range(2)]
        EB = [sb("EB%d" % d, [128, 256]) for d in range(2)]
        WST = [sb("WST%d" % d, [128, 256]) for d in range(2)]
        DEC = [sb("DEC%d" % d, [128, 256]) for d in range(2)]
        Cst = [sb("Cst%d" % d, [128, 4, 2, 257]) for d in range(2)]
        Cb = [[sb("Cb%d_%d" % (d, v), [128, 4, 2, 257], BF16) for v in range(2)] for d in range(2)]
        cbver = [0, 0]
        NQ, NK = 2, 4
        qc = [sb("qc%d" % i, [128, 8, 128], BF16) for i in range(NQ)]
        kc = [sb("kc%d" % i, [128, 8, 128], BF16) for i in range(NK)]
        vc = [sb("vc%d" % i, [128, 4, 257], BF16) for i in range(NK)]
        ktc = [sb("ktc%d" % i, [128, 1024], BF16) for i in range(NK)]
        gac = [sb("gac%d" % i, [128, 1024], BF16) for i in range(2)]
        hfr = [sb("hfr%d" % i, [128, 1024]) for i in range(2)]
        hfw = [sb("hfw%d" % i, [128, 1024]) for i in range(2)]
        yasc = [sb("yasc%d" % i, [128, 8, 128], BF16) for i in range(2)]
        NWB = 4
        dg = [sb("dg%d" % i, [128, 128]) for i in range(NWB)]
        Dm = [sb("Dm%d" % i, [128, 128]) for i in range(NWB)]
        Wm = [sb("Wm%d" % i, [128, 128], BF16) for i in range(NWB)]
        vw = [sb("vw%d" % i, [128, 257], BF16) for i in range(NWB)]
        tmpc = [sb("tmpc%d" % i, [128, 257]) for i in range(NWB)]
        tot = [sb("tot%d" % i, [128, 257]) for i in range(NWB)]
        sm = [sb("sm%d" % i, [128, 4]) for i in range(NWB)]
        hs = [sb("hs%d" % i, [128, 256]) for i in range(NWB)]
        sqv = [sb("sqv%d" % i, [128, 256]) for i in range(NWB)]
        yab = [sb("yab%d" % i, [128, 256], BF16) for i in range(NWB)]
        pD = ps("pD", [128, 512])
        pST = ps("pST", [128, 512])
        pI = ps("pI", [128, 512])
        pC = ps("pC", [128, 512])
        pU = [ps("pU%d" % i, [128, 512]) for i in range(2)]
        ptrk = ps("ptrk", [128, 1024], BF16)
        ptry = ps("ptry", [128, 1024], BF16)

        P.dma("sp", gall[:], T["gall_d"], w=["gall"])
        P.dma("sp", mlng[:], T["mlng_rep"], w=["mlng"])
        P.dma("sp", tri[:], T["c_tri"], w=["tri"])
        P.dma("sp", identf[:], T["c_ident"], w=["identf"])
        P.dve(lambda e: e.tensor_copy(out=identb[:], in_=identf[:]), r=["identf"], w=["identb"])
        P.dve(lambda e: e.memset(onesf[:], 1.0), w=["onesf"])
        for i in range(NK):
            P.dve(lambda e, i=i: e.memset(vc[i][:, :, 256:257], 1.0), w=[("vc1", i)])
        for d in range(2):
            P.dve(lambda e, d=d: e.memset(Cst[d][:], 0.0), w=[("Cst", d, h, k) for h in range(4) for k in range(2)])
            P.dve(lambda e, d=d: e.memset(Cb[d][0][:], 0.0), w=[("Cb", d, 0, h) for h in range(4)])

        g4 = gall[:, :].rearrange("p (c g h) -> p c g h", g=4, h=4)
        v3 = lambda t: t[:, :].rearrange("p (c h) -> p c h", h=4)
        for d in range(2):
            i_d = g4[:, :, 2 * d, :]
            f_d = g4[:, :, 2 * d + 1, :]
            P.act(lambda e, d=d, f_d=f_d: e.activation(out=v3(LFt[d]), in_=f_d, func=AF.Exp, scale=-1.0),
                  r=["gall"], w=[("LFt", d)])
            P.act(lambda e, d=d: e.activation(out=LFt[d][:, :], in_=LFt[d][:, :], func=AF.Ln, bias=1.0),
                  r=[("LFt", d)], w=[("LFt", d)])
            P.pe(lambda e, d=d: e.matmul(pI[:, 0:256], tri[:, d, :], LFt[d][:, :], start=True, stop=True),
                 r=["tri", ("LFt", d)], w=["pI"])
            P.pe(lambda e, d=d: e.matmul(pC[:, 0:256], onesf[:, :], LFt[d][:, :], start=True, stop=True),
                 r=["onesf", ("LFt", d)], w=["pC"])
            P.dve(lambda e, d=d: e.tensor_scalar(out=Bc[d][:, :], in0=pI[:, 0:256], scalar1=-1.0, scalar2=None, op0=ALU.mult),
                  r=["pI"], w=[("Bc", d)])
            P.dve(lambda e, d=d, i_d=i_d: e.tensor_tensor(out=v3(Aa[d]), in0=pI[:, 0:256].rearrange("p (c h) -> p c h", h=4),
                                                         in1=i_d, op=ALU.add),
                  r=["pI", "gall"], w=[("Aa", d)])
            P.act(lambda e, d=d: e.activation(out=EB[d][:, :], in_=Bc[d][:, :], func=AF.Exp),
                  r=[("Bc", d)], w=[("EB", d)])
            P.dve(lambda e, d=d: e.tensor_copy(out=DEC[d][:, :], in_=pC[:, 0:256]), r=["pC"], w=[("DEC", d)])
            P.dve(lambda e, d=d: e.tensor_tensor(out=WST[d][:, :], in0=Aa[d][:, :], in1=DEC[d][:, :], op=ALU.subtract),
                  r=[("Aa", d), ("DEC", d)], w=[("WST", d)])
            P.act(lambda e, d=d: e.activation(out=WST[d][:, :], in_=WST[d][:, :], func=AF.Exp),
                  r=[("WST", d)], w=[("WST", d)])
            P.act(lambda e, d=d: e.activation(out=DEC[d][:, :], in_=DEC[d][:, :], func=AF.Exp, scale=-1.0),
                  r=[("DEC", d), ("WST", d)], w=[("DEC", d)])

        cnt = {"w": 0, "k": 0, "q": 0, "g": 0, "cp": 0}

        def load_chunk(c, need_q, need_bw_out):
            ks = cnt["k"] % NK
            cnt["k"] += 1
            cs = slice(c * 128, (c + 1) * 128)
            P.dma("sp", kc[ks][:], mkT[:, cs].rearrange("(f p) t -> p f t", p=128), w=[("kc", ks)])
            P.dma("sp", vc[ks][:, :, 0:256], mv[cs, :].rearrange("p (h f) -> p h f", h=4), w=[("vc", ks)])
            for f in range(8):
                P.pe(lambda e, f=f: e.transpose(out=ptrk[:, f * 128:(f + 1) * 128], in_=kc[ks][:, f, :], identity=identb[:]),
                     r=[("kc", ks), "identb"], w=["ptrk"])
            if cnt["cp"] % 2 == 0:
                P.act(lambda e: e.activation(out=ktc[ks][:, :], in_=ptrk[:, :], func=AF.Copy), r=["ptrk"], w=[("ktc", ks)])
            else:
                P.dve(lambda e: e.tensor_copy(out=ktc[ks][:, :], in_=ptrk[:, :]), r=["ptrk"], w=[("ktc", ks)])
            cnt["cp"] += 1
            ctx = {"c": c, "ks": ks}
            if need_q:
                qs = cnt["q"] % NQ
                cnt["q"] += 1
                P.dma("sp", qc[qs][:], mqT[:, cs].rearrange("(f p) t -> p f t", p=128), w=[("qc", qs)])
                ctx["qs"] = qs
            return ctx

        def load_bw_extra(ctx):
            c = ctx["c"]
            cs = slice(c * 128, (c + 1) * 128)
            gs = cnt["g"] % 2
            cnt["g"] += 1
            P.dma("sp", gac[gs][:], gA[cs, :], w=[("gac", gs)])
            P.dma("sp", hfr[gs][:], hfwd[cs, :], r=[("hfwd", c)], w=[("hfr", gs)])
            ctx["gs"] = gs

        def output_A(ck, h, d):
            i = cnt["w"] % NWB
            cnt["w"] += 1
            c, ks, qs = ck["c"], ck["ks"], ck["qs"]
            col = c * 4 + h
            cbv = Cb[d][cbver[d]]
            cbk = ("Cb", d, cbver[d], h)
            P.dve(lambda e: e.tensor_scalar(out=dg[i][:], in0=identf[:], scalar1=Bc[d][:, col:col + 1], scalar2=None, op0=ALU.mult),
                  r=["identf", ("Bc", d)], w=[("dg", i)])
            P.pe(lambda e: e.matmul(pD[:, 0:128], onesf[:, :], dg[i][:, :], start=True, stop=False), r=["onesf", ("dg", i)], w=["pD"])
            P.pe(lambda e: e.matmul(pD[:, 0:128], identf[:, :], tri[:, 2 + d, :], start=False, stop=True), r=["identf", "tri"], w=["pD"])
            P.act(lambda e: e.activation(out=Dm[i][:], in_=pD[:, 0:128], func=AF.Exp, bias=Aa[d][:, col:col + 1]),
                  r=["pD", ("Aa", d)], w=[("Dm", i)])
            for dkc in range(2):
                P.pe(lambda e, dkc=dkc: e.matmul(pST[:, 0:128], kc[ks][:, h * 2 + dkc, :], qc[qs][:, h * 2 + dkc, :],
                                                 start=(dkc == 0), stop=(dkc == 1)),
                     r=[("kc", ks), ("qc", qs)], w=["pST"])
            for dkc in range(2):
                P.pe(lambda e, dkc=dkc: e.matmul(pC[:, 0:257], qc[qs][:, h * 2 + dkc, :], cbv[:, h, dkc, :],
                                                 start=(dkc == 0), stop=(dkc == 1)),
                     r=[("qc", qs), cbk], w=["pC"])
            P.act(lambda e: e.activation(out=tmpc[i][:], in_=pC[:, 0:257], func=AF.Copy, scale=EB[d][:, col:col + 1]),
                  r=["pC", ("EB", d)], w=[("tmpc", i)])
            P.dve(lambda e: e.tensor_tensor(out=Wm[i][:], in0=pST[:, 0:128], in1=Dm[i][:], op=ALU.mult),
                  r=["pST", ("Dm", i)], w=[("Wm", i)])
            P.pe(lambda e: e.matmul(pI[:, 0:257], Wm[i][:, :], vc[ks][:, h, :], start=True, stop=True),
                 r=[("Wm", i), ("vc", ks), ("vc1", ks)], w=["pI"])
            P.dve(lambda e: e.tensor_tensor(out=tot[i][:], in0=tmpc[i][:], in1=pI[:, 0:257], op=ALU.add),
                  r=[("tmpc", i), "pI"], w=[("tot", i)])
            return (ck, h, d, i)

        def output_B(ctx, ws):
            ck, h, d, i = ctx
            c = ck["c"]
            hsl = slice(h * 256, (h + 1) * 256)
            P.dve(lambda e: e.tensor_scalar(out=sm[i][:, 3:4], in0=tot[i][:, 256:257], scalar1=-1.0, scalar2=1.0, op0=ALU.mult, op1=ALU.max),
                  r=[("tot", i)], w=[("sm", i)])
            P.dve(lambda e: e.tensor_tensor(out=sm[i][:, 0:1], in0=tot[i][:, 256:257], in1=sm[i][:, 3:4], op=ALU.max),
                  r=[("tot", i), ("sm", i)], w=[("sm", i)])
            P.dve(lambda e: e.reciprocal(out=sm[i][:, 1:2], in_=sm[i][:, 0:1]), r=[("sm", i)], w=[("sm", i)])
            if d == 0:
                P.act(lambda e: e.activation(out=hfw[ws][:, hsl], in_=tot[i][:, 0:256], func=AF.Copy, scale=sm[i][:, 1:2]),
                      r=[("tot", i), ("sm", i)], w=[("hfw", ws, h)])
                return
            gs = ck["gs"]
            P.dve(lambda e: e.scalar_tensor_tensor(out=hs[i][:], in0=tot[i][:, 0:256], scalar=sm[i][:, 1:2], in1=hfr[gs][:, hsl],
                                                   op0=ALU.mult, op1=ALU.add),
                  r=[("tot", i), ("sm", i), ("hfr", gs)], w=[("hs", i)])
            P.dve(lambda e: e.tensor_tensor(out=sqv[i][:], in0=hs[i][:], in1=hs[i][:], op=ALU.mult), r=[("hs", i)], w=[("sqv", i)])
            P.dve(lambda e: e.reduce_sum(out=sm[i][:, 2:3], in_=sqv[i][:], axis=AX.X), r=[("sqv", i)], w=[("smb", i)])
            P.act(lambda e: e.activation(out=sm[i][:, 2:3], in_=sm[i][:, 2:3], func=AF.Ln, scale=1.0 / 256.0, bias=NORM_EPS),
                  r=[("smb", i)], w=[("smb", i)])
            P.act(lambda e: e.activation(out=sm[i][:, 2:3], in_=sm[i][:, 2:3], func=AF.Exp, scale=-0.5),
                  r=[("smb", i)], w=[("smb", i)])
            P.dve(lambda e: e.scalar_tensor_tensor(out=sqv[i][:], in0=hs[i][:], scalar=sm[i][:, 2:3],
                                                   in1=mlng[:, hsl], op0=ALU.mult, op1=ALU.mult),
                  r=[("hs", i), ("smb", i), "mlng"], w=[("sqv", i)])
            P.dve(lambda e: e.tensor_tensor(out=yab[i][:], in0=sqv[i][:], in1=gac[gs][:, hsl], op=ALU.mult),
                  r=[("sqv", i), ("gac", gs)], w=[("yab", i)])
            for k in range(2):
                f = h * 2 + k
                P.pe(lambda e, k=k, f=f: e.transpose(out=ptry[:, f * 128:(f + 1) * 128], in_=yab[i][:, k * 128:(k + 1) * 128],
                                                     identity=identb[:]), r=[("yab", i), "identb"], w=["ptry"])

        def update_step(ck, h, d):
            i = cnt["w"] % NWB
            cnt["w"] += 1
            c, ks = ck["c"], ck["ks"]
            col = c * 4 + h
            nv = 1 - cbver[d]
            nb_ = Cb[d][nv]
            nbk = ("Cb", d, nv, h)
            P.act(lambda e: e.activation(out=vw[i][:], in_=vc[ks][:, h, :], func=AF.Copy, scale=WST[d][:, col:col + 1]),
                  r=[("vc", ks), ("vc1", ks), ("WST", d)], w=[("vw", i)])
            for dkc in range(2):
                f = h * 2 + dkc
                P.pe(lambda e, dkc=dkc, f=f: e.matmul(pU[dkc][:, 0:257], ktc[ks][:, f * 128:(f + 1) * 128], vw[i][:, :], start=True, stop=True),
                     r=[("ktc", ks), ("vw", i)], w=[("pU", dkc)])
                P.dve(lambda e, dkc=dkc: e.scalar_tensor_tensor(out=Cst[d][:, h, dkc, :], in0=Cst[d][:, h, dkc, :], scalar=DEC[d][:, col:col + 1],
                                                                in1=pU[dkc][:, 0:257], op0=ALU.mult, op1=ALU.add),
                      r=[("Cst", d, h, dkc), ("DEC", d), ("pU", dkc)], w=[("Cst", d, h, dkc)])
                P.act(lambda e, dkc=dkc: e.activation(out=nb_[:, h, dkc, :], in_=Cst[d][:, h, dkc, :], func=AF.Copy),
                      r=[("Cst", d, h, dkc)], w=[nbk])

        def loads_for(step):
            if step < 32:
                return [load_chunk(step, True, False), load_chunk(63 - step, False, False)]
            return [load_chunk(63 - step, True, True)]

        nxt = loads_for(0)
        for step in range(64):
            cur = nxt
            if step + 1 < 64:
                nxt = loads_for(step + 1)
            ws = step % 2
            if step < 32:
                ca, cbk_ = cur
                ctxs = [output_A(ca, h, 0) for h in range(4)]
                if step < 31:
                    for h in range(4):
                        update_step(ca, h, 0)
                    cbver[0] = 1 - cbver[0]
                for h in range(4):
                    update_step(cbk_, h, 1)
                cbver[1] = 1 - cbver[1]
                for cx in ctxs:
                    output_B(cx, ws)
                c = ca["c"]
                P.dma("sp", hfwd[c * 128:(c + 1) * 128, :], hfw[ws][:, :], r=[("hfw", ws, h) for h in range(4)], w=[("hfwd", c)])
                if step == 31:
                    load_bw_extra(nxt[0])
            else:
                ca = cur[0]
                c = ca["c"]
                ctxs = [output_A(ca, h, 1) for h in range(4)]
                if c > 0:
                    for h in range(4):
                        update_step(ca, h, 1)
                    cbver[1] = 1 - cbver[1]
                for cx in ctxs:
                    output_B(cx, ws)
                P.act(lambda e, ws=ws: e.activation(out=yasc[ws][:, :, :], in_=ptry[:, :].rearrange("p (f t) -> p f t", f=8), func=AF.Copy),
                      r=["ptry"], w=[("yasc", ws)])
                P.dma("sp", yaT[:, c * 128:(c + 1) * 128].rearrange("(f p) t -> p f t", p=128), yasc[ws][:, :, :], r=[("yasc", ws)])
                if step + 1 < 64:
                    load_bw_extra(nxt[0])
        P.flush()


def phase_D(nc, P, T):
    dqT, dkT, dv, gB, ybT = T["dqT"], T["dkT"], T["dv"], T["gB"], T["ybT"]
    with contextlib.ExitStack() as st:
        sb = lambda name, shape, dt=F32: st.enter_context(nc.sbuf_tensor("D_" + name, list(shape), dt))
        ps = lambda name, shape, dt=F32: st.enter_context(nc.psum_tensor("D_" + name, list(shape), dt))
        kT = [sb("dkT%d" % i, [128, S], BF16) for i in range(2)]
        va = [sb("dva%d" % i, [128, 64, 129], BF16) for i in range(2)]
        qT = [sb("dqT%d" % i, [128, OWN], BF16) for i in range(2)]
        NE = 3
        E = [sb("dE%d" % i, [128, 1024], BF16) for i in range(NE)]
        gbt = [sb("dgb%d" % i, [128, 4, 128], BF16) for i in range(2)]
        lamr = sb("lamr", [128, 256])
        ltmp = sb("ltmp", [128, 128])
        lam = sb("lam", [128, 4])
        subg = sb("subg", [128, 128])
        identf = sb("identf", [128, 128])
        identb = sb("identb", [128, 128], BF16)
        r12 = [sb("r12_%d" % i, [128, 4]) for i in range(2)]
        o1 = [sb("o1_%d" % i, [128, 128]) for i in range(2)]
        o2 = [sb("o2_%d" % i, [128, 128]) for i in range(2)]
        sq = [sb("sq_%d" % i, [128, 128]) for i in range(2)]
        ybq = [sb("ybq%d" % i, [128, 128], BF16) for i in range(8)]
        ybs = [sb("ybs%d" % i, [128, 512], BF16) for i in range(2)]
        pS = [ps("pS%d" % i, [128, 1024]) for i in range(2)]
        pacc = [ps("pacc%d" % i, [128, 512]) for i in range(3)]
        ptr = ps("dptr", [128, 512], BF16)

        P.dma("sp", lamr[:], T["lam_rep"], w=["lamr"])
        P.dma("sp", subg[:], T["subln_rep"], w=["subg"])
        P.dma("sp", identf[:], T["c_ident"], w=["identf"])
        P.dve(lambda e: e.tensor_copy(out=identb[:], in_=identf[:]), r=["identf"], w=["identb"])
        P.dve(lambda e: e.tensor_scalar(out=subg[:], in0=subg[:], scalar1=(1.0 - LAMBDA_INIT), scalar2=None, op0=ALU.mult),
              r=["subg"], w=["subg"])
        P.dve(lambda e: e.tensor_tensor(out=ltmp[:, 0:64], in0=lamr[:, 0:64], in1=lamr[:, 64:128], op=ALU.mult),
              r=["lamr"], w=["ltmp"])
        P.dve(lambda e: e.tensor_tensor(out=ltmp[:, 64:128], in0=lamr[:, 128:192], in1=lamr[:, 192:256], op=ALU.mult),
              r=["lamr"], w=["ltmp"])
        P.dve(lambda e: e.reduce_sum(out=lam[:, 0:2], in_=ltmp[:, :].rearrange("p (a b) -> p a b", a=2), axis=AX.X),
              r=["ltmp"], w=["lam"])
        P.act(lambda e: e.activation(out=lam[:, 0:2], in_=lam[:, 0:2], func=AF.Exp), r=["lam"], w=["lam"])
        P.dve(lambda e: e.tensor_tensor(out=lam[:, 2:3], in0=lam[:, 0:1], in1=lam[:, 1:2], op=ALU.subtract),
              r=["lam"], w=["lam"])
        P.dve(lambda e: e.tensor_scalar(out=lam[:, 3:4], in0=lam[:, 2:3], scalar1=LAMBDA_INIT, scalar2=-1.0,
                                        op0=ALU.add, op1=ALU.mult), r=["lam"], w=["lam"])
        for i in range(2):
            P.dve(lambda e, i=i: e.memset(va[i][:, :, 128:129], 1.0), w=[("va1", i)])

        def acc_ap(a, lo, hi):
            return pacc[a // 3][:, (a % 3) * 129 + lo:(a % 3) * 129 + hi]

        accS = [sb("accS%d" % i, [128, 8 * 129]) for i in range(2)]
        mhalf = sb("mhalf", [128, 1])
        P.dve(lambda e: e.memset(mhalf[:], -0.5), w=["mhalf"])

        def load_head(h):
            hs = h % 2
            P.dma("sp", kT[hs][:], dkT[h * 128:(h + 1) * 128, :], w=[("kT", hs)])
            P.dma("sp", qT[hs][:], dqT[h * 128:(h + 1) * 128, :], w=[("qT", hs)])
            P.dma("sp", va[hs][:, :, 0:128], dv[:, h * 128:(h + 1) * 128].rearrange("(kb p) c -> p kb c", p=128),
                  w=[("va", hs)])

        def load_gb(h, qt):
            gs = (h * 8 + qt) % 2
            P.dma("sp", gbt[gs][:], gB[qt * 512:(qt + 1) * 512, h * 128:(h + 1) * 128].rearrange("(u p) c -> p u c", p=128),
                  w=[("gbt", gs)])

        def qk(i, h, qt, kb):
            sl, hs = i % 2, h % 2
            for j in range(2):
                P.pe(lambda e, j=j: e.matmul(
                    pS[sl][:, j * 512:(j + 1) * 512], kT[hs][j * 64:(j + 1) * 64, kb * 128:(kb + 1) * 128],
                    qT[hs][j * 64:(j + 1) * 64, qt * 512:(qt + 1) * 512], start=True, stop=True),
                    r=[("kT", hs), ("qT", hs)], w=[("pS", sl)])

        def ex(i):
            sl, es = i % 2, i % NE
            P.act(lambda e: e.activation(out=E[es][:, :], in_=pS[sl][:, :], func=AF.Exp), r=[("pS", sl)], w=[("E", es)])

        def pv(i, h, kb):
            es, hs = i % NE, h % 2
            for a in range(8):
                j, u = a // 4, a % 4
                P.pe(lambda e, a=a, j=j, u=u: e.matmul(
                    acc_ap(a, 0, 129), E[es][:, j * 512 + u * 128:j * 512 + (u + 1) * 128], va[hs][:, kb, :],
                    start=(kb == 0 and a % 3 == 0), stop=(kb == 63), skip_group_check=True),
                    r=[("E", es), ("va", hs), ("va1", hs)], w=[("accb", a // 3)])

        epi = [0]

        def epilogue(h, qt):
            gs = (h * 8 + qt) % 2
            ys = gs
            ai = gs
            A = accS[ai]
            for b in range(3):
                n = 387 if b < 2 else 258
                P.dve(lambda e, b=b, n=n: e.tensor_copy(out=A[:, b * 387:b * 387 + n], in_=pacc[b][:, 0:n]),
                      r=[("accb", b)], w=[("accS", ai)])
            sa = lambda a, lo, hi: A[:, a * 129 + lo:a * 129 + hi]
            for u in range(4):
                ep = epi[0] % 2
                epi[0] += 1
                a0, a1 = u, 4 + u
                P.dve(lambda e, ep=ep, a0=a0: e.reciprocal(out=r12[ep][:, 0:1], in_=sa(a0, 128, 129)),
                      r=[("accS", ai)], w=[("r12", ep)])
                P.dve(lambda e, ep=ep, a1=a1: e.reciprocal(out=r12[ep][:, 1:2], in_=sa(a1, 128, 129)),
                      r=[("accS", ai)], w=[("r12", ep)])
                P.dve(lambda e, ep=ep: e.tensor_tensor(out=r12[ep][:, 2:3], in0=r12[ep][:, 1:2], in1=lam[:, 3:4], op=ALU.mult),
                      r=[("r12", ep), "lam"], w=[("r12", ep)])
                P.dve(lambda e, ep=ep, a0=a0: e.tensor_scalar(out=o1[ep][:], in0=sa(a0, 0, 128), scalar1=r12[ep][:, 0:1],
                                                              scalar2=None, op0=ALU.mult),
                      r=[("accS", ai), ("r12", ep)], w=[("o1", ep)])
                P.dve(lambda e, ep=ep, a1=a1: e.scalar_tensor_tensor(out=o2[ep][:], in0=sa(a1, 0, 128), scalar=r12[ep][:, 2:3],
                                                                     in1=o1[ep][:], op0=ALU.mult, op1=ALU.add),
                      r=[("accS", ai), ("r12", ep), ("o1", ep)], w=[("o2", ep)])
                P.dve(lambda e, ep=ep: e.tensor_tensor(out=sq[ep][:], in0=o2[ep][:], in1=o2[ep][:], op=ALU.mult),
                      r=[("o2", ep)], w=[("sq", ep)])
                P.dve(lambda e, ep=ep: e.reduce_sum(out=r12[ep][:, 3:4], in_=sq[ep][:], axis=AX.X),
                      r=[("sq", ep)], w=[("r12b", ep)])
                P.dve(lambda e, ep=ep: e.tensor_scalar(out=r12[ep][:, 3:4], in0=r12[ep][:, 3:4], scalar1=1.0 / 128.0, scalar2=NORM_EPS,
                                                       op0=ALU.mult, op1=ALU.add), r=[("r12b", ep)], w=[("r12b", ep)])
                P.pool(lambda e, ep=ep: e.tensor_tensor(out=r12[ep][:, 3:4], in0=r12[ep][:, 3:4], in1=mhalf[:, 0:1], op=ALU.pow),
                       r=[("r12b", ep), "mhalf"], w=[("r12b", ep)])
                P.dve(lambda e, ep=ep: e.scalar_tensor_tensor(out=o1[ep][:], in0=o2[ep][:], scalar=r12[ep][:, 3:4],
                                                              in1=subg[:], op0=ALU.mult, op1=ALU.mult),
                      r=[("o2", ep), ("r12b", ep), "subg"], w=[("o1", ep)])
                yq = gs * 4 + u
                P.dve(lambda e, ep=ep, u=u, yq=yq: e.tensor_tensor(out=ybq[yq][:], in0=o1[ep][:], in1=gbt[gs][:, u, :], op=ALU.mult),
                      r=[("o1", ep), ("gbt", gs)], w=[("ybq", yq)])

        def epilogue2(h, qt):
            gs = (h * 8 + qt) % 2
            ys = gs
            for u in range(4):
                yq = gs * 4 + u
                P.pe(lambda e, u=u, yq=yq: e.transpose(out=ptr[:, u * 128:(u + 1) * 128], in_=ybq[yq][:], identity=identb[:]),
                     r=[("ybq", yq), "identb"], w=["ptr"])
            P.dve(lambda e: e.tensor_copy(out=ybs[ys][:, :], in_=ptr[:, :]), r=["ptr"], w=[("ybs", ys)])
            P.dma("sp", ybT[h * 128:(h + 1) * 128, qt * 512:(qt + 1) * 512], ybs[ys][:], r=[("ybs", ys)])

        blocks = [(h, qt, kb) for h in range(8) for qt in range(8) for kb in range(64)]
        nb = len(blocks)
        load_head(0)
        load_gb(0, 0)
        qk(0, *blocks[0])
        qk(1, *blocks[1])
        for i, (h, qt, kb) in enumerate(blocks):
            if kb == 0 and qt == 0 and h + 1 < 8:
                load_head(h + 1)
            if kb == 0:
                nxt = h * 8 + qt + 1
                if nxt < 64:
                    load_gb(nxt // 8, nxt % 8)
            ex(i)
            pv(i, h, kb)
            if i + 2 < nb:
                qk(i + 2, *blocks[i + 2])
            if kb == 63:
                epilogue(h, qt)
            if kb == 40 and (h, qt) != (0, 0):
                pq = h * 8 + qt - 1
                epilogue2(pq // 8, pq % 8)
        epilogue2(7, 7)
        P.flush()


def phase_O(nc, P, T):
    x, out, yaT, ybT, sgT = T["x"], T["out"], T["yaT"], T["ybT"], T["sgT"]
    with contextlib.ExitStack() as st:
        sb = lambda name, shape, dt=F32: st.enter_context(nc.sbuf_tensor("O_" + name, list(shape), dt))
        ps = lambda name, shape, dt=F32: st.enter_context(nc.psum_tensor("O_" + name, list(shape), dt))
        Wa = sb("Wa", [128, 8, D], BF16)
        Wb = sb("Wb", [128, 8, D], BF16)
        Wo = sb("Wo", [128, 8, D], BF16)
        fing = sb("fing", [128, D])
        ya = [sb("ya%d" % i, [128, 8, 512], BF16) for i in range(2)]
        yb = [sb("yb%d" % i, [128, 8, 512], BF16) for i in range(2)]
        sa = [sb("sa%d" % i, [128, 8, 512], BF16) for i in range(2)]
        sbb = [sb("sb%d" % i, [128, 8, 512], BF16) for i in range(2)]
        mixT = [sb("mixT%d" % i, [128, 8, 512], BF16) for i in range(2)]
        t1 = [sb("t1_%d" % i, [128, 512]) for i in range(2)]
        t2 = [sb("t2_%d" % i, [128, 512]) for i in range(2)]
        xt = [sb("xt%d" % i, [128, D]) for i in range(2)]
        xo = [sb("xo%d" % i, [128, D]) for i in range(2)]
        junk = sb("junk", [128, D], BF16)
        ssq = [sb("ssq%d" % i, [128, 2]) for i in range(2)]
        pa = [ps("pa%d" % i, [128, 512]) for i in range(2)]
        pb = [ps("pb%d" % i, [128, 512]) for i in range(2)]
        po = [ps("po%d" % i, [128, 512]) for i in range(4)]

        for (W, src, key) in ((Wa, T["w_a"], "Wa"), (Wb, T["w_b"], "Wb"), (Wo, T["w_o"], "Wo")):
            for hh in range(2):
                P.dma("pool", W[:, :, hh * 512:(hh + 1) * 512],
                      src[:, hh * 512:(hh + 1) * 512].rearrange("(kc p) c -> p kc c", p=128), w=[(key, hh)])
        P.dma("sp", fing[:], T["fing_rep"], w=["fing"])
        wkeys = lambda k: [(k, 0), (k, 1)]
        cnt = {"p": 0, "o": 0, "x": 0}
        for tt in range(8):
            sl = tt % 2
            ts_ = slice(tt * 512, (tt + 1) * 512)
            P.dma("sp", ya[sl][:], yaT[:, ts_].rearrange("(cc p) t -> p cc t", p=128), w=[("ya", sl)])
            P.dma("sp", yb[sl][:], ybT[:, ts_].rearrange("(cc p) t -> p cc t", p=128), w=[("yb", sl)])
            P.dma("sp", sa[sl][:], sgT[0, :, ts_].rearrange("(cc p) t -> p cc t", p=128), w=[("sa", sl)])
            P.dma("sp", sbb[sl][:], sgT[1, :, ts_].rearrange("(cc p) t -> p cc t", p=128), w=[("sb", sl)])
            for dd in range(8):
                i = cnt["p"] % 2
                cnt["p"] += 1
                for cc in range(8):
                    P.pe(lambda e, i=i, cc=cc, dd=dd, sl=sl: e.matmul(pa[i][:, :], Wa[:, cc, dd * 128:(dd + 1) * 128], ya[sl][:, cc, :],
                                                                      start=(cc == 0), stop=(cc == 7)),
                         r=wkeys("Wa") + [("ya", sl)], w=[("pa", i)])
                for cc in range(8):
                    P.pe(lambda e, i=i, cc=cc, dd=dd, sl=sl: e.matmul(pb[i][:, :], Wb[:, cc, dd * 128:(dd + 1) * 128], yb[sl][:, cc, :],
                                                                      start=(cc == 0), stop=(cc == 7)),
                         r=wkeys("Wb") + [("yb", sl)], w=[("pb", i)])
                P.dve(lambda e, i=i, dd=dd, sl=sl: e.tensor_tensor(out=t1[i][:], in0=pa[i][:, :], in1=sa[sl][:, dd, :], op=ALU.mult),
                      r=[("pa", i), ("sa", sl)], w=[("t1", i)])
                P.dve(lambda e, i=i, dd=dd, sl=sl: e.tensor_tensor(out=t2[i][:], in0=pb[i][:, :], in1=sbb[sl][:, dd, :], op=ALU.mult),
                      r=[("pb", i), ("sb", sl)], w=[("t2", i)])
                P.pool(lambda e, i=i, dd=dd, sl=sl: e.tensor_tensor(out=mixT[sl][:, dd, :], in0=t1[i][:], in1=t2[i][:], op=ALU.add),
                       r=[("t1", i), ("t2", i)], w=[("mixT", sl)])
            for u in range(4):
                xs = cnt["x"] % 2
                cnt["x"] += 1
                r0 = tt * 512 + u * 128
                P.dma("sp", xt[xs][:], x[r0:r0 + 128, :], w=[("xt", xs)])
                for eg in range(2):
                    o = cnt["o"] % 4
                    cnt["o"] += 1
                    for dd in range(8):
                        P.pe(lambda e, o=o, dd=dd, sl=sl, u=u, eg=eg: e.matmul(
                            po[o][:, :], mixT[sl][:, dd, u * 128:(u + 1) * 128], Wo[:, dd, eg * 512:(eg + 1) * 512],
                            start=(dd == 0), stop=(dd == 7)), r=[("mixT", sl), ("Wo", eg)], w=[("po", o)])
                    P.dve(lambda e, o=o, xs=xs, eg=eg: e.tensor_tensor(out=xo[xs][:, eg * 512:(eg + 1) * 512], in0=po[o][:, :],
                                                                       in1=xt[xs][:, eg * 512:(eg + 1) * 512], op=ALU.add),
                          r=[("po", o), ("xt", xs)], w=[("xo", xs, eg)])
                P.act(lambda e, xs=xs: e.activation(out=junk[:], in_=xo[xs][:], func=AF.Square, accum_out=ssq[xs][:, 0:1]),
                      r=[("xo", xs, 0), ("xo", xs, 1)], w=[("ssq", xs)])
                P.act(lambda e, xs=xs: e.activation(out=ssq[xs][:, 0:1], in_=ssq[xs][:, 0:1], func=AF.Sqrt, scale=1.0 / D, bias=NORM_EPS),
                      r=[("ssq", xs)], w=[("ssq", xs)])
                P.dve(lambda e, xs=xs: e.reciprocal(out=ssq[xs][:, 1:2], in_=ssq[xs][:, 0:1]), r=[("ssq", xs)], w=[("ssq", xs)])
                P.dve(lambda e, xs=xs: e.scalar_tensor_tensor(out=xo[xs][:], in0=xo[xs][:], scalar=ssq[xs][:, 1:2], in1=fing[:],
                                                              op0=ALU.mult, op1=ALU.mult),
                      r=[("xo", xs, 0), ("xo", xs, 1), ("ssq", xs), "fing"], w=[("xo", xs, 0), ("xo", xs, 1)])
                P.dma("sp", out[r0:r0 + 128, :], xo[xs][:], r=[("xo", xs, 0), ("xo", xs, 1)])
        P.flush()


def make_in_maps(x, positions, norm_g, w_in, ml_gate_b, ml_conv_w, ml_norm_g, da_lambda,
                 da_subln_g, gate_b, w_branch_a, w_branch_b, w_out, final_g):
    f32 = np.float32
    x = np.asarray(x, f32)
    positions = np.asarray(positions, np.int32)
    w_in0 = np.ascontiguousarray(np.asarray(w_in, f32)[0])
    gb0 = np.asarray(ml_gate_b, f32)[0]
    cw0 = np.asarray(ml_conv_w, f32)[0]
    w_in1 = w_in0.copy()
    ag = w_in0[:, C_AG:C_AG + 16].reshape(D, 4, 4)
    w_in1[:, C_AG:C_AG + 16] = ag[:, [2, 3, 0, 1], :].reshape(D, 16)
    gb1 = gb0[[2, 3, 0, 1], :]
    cw1 = cw0[::-1, :]
    rep = lambda v, n=128: np.ascontiguousarray(np.broadcast_to(np.asarray(v, f32).reshape(1, -1), (n, np.asarray(v).size)))
    ident = np.eye(128, dtype=f32)
    rsw = np.zeros((128, 128), f32)
    for r in range(128):
        m = r % 64
        if m < 32:
            rsw[r + 32, r] = -1.0
        else:
            rsw[r - 32, r] = 1.0
    invf = (10000.0 ** (-np.arange(0, 64, 2, dtype=f32) / f32(64))).astype(f32)
    invf_p = np.array([invf[(p % 64) % 32] for p in range(128)], f32).reshape(128, 1)
    ii = np.arange(128)
    U = (ii[:, None] <= ii[None, :]).astype(f32)
    L = (ii[:, None] >= ii[None, :]).astype(f32)
    NEG = -30000.0
    tri = np.stack([U, L, (1 - U) * NEG, (1 - L) * NEG], axis=1).astype(f32)
    common = {
        "normg_rep": rep(np.asarray(norm_g, f32)[0]),
        "mlng_rep": rep(np.asarray(ml_norm_g, f32)[0].reshape(-1)),
        "lam_rep": rep(np.asarray(da_lambda, f32)[0].reshape(-1)),
        "subln_rep": rep(np.asarray(da_subln_g, f32)[0]),
        "gb_fm": np.ascontiguousarray(np.asarray(gate_b, f32)[0].reshape(2, 8, 128).transpose(2, 0, 1)),
        "w_a": np.ascontiguousarray(np.asarray(w_branch_a, f32)[0]),
        "w_b": np.ascontiguousarray(np.asarray(w_branch_b, f32)[0]),
        "w_o": np.ascontiguousarray(np.asarray(w_out, f32)[0]),
        "fing_rep": rep(np.asarray(final_g, f32)),
        "c_ident": ident, "c_rswap": rsw, "c_invf": invf_p, "c_tri": tri,
    }
    in_maps = []
    for core in range(NCORES):
        b, half = core // 2, core % 2
        xb = x[b]
        pb = positions[b]
        if half == 1:
            xb = xb[::-1]
            pb = pb[::-1]
        gbx = gb1 if half else gb0
        cwx = cw1 if half else cw0
        m = dict(common)
        m["x"] = np.ascontiguousarray(xb)
        m["posr"] = np.ascontiguousarray(np.broadcast_to(pb.reshape(1, S), (128, S))).astype(np.int32)
        m["w_in"] = w_in1 if half else w_in0
        m["gateb_rep"] = rep(gbx.reshape(-1))
        m["convw"] = np.ascontiguousarray(cwx.reshape(5, 16, 128).transpose(2, 1, 0))
        in_maps.append(m)
    return in_maps


_NC_CACHE = {}


def kernel(**inputs):
    in_maps = make_in_maps(**inputs)
    if "nc" not in _NC_CACHE:
        _NC_CACHE["nc"] = build_nc()
    nc = _NC_CACHE["nc"]
    res = run_bass_kernel_spmd(nc, in_maps, core_ids=list(range(NCORES)))
    B = 4
    outp = np.empty((B, S, D), np.float32)
    for core in range(NCORES):
        b, half = core // 2, core % 2
        o = np.asarray(res.results[core]["out"], np.float32)
        if half == 0:
            outp[b, :OWN] = o
        else:
            outp[b, OWN:] = o[::-1]
    return outp
```

```python
import contextlib
import math
import numpy as np
import concourse.bass as bass
import concourse.mybir as mybir
from concourse.bass_utils import run_bass_kernel_spmd

F32, BF16, I32 = mybir.dt.float32, mybir.dt.bfloat16, mybir.dt.int32
AF = mybir.ActivationFunctionType
ALU = mybir.AluOpType
AX = mybir.AxisListType

D = 1024
S = 8192
OWN = 4096
NCORES = 8
PROJ = 11280
C_AQ, C_AK, C_AV, C_AO, C_AZ, C_AG = 0, 1024, 2048, 3072, 4096, 5120
C_BQ, C_BK, C_BV, C_BZ, C_GA, C_GB = 5136, 6160, 7184, 8208, 9232, 10256
NORM_EPS = 1e-6
LAMBDA_INIT = 0.8 - 0.6 * math.exp(-0.3 * 0)
SAME_ENGINE_SYNC = True
SAME_ENGINE_RAW_ONLY = False


class _Op:
    __slots__ = ("eng", "fn", "dma", "clock", "idx", "waits", "signal", "semval", "sem", "know")


class Prog:
    ENGS = ("sp", "act", "dve", "pool", "pe")

    def __init__(self, nc, stack, n_dma=8):
        self.nc = nc
        self.sem = {e: stack.enter_context(nc.semaphore("cs_" + e)) for e in self.ENGS}
        self.dsem = {q: [stack.enter_context(nc.semaphore("ds_%s_%d" % (q, i))) for i in range(n_dma)]
                     for q in ("sp", "pool", "act")}
        self.n_dma = n_dma
        self.dcount = {q: 0 for q in self.dsem}
        self.dlast = {}
        self.cnt = {e: 0 for e in self.ENGS}
        self.sigcnt = {e: 0 for e in self.ENGS}
        self.know = {e: {} for e in self.ENGS}
        self.pending = {e: [] for e in self.ENGS}
        self.last_w = {}
        self.readers = {}
        self.nops = 0

    def add(self, eng, fn, reads=(), writes=(), dma=False, extra_deps=()):
        op = _Op()
        op.eng, op.fn, op.dma, op.signal, op.semval = eng, fn, dma, False, None
        self.nops += 1
        deps = []
        seen = set()

        raw = set()

        def push(d, is_raw=False):
            if d is None:
                return
            if is_raw:
                raw.add(id(d))
            if id(d) not in seen:
                seen.add(id(d))
                deps.append(d)

        for k in reads:
            push(self.last_w.get(k), True)
        for k in writes:
            push(self.last_w.get(k))
            for r in self.readers.get(k, ()):
                push(r)
        for d in extra_deps:
            push(d, True)
        if dma:
            slot = self.dcount[eng] % self.n_dma
            self.dcount[eng] += 1
            op.clock = (eng, slot)
            prev = self.dlast.get(op.clock)
            op.idx = (prev.idx + 1) if prev is not None else 1
            op.sem = self.dsem[eng][slot]
            op.semval = 16 * op.idx
            push(prev)
            self.dlast[op.clock] = op
        else:
            self.cnt[eng] += 1
            op.clock = eng
            op.idx = self.cnt[eng]
            op.sem = self.sem[eng]
        know = self.know[eng]
        waits = []
        for d in deps:
            if (not d.dma) and (not dma) and d.eng == eng:
                if eng == "pe" or not SAME_ENGINE_SYNC:
                    continue
                if SAME_ENGINE_RAW_ONLY and id(d) not in raw:
                    continue
            if know.get(d.clock, 0) >= d.idx:
                continue
            waits.append(d)
            d.signal = True
            for c, v in d.know.items():
                if know.get(c, 0) < v:
                    know[c] = v
            if know.get(d.clock, 0) < d.idx:
                know[d.clock] = d.idx
        op.waits = waits
        op.know = dict(know)
        for k in writes:
            self.last_w[k] = op
            self.readers[k] = []
        for k in reads:
            self.readers.setdefault(k, []).append(op)
        self.pending[eng].append(op)
        return op

    def pe(self, fn, r=(), w=()):
        return self.add("pe", fn, r, w)

    def act(self, fn, r=(), w=()):
        return self.add("act", fn, r, w)

    def dve(self, fn, r=(), w=()):
        return self.add("dve", fn, r, w)

    def pool(self, fn, r=(), w=()):
        return self.add("pool", fn, r, w)

    def dma(self, q, out, in_, r=(), w=()):
        return self.add(q, lambda e: e.dma_start(out=out, in_=in_), r, w, dma=True)

    def flush(self, final=False):
        outstanding = [d for d in self.dlast.values()]
        self.add("sp", None, extra_deps=outstanding)
        for e in self.ENGS:
            for op in self.pending[e]:
                if not op.dma and op.signal:
                    self.sigcnt[e] += 1
                    op.semval = self.sigcnt[e]
                elif not op.dma:
                    op.semval = None
        pend = self.pending
        sems = self.sem

        def make_body(eng):
            ops = pend[eng]

            def body(e):
                for op in ops:
                    for d in op.waits:
                        assert d.semval is not None
                        e.wait_ge(d.sem, d.semval)
                    if op.fn is None:
                        continue
                    ins = op.fn(e)
                    if op.dma:
                        ins.then_inc(op.sem, 16)
                    elif op.signal:
                        ins.then_inc(sems[eng], 1)
            return body

        with self.nc.Block() as block:
            block.sync(make_body("sp"))
            block.scalar(make_body("act"))
            block.vector(make_body("dve"))
            block.gpsimd(make_body("pool"))
            block.tensor(make_body("pe"))
        self.pending = {e: [] for e in self.ENGS}
        self.last_w = {}
        self.readers = {}
        full = {}
        for e in self.ENGS:
            full[e] = self.cnt[e]
        for c, d in self.dlast.items():
            full[c] = d.idx
        self.know = {e: dict(full) for e in self.ENGS}


def build_nc(debug=False, phases=("P", "M", "D", "O")):
    nc = bass.Bass("TRN2", target_bir_lowering=False)
    IN = lambda name, shape, dt=F32: nc.dram_tensor(name, list(shape), dt, kind="ExternalInput").ap()
    skind = "ExternalOutput" if debug else "Internal"
    SCR = lambda name, shape, dt=BF16: nc.dram_tensor(name, list(shape), dt, kind=skind).ap()

    x = IN("x", [S, D])
    posr = IN("posr", [128, S], I32)
    w_in = IN("w_in", [D, PROJ])
    normg_rep = IN("normg_rep", [128, D])
    gateb_rep = IN("gateb_rep", [128, 16])
    convw = IN("convw", [128, 16, 5])
    mlng_rep = IN("mlng_rep", [128, D])
    lam_rep = IN("lam_rep", [128, 256])
    subln_rep = IN("subln_rep", [128, 128])
    gb_fm = IN("gb_fm", [128, 2, 8])
    w_a = IN("w_a", [D, D])
    w_b = IN("w_b", [D, D])
    w_o = IN("w_o", [D, D])
    fing_rep = IN("fing_rep", [128, D])
    c_ident = IN("c_ident", [128, 128])
    c_rswap = IN("c_rswap", [128, 128])
    c_invf = IN("c_invf", [128, 1])
    c_tri = IN("c_tri", [128, 4, 128])
    out = nc.dram_tensor("out", [OWN, D], F32, kind="ExternalOutput").ap()

    mqT = SCR("mqT", [D, S])
    mkT = SCR("mkT", [D, S])
    mv = SCR("mv", [S, D])
    dqT = SCR("dqT", [D, OWN])
    dkT = SCR("dkT", [D, S])
    dv = SCR("dv", [S, D])
    gA = SCR("gA", [OWN, D])
    gB = SCR("gB", [OWN, D])
    sgT = SCR("sgT", [2, D, OWN])
    gall_d = SCR("gall_d", [128, 64 * 16], F32)
    yaT = SCR("yaT", [D, OWN])
    ybT = SCR("ybT", [D, OWN])
    hfwd = SCR("hfwd", [OWN, D], F32)

    with contextlib.ExitStack() as top:
        P = Prog(nc, top)
        if "P" in phases:
            phase_P(nc, P, locals())
        if "M" in phases:
            phase_M(nc, P, locals())
        if "M2" in phases:
            phase_M2(nc, P, locals())
        if "D" in phases:
            phase_D(nc, P, locals())
        if "O" in phases:
            phase_O(nc, P, locals())
    return nc


def phase_P(nc, P, T):
    x, posr, w_in = T["x"], T["posr"], T["w_in"]
    with contextlib.ExitStack() as st:
        sb = lambda name, shape, dt=F32: st.enter_context(nc.sbuf_tensor("P_" + name, list(shape), dt))
        ps = lambda name, shape, dt=F32: st.enter_context(nc.psum_tensor("P_" + name, list(shape), dt))
        hT = sb("hT", [128, 8, 2048], BF16)
        NX = 4
        xt = [sb("xt%d" % i, [128, D]) for i in range(NX)]
        xn = [sb("xn%d" % i, [128, D], BF16) for i in range(NX)]
        junk = sb("junk", [128, D], BF16)
        ss = [sb("ss%d" % i, [128, 1]) for i in range(NX)]
        rstd = [sb("rstd%d" % i, [128, 1]) for i in range(NX)]
        grep = sb("grep", [128, D])
        NW = 3
        wB = [sb("wB%d" % i, [128, 8, 512], BF16) for i in range(NW)]
        wG = sb("wG", [128, 8, 16], BF16)
        NWK = 5
        WK = [sb("WK%d" % i, [128, 2054]) for i in range(NWK)]
        OB = [sb("OB%d" % i, [128, 2050], BF16) for i in range(2)]
        OBT = [sb("OBT%d" % i, [128, 8192], BF16) for i in range(2)]
        obt_slot = [0]
        cosT = sb("cosT", [128, 2048])
        sinT = sb("sinT", [128, 2048])
        gall = sb("gall", [128, 64 * 16])
        gbrep = sb("gbrep", [128, 16])
        cw = sb("cw", [128, 16, 5])
        carry = sb("carry", [128, 16, 4])
        identf = sb("identf", [128, 128])
        identb = sb("identb", [128, 128], BF16)
        rswap = sb("rswap", [128, 128])
        invf = sb("invf", [128, 1])
        gbfm = sb("gbfm", [128, 2, 8])
        posi = sb("posi", [128, 2048], I32)
        ptr = [ps("ptr%d" % i, [128, D], BF16) for i in range(2)]
        pp = [ps("pp%d" % i, [128, 512]) for i in range(4)]
        prot = [ps("prot%d" % i, [128, 512]) for i in range(2)]

        P.dma("sp", grep[:], T["normg_rep"], w=["grep"])
        P.dma("sp", gbrep[:], T["gateb_rep"], w=["gbrep"])
        P.dma("sp", cw[:], T["convw"], w=["cw"])
        P.dma("sp", identf[:], T["c_ident"], w=["identf"])
        P.dma("sp", rswap[:], T["c_rswap"], w=["rswap"])
        P.dma("sp", invf[:], T["c_invf"], w=["invf"])
        P.dma("sp", gbfm[:], T["gb_fm"], w=["gbfm"])
        P.dma("pool", wG[:], w_in[:, C_AG:C_AG + 16].rearrange("(kc p) c -> p kc c", p=128), w=["wG"])
        P.dve(lambda e: e.tensor_copy(out=identb[:], in_=identf[:]), r=["identf"], w=["identb"])
        P.dve(lambda e: e.memset(carry[:], 0.0), w=["carry"])
        for i in range(NWK):
            P.dve(lambda e, i=i: e.memset(WK[i][:, 2052:2054], 0.0), w=[("WKz", i), ("WK", i)])

        wslot = [0]

        def load_w(col0, ncols):
            s = wslot[0] % NW
            wslot[0] += 1
            P.dma("pool", wB[s][:, :, 0:ncols],
                  w_in[:, col0:col0 + ncols].rearrange("(kc p) c -> p kc c", p=128), w=[("wB", s)])
            return s

        ppslot = [0]

        def next_pp():
            s = ppslot[0] % 4
            ppslot[0] += 1
            return s

        wkslot = [0]

        def next_wk():
            s = wkslot[0] % NWK
            wkslot[0] += 1
            return s

        obslot = [0]

        def next_ob():
            s = obslot[0] % 2
            obslot[0] += 1
            return s

        def fm_mm(ws, wc, tq):
            s = next_pp()
            for kc in range(8):
                P.pe(lambda e, s=s, ws=ws, wc=wc, kc=kc, tq=tq: e.matmul(
                    pp[s][:, :], wB[ws][:, kc, wc * 128:(wc + 1) * 128], hT[:, kc, tq * 512:(tq + 1) * 512],
                    start=(kc == 0), stop=(kc == 7)),
                    r=[("wB", ws), "hT"], w=[("pp", s)])
            return s

        def tm_mm(ws, ncols, tt):
            s = next_pp()
            for kc in range(8):
                P.pe(lambda e, s=s, ws=ws, kc=kc, tt=tt, ncols=ncols: e.matmul(
                    pp[s][:, 0:ncols], hT[:, kc, tt * 128:(tt + 1) * 128], wB[ws][:, kc, 0:ncols],
                    start=(kc == 0), stop=(kc == 7)),
                    r=[("wB", ws), "hT"], w=[("pp", s)])
            return s

        def load_x(stile_, tt_):
            r0_ = stile_ * 2048 + tt_ * 128
            P.dma("sp", xt[tt_ % NX][:], x[r0_:r0_ + 128, :], w=[("xt", tt_ % NX)])

        for stile in range(4):
            own = stile < 2
            t0 = stile * 2048
            last = stile == 3
            if stile == 0:
                for tt in range(NX):
                    load_x(0, tt)
            def stage_a(tt):
                sl = tt % NX
                pl = tt % 2
                P.act(lambda e, sl=sl: e.activation(out=junk[:], in_=xt[sl][:], func=AF.Square, accum_out=ss[sl][:]),
                      r=[("xt", sl)], w=[("ss", sl)])
                P.act(lambda e, sl=sl: e.activation(out=ss[sl][:], in_=ss[sl][:], func=AF.Sqrt,
                                                    scale=1.0 / D, bias=NORM_EPS),
                      r=[("ss", sl)], w=[("ss", sl)])
                P.dve(lambda e, sl=sl: e.reciprocal(out=rstd[sl][:], in_=ss[sl][:]), r=[("ss", sl)], w=[("rstd", sl)])
                P.dve(lambda e, sl=sl: e.scalar_tensor_tensor(out=xn[sl][:], in0=xt[sl][:], scalar=rstd[sl][:],
                                                              in1=grep[:], op0=ALU.mult, op1=ALU.mult),
                      r=[("xt", sl), ("rstd", sl), "grep"], w=[("xn", sl)])
                for kc in range(8):
                    P.pe(lambda e, sl=sl, kc=kc, pl=pl: e.transpose(out=ptr[pl][:, kc * 128:(kc + 1) * 128],
                                                                    in_=xn[sl][:, kc * 128:(kc + 1) * 128],
                                                                    identity=identb[:]),
                         r=[("xn", sl), "identb"], w=[("ptr", pl)])
                if tt + NX < 16:
                    load_x(stile, tt + NX)

            def stage_b(tt):
                pl = tt % 2
                if tt % 2 == 0:
                    P.act(lambda e, pl=pl, tt=tt: e.activation(
                        out=hT[:, :, tt * 128:(tt + 1) * 128],
                        in_=ptr[pl][:, :].rearrange("p (k t) -> p k t", k=8), func=AF.Copy),
                        r=[("ptr", pl)], w=["hT"])
                else:
                    P.dve(lambda e, pl=pl, tt=tt: e.tensor_copy(
                        out=hT[:, :, tt * 128:(tt + 1) * 128],
                        in_=ptr[pl][:, :].rearrange("p (k t) -> p k t", k=8)),
                        r=[("ptr", pl)], w=["hT"])

            stage_a(0)
            for tt in range(16):
                if tt + 1 < 16:
                    stage_a(tt + 1)
                stage_b(tt)
            if stile < 3:
                for tt in range(NX):
                    load_x(stile + 1, tt)

            P.dma("sp", posi[:], posr[:, t0:t0 + 2048], w=["posi"])
            build_rope_tables(P, posi, invf, cosT, sinT, WK, next_wk)

            for tt in range(16):
                s = next_pp()
                for kc in range(8):
                    P.pe(lambda e, s=s, kc=kc, tt=tt: e.matmul(
                        pp[s][:, 0:16], hT[:, kc, tt * 128:(tt + 1) * 128], wG[:, kc, :],
                        start=(kc == 0), stop=(kc == 7)), r=["wG", "hT"], w=[("pp", s)])
                gt = stile * 16 + tt
                P.dve(lambda e, s=s, gt=gt: e.tensor_tensor(out=gall[:, gt * 16:(gt + 1) * 16], in0=pp[s][:, 0:16],
                                                            in1=gbrep[:], op=ALU.add),
                      r=[("pp", s), "gbrep"], w=["gall"])

            def gen_tm_v():
                for (c0, dst) in ((C_AV, T["mv"]), (C_BV, T["dv"])):
                    for cg in range(2):
                        ws = load_w(c0 + cg * 512, 512)
                        ob = obt_slot[0] % 2
                        obt_slot[0] += 1
                        for tt in range(16):
                            s = tm_mm(ws, 512, tt)
                            if True:
                                P.act(lambda e, s=s, ob=ob, tt=tt: e.activation(
                                    out=OBT[ob][:, tt * 512:(tt + 1) * 512], in_=pp[s][:, :], func=AF.Copy),
                                    r=[("pp", s)], w=[("OBT", ob)])
                            else:
                                P.dve(lambda e, s=s, ob=ob, tt=tt: e.tensor_copy(
                                    out=OBT[ob][:, tt * 512:(tt + 1) * 512], in_=pp[s][:, :]),
                                    r=[("pp", s)], w=[("OBT", ob)])
                            if tt == 15:
                                P.dma("sp", dst[t0:t0 + 2048, cg * 512:(cg + 1) * 512].rearrange("(tt p) c -> p tt c", p=128),
                                      OBT[ob][:, :].rearrange("p (tt c) -> p tt c", c=512), r=[("OBT", ob)])
                            yield

            def gen_conv():
                prev2 = [None]
                for fam, (c0, dst, ksc) in enumerate(((C_AQ, T["mqT"], 1.0), (C_AK, T["mkT"], 1.0 / 16.0))):
                    for g4 in range(2):
                        ws = load_w(c0 + g4 * 512, 512)
                        for wc in range(4):
                            fc = fam * 8 + g4 * 4 + wc
                            row0 = (g4 * 4 + wc) * 128
                            stg = next_wk()
                            acc = next_wk()
                            sg = next_wk()
                            P.dve(lambda e, stg=stg, fc=fc: e.tensor_copy(out=WK[stg][:, 0:4], in_=carry[:, fc, :]),
                                  r=["carry"], w=[("WK", stg)])
                            for tq in range(4):
                                s = fm_mm(ws, wc, tq)
                                P.act(lambda e, s=s, stg=stg, tq=tq: e.activation(
                                    out=WK[stg][:, 4 + tq * 512:4 + (tq + 1) * 512], in_=pp[s][:, :], func=AF.Copy),
                                    r=[("pp", s)], w=[("WK", stg)])
                            P.dve(lambda e, stg=stg, fc=fc: e.tensor_copy(out=carry[:, fc, :], in_=WK[stg][:, 2048:2052]),
                                  r=[("WK", stg)], w=["carry"])
                            nout = 2050 if last else 2048
                            P.act(lambda e, stg=stg, acc=acc, fc=fc, nout=nout: e.activation(
                                out=WK[acc][:, 0:nout], in_=WK[stg][:, 0:nout], func=AF.Copy, scale=cw[:, fc, 0:1]),
                                r=[("WK", stg), "cw", ("WKz", stg)], w=[("WK", acc)])
                            for j in range(1, 5):
                                P.dve(lambda e, stg=stg, acc=acc, fc=fc, nout=nout, j=j: e.scalar_tensor_tensor(
                                    out=WK[acc][:, 0:nout], in0=WK[stg][:, j:j + nout], scalar=cw[:, fc, j:j + 1],
                                    in1=WK[acc][:, 0:nout], op0=ALU.mult, op1=ALU.add),
                                    r=[("WK", stg), "cw", ("WK", acc)], w=[("WK", acc)])

                            def part2(acc=acc, sg=sg, nout=nout, ksc=ksc, dst=dst, row0=row0):
                                ob = next_ob()
                                P.act(lambda e: e.activation(
                                    out=WK[sg][:, 0:nout], in_=WK[acc][:, 0:nout], func=AF.Sigmoid),
                                    r=[("WK", acc)], w=[("WK", sg)])
                                P.dve(lambda e: e.scalar_tensor_tensor(
                                    out=OB[ob][:, 0:nout], in0=WK[acc][:, 0:nout], scalar=ksc, in1=WK[sg][:, 0:nout],
                                    op0=ALU.mult, op1=ALU.mult), r=[("WK", acc), ("WK", sg)], w=[("OB", ob)])
                                if stile == 0:
                                    P.dma("sp", dst[row0:row0 + 128, 0:nout - 2], OB[ob][:, 2:nout], r=[("OB", ob)])
                                else:
                                    P.dma("sp", dst[row0:row0 + 128, t0 - 2:t0 - 2 + nout], OB[ob][:, 0:nout], r=[("OB", ob)])

                            if prev2[0] is not None:
                                prev2[0]()
                            prev2[0] = part2
                            yield
                if prev2[0] is not None:
                    prev2[0]()

            g_tm = gen_tm_v()
            for _ in gen_conv():
                for _k in range(4):
                    next(g_tm, None)
            for _ in g_tm:
                pass

            fams = [(C_BK, T["dkT"], 1.0, t0)]
            if own:
                fams.append((C_BQ, T["dqT"], 0.125, t0))
            pend = [None]
            for (c0, dst, sc, tcol) in fams:
                for g4 in range(2):
                    ws = load_w(c0 + g4 * 512, 512)
                    for wc in range(4):
                        row0 = (g4 * 4 + wc) * 128
                        ob = next_ob()
                        for tq in range(4):
                            s = fm_mm(ws, wc, tq)
                            wk = next_wk()
                            P.act(lambda e, s=s, wk=wk: e.activation(out=WK[wk][:, 0:512], in_=pp[s][:, :], func=AF.Copy),
                                  r=[("pp", s)], w=[("WK", wk)])

                            def tail(wk=wk, tq=tq, ob=ob, sc=sc, dst=dst, row0=row0, tcol=tcol):
                                pr = tq % 2
                                cs = slice(tq * 512, (tq + 1) * 512)
                                P.pe(lambda e: e.matmul(prot[pr][:, :], rswap[:, :], WK[wk][:, 0:512], start=True, stop=True),
                                     r=[("WK", wk), "rswap"], w=[("prot", pr)])
                                P.dve(lambda e: e.scalar_tensor_tensor(
                                    out=WK[wk][:, 512:1024], in0=WK[wk][:, 0:512], scalar=sc, in1=cosT[:, cs],
                                    op0=ALU.mult, op1=ALU.mult), r=[("WK", wk), "cosT"], w=[("WK", wk)])
                                P.dve(lambda e: e.scalar_tensor_tensor(
                                    out=WK[wk][:, 1024:1536], in0=prot[pr][:, :], scalar=sc, in1=sinT[:, cs],
                                    op0=ALU.mult, op1=ALU.mult), r=[("prot", pr), "sinT"], w=[("WK", wk)])
                                P.dve(lambda e: e.tensor_tensor(
                                    out=OB[ob][:, cs], in0=WK[wk][:, 512:1024], in1=WK[wk][:, 1024:1536], op=ALU.add),
                                    r=[("WK", wk)], w=[("OB", ob)])
                                if tq == 3:
                                    P.dma("sp", dst[row0:row0 + 128, tcol:tcol + 2048], OB[ob][:, 0:2048], r=[("OB", ob)])

                            if pend[0] is not None:
                                pend[0]()
                            pend[0] = tail
            if pend[0] is not None:
                pend[0]()

            if own:
                for gi, c0 in enumerate((C_GA, C_GB)):
                    for g4 in range(2):
                        ws = load_w(c0 + g4 * 512, 512)
                        for wc in range(4):
                            cc = g4 * 4 + wc
                            ob = next_ob()
                            for tq in range(4):
                                s = fm_mm(ws, wc, tq)
                                P.act(lambda e, s=s, ob=ob, tq=tq, gi=gi, cc=cc: e.activation(
                                    out=OB[ob][:, tq * 512:(tq + 1) * 512], in_=pp[s][:, :], func=AF.Sigmoid,
                                    bias=gbfm[:, gi, cc:cc + 1]), r=[("pp", s), "gbfm"], w=[("OB", ob)])
                            P.dma("sp", T["sgT"][gi, cc * 128:(cc + 1) * 128, t0:t0 + 2048], OB[ob][:, 0:2048],
                                  r=[("OB", ob)])
                for cg in range(2):
                    wso = load_w(C_AO + cg * 512, 512)
                    wsz = load_w(C_AZ + cg * 512, 512)
                    ob = obt_slot[0] % 2
                    obt_slot[0] += 1
                    for tt in range(16):
                        so = tm_mm(wso, 512, tt)
                        sz = tm_mm(wsz, 512, tt)
                        wk = next_wk()
                        P.act(lambda e, so=so, wk=wk: e.activation(out=WK[wk][:, 0:512], in_=pp[so][:, :], func=AF.Sigmoid),
                              r=[("pp", so)], w=[("WK", wk)])
                        P.act(lambda e, sz=sz, wk=wk: e.activation(out=WK[wk][:, 512:1024], in_=pp[sz][:, :], func=AF.Sigmoid),
                              r=[("pp", sz)], w=[("WK", wk)])
                        P.dve(lambda e, sz=sz, wk=wk: e.tensor_tensor(out=WK[wk][:, 512:1024], in0=pp[sz][:, :],
                                                                      in1=WK[wk][:, 512:1024], op=ALU.mult),
                              r=[("pp", sz), ("WK", wk)], w=[("WK", wk)])
                        P.dve(lambda e, wk=wk, ob=ob, tt=tt: e.tensor_tensor(
                            out=OBT[ob][:, tt * 512:(tt + 1) * 512], in0=WK[wk][:, 0:512], in1=WK[wk][:, 512:1024],
                            op=ALU.mult), r=[("WK", wk)], w=[("OBT", ob)])
                    P.dma("sp", T["gA"][t0:t0 + 2048, cg * 512:(cg + 1) * 512].rearrange("(tt p) c -> p tt c", p=128),
                          OBT[ob][:, :].rearrange("p (tt c) -> p tt c", c=512), r=[("OBT", ob)])
                for cg in range(2):
                    wsz = load_w(C_BZ + cg * 512, 512)
                    ob = obt_slot[0] % 2
                    obt_slot[0] += 1
                    for tt in range(16):
                        sz = tm_mm(wsz, 512, tt)
                        wk = next_wk()
                        P.act(lambda e, sz=sz, wk=wk: e.activation(out=WK[wk][:, 0:512], in_=pp[sz][:, :], func=AF.Sigmoid),
                              r=[("pp", sz)], w=[("WK", wk)])
                        P.dve(lambda e, sz=sz, wk=wk, ob=ob, tt=tt: e.tensor_tensor(
                            out=OBT[ob][:, tt * 512:(tt + 1) * 512], in0=pp[sz][:, :], in1=WK[wk][:, 0:512],
                            op=ALU.mult), r=[("pp", sz), ("WK", wk)], w=[("OBT", ob)])
                    P.dma("sp", T["gB"][t0:t0 + 2048, cg * 512:(cg + 1) * 512].rearrange("(tt p) c -> p tt c", p=128),
                          OBT[ob][:, :].rearrange("p (tt c) -> p tt c", c=512), r=[("OBT", ob)])

        P.dma("sp", T["gall_d"][:, :], gall[:, :], r=["gall"])
        P.flush()


def build_rope_tables(P, posi, invf, cosT, sinT, WK, next_wk):
    TWO_PI = 2.0 * math.pi
    C1 = 6.28125
    C2 = TWO_PI - C1
    a = next_wk()
    b = next_wk()
    c = next_wk()
    A, Bk, R = WK[a], WK[b], WK[c]
    N = 2048
    P.dve(lambda e: e.tensor_copy(out=A[:, 0:N], in_=posi[:, :]), r=["posi"], w=[("WK", a)])
    P.dve(lambda e: e.tensor_scalar(out=A[:, 0:N], in0=A[:, 0:N], scalar1=invf[:, 0:1], scalar2=None, op0=ALU.mult),
          r=[("WK", a), "invf"], w=[("WK", a)])
    ki = posi
    P.dve(lambda e: e.tensor_scalar(out=Bk[:, 0:N], in0=A[:, 0:N], scalar1=1.0 / TWO_PI, scalar2=None, op0=ALU.mult),
          r=[("WK", a)], w=[("WK", b)])
    P.dve(lambda e: e.tensor_copy(out=ki[:, :], in_=Bk[:, 0:N]), r=[("WK", b)], w=["posi"])
    P.dve(lambda e: e.tensor_copy(out=Bk[:, 0:N], in_=ki[:, :]), r=["posi"], w=[("WK", b)])
    P.dve(lambda e: e.scalar_tensor_tensor(out=R[:, 0:N], in0=Bk[:, 0:N], scalar=-C1, in1=A[:, 0:N],
                                           op0=ALU.mult, op1=ALU.add), r=[("WK", a), ("WK", b)], w=[("WK", c)])
    P.dve(lambda e: e.scalar_tensor_tensor(out=R[:, 0:N], in0=Bk[:, 0:N], scalar=-C2, in1=R[:, 0:N],
                                           op0=ALU.mult, op1=ALU.add), r=[("WK", b), ("WK", c)], w=[("WK", c)])

    def wrap(X, key):
        P.dve(lambda e: e.tensor_scalar(out=Bk[:, 0:N], in0=X[:, 0:N], scalar1=math.pi, scalar2=-TWO_PI,
                                        op0=ALU.is_gt, op1=ALU.mult), r=[key], w=[("WK", b)])
        P.dve(lambda e: e.tensor_tensor(out=X[:, 0:N], in0=X[:, 0:N], in1=Bk[:, 0:N], op=ALU.add),
              r=[key, ("WK", b)], w=[key])
        P.dve(lambda e: e.tensor_scalar(out=Bk[:, 0:N], in0=X[:, 0:N], scalar1=-math.pi, scalar2=TWO_PI,
                                        op0=ALU.is_lt, op1=ALU.mult), r=[key], w=[("WK", b)])
        P.dve(lambda e: e.tensor_tensor(out=X[:, 0:N], in0=X[:, 0:N], in1=Bk[:, 0:N], op=ALU.add),
              r=[key, ("WK", b)], w=[key])

    wrap(R, ("WK", c))
    P.act(lambda e: e.activation(out=sinT[:, :], in_=R[:, 0:N], func=AF.Sin), r=[("WK", c)], w=["sinT"])
    P.dve(lambda e: e.tensor_scalar(out=R[:, 0:N], in0=R[:, 0:N], scalar1=math.pi / 2, scalar2=None, op0=ALU.add),
          r=[("WK", c)], w=[("WK", c)])
    wrap(R, ("WK", c))
    P.act(lambda e: e.activation(out=cosT[:, :], in_=R[:, 0:N], func=AF.Sin), r=[("WK", c)], w=["cosT"])


def phase_M(nc, P, T):
    mqT, mkT, mv, gA, yaT = T["mqT"], T["mkT"], T["mv"], T["gA"], T["yaT"]
    with contextlib.ExitStack() as st:
        sb = lambda name, shape, dt=F32: st.enter_context(nc.sbuf_tensor("M_" + name, list(shape), dt))
        ps = lambda name, shape, dt=F32: st.enter_context(nc.psum_tensor("M_" + name, list(shape), dt))
        qT = sb("qT", [128, 2, OWN], BF16)
        kT = sb("kT", [128, 2, S], BF16)
        va = sb("va", [128, 64, 257], BF16)
        ktok = sb("ktok", [128, 64, 256], BF16)
        hacc = sb("hacc", [128, 32, 256])
        gat = [sb("gat%d" % i, [128, 256], BF16) for i in range(2)]
        gall = sb("gall", [128, 1024])
        mlng = sb("mlng", [128, D])
        tri = sb("tri", [128, 4, 128])
        identf = sb("identf", [128, 128])
        identb = sb("identb", [128, 128], BF16)
        onesf = sb("onesf", [128, 128])
        LFt = [sb("LFt%d" % d, [128, 256]) for d in range(2)]
        Bc = [sb("Bc%d" % d, [128, 256]) for d in range(2)]
        Aa = [sb("Aa%d" % d, [128, 256]) for d in range(2)]
        EB = [sb("EB%d" % d, [128, 256]) for d in range(2)]
        WST = [sb("WST%d" % d, [128, 256]) for d in range(2)]
        DEC = [sb("DEC%d" % d, [128, 256]) for d in range(2)]
        Cst = [sb("Cst%d" % d, [128, 2, 257]) for d in range(2)]
        Cb = [[sb("Cb%d_%d" % (d, v), [128, 2, 257], BF16) for v in range(2)] for d in range(2)]
        cbver = [0, 0]
        dg = [sb("dg%d" % i, [128, 128]) for i in range(2)]
        Dm = [sb("Dm%d" % i, [128, 128]) for i in range(2)]
        Wm = [sb("Wm%d" % i, [128, 128], BF16) for i in range(2)]
        vw = [sb("vw%d" % i, [128, 257], BF16) for i in range(2)]
        tmpc = [sb("tmpc%d" % i, [128, 257]) for i in range(2)]
        tot = [sb("tot%d" % i, [128, 257]) for i in range(2)]
        sm = [sb("sm%d" % i, [128, 4]) for i in range(2)]
        hs = [sb("hs%d" % i, [128, 256]) for i in range(2)]
        sqv = [sb("sqv%d" % i, [128, 256]) for i in range(2)]
        yab = [sb("yab%d" % i, [128, 256], BF16) for i in range(2)]
        yas = [sb("yas%d" % i, [128, 2, 128], BF16) for i in range(2)]
        pD = ps("pD", [128, 512])
        pST = ps("pST", [128, 512])
        pIs = [ps("pI%d" % i, [128, 512]) for i in range(2)]
        pI = pIs[0]
        pC = ps("pC", [128, 512])
        pU = [ps("pU%d" % i, [128, 512]) for i in range(2)]
        ptrk = ps("ptrk", [128, 1024], BF16)
        ptry = ptrk

        P.dma("sp", gall[:], T["gall_d"], w=["gall"])
        P.dma("sp", mlng[:], T["mlng_rep"], w=["mlng"])
        P.dma("sp", tri[:], T["c_tri"], w=["tri"])
        P.dma("sp", identf[:], T["c_ident"], w=["identf"])
        P.dve(lambda e: e.tensor_copy(out=identb[:], in_=identf[:]), r=["identf"], w=["identb"])
        P.dve(lambda e: e.memset(onesf[:], 1.0), w=["onesf"])
        P.dve(lambda e: e.memset(va[:, :, 256:257], 1.0), w=["va1"])

        g4 = gall[:, :].rearrange("p (c g h) -> p c g h", g=4, h=4)
        v3 = lambda t: t[:, :].rearrange("p (c h) -> p c h", h=4)
        for d in range(2):
            i_d = g4[:, :, 2 * d, :]
            f_d = g4[:, :, 2 * d + 1, :]
            P.act(lambda e, d=d, f_d=f_d: e.activation(out=v3(LFt[d]), in_=f_d, func=AF.Exp, scale=-1.0),
                  r=["gall"], w=[("LFt", d)])
            P.act(lambda e, d=d: e.activation(out=LFt[d][:, :], in_=LFt[d][:, :], func=AF.Ln, bias=1.0),
                  r=[("LFt", d)], w=[("LFt", d)])
            P.pe(lambda e, d=d: e.matmul(pI[:, 0:256], tri[:, d, :], LFt[d][:, :], start=True, stop=True),
                 r=["tri", ("LFt", d)], w=[("pI", 0)])
            P.pe(lambda e, d=d: e.matmul(pC[:, 0:256], onesf[:, :], LFt[d][:, :], start=True, stop=True),
                 r=["onesf", ("LFt", d)], w=["pC"])
            P.dve(lambda e, d=d: e.tensor_scalar(out=Bc[d][:, :], in0=pI[:, 0:256], scalar1=-1.0, scalar2=None, op0=ALU.mult),
                  r=[("pI", 0)], w=[("Bc", d)])
            P.dve(lambda e, d=d, i_d=i_d: e.tensor_tensor(out=v3(Aa[d]), in0=pI[:, 0:256].rearrange("p (c h) -> p c h", h=4),
                                                         in1=i_d, op=ALU.add),
                  r=[("pI", 0), "gall"], w=[("Aa", d)])
            P.act(lambda e, d=d: e.activation(out=EB[d][:, :], in_=Bc[d][:, :], func=AF.Exp),
                  r=[("Bc", d)], w=[("EB", d)])
            P.dve(lambda e, d=d: e.tensor_copy(out=DEC[d][:, :], in_=pC[:, 0:256]), r=["pC"], w=[("DEC", d)])
            P.dve(lambda e, d=d: e.tensor_tensor(out=WST[d][:, :], in0=Aa[d][:, :], in1=DEC[d][:, :], op=ALU.subtract),
                  r=[("Aa", d), ("DEC", d)], w=[("WST", d)])
            P.act(lambda e, d=d: e.activation(out=WST[d][:, :], in_=WST[d][:, :], func=AF.Exp),
                  r=[("WST", d)], w=[("WST", d)])
            P.act(lambda e, d=d: e.activation(out=DEC[d][:, :], in_=DEC[d][:, :], func=AF.Exp, scale=-1.0),
                  r=[("DEC", d), ("WST", d)], w=[("DEC", d)])

        cnt = {"o": 0, "u": 0, "g": 0, "y": 0}

        def output_A1(h, d, c):
            i = cnt["o"] % 2
            cnt["o"] += 1
            col = c * 4 + h
            cs = slice(c * 128, (c + 1) * 128)
            pIx = pIs[i]
            P.dve(lambda e: e.tensor_scalar(out=dg[i][:], in0=identf[:], scalar1=Bc[d][:, col:col + 1], scalar2=None, op0=ALU.mult),
                  r=["identf", ("Bc", d)], w=[("dg", i)])
            P.pe(lambda e: e.matmul(pD[:, 0:128], onesf[:, :], dg[i][:, :], start=True, stop=False), r=["onesf", ("dg", i)], w=["pD"])
            P.pe(lambda e: e.matmul(pD[:, 0:128], identf[:, :], tri[:, 2 + d, :], start=False, stop=True), r=["identf", "tri"], w=["pD"])
            P.act(lambda e: e.activation(out=Dm[i][:], in_=pD[:, 0:128], func=AF.Exp, bias=Aa[d][:, col:col + 1]),
                  r=["pD", ("Aa", d)], w=[("Dm", i)])
            for dkc in range(2):
                P.pe(lambda e, dkc=dkc: e.matmul(pST[:, 0:128], kT[:, dkc, cs], qT[:, dkc, cs], start=(dkc == 0), stop=(dkc == 1)),
                     r=["kT", "qT"], w=["pST"])
            P.dve(lambda e: e.tensor_tensor(out=Wm[i][:], in0=pST[:, 0:128], in1=Dm[i][:], op=ALU.mult),
                  r=["pST", ("Dm", i)], w=[("Wm", i)])
            P.pe(lambda e: e.matmul(pIx[:, 0:257], Wm[i][:, :], va[:, c, :], start=True, stop=True),
                 r=[("Wm", i), "va", "va1"], w=[("pI", i)])
            return (h, d, c, i)

        def output_A2(ctx):
            h, d, c, i = ctx
            col = c * 4 + h
            cs = slice(c * 128, (c + 1) * 128)
            pIx = pIs[i]
            cbv = Cb[d][cbver[d]]
            cbk = ("Cb", d, cbver[d])
            for dkc in range(2):
                P.pe(lambda e, dkc=dkc: e.matmul(pC[:, 0:257], qT[:, dkc, cs], cbv[:, dkc, :], start=(dkc == 0), stop=(dkc == 1)),
                     r=["qT", cbk], w=["pC"])
            P.act(lambda e: e.activation(out=tmpc[i][:], in_=pC[:, 0:257], func=AF.Copy, scale=EB[d][:, col:col + 1]),
                  r=["pC", ("EB", d)], w=[("tmpc", i)])
            P.dve(lambda e: e.tensor_tensor(out=tot[i][:], in0=tmpc[i][:], in1=pIx[:, 0:257], op=ALU.add),
                  r=[("tmpc", i), ("pI", i)], w=[("tot", i)])

        def output_B(ctx):
            h, d, c, i = ctx
            col = c * 4 + h
            P.dve(lambda e: e.tensor_scalar(out=sm[i][:, 3:4], in0=tot[i][:, 256:257], scalar1=-1.0, scalar2=1.0, op0=ALU.mult, op1=ALU.max),
                  r=[("tot", i)], w=[("sm", i)])
            P.dve(lambda e: e.tensor_tensor(out=sm[i][:, 0:1], in0=tot[i][:, 256:257], in1=sm[i][:, 3:4], op=ALU.max),
                  r=[("tot", i), ("sm", i)], w=[("sm", i)])
            P.dve(lambda e: e.reciprocal(out=sm[i][:, 1:2], in_=sm[i][:, 0:1]), r=[("sm", i)], w=[("sm", i)])
            if d == 0:
                P.dve(lambda e: e.tensor_scalar(out=hacc[:, c, :], in0=tot[i][:, 0:256], scalar1=sm[i][:, 1:2], scalar2=None, op0=ALU.mult),
                      r=[("tot", i), ("sm", i)], w=[("hacc", c)])
                return
            gi = cnt["g"] % 2
            cnt["g"] += 1
            P.dma("sp", gat[gi][:], gA[c * 128:(c + 1) * 128, h * 256:(h + 1) * 256], w=[("gat", gi)])
            P.dve(lambda e: e.scalar_tensor_tensor(out=hs[i][:], in0=tot[i][:, 0:256], scalar=sm[i][:, 1:2], in1=hacc[:, c, :],
                                                   op0=ALU.mult, op1=ALU.add),
                  r=[("tot", i), ("sm", i), ("hacc", c)], w=[("hs", i)])
            P.dve(lambda e: e.tensor_tensor(out=sqv[i][:], in0=hs[i][:], in1=hs[i][:], op=ALU.mult), r=[("hs", i)], w=[("sqv", i)])
            P.dve(lambda e: e.reduce_sum(out=sm[i][:, 2:3], in_=sqv[i][:], axis=AX.X), r=[("sqv", i)], w=[("smb", i)])
            P.act(lambda e: e.activation(out=sm[i][:, 2:3], in_=sm[i][:, 2:3], func=AF.Ln, scale=1.0 / 256.0, bias=NORM_EPS),
                  r=[("smb", i)], w=[("smb", i)])
            P.act(lambda e: e.activation(out=sm[i][:, 2:3], in_=sm[i][:, 2:3], func=AF.Exp, scale=-0.5),
                  r=[("smb", i)], w=[("smb", i)])
            P.dve(lambda e: e.scalar_tensor_tensor(out=sqv[i][:], in0=hs[i][:], scalar=sm[i][:, 2:3],
                                                   in1=mlng[:, h * 256:(h + 1) * 256], op0=ALU.mult, op1=ALU.mult),
                  r=[("hs", i), ("smb", i), "mlng"], w=[("sqv", i)])
            P.dve(lambda e: e.tensor_tensor(out=yab[i][:], in0=sqv[i][:], in1=gat[gi][:], op=ALU.mult),
                  r=[("sqv", i), ("gat", gi)], w=[("yab", i)])
            def b2():
                for k in range(2):
                    P.pe(lambda e, k=k: e.transpose(out=ptry[:, k * 128:(k + 1) * 128], in_=yab[i][:, k * 128:(k + 1) * 128],
                                                    identity=identb[:]), r=[("yab", i), "identb"], w=["ptrk"])
                P.act(lambda e: e.activation(out=yas[i][:, :, :], in_=ptry[:, 0:256].rearrange("p (k t) -> p k t", k=2), func=AF.Copy),
                      r=["ptrk"], w=[("yas", i)])
                P.dma("sp", yaT[h * 256:(h + 1) * 256, c * 128:(c + 1) * 128].rearrange("(k p) t -> p k t", p=128),
                      yas[i][:, :, :], r=[("yas", i)])
            return b2

        def update_step(h, d, c):
            i = cnt["u"] % 2
            cnt["u"] += 1
            col = c * 4 + h
            nv = 1 - cbver[d]
            nb_ = Cb[d][nv]
            nbk = ("Cb", d, nv)
            P.act(lambda e: e.activation(out=vw[i][:], in_=va[:, c, :], func=AF.Copy, scale=WST[d][:, col:col + 1]),
                  r=["va", "va1", ("WST", d)], w=[("vw", i)])
            for dkc in range(2):
                P.pe(lambda e, dkc=dkc: e.matmul(pU[dkc][:, 0:257], ktok[:, c, dkc * 128:(dkc + 1) * 128], vw[i][:, :], start=True, stop=True),
                     r=[("ktok", c), ("vw", i)], w=[("pU", dkc)])
                P.dve(lambda e, dkc=dkc: e.scalar_tensor_tensor(out=Cst[d][:, dkc, :], in0=Cst[d][:, dkc, :], scalar=DEC[d][:, col:col + 1],
                                                                in1=pU[dkc][:, 0:257], op0=ALU.mult, op1=ALU.add),
                      r=[("Cst", d, dkc), ("DEC", d), ("pU", dkc)], w=[("Cst", d, dkc)])
                P.act(lambda e, dkc=dkc, nb_=nb_: e.activation(out=nb_[:, dkc, :], in_=Cst[d][:, dkc, :], func=AF.Copy),
                      r=[("Cst", d, dkc)], w=[nbk])
            cbver[d] = nv

        for h in range(4):
            for dkc in range(2):
                r0 = h * 256 + dkc * 128
                P.dma("sp", qT[:, dkc, :], mqT[r0:r0 + 128, 0:OWN], w=["qT"])
                P.dma("sp", kT[:, dkc, :], mkT[r0:r0 + 128, :], w=["kT"])
            P.dma("sp", va[:, :, 0:256], mv[:, h * 256:(h + 1) * 256].rearrange("(c p) f -> p c f", p=128), w=["va"])
            for d in range(2):
                P.dve(lambda e, d=d: e.memset(Cst[d][:], 0.0), w=[("Cst", d, 0), ("Cst", d, 1)])
                P.dve(lambda e, d=d: e.memset(Cb[d][0][:], 0.0), w=[("Cb", d, 0)])
                cbver[d] = 0
            for c in range(64):
                for dkc in range(2):
                    P.pe(lambda e, c=c, dkc=dkc: e.transpose(out=ptrk[:, dkc * 128:(dkc + 1) * 128],
                                                             in_=kT[:, dkc, c * 128:(c + 1) * 128], identity=identb[:]),
                         r=["kT", "identb"], w=["ptrk"])
                if c % 2 == 0:
                    P.act(lambda e, c=c: e.activation(out=ktok[:, c, :], in_=ptrk[:, 0:256], func=AF.Copy), r=["ptrk"], w=[("ktok", c)])
                else:
                    P.dve(lambda e, c=c: e.tensor_copy(out=ktok[:, c, :], in_=ptrk[:, 0:256]), r=["ptrk"], w=[("ktok", c)])
            ctx = output_A1(h, 0, 0)
            for i in range(32):
                output_A2(ctx)
                if i < 31:
                    update_step(h, 0, i)
                update_step(h, 1, 63 - i)
                nctx = output_A1(h, 0, i + 1) if i < 31 else output_A1(h, 1, 31)
                output_B(ctx)
                ctx = nctx
            pb2 = None
            for i in range(32, 64):
                c = 63 - i
                output_A2(ctx)
                if c > 0:
                    update_step(h, 1, c)
                    nctx = output_A1(h, 1, c - 1)
                if pb2 is not None:
                    pb2()
                pb2 = output_B(ctx)
                ctx = nctx
            pb2()
        P.flush()


def phase_M2(nc, P, T):
    mqT, mkT, mv, gA, yaT, hfwd = T["mqT"], T["mkT"], T["mv"], T["gA"], T["yaT"], T["hfwd"]
    with contextlib.ExitStack() as st:
        sb = lambda name, shape, dt=F32: st.enter_context(nc.sbuf_tensor("M2_" + name, list(shape), dt))
        ps = lambda name, shape, dt=F32: st.enter_context(nc.psum_tensor("M2_" + name, list(shape), dt))
        gall = sb("gall", [128, 1024])
        mlng = sb("mlng", [128, D])
        tri = sb("tri", [128, 4, 128])
        identf = sb("identf", [128, 128])
        identb = sb("identb", [128, 128], BF16)
        onesf = sb("onesf", [128, 128])
        LFt = [sb("LFt%d" % d, [128, 256]) for d in range(2)]
        Bc = [sb("Bc%d" % d, [128, 256]) for d in range(2)]
        Aa = [sb("Aa%d" % d, [128, 256]) for d in range(2)]
        EB = [sb("EB%d" % d, [128, 256]) for d in range(2)]
        WST = [sb("WST%d" % d, [128, 256]) for d in range(2)]
        DEC = [sb("DEC%d" % d, [128, 256]) for d in range(2)]
        Cst = [sb("Cst%d" % d, [128, 4, 2, 257]) for d in range(2)]
        Cb = [[sb("Cb%d_%d" % (d, v), [128, 4, 2, 257], BF16) for v in range(2)] for d in range(2)]
        cbver = [0, 0]
        NQ, NK = 2, 4
        qc = [sb("qc%d" % i, [128, 8, 128], BF16) for i in range(NQ)]
        kc = [sb("kc%d" % i, [128, 8, 128], BF16) for i in range(NK)]
        vc = [sb("vc%d" % i, [128, 4, 257], BF16) for i in range(NK)]
        ktc = [sb("ktc%d" % i, [128, 1024], BF16) for i in range(NK)]
        gac = [sb("gac%d" % i, [128, 1024], BF16) for i in range(2)]
        hfr = [sb("hfr%d" % i, [128, 1024]) for i in range(2)]
        hfw = [sb("hfw%d" % i, [128, 1024]) for i in range(2)]
        yasc = [sb("yasc%d" % i, [128, 8, 128], BF16) for i in range(2)]
        NWB = 4
        dg = [sb("dg%d" % i, [128, 128]) for i in range(NWB)]
        Dm = [sb("Dm%d" % i, [128, 128]) for i in range(NWB)]
        Wm = [sb("Wm%d" % i, [128, 128], BF16) for i in range(NWB)]
        vw = [sb("vw%d" % i, [128, 257], BF16) for i in range(NWB)]
        tmpc = [sb("tmpc%d" % i, [128, 257]) for i in range(NWB)]
        tot = [sb("tot%d" % i, [128, 257]) for i in range(NWB)]
        sm = [sb("sm%d" % i, [128, 4]) for i in range(NWB)]
        hs = [sb("hs%d" % i, [128, 256]) for i in range(NWB)]
        sqv = [sb("sqv%d" % i, [128, 256]) for i in range(NWB)]
        yab = [sb("yab%d" % i, [128, 256], BF16) for i in range(NWB)]
        pD = ps("pD", [128, 512])
        pST = ps("pST", [128, 512])
        pI = ps("pI", [128, 512])
        pC = ps("pC", [128, 512])
        pU = [ps("pU%d" % i, [128, 512]) for i in range(2)]
        ptrk = ps("ptrk", [128, 1024], BF16)
        ptry = ps("ptry", [128, 1024], BF16)

        P.dma("sp", gall[:], T["gall_d"], w=["gall"])
        P.dma("sp", mlng[:], T["mlng_rep"], w=["mlng"])
        P.dma("sp", tri[:], T["c_tri"], w=["tri"])
        P.dma("sp", identf[:], T["c_ident"], w=["identf"])
        P.dve(lambda e: e.tensor_copy(out=identb[:], in_=identf[:]), r=["identf"], w=["identb"])
        P.dve(lambda e: e.memset(onesf[:], 1.0), w=["onesf"])
        for i in range(NK):
            P.dve(lambda e, i=i: e.memset(vc[i][:, :, 256:257], 1.0), w=[("vc1", i)])
        for d in range(2):
            P.dve(lambda e, d=d: e.memset(Cst[d][:], 0.0), w=[("Cst", d, h, k) for h in range(4) for k in range(2)])
            P.dve(lambda e, d=d: e.memset(Cb[d][0][:], 0.0), w=[("Cb", d, 0, h) for h in range(4)])

        g4 = gall[:, :].rearrange("p (c g h) -> p c g h", g=4, h=4)
        v3 = lambda t: t[:, :].rearrange("p (c h) -> p c h", h=4)
        for d in range(2):
            i_d = g4[:, :, 2 * d, :]
            f_d = g4[:, :, 2 * d + 1, :]
            P.act(lambda e, d=d, f_d=f_d: e.activation(out=v3(LFt[d]), in_=f_d, func=AF.Exp, scale=-1.0),
                  r=["gall"], w=[("LFt", d)])
            P.act(lambda e, d=d: e.activation(out=LFt[d][:, :], in_=LFt[d][:, :], func=AF.Ln, bias=1.0),
                  r=[("LFt", d)], w=[("LFt", d)])
            P.pe(lambda e, d=d: e.matmul(pI[:, 0:256], tri[:, d, :], LFt[d][:, :], start=True, stop=True),
                 r=["tri", ("LFt", d)], w=["pI"])
            P.pe(lambda e, d=d: e.matmul(pC[:, 0:256], onesf[:, :], LFt[d][:, :], start=True, stop=True),
                 r=["onesf", ("LFt", d)], w=["pC"])
            P.dve(lambda e, d=d: e.tensor_scalar(out=Bc[d][:, :], in0=pI[:, 0:256], scalar1=-1.0, scalar2=None, op0=ALU.mult),
                  r=["pI"], w=[("Bc", d)])
            P.dve(lambda e, d=d, i_d=i_d: e.tensor_tensor(out=v3(Aa[d]), in0=pI[:, 0:256].rearrange("p (c h) -> p c h", h=4),
                                                         in1=i_d, op=ALU.add),
                  r=["pI", "gall"], w=[("Aa", d)])
            P.act(lambda e, d=d: e.activation(out=EB[d][:, :], in_=Bc[d][:, :], func=AF.Exp),
                  r=[("Bc", d)], w=[("EB", d)])
            P.dve(lambda e, d=d: e.tensor_copy(out=DEC[d][:, :], in_=pC[:, 0:256]), r=["pC"], w=[("DEC", d)])
            P.dve(lambda e, d=d: e.tensor_tensor(out=WST[d][:, :], in0=Aa[d][:, :], in1=DEC[d][:, :], op=ALU.subtract),
                  r=[("Aa", d), ("DEC", d)], w=[("WST", d)])
            P.act(lambda e, d=d: e.activation(out=WST[d][:, :], in_=WST[d][:, :], func=AF.Exp),
                  r=[("WST", d)], w=[("WST", d)])
            P.act(lambda e, d=d: e.activation(out=DEC[d][:, :], in_=DEC[d][:, :], func=AF.Exp, scale=-1.0),
                  r=[("DEC", d), ("WST", d)], w=[("DEC", d)])

        cnt = {"w": 0, "k": 0, "q": 0, "g": 0, "cp": 0}

        def load_chunk(c, need_q, need_bw_out):
            ks = cnt["k"] % NK
            cnt["k"] += 1
            cs = slice(c * 128, (c + 1) * 128)
            P.dma("sp", kc[ks][:], mkT[:, cs].rearrange("(f p) t -> p f t", p=128), w=[("kc", ks)])
            P.dma("sp", vc[ks][:, :, 0:256], mv[cs, :].rearrange("p (h f) -> p h f", h=4), w=[("vc", ks)])
            for f in range(8):
                P.pe(lambda e, f=f: e.transpose(out=ptrk[:, f * 128:(f + 1) * 128], in_=kc[ks][:, f, :], identity=identb[:]),
                     r=[("kc", ks), "identb"], w=["ptrk"])
            if cnt["cp"] % 2 == 0:
                P.act(lambda e: e.activation(out=ktc[ks][:, :], in_=ptrk[:, :], func=AF.Copy), r=["ptrk"], w=[("ktc", ks)])
            else:
                P.dve(lambda e: e.tensor_copy(out=ktc[ks][:, :], in_=ptrk[:, :]), r=["ptrk"], w=[("ktc", ks)])
            cnt["cp"] += 1
            ctx = {"c": c, "ks": ks}
            if need_q:
                qs = cnt["q"] % NQ
                cnt["q"] += 1
                P.dma("sp", qc[qs][:], mqT[:, cs].rearrange("(f p) t -> p f t", p=128), w=[("qc", qs)])
                ctx["qs"] = qs
            return ctx

        def load_bw_extra(ctx):
            c = ctx["c"]
            cs = slice(c * 128, (c + 1) * 128)
            gs = cnt["g"] % 2
            cnt["g"] += 1
            P.dma("sp", gac[gs][:], gA[cs, :], w=[("gac", gs)])
            P.dma("sp", hfr[gs][:], hfwd[cs, :], r=[("hfwd", c)], w=[("hfr", gs)])
            ctx["gs"] = gs

        def output_A(ck, h, d):
            i = cnt["w"] % NWB
            cnt["w"] += 1
            c, ks, qs = ck["c"], ck["ks"], ck["qs"]
            col = c * 4 + h
            cbv = Cb[d][cbver[d]]
            cbk = ("Cb", d, cbver[d], h)
            P.dve(lambda e: e.tensor_scalar(out=dg[i][:], in0=identf[:], scalar1=Bc[d][:, col:col + 1], scalar2=None, op0=ALU.mult),
                  r=["identf", ("Bc", d)], w=[("dg", i)])
            P.pe(lambda e: e.matmul(pD[:, 0:128], onesf[:, :], dg[i][:, :], start=True, stop=False), r=["onesf", ("dg", i)], w=["pD"])
            P.pe(lambda e: e.matmul(pD[:, 0:128], identf[:, :], tri[:, 2 + d, :], start=False, stop=True), r=["identf", "tri"], w=["pD"])
            P.act(lambda e: e.activation(out=Dm[i][:], in_=pD[:, 0:128], func=AF.Exp, bias=Aa[d][:, col:col + 1]),
                  r=["pD", ("Aa", d)], w=[("Dm", i)])
            for dkc in range(2):
                P.pe(lambda e, dkc=dkc: e.matmul(pST[:, 0:128], kc[ks][:, h * 2 + dkc, :], qc[qs][:, h * 2 + dkc, :],
                                                 start=(dkc == 0), stop=(dkc == 1)),
                     r=[("kc", ks), ("qc", qs)], w=["pST"])
            for dkc in range(2):
                P.pe(lambda e, dkc=dkc: e.matmul(pC[:, 0:257], qc[qs][:, h * 2 + dkc, :], cbv[:, h, dkc, :],
                                                 start=(dkc == 0), stop=(dkc == 1)),
                     r=[("qc", qs), cbk], w=["pC"])
            P.act(lambda e: e.activation(out=tmpc[i][:], in_=pC[:, 0:257], func=AF.Copy, scale=EB[d][:, col:col + 1]),
                  r=["pC", ("EB", d)], w=[("tmpc", i)])
            P.dve(lambda e: e.tensor_tensor(out=Wm[i][:], in0=pST[:, 0:128], in1=Dm[i][:], op=ALU.mult),
                  r=["pST", ("Dm", i)], w=[("Wm", i)])
            P.pe(lambda e: e.matmul(pI[:, 0:257], Wm[i][:, :], vc[ks][:, h, :], start=True, stop=True),
                 r=[("Wm", i), ("vc", ks), ("vc1", ks)], w=["pI"])
            P.dve(lambda e: e.tensor_tensor(out=tot[i][:], in0=tmpc[i][:], in1=pI[:, 0:257], op=ALU.add),
                  r=[("tmpc", i), "pI"], w=[("tot", i)])
            return (ck, h, d, i)

        def output_B(ctx, ws):
            ck, h, d, i = ctx
            c = ck["c"]
            hsl = slice(h * 256, (h + 1) * 256)
            P.dve(lambda e: e.tensor_scalar(out=sm[i][:, 3:4], in0=tot[i][:, 256:257], scalar1=-1.0, scalar2=1.0, op0=ALU.mult, op1=ALU.max),
                  r=[("tot", i)], w=[("sm", i)])
            P.dve(lambda e: e.tensor_tensor(out=sm[i][:, 0:1], in0=tot[i][:, 256:257], in1=sm[i][:, 3:4], op=ALU.max),
                  r=[("tot", i), ("sm", i)], w=[("sm", i)])
            P.dve(lambda e: e.reciprocal(out=sm[i][:, 1:2], in_=sm[i][:, 0:1]), r=[("sm", i)], w=[("sm", i)])
            if d == 0:
                P.act(lambda e: e.activation(out=hfw[ws][:, hsl], in_=tot[i][:, 0:256], func=AF.Copy, scale=sm[i][:, 1:2]),
                      r=[("tot", i), ("sm", i)], w=[("hfw", ws, h)])
                return
            gs = ck["gs"]
            P.dve(lambda e: e.scalar_tensor_tensor(out=hs[i][:], in0=tot[i][:, 0:256], scalar=sm[i][:, 1:2], in1=hfr[gs][:, hsl],
                                                   op0=ALU.mult, op1=ALU.add),
                  r=[("tot", i), ("sm", i), ("hfr", gs)], w=[("hs", i)])
            P.dve(lambda e: e.tensor_tensor(out=sqv[i][:], in0=hs[i][:], in1=hs[i][:], op=ALU.mult), r=[("hs", i)], w=[("sqv", i)])
            P.dve(lambda e: e.reduce_sum(out=sm[i][:, 2:3], in_=sqv[i][:], axis=AX.X), r=[("sqv", i)], w=[("smb", i)])
            P.act(lambda e: e.activation(out=sm[i][:, 2:3], in_=sm[i][:, 2:3], func=AF.Ln, scale=1.0 / 256.0, bias=NORM_EPS),
                  r=[("smb", i)], w=[("smb", i)])
            P.act(lambda e: e.activation(out=sm[i][:, 2:3], in_=sm[i][:, 2:3], func=AF.Exp, scale=-0.5),
                  r=[("smb", i)], w=[("smb", i)])
            P.dve(lambda e: e.scalar_tensor_tensor(out=sqv[i][:], in0=hs[i][:], scalar=sm[i][:, 2:3],
                                                   in1=mlng[:, hsl], op0=ALU.mult, op1=ALU.mult),
                  r=[("hs", i), ("smb", i), "mlng"], w=[("sqv", i)])
            P.dve(lambda e: e.tensor_tensor(out=yab[i][:], in0=sqv[i][:], in1=gac[gs][:, hsl], op=ALU.mult),
                  r=[("sqv", i), ("gac", gs)], w=[("yab", i)])
            for k in range(2):
                f = h * 2 + k
                P.pe(lambda e, k=k, f=f: e.transpose(out=ptry[:, f * 128:(f + 1) * 128], in_=yab[i][:, k * 128:(k + 1) * 128],
                                                     identity=identb[:]), r=[("yab", i), "identb"], w=["ptry"])

        def update_step(ck, h, d):
            i = cnt["w"] % NWB
            cnt["w"] += 1
            c, ks = ck["c"], ck["ks"]
            col = c * 4 + h
            nv = 1 - cbver[d]
            nb_ = Cb[d][nv]
            nbk = ("Cb", d, nv, h)
            P.act(lambda e: e.activation(out=vw[i][:], in_=vc[ks][:, h, :], func=AF.Copy, scale=WST[d][:, col:col + 1]),
                  r=[("vc", ks), ("vc1", ks), ("WST", d)], w=[("vw", i)])
            for dkc in range(2):
                f = h * 2 + dkc
                P.pe(lambda e, dkc=dkc, f=f: e.matmul(pU[dkc][:, 0:257], ktc[ks][:, f * 128:(f + 1) * 128], vw[i][:, :], start=True, stop=True),
                     r=[("ktc", ks), ("vw", i)], w=[("pU", dkc)])
                P.dve(lambda e, dkc=dkc: e.scalar_tensor_tensor(out=Cst[d][:, h, dkc, :], in0=Cst[d][:, h, dkc, :], scalar=DEC[d][:, col:col + 1],
                                                                in1=pU[dkc][:, 0:257], op0=ALU.mult, op1=ALU.add),
                      r=[("Cst", d, h, dkc), ("DEC", d), ("pU", dkc)], w=[("Cst", d, h, dkc)])
                P.act(lambda e, dkc=dkc: e.activation(out=nb_[:, h, dkc, :], in_=Cst[d][:, h, dkc, :], func=AF.Copy),
                      r=[("Cst", d, h, dkc)], w=[nbk])

        def loads_for(step):
            if step < 32:
                return [load_chunk(step, True, False), load_chunk(63 - step, False, False)]
            return [load_chunk(63 - step, True, True)]

        nxt = loads_for(0)
        for step in range(64):
            cur = nxt
            if step + 1 < 64:
                nxt = loads_for(step + 1)
            ws = step % 2
            if step < 32:
                ca, cbk_ = cur
                ctxs = [output_A(ca, h, 0) for h in range(4)]
                if step < 31:
                    for h in range(4):
                        update_step(ca, h, 0)
                    cbver[0] = 1 - cbver[0]
                for h in range(4):
                    update_step(cbk_, h, 1)
                cbver[1] = 1 - cbver[1]
                for cx in ctxs:
                    output_B(cx, ws)
                c = ca["c"]
                P.dma("sp", hfwd[c * 128:(c + 1) * 128, :], hfw[ws][:, :], r=[("hfw", ws, h) for h in range(4)], w=[("hfwd", c)])
                if step == 31:
                    load_bw_extra(nxt[0])
            else:
                ca = cur[0]
                c = ca["c"]
                ctxs = [output_A(ca, h, 1) for h in range(4)]
                if c > 0:
                    for h in range(4):
                        update_step(ca, h, 1)
                    cbver[1] = 1 - cbver[1]
                for cx in ctxs:
                    output_B(cx, ws)
                P.act(lambda e, ws=ws: e.activation(out=yasc[ws][:, :, :], in_=ptry[:, :].rearrange("p (f t) -> p f t", f=8), func=AF.Copy),
                      r=["ptry"], w=[("yasc", ws)])
                P.dma("sp", yaT[:, c * 128:(c + 1) * 128].rearrange("(f p) t -> p f t", p=128), yasc[ws][:, :, :], r=[("yasc", ws)])
                if step + 1 < 64:
                    load_bw_extra(nxt[0])
        P.flush()


def phase_D(nc, P, T):
    dqT, dkT, dv, gB, ybT = T["dqT"], T["dkT"], T["dv"], T["gB"], T["ybT"]
    with contextlib.ExitStack() as st:
        sb = lambda name, shape, dt=F32: st.enter_context(nc.sbuf_tensor("D_" + name, list(shape), dt))
        ps = lambda name, shape, dt=F32: st.enter_context(nc.psum_tensor("D_" + name, list(shape), dt))
        kT = [sb("dkT%d" % i, [128, S], BF16) for i in range(2)]
        va = [sb("dva%d" % i, [128, 64, 129], BF16) for i in range(2)]
        qT = [sb("dqT%d" % i, [128, OWN], BF16) for i in range(2)]
        NE = 3
        E = [sb("dE%d" % i, [128, 1024], BF16) for i in range(NE)]
        gbt = [sb("dgb%d" % i, [128, 4, 128], BF16) for i in range(2)]
        lamr = sb("lamr", [128, 256])
        ltmp = sb("ltmp", [128, 128])
        lam = sb("lam", [128, 4])
        subg = sb("subg", [128, 128])
        identf = sb("identf", [128, 128])
        identb = sb("identb", [128, 128], BF16)
        r12 = [sb("r12_%d" % i, [128, 4]) for i in range(2)]
        o1 = [sb("o1_%d" % i, [128, 128]) for i in range(2)]
        o2 = [sb("o2_%d" % i, [128, 128]) for i in range(2)]
        sq = [sb("sq_%d" % i, [128, 128]) for i in range(2)]
        ybq = [sb("ybq%d" % i, [128, 128], BF16) for i in range(8)]
        ybs = [sb("ybs%d" % i, [128, 512], BF16) for i in range(2)]
        pS = [ps("pS%d" % i, [128, 1024]) for i in range(2)]
        pacc = [ps("pacc%d" % i, [128, 512]) for i in range(3)]
        ptr = ps("dptr", [128, 512], BF16)

        P.dma("sp", lamr[:], T["lam_rep"], w=["lamr"])
        P.dma("sp", subg[:], T["subln_rep"], w=["subg"])
        P.dma("sp", identf[:], T["c_ident"], w=["identf"])
        P.dve(lambda e: e.tensor_copy(out=identb[:], in_=identf[:]), r=["identf"], w=["identb"])
        P.dve(lambda e: e.tensor_scalar(out=subg[:], in0=subg[:], scalar1=(1.0 - LAMBDA_INIT), scalar2=None, op0=ALU.mult),
              r=["subg"], w=["subg"])
        P.dve(lambda e: e.tensor_tensor(out=ltmp[:, 0:64], in0=lamr[:, 0:64], in1=lamr[:, 64:128], op=ALU.mult),
              r=["lamr"], w=["ltmp"])
        P.dve(lambda e: e.tensor_tensor(out=ltmp[:, 64:128], in0=lamr[:, 128:192], in1=lamr[:, 192:256], op=ALU.mult),
              r=["lamr"], w=["ltmp"])
        P.dve(lambda e: e.reduce_sum(out=lam[:, 0:2], in_=ltmp[:, :].rearrange("p (a b) -> p a b", a=2), axis=AX.X),
              r=["ltmp"], w=["lam"])
        P.act(lambda e: e.activation(out=lam[:, 0:2], in_=lam[:, 0:2], func=AF.Exp), r=["lam"], w=["lam"])
        P.dve(lambda e: e.tensor_tensor(out=lam[:, 2:3], in0=lam[:, 0:1], in1=lam[:, 1:2], op=ALU.subtract),
              r=["lam"], w=["lam"])
        P.dve(lambda e: e.tensor_scalar(out=lam[:, 3:4], in0=lam[:, 2:3], scalar1=LAMBDA_INIT, scalar2=-1.0,
                                        op0=ALU.add, op1=ALU.mult), r=["lam"], w=["lam"])
        for i in range(2):
            P.dve(lambda e, i=i: e.memset(va[i][:, :, 128:129], 1.0), w=[("va1", i)])

        def acc_ap(a, lo, hi):
            return pacc[a // 3][:, (a % 3) * 129 + lo:(a % 3) * 129 + hi]

        accS = [sb("accS%d" % i, [128, 8 * 129]) for i in range(2)]
        mhalf = sb("mhalf", [128, 1])
        P.dve(lambda e: e.memset(mhalf[:], -0.5), w=["mhalf"])

        def load_head(h):
            hs = h % 2
            P.dma("sp", kT[hs][:], dkT[h * 128:(h + 1) * 128, :], w=[("kT", hs)])
            P.dma("sp", qT[hs][:], dqT[h * 128:(h + 1) * 128, :], w=[("qT", hs)])
            P.dma("sp", va[hs][:, :, 0:128], dv[:, h * 128:(h + 1) * 128].rearrange("(kb p) c -> p kb c", p=128),
                  w=[("va", hs)])

        def load_gb(h, qt):
            gs = (h * 8 + qt) % 2
            P.dma("sp", gbt[gs][:], gB[qt * 512:(qt + 1) * 512, h * 128:(h + 1) * 128].rearrange("(u p) c -> p u c", p=128),
                  w=[("gbt", gs)])

        def qk(i, h, qt, kb):
            sl, hs = i % 2, h % 2
            for j in range(2):
                P.pe(lambda e, j=j: e.matmul(
                    pS[sl][:, j * 512:(j + 1) * 512], kT[hs][j * 64:(j + 1) * 64, kb * 128:(kb + 1) * 128],
                    qT[hs][j * 64:(j + 1) * 64, qt * 512:(qt + 1) * 512], start=True, stop=True),
                    r=[("kT", hs), ("qT", hs)], w=[("pS", sl)])

        def ex(i):
            sl, es = i % 2, i % NE
            P.act(lambda e: e.activation(out=E[es][:, :], in_=pS[sl][:, :], func=AF.Exp), r=[("pS", sl)], w=[("E", es)])

        def pv(i, h, kb):
            es, hs = i % NE, h % 2
            for a in range(8):
                j, u = a // 4, a % 4
                P.pe(lambda e, a=a, j=j, u=u: e.matmul(
                    acc_ap(a, 0, 129), E[es][:, j * 512 + u * 128:j * 512 + (u + 1) * 128], va[hs][:, kb, :],
                    start=(kb == 0 and a % 3 == 0), stop=(kb == 63), skip_group_check=True),
                    r=[("E", es), ("va", hs), ("va1", hs)], w=[("accb", a // 3)])

        epi = [0]

        def epilogue(h, qt):
            gs = (h * 8 + qt) % 2
            ys = gs
            ai = gs
            A = accS[ai]
            for b in range(3):
                n = 387 if b < 2 else 258
                P.dve(lambda e, b=b, n=n: e.tensor_copy(out=A[:, b * 387:b * 387 + n], in_=pacc[b][:, 0:n]),
                      r=[("accb", b)], w=[("accS", ai)])
            sa = lambda a, lo, hi: A[:, a * 129 + lo:a * 129 + hi]
            for u in range(4):
                ep = epi[0] % 2
                epi[0] += 1
                a0, a1 = u, 4 + u
                P.dve(lambda e, ep=ep, a0=a0: e.reciprocal(out=r12[ep][:, 0:1], in_=sa(a0, 128, 129)),
                      r=[("accS", ai)], w=[("r12", ep)])
                P.dve(lambda e, ep=ep, a1=a1: e.reciprocal(out=r12[ep][:, 1:2], in_=sa(a1, 128, 129)),
                      r=[("accS", ai)], w=[("r12", ep)])
                P.dve(lambda e, ep=ep: e.tensor_tensor(out=r12[ep][:, 2:3], in0=r12[ep][:, 1:2], in1=lam[:, 3:4], op=ALU.mult),
                      r=[("r12", ep), "lam"], w=[("r12", ep)])
                P.dve(lambda e, ep=ep, a0=a0: e.tensor_scalar(out=o1[ep][:], in0=sa(a0, 0, 128), scalar1=r12[ep][:, 0:1],
                                                              scalar2=None, op0=ALU.mult),
                      r=[("accS", ai), ("r12", ep)], w=[("o1", ep)])
                P.dve(lambda e, ep=ep, a1=a1: e.scalar_tensor_tensor(out=o2[ep][:], in0=sa(a1, 0, 128), scalar=r12[ep][:, 2:3],
                                                                     in1=o1[ep][:], op0=ALU.mult, op1=ALU.add),
                      r=[("accS", ai), ("r12", ep), ("o1", ep)], w=[("o2", ep)])
                P.dve(lambda e, ep=ep: e.tensor_tensor(out=sq[ep][:], in0=o2[ep][:], in1=o2[ep][:], op=ALU.mult),
                      r=[("o2", ep)], w=[("sq", ep)])
                P.dve(lambda e, ep=ep: e.reduce_sum(out=r12[ep][:, 3:4], in_=sq[ep][:], axis=AX.X),
                      r=[("sq", ep)], w=[("r12b", ep)])
                P.dve(lambda e, ep=ep: e.tensor_scalar(out=r12[ep][:, 3:4], in0=r12[ep][:, 3:4], scalar1=1.0 / 128.0, scalar2=NORM_EPS,
                                                       op0=ALU.mult, op1=ALU.add), r=[("r12b", ep)], w=[("r12b", ep)])
                P.pool(lambda e, ep=ep: e.tensor_tensor(out=r12[ep][:, 3:4], in0=r12[ep][:, 3:4], in1=mhalf[:, 0:1], op=ALU.pow),
                       r=[("r12b", ep), "mhalf"], w=[("r12b", ep)])
                P.dve(lambda e, ep=ep: e.scalar_tensor_tensor(out=o1[ep][:], in0=o2[ep][:], scalar=r12[ep][:, 3:4],
                                                              in1=subg[:], op0=ALU.mult, op1=ALU.mult),
                      r=[("o2", ep), ("r12b", ep), "subg"], w=[("o1", ep)])
                yq = gs * 4 + u
                P.dve(lambda e, ep=ep, u=u, yq=yq: e.tensor_tensor(out=ybq[yq][:], in0=o1[ep][:], in1=gbt[gs][:, u, :], op=ALU.mult),
                      r=[("o1", ep), ("gbt", gs)], w=[("ybq", yq)])

        def epilogue2(h, qt):
            gs = (h * 8 + qt) % 2
            ys = gs
            for u in range(4):
                yq = gs * 4 + u
                P.pe(lambda e, u=u, yq=yq: e.transpose(out=ptr[:, u * 128:(u + 1) * 128], in_=ybq[yq][:], identity=identb[:]),
                     r=[("ybq", yq), "identb"], w=["ptr"])
            P.dve(lambda e: e.tensor_copy(out=ybs[ys][:, :], in_=ptr[:, :]), r=["ptr"], w=[("ybs", ys)])
            P.dma("sp", ybT[h * 128:(h + 1) * 128, qt * 512:(qt + 1) * 512], ybs[ys][:], r=[("ybs", ys)])

        blocks = [(h, qt, kb) for h in range(8) for qt in range(8) for kb in range(64)]
        nb = len(blocks)
        load_head(0)
        load_gb(0, 0)
        qk(0, *blocks[0])
        qk(1, *blocks[1])
        for i, (h, qt, kb) in enumerate(blocks):
            if kb == 0 and qt == 0 and h + 1 < 8:
                load_head(h + 1)
            if kb == 0:
                nxt = h * 8 + qt + 1
                if nxt < 64:
                    load_gb(nxt // 8, nxt % 8)
            ex(i)
            pv(i, h, kb)
            if i + 2 < nb:
                qk(i + 2, *blocks[i + 2])
            if kb == 63:
                epilogue(h, qt)
            if kb == 40 and (h, qt) != (0, 0):
                pq = h * 8 + qt - 1
                epilogue2(pq // 8, pq % 8)
        epilogue2(7, 7)
        P.flush()


def phase_O(nc, P, T):
    x, out, yaT, ybT, sgT = T["x"], T["out"], T["yaT"], T["ybT"], T["sgT"]
    with contextlib.ExitStack() as st:
        sb = lambda name, shape, dt=F32: st.enter_context(nc.sbuf_tensor("O_" + name, list(shape), dt))
        ps = lambda name, shape, dt=F32: st.enter_context(nc.psum_tensor("O_" + name, list(shape), dt))
        Wa = sb("Wa", [128, 8, D], BF16)
        Wb = sb("Wb", [128, 8, D], BF16)
        Wo = sb("Wo", [128, 8, D], BF16)
        fing = sb("fing", [128, D])
        ya = [sb("ya%d" % i, [128, 8, 512], BF16) for i in range(2)]
        yb = [sb("yb%d" % i, [128, 8, 512], BF16) for i in range(2)]
        sa = [sb("sa%d" % i, [128, 8, 512], BF16) for i in range(2)]
        sbb = [sb("sb%d" % i, [128, 8, 512], BF16) for i in range(2)]
        mixT = [sb("mixT%d" % i, [128, 8, 512], BF16) for i in range(2)]
        t1 = [sb("t1_%d" % i, [128, 512]) for i in range(2)]
        t2 = [sb("t2_%d" % i, [128, 512]) for i in range(2)]
        xt = [sb("xt%d" % i, [128, D]) for i in range(2)]
        xo = [sb("xo%d" % i, [128, D]) for i in range(2)]
        junk = sb("junk", [128, D], BF16)
        ssq = [sb("ssq%d" % i, [128, 2]) for i in range(2)]
        pa = [ps("pa%d" % i, [128, 512]) for i in range(2)]
        pb = [ps("pb%d" % i, [128, 512]) for i in range(2)]
        po = [ps("po%d" % i, [128, 512]) for i in range(4)]

        for (W, src, key) in ((Wa, T["w_a"], "Wa"), (Wb, T["w_b"], "Wb"), (Wo, T["w_o"], "Wo")):
            for hh in range(2):
                P.dma("pool", W[:, :, hh * 512:(hh + 1) * 512],
                      src[:, hh * 512:(hh + 1) * 512].rearrange("(kc p) c -> p kc c", p=128), w=[(key, hh)])
        P.dma("sp", fing[:], T["fing_rep"], w=["fing"])
        wkeys = lambda k: [(k, 0), (k, 1)]
        cnt = {"p": 0, "o": 0, "x": 0}
        for tt in range(8):
            sl = tt % 2
            ts_ = slice(tt * 512, (tt + 1) * 512)
            P.dma("sp", ya[sl][:], yaT[:, ts_].rearrange("(cc p) t -> p cc t", p=128), w=[("ya", sl)])
            P.dma("sp", yb[sl][:], ybT[:, ts_].rearrange("(cc p) t -> p cc t", p=128), w=[("yb", sl)])
            P.dma("sp", sa[sl][:], sgT[0, :, ts_].rearrange("(cc p) t -> p cc t", p=128), w=[("sa", sl)])
            P.dma("sp", sbb[sl][:], sgT[1, :, ts_].rearrange("(cc p) t -> p cc t", p=128), w=[("sb", sl)])
            for dd in range(8):
                i = cnt["p"] % 2
                cnt["p"] += 1
                for cc in range(8):
                    P.pe(lambda e, i=i, cc=cc, dd=dd, sl=sl: e.matmul(pa[i][:, :], Wa[:, cc, dd * 128:(dd + 1) * 128], ya[sl][:, cc, :],
                                                                      start=(cc == 0), stop=(cc == 7)),
                         r=wkeys("Wa") + [("ya", sl)], w=[("pa", i)])
                for cc in range(8):
                    P.pe(lambda e, i=i, cc=cc, dd=dd, sl=sl: e.matmul(pb[i][:, :], Wb[:, cc, dd * 128:(dd + 1) * 128], yb[sl][:, cc, :],
                                                                      start=(cc == 0), stop=(cc == 7)),
                         r=wkeys("Wb") + [("yb", sl)], w=[("pb", i)])
                P.dve(lambda e, i=i, dd=dd, sl=sl: e.tensor_tensor(out=t1[i][:], in0=pa[i][:, :], in1=sa[sl][:, dd, :], op=ALU.mult),
                      r=[("pa", i), ("sa", sl)], w=[("t1", i)])
                P.dve(lambda e, i=i, dd=dd, sl=sl: e.tensor_tensor(out=t2[i][:], in0=pb[i][:, :], in1=sbb[sl][:, dd, :], op=ALU.mult),
                      r=[("pb", i), ("sb", sl)], w=[("t2", i)])
                P.pool(lambda e, i=i, dd=dd, sl=sl: e.tensor_tensor(out=mixT[sl][:, dd, :], in0=t1[i][:], in1=t2[i][:], op=ALU.add),
                       r=[("t1", i), ("t2", i)], w=[("mixT", sl)])
            for u in range(4):
                xs = cnt["x"] % 2
                cnt["x"] += 1
                r0 = tt * 512 + u * 128
                P.dma("sp", xt[xs][:], x[r0:r0 + 128, :], w=[("xt", xs)])
                for eg in range(2):
                    o = cnt["o"] % 4
                    cnt["o"] += 1
                    for dd in range(8):
                        P.pe(lambda e, o=o, dd=dd, sl=sl, u=u, eg=eg: e.matmul(
                            po[o][:, :], mixT[sl][:, dd, u * 128:(u + 1) * 128], Wo[:, dd, eg * 512:(eg + 1) * 512],
                            start=(dd == 0), stop=(dd == 7)), r=[("mixT", sl), ("Wo", eg)], w=[("po", o)])
                    P.dve(lambda e, o=o, xs=xs, eg=eg: e.tensor_tensor(out=xo[xs][:, eg * 512:(eg + 1) * 512], in0=po[o][:, :],
                                                                       in1=xt[xs][:, eg * 512:(eg + 1) * 512], op=ALU.add),
                          r=[("po", o), ("xt", xs)], w=[("xo", xs, eg)])
                P.act(lambda e, xs=xs: e.activation(out=junk[:], in_=xo[xs][:], func=AF.Square, accum_out=ssq[xs][:, 0:1]),
                      r=[("xo", xs, 0), ("xo", xs, 1)], w=[("ssq", xs)])
                P.act(lambda e, xs=xs: e.activation(out=ssq[xs][:, 0:1], in_=ssq[xs][:, 0:1], func=AF.Sqrt, scale=1.0 / D, bias=NORM_EPS),
                      r=[("ssq", xs)], w=[("ssq", xs)])
                P.dve(lambda e, xs=xs: e.reciprocal(out=ssq[xs][:, 1:2], in_=ssq[xs][:, 0:1]), r=[("ssq", xs)], w=[("ssq", xs)])
                P.dve(lambda e, xs=xs: e.scalar_tensor_tensor(out=xo[xs][:], in0=xo[xs][:], scalar=ssq[xs][:, 1:2], in1=fing[:],
                                                              op0=ALU.mult, op1=ALU.mult),
                      r=[("xo", xs, 0), ("xo", xs, 1), ("ssq", xs), "fing"], w=[("xo", xs, 0), ("xo", xs, 1)])
                P.dma("sp", out[r0:r0 + 128, :], xo[xs][:], r=[("xo", xs, 0), ("xo", xs, 1)])
        P.flush()


def make_in_maps(x, positions, norm_g, w_in, ml_gate_b, ml_conv_w, ml_norm_g, da_lambda,
                 da_subln_g, gate_b, w_branch_a, w_branch_b, w_out, final_g):
    f32 = np.float32
    x = np.asarray(x, f32)
    positions = np.asarray(positions, np.int32)
    w_in0 = np.ascontiguousarray(np.asarray(w_in, f32)[0])
    gb0 = np.asarray(ml_gate_b, f32)[0]
    cw0 = np.asarray(ml_conv_w, f32)[0]
    w_in1 = w_in0.copy()
    ag = w_in0[:, C_AG:C_AG + 16].reshape(D, 4, 4)
    w_in1[:, C_AG:C_AG + 16] = ag[:, [2, 3, 0, 1], :].reshape(D, 16)
    gb1 = gb0[[2, 3, 0, 1], :]
    cw1 = cw0[::-1, :]
    rep = lambda v, n=128: np.ascontiguousarray(np.broadcast_to(np.asarray(v, f32).reshape(1, -1), (n, np.asarray(v).size)))
    ident = np.eye(128, dtype=f32)
    rsw = np.zeros((128, 128), f32)
    for r in range(128):
        m = r % 64
        if m < 32:
            rsw[r + 32, r] = -1.0
        else:
            rsw[r - 32, r] = 1.0
    invf = (10000.0 ** (-np.arange(0, 64, 2, dtype=f32) / f32(64))).astype(f32)
    invf_p = np.array([invf[(p % 64) % 32] for p in range(128)], f32).reshape(128, 1)
    ii = np.arange(128)
    U = (ii[:, None] <= ii[None, :]).astype(f32)
    L = (ii[:, None] >= ii[None, :]).astype(f32)
    NEG = -30000.0
    tri = np.stack([U, L, (1 - U) * NEG, (1 - L) * NEG], axis=1).astype(f32)
    common = {
        "normg_rep": rep(np.asarray(norm_g, f32)[0]),
        "mlng_rep": rep(np.asarray(ml_norm_g, f32)[0].reshape(-1)),
        "lam_rep": rep(np.asarray(da_lambda, f32)[0].reshape(-1)),
        "subln_rep": rep(np.asarray(da_subln_g, f32)[0]),
        "gb_fm": np.ascontiguousarray(np.asarray(gate_b, f32)[0].reshape(2, 8, 128).transpose(2, 0, 1)),
        "w_a": np.ascontiguousarray(np.asarray(w_branch_a, f32)[0]),
        "w_b": np.ascontiguousarray(np.asarray(w_branch_b, f32)[0]),
        "w_o": np.ascontiguousarray(np.asarray(w_out, f32)[0]),
        "fing_rep": rep(np.asarray(final_g, f32)),
        "c_ident": ident, "c_rswap": rsw, "c_invf": invf_p, "c_tri": tri,
    }
    in_maps = []
    for core in range(NCORES):
        b, half = core // 2, core % 2
        xb = x[b]
        pb = positions[b]
        if half == 1:
            xb = xb[::-1]
            pb = pb[::-1]
        gbx = gb1 if half else gb0
        cwx = cw1 if half else cw0
        m = dict(common)
        m["x"] = np.ascontiguousarray(xb)
        m["posr"] = np.ascontiguousarray(np.broadcast_to(pb.reshape(1, S), (128, S))).astype(np.int32)
        m["w_in"] = w_in1 if half else w_in0
        m["gateb_rep"] = rep(gbx.reshape(-1))
        m["convw"] = np.ascontiguousarray(cwx.reshape(5, 16, 128).transpose(2, 1, 0))
        in_maps.append(m)
    return in_maps


_NC_CACHE = {}


def kernel(**inputs):
    in_maps = make_in_maps(**inputs)
    if "nc" not in _NC_CACHE:
        _NC_CACHE["nc"] = build_nc()
    nc = _NC_CACHE["nc"]
    res = run_bass_kernel_spmd(nc, in_maps, core_ids=list(range(NCORES)))
    B = 4
    outp = np.empty((B, S, D), np.float32)
    for core in range(NCORES):
        b, half = core // 2, core % 2
        o = np.asarray(res.results[core]["out"], np.float32)
        if half == 0:
            outp[b, :OWN] = o
        else:
            outp[b, OWN:] = o[::-1]
    return outp
```

```python
import contextlib
import math
import numpy as np
import concourse.bass as bass
import concourse.mybir as mybir
from concourse.bass_utils import run_bass_kernel_spmd

F32, BF16, I32 = mybir.dt.float32, mybir.dt.bfloat16, mybir.dt.int32
AF = mybir.ActivationFunctionType
ALU = mybir.AluOpType
AX = mybir.AxisListType

D = 1024
S = 8192
OWN = 4096
NCORES = 8
PROJ = 11280
C_AQ, C_AK, C_AV, C_AO, C_AZ, C_AG = 0, 1024, 2048, 3072, 4096, 5120
C_BQ, C_BK, C_BV, C_BZ, C_GA, C_GB = 5136, 6160, 7184, 8208, 9232, 10256
NORM_EPS = 1e-6
LAMBDA_INIT = 0.8 - 0.6 * math.exp(-0.3 * 0)
SAME_ENGINE_SYNC = True
SAME_ENGINE_RAW_ONLY = False


class _Op:
    __slots__ = ("eng", "fn", "dma", "clock", "idx", "waits", "signal", "semval", "sem", "know")


class Prog:
    ENGS = ("sp", "act", "dve", "pool", "pe")

    def __init__(self, nc, stack, n_dma=8):
        self.nc = nc
        self.sem = {e: stack.enter_context(nc.semaphore("cs_" + e)) for e in self.ENGS}
        self.dsem = {q: [stack.enter_context(nc.semaphore("ds_%s_%d" % (q, i))) for i in range(n_dma)]
                     for q in ("sp", "pool", "act")}
        self.n_dma = n_dma
        self.dcount = {q: 0 for q in self.dsem}
        self.dlast = {}
        self.cnt = {e: 0 for e in self.ENGS}
        self.sigcnt = {e: 0 for e in self.ENGS}
        self.know = {e: {} for e in self.ENGS}
        self.pending = {e: [] for e in self.ENGS}
        self.last_w = {}
        self.readers = {}
        self.nops = 0

    def add(self, eng, fn, reads=(), writes=(), dma=False, extra_deps=()):
        op = _Op()
        op.eng, op.fn, op.dma, op.signal, op.semval = eng, fn, dma, False, None
        self.nops += 1
        deps = []
        seen = set()

        raw = set()

        def push(d, is_raw=False):
            if d is None:
                return
            if is_raw:
                raw.add(id(d))
            if id(d) not in seen:
                seen.add(id(d))
                deps.append(d)

        for k in reads:
            push(self.last_w.get(k), True)
        for k in writes:
            push(self.last_w.get(k))
            for r in self.readers.get(k, ()):
                push(r)
        for d in extra_deps:
            push(d, True)
        if dma:
            slot = self.dcount[eng] % self.n_dma
            self.dcount[eng] += 1
            op.clock = (eng, slot)
            prev = self.dlast.get(op.clock)
            op.idx = (prev.idx + 1) if prev is not None else 1
            op.sem = self.dsem[eng][slot]
            op.semval = 16 * op.idx
            push(prev)
            self.dlast[op.clock] = op
        else:
            self.cnt[eng] += 1
            op.clock = eng
            op.idx = self.cnt[eng]
            op.sem = self.sem[eng]
        know = self.know[eng]
        waits = []
        for d in deps:
            if (not d.dma) and (not dma) and d.eng == eng:
                if eng == "pe" or not SAME_ENGINE_SYNC:
                    continue
                if SAME_ENGINE_RAW_ONLY and id(d) not in raw:
                    continue
            if know.get(d.clock, 0) >= d.idx:
                continue
            waits.append(d)
            d.signal = True
            for c, v in d.know.items():
                if know.get(c, 0) < v:
                    know[c] = v
            if know.get(d.clock, 0) < d.idx:
                know[d.clock] = d.idx
        op.waits = waits
        op.know = dict(know)
        for k in writes:
            self.last_w[k] = op
            self.readers[k] = []
        for k in reads:
            self.readers.setdefault(k, []).append(op)
        self.pending[eng].append(op)
        return op

    def pe(self, fn, r=(), w=()):
        return self.add("pe", fn, r, w)

    def act(self, fn, r=(), w=()):
        return self.add("act", fn, r, w)

    def dve(self, fn, r=(), w=()):
        return self.add("dve", fn, r, w)

    def pool(self, fn, r=(), w=()):
        return self.add("pool", fn, r, w)

    def dma(self, q, out, in_, r=(), w=()):
        return self.add(q, lambda e: e.dma_start(out=out, in_=in_), r, w, dma=True)

    def flush(self, final=False):
        outstanding = [d for d in self.dlast.values()]
        self.add("sp", None, extra_deps=outstanding)
        for e in self.ENGS:
            for op in self.pending[e]:
                if not op.dma and op.signal:
                    self.sigcnt[e] += 1
                    op.semval = self.sigcnt[e]
                elif not op.dma:
                    op.semval = None
        pend = self.pending
        sems = self.sem

        def make_body(eng):
            ops = pend[eng]

            def body(e):
                for op in ops:
                    for d in op.waits:
                        assert d.semval is not None
                        e.wait_ge(d.sem, d.semval)
                    if op.fn is None:
                        continue
                    ins = op.fn(e)
                    if op.dma:
                        ins.then_inc(op.sem, 16)
                    elif op.signal:
                        ins.then_inc(sems[eng], 1)
            return body

        with self.nc.Block() as block:
            block.sync(make_body("sp"))
            block.scalar(make_body("act"))
            block.vector(make_body("dve"))
            block.gpsimd(make_body("pool"))
            block.tensor(make_body("pe"))
        self.pending = {e: [] for e in self.ENGS}
        self.last_w = {}
        self.readers = {}
        full = {}
        for e in self.ENGS:
            full[e] = self.cnt[e]
        for c, d in self.dlast.items():
            full[c] = d.idx
        self.know = {e: dict(full) for e in self.ENGS}


def build_nc(debug=False, phases=("P", "M", "D", "O")):
    nc = bass.Bass("TRN2", target_bir_lowering=False)
    IN = lambda name, shape, dt=F32: nc.dram_tensor(name, list(shape), dt, kind="ExternalInput").ap()
    skind = "ExternalOutput" if debug else "Internal"
    SCR = lambda name, shape, dt=BF16: nc.dram_tensor(name, list(shape), dt, kind=skind).ap()

    x = IN("x", [S, D])
    posr = IN("posr", [128, S], I32)
    w_in = IN("w_in", [D, PROJ])
    normg_rep = IN("normg_rep", [128, D])
    gateb_rep = IN("gateb_rep", [128, 16])
    convw = IN("convw", [128, 16, 5])
    mlng_rep = IN("mlng_rep", [128, D])
    lam_rep = IN("lam_rep", [128, 256])
    subln_rep = IN("subln_rep", [128, 128])
    gb_fm = IN("gb_fm", [128, 2, 8])
    w_a = IN("w_a", [D, D])
    w_b = IN("w_b", [D, D])
    w_o = IN("w_o", [D, D])
    fing_rep = IN("fing_rep", [128, D])
    c_ident = IN("c_ident", [128, 128])
    c_rswap = IN("c_rswap", [128, 128])
    c_invf = IN("c_invf", [128, 1])
    c_tri = IN("c_tri", [128, 4, 128])
    out = nc.dram_tensor("out", [OWN, D], F32, kind="ExternalOutput").ap()

    mqT = SCR("mqT", [D, S])
    mkT = SCR("mkT", [D, S])
    mv = SCR("mv", [S, D])
    dqT = SCR("dqT", [D, OWN])
    dkT = SCR("dkT", [D, S])
    dv = SCR("dv", [S, D])
    gA = SCR("gA", [OWN, D])
    gB = SCR("gB", [OWN, D])
    sgT = SCR("sgT", [2, D, OWN])
    gall_d = SCR("gall_d", [128, 64 * 16], F32)
    yaT = SCR("yaT", [D, OWN])
    ybT = SCR("ybT", [D, OWN])
    hfwd = SCR("hfwd", [OWN, D], F32)

    with contextlib.ExitStack() as top:
        P = Prog(nc, top)
        if "P" in phases:
            phase_P(nc, P, locals())
        if "M" in phases:
            phase_M(nc, P, locals())
        if "M2" in phases:
            phase_M2(nc, P, locals())
        ow = None
        if "O" in phases:
            ow = {k: top.enter_context(nc.sbuf_tensor("OW_" + k, [128, 8, D], BF16)) for k in ("Wa", "Wb", "Wo")}
            for (k, srcw) in (("Wa", w_a), ("Wb", w_b), ("Wo", w_o)):
                for hh in range(2):
                    P.dma("pool", ow[k][:, :, hh * 512:(hh + 1) * 512],
                          srcw[:, hh * 512:(hh + 1) * 512].rearrange("(kc p) c -> p kc c", p=128), w=[(k, hh)])
        if "D" in phases:
            phase_D(nc, P, locals())
        if "O" in phases:
            phase_O(nc, P, locals())
    return nc


def phase_P(nc, P, T):
    x, posr, w_in = T["x"], T["posr"], T["w_in"]
    with contextlib.ExitStack() as st:
        sb = lambda name, shape, dt=F32: st.enter_context(nc.sbuf_tensor("P_" + name, list(shape), dt))
        ps = lambda name, shape, dt=F32: st.enter_context(nc.psum_tensor("P_" + name, list(shape), dt))
        hT = sb("hT", [128, 8, 2048], BF16)
        NX = 4
        xt = [sb("xt%d" % i, [128, D]) for i in range(NX)]
        xn = [sb("xn%d" % i, [128, D], BF16) for i in range(NX)]
        junk = sb("junk", [128, D], BF16)
        ss = [sb("ss%d" % i, [128, 1]) for i in range(NX)]
        rstd = [sb("rstd%d" % i, [128, 1]) for i in range(NX)]
        grep = sb("grep", [128, D])
        NW = 3
        wB = [sb("wB%d" % i, [128, 8, 512], BF16) for i in range(NW)]
        wG = sb("wG", [128, 8, 16], BF16)
        NWK = 5
        WK = [sb("WK%d" % i, [128, 2054]) for i in range(NWK)]
        OB = [sb("OB%d" % i, [128, 2050], BF16) for i in range(2)]
        OBT = [sb("OBT%d" % i, [128, 8192], BF16) for i in range(2)]
        obt_slot = [0]
        cosT = sb("cosT", [128, 2048])
        sinT = sb("sinT", [128, 2048])
        gall = sb("gall", [128, 64 * 16])
        gbrep = sb("gbrep", [128, 16])
        cw = sb("cw", [128, 16, 5])
        carry = sb("carry", [128, 16, 4])
        identf = sb("identf", [128, 128])
        identb = sb("identb", [128, 128], BF16)
        rswap = sb("rswap", [128, 128])
        invf = sb("invf", [128, 1])
        gbfm = sb("gbfm", [128, 2, 8])
        posi = sb("posi", [128, 2048], I32)
        ptr = [ps("ptr%d" % i, [128, D], BF16) for i in range(2)]
        pp = [ps("pp%d" % i, [128, 512]) for i in range(4)]
        prot = [ps("prot%d" % i, [128, 512]) for i in range(2)]

        P.dma("sp", grep[:], T["normg_rep"], w=["grep"])
        P.dma("sp", gbrep[:], T["gateb_rep"], w=["gbrep"])
        P.dma("sp", cw[:], T["convw"], w=["cw"])
        P.dma("sp", identf[:], T["c_ident"], w=["identf"])
        P.dma("sp", rswap[:], T["c_rswap"], w=["rswap"])
        P.dma("sp", invf[:], T["c_invf"], w=["invf"])
        P.dma("sp", gbfm[:], T["gb_fm"], w=["gbfm"])
        P.dma("pool", wG[:], w_in[:, C_AG:C_AG + 16].rearrange("(kc p) c -> p kc c", p=128), w=["wG"])
        P.dve(lambda e: e.tensor_copy(out=identb[:], in_=identf[:]), r=["identf"], w=["identb"])
        P.dve(lambda e: e.memset(carry[:], 0.0), w=["carry"])
        for i in range(NWK):
            P.dve(lambda e, i=i: e.memset(WK[i][:, 2052:2054], 0.0), w=[("WKz", i), ("WK", i)])

        wslot = [0]

        def load_w(col0, ncols):
            s = wslot[0] % NW
            wslot[0] += 1
            P.dma("pool", wB[s][:, :, 0:ncols],
                  w_in[:, col0:col0 + ncols].rearrange("(kc p) c -> p kc c", p=128), w=[("wB", s)])
            return s

        ppslot = [0]

        def next_pp():
            s = ppslot[0] % 4
            ppslot[0] += 1
            return s

        wkslot = [0]

        def next_wk():
            s = wkslot[0] % NWK
            wkslot[0] += 1
            return s

        obslot = [0]

        def next_ob():
            s = obslot[0] % 2
            obslot[0] += 1
            return s

        def fm_mm(ws, wc, tq):
            s = next_pp()
            for kc in range(8):
                P.pe(lambda e, s=s, ws=ws, wc=wc, kc=kc, tq=tq: e.matmul(
                    pp[s][:, :], wB[ws][:, kc, wc * 128:(wc + 1) * 128], hT[:, kc, tq * 512:(tq + 1) * 512],
                    start=(kc == 0), stop=(kc == 7)),
                    r=[("wB", ws), "hT"], w=[("pp", s)])
            return s

        def tm_mm(ws, ncols, tt):
            s = next_pp()
            for kc in range(8):
                P.pe(lambda e, s=s, ws=ws, kc=kc, tt=tt, ncols=ncols: e.matmul(
                    pp[s][:, 0:ncols], hT[:, kc, tt * 128:(tt + 1) * 128], wB[ws][:, kc, 0:ncols],
                    start=(kc == 0), stop=(kc == 7)),
                    r=[("wB", ws), "hT"], w=[("pp", s)])
            return s

        def load_x(stile_, tt_):
            r0_ = stile_ * 2048 + tt_ * 128
            P.dma("sp", xt[tt_ % NX][:], x[r0_:r0_ + 128, :], w=[("xt", tt_ % NX)])

        for stile in range(4):
            own = stile < 2
            t0 = stile * 2048
            last = stile == 3
            if stile == 0:
                for tt in range(NX):
                    load_x(0, tt)
            def stage_a(tt):
                sl = tt % NX
                pl = tt % 2
                P.act(lambda e, sl=sl: e.activation(out=junk[:], in_=xt[sl][:], func=AF.Square, accum_out=ss[sl][:]),
                      r=[("xt", sl)], w=[("ss", sl)])
                P.act(lambda e, sl=sl: e.activation(out=ss[sl][:], in_=ss[sl][:], func=AF.Sqrt,
                                                    scale=1.0 / D, bias=NORM_EPS),
                      r=[("ss", sl)], w=[("ss", sl)])
                P.dve(lambda e, sl=sl: e.reciprocal(out=rstd[sl][:], in_=ss[sl][:]), r=[("ss", sl)], w=[("rstd", sl)])
                P.dve(lambda e, sl=sl: e.scalar_tensor_tensor(out=xn[sl][:], in0=xt[sl][:], scalar=rstd[sl][:],
                                                              in1=grep[:], op0=ALU.mult, op1=ALU.mult),
                      r=[("xt", sl), ("rstd", sl), "grep"], w=[("xn", sl)])
                for kc in range(8):
                    P.pe(lambda e, sl=sl, kc=kc, pl=pl: e.transpose(out=ptr[pl][:, kc * 128:(kc + 1) * 128],
                                                                    in_=xn[sl][:, kc * 128:(kc + 1) * 128],
                                                                    identity=identb[:]),
                         r=[("xn", sl), "identb"], w=[("ptr", pl)])
                if tt + NX < 16:
                    load_x(stile, tt + NX)

            def stage_b(tt):
                pl = tt % 2
                if tt % 2 == 0:
                    P.act(lambda e, pl=pl, tt=tt: e.activation(
                        out=hT[:, :, tt * 128:(tt + 1) * 128],
                        in_=ptr[pl][:, :].rearrange("p (k t) -> p k t", k=8), func=AF.Copy),
                        r=[("ptr", pl)], w=["hT"])
                else:
                    P.dve(lambda e, pl=pl, tt=tt: e.tensor_copy(
                        out=hT[:, :, tt * 128:(tt + 1) * 128],
                        in_=ptr[pl][:, :].rearrange("p (k t) -> p k t", k=8)),
                        r=[("ptr", pl)], w=["hT"])

            stage_a(0)
            for tt in range(16):
                if tt + 1 < 16:
                    stage_a(tt + 1)
                stage_b(tt)
            if stile < 3:
                for tt in range(NX):
                    load_x(stile + 1, tt)

            P.dma("sp", posi[:], posr[:, t0:t0 + 2048], w=["posi"])
            build_rope_tables(P, posi, invf, cosT, sinT, WK, next_wk)

            for tt in range(16):
                s = next_pp()
                for kc in range(8):
                    P.pe(lambda e, s=s, kc=kc, tt=tt: e.matmul(
                        pp[s][:, 0:16], hT[:, kc, tt * 128:(tt + 1) * 128], wG[:, kc, :],
                        start=(kc == 0), stop=(kc == 7)), r=["wG", "hT"], w=[("pp", s)])
                gt = stile * 16 + tt
                P.dve(lambda e, s=s, gt=gt: e.tensor_tensor(out=gall[:, gt * 16:(gt + 1) * 16], in0=pp[s][:, 0:16],
                                                            in1=gbrep[:], op=ALU.add),
                      r=[("pp", s), "gbrep"], w=["gall"])

            def gen_tm_v():
                for (c0, dst) in ((C_AV, T["mv"]), (C_BV, T["dv"])):
                    for cg in range(2):
                        ws = load_w(c0 + cg * 512, 512)
                        ob = obt_slot[0] % 2
                        obt_slot[0] += 1
                        for tt in range(16):
                            s = tm_mm(ws, 512, tt)
                            if True:
                                P.act(lambda e, s=s, ob=ob, tt=tt: e.activation(
                                    out=OBT[ob][:, tt * 512:(tt + 1) * 512], in_=pp[s][:, :], func=AF.Copy),
                                    r=[("pp", s)], w=[("OBT", ob)])
                            else:
                                P.dve(lambda e, s=s, ob=ob, tt=tt: e.tensor_copy(
                                    out=OBT[ob][:, tt * 512:(tt + 1) * 512], in_=pp[s][:, :]),
                                    r=[("pp", s)], w=[("OBT", ob)])
                            if tt == 15:
                                P.dma("sp", dst[t0:t0 + 2048, cg * 512:(cg + 1) * 512].rearrange("(tt p) c -> p tt c", p=128),
                                      OBT[ob][:, :].rearrange("p (tt c) -> p tt c", c=512), r=[("OBT", ob)])
                            yield

            def gen_conv():
                prev2 = [None]
                for fam, (c0, dst, ksc) in enumerate(((C_AQ, T["mqT"], 1.0), (C_AK, T["mkT"], 1.0 / 16.0))):
                    for g4 in range(2):
                        ws = load_w(c0 + g4 * 512, 512)
                        for wc in range(4):
                            fc = fam * 8 + g4 * 4 + wc
                            row0 = (g4 * 4 + wc) * 128
                            stg = next_wk()
                            acc = next_wk()
                            sg = next_wk()
                            P.dve(lambda e, stg=stg, fc=fc: e.tensor_copy(out=WK[stg][:, 0:4], in_=carry[:, fc, :]),
                                  r=["carry"], w=[("WK", stg)])
                            for tq in range(4):
                                s = fm_mm(ws, wc, tq)
                                P.act(lambda e, s=s, stg=stg, tq=tq: e.activation(
                                    out=WK[stg][:, 4 + tq * 512:4 + (tq + 1) * 512], in_=pp[s][:, :], func=AF.Copy),
                                    r=[("pp", s)], w=[("WK", stg)])
                            P.dve(lambda e, stg=stg, fc=fc: e.tensor_copy(out=carry[:, fc, :], in_=WK[stg][:, 2048:2052]),
                                  r=[("WK", stg)], w=["carry"])
                            nout = 2050 if last else 2048
                            P.act(lambda e, stg=stg, acc=acc, fc=fc, nout=nout: e.activation(
                                out=WK[acc][:, 0:nout], in_=WK[stg][:, 0:nout], func=AF.Copy, scale=cw[:, fc, 0:1]),
                                r=[("WK", stg), "cw", ("WKz", stg)], w=[("WK", acc)])
                            for j in range(1, 5):
                                P.dve(lambda e, stg=stg, acc=acc, fc=fc, nout=nout, j=j: e.scalar_tensor_tensor(
                                    out=WK[acc][:, 0:nout], in0=WK[stg][:, j:j + nout], scalar=cw[:, fc, j:j + 1],
                                    in1=WK[acc][:, 0:nout], op0=ALU.mult, op1=ALU.add),
                                    r=[("WK", stg), "cw", ("WK", acc)], w=[("WK", acc)])

                            def part2(acc=acc, sg=sg, nout=nout, ksc=ksc, dst=dst, row0=row0):
                                ob = next_ob()
                                P.act(lambda e: e.activation(
                                    out=WK[sg][:, 0:nout], in_=WK[acc][:, 0:nout], func=AF.Sigmoid),
                                    r=[("WK", acc)], w=[("WK", sg)])
                                P.dve(lambda e: e.scalar_tensor_tensor(
                                    out=OB[ob][:, 0:nout], in0=WK[acc][:, 0:nout], scalar=ksc, in1=WK[sg][:, 0:nout],
                                    op0=ALU.mult, op1=ALU.mult), r=[("WK", acc), ("WK", sg)], w=[("OB", ob)])
                                if stile == 0:
                                    P.dma("sp", dst[row0:row0 + 128, 0:nout - 2], OB[ob][:, 2:nout], r=[("OB", ob)])
                                else:
                                    P.dma("sp", dst[row0:row0 + 128, t0 - 2:t0 - 2 + nout], OB[ob][:, 0:nout], r=[("OB", ob)])

                            if prev2[0] is not None:
                                prev2[0]()
                            prev2[0] = part2
                            yield
                if prev2[0] is not None:
                    prev2[0]()

            g_tm = gen_tm_v()
            for _ in gen_conv():
                for _k in range(4):
                    next(g_tm, None)
            for _ in g_tm:
                pass

            fams = [(C_BK, T["dkT"], 1.0, t0)]
            if own:
                fams.append((C_BQ, T["dqT"], 0.125, t0))
            pend = [None]
            for (c0, dst, sc, tcol) in fams:
                for g4 in range(2):
                    ws = load_w(c0 + g4 * 512, 512)
                    for wc in range(4):
                        row0 = (g4 * 4 + wc) * 128
                        ob = next_ob()
                        for tq in range(4):
                            s = fm_mm(ws, wc, tq)
                            wk = next_wk()
                            P.act(lambda e, s=s, wk=wk: e.activation(out=WK[wk][:, 0:512], in_=pp[s][:, :], func=AF.Copy),
                                  r=[("pp", s)], w=[("WK", wk)])

                            def tail(wk=wk, tq=tq, ob=ob, sc=sc, dst=dst, row0=row0, tcol=tcol):
                                pr = tq % 2
                                cs = slice(tq * 512, (tq + 1) * 512)
                                P.pe(lambda e: e.matmul(prot[pr][:, :], rswap[:, :], WK[wk][:, 0:512], start=True, stop=True),
                                     r=[("WK", wk), "rswap"], w=[("prot", pr)])
                                P.dve(lambda e: e.scalar_tensor_tensor(
                                    out=WK[wk][:, 512:1024], in0=WK[wk][:, 0:512], scalar=sc, in1=cosT[:, cs],
                                    op0=ALU.mult, op1=ALU.mult), r=[("WK", wk), "cosT"], w=[("WK", wk)])
                                P.dve(lambda e: e.scalar_tensor_tensor(
                                    out=WK[wk][:, 1024:1536], in0=prot[pr][:, :], scalar=sc, in1=sinT[:, cs],
                                    op0=ALU.mult, op1=ALU.mult), r=[("prot", pr), "sinT"], w=[("WK", wk)])
                                P.dve(lambda e: e.tensor_tensor(
                                    out=OB[ob][:, cs], in0=WK[wk][:, 512:1024], in1=WK[wk][:, 1024:1536], op=ALU.add),
                                    r=[("WK", wk)], w=[("OB", ob)])
                                if tq == 3:
                                    P.dma("sp", dst[row0:row0 + 128, tcol:tcol + 2048], OB[ob][:, 0:2048], r=[("OB", ob)])

                            if pend[0] is not None:
                                pend[0]()
                            pend[0] = tail
            if pend[0] is not None:
                pend[0]()

            if own:
                for gi, c0 in enumerate((C_GA, C_GB)):
                    for g4 in range(2):
                        ws = load_w(c0 + g4 * 512, 512)
                        for wc in range(4):
                            cc = g4 * 4 + wc
                            ob = next_ob()
                            for tq in range(4):
                                s = fm_mm(ws, wc, tq)
                                P.act(lambda e, s=s, ob=ob, tq=tq, gi=gi, cc=cc: e.activation(
                                    out=OB[ob][:, tq * 512:(tq + 1) * 512], in_=pp[s][:, :], func=AF.Sigmoid,
                                    bias=gbfm[:, gi, cc:cc + 1]), r=[("pp", s), "gbfm"], w=[("OB", ob)])
                            P.dma("sp", T["sgT"][gi, cc * 128:(cc + 1) * 128, t0:t0 + 2048], OB[ob][:, 0:2048],
                                  r=[("OB", ob)])
                for cg in range(2):
                    wso = load_w(C_AO + cg * 512, 512)
                    wsz = load_w(C_AZ + cg * 512, 512)
                    ob = obt_slot[0] % 2
                    obt_slot[0] += 1
                    for tt in range(16):
                        so = tm_mm(wso, 512, tt)
                        sz = tm_mm(wsz, 512, tt)
                        wk = next_wk()
                        P.act(lambda e, so=so, wk=wk: e.activation(out=WK[wk][:, 0:512], in_=pp[so][:, :], func=AF.Sigmoid),
                              r=[("pp", so)], w=[("WK", wk)])
                        P.act(lambda e, sz=sz, wk=wk: e.activation(out=WK[wk][:, 512:1024], in_=pp[sz][:, :], func=AF.Sigmoid),
                              r=[("pp", sz)], w=[("WK", wk)])
                        P.dve(lambda e, sz=sz, wk=wk: e.tensor_tensor(out=WK[wk][:, 512:1024], in0=pp[sz][:, :],
                                                                      in1=WK[wk][:, 512:1024], op=ALU.mult),
                              r=[("pp", sz), ("WK", wk)], w=[("WK", wk)])
                        P.dve(lambda e, wk=wk, ob=ob, tt=tt: e.tensor_tensor(
                            out=OBT[ob][:, tt * 512:(tt + 1) * 512], in0=WK[wk][:, 0:512], in1=WK[wk][:, 512:1024],
                            op=ALU.mult), r=[("WK", wk)], w=[("OBT", ob)])
                    P.dma("sp", T["gA"][t0:t0 + 2048, cg * 512:(cg + 1) * 512].rearrange("(tt p) c -> p tt c", p=128),
                          OBT[ob][:, :].rearrange("p (tt c) -> p tt c", c=512), r=[("OBT", ob)])
                for cg in range(2):
                    wsz = load_w(C_BZ + cg * 512, 512)
                    ob = obt_slot[0] % 2
                    obt_slot[0] += 1
                    for tt in range(16):
                        sz = tm_mm(wsz, 512, tt)
                        wk = next_wk()
                        P.act(lambda e, sz=sz, wk=wk: e.activation(out=WK[wk][:, 0:512], in_=pp[sz][:, :], func=AF.Sigmoid),
                              r=[("pp", sz)], w=[("WK", wk)])
                        P.dve(lambda e, sz=sz, wk=wk, ob=ob, tt=tt: e.tensor_tensor(
                            out=OBT[ob][:, tt * 512:(tt + 1) * 512], in0=pp[sz][:, :], in1=WK[wk][:, 0:512],
                            op=ALU.mult), r=[("pp", sz), ("WK", wk)], w=[("OBT", ob)])
                    P.dma("sp", T["gB"][t0:t0 + 2048, cg * 512:(cg + 1) * 512].rearrange("(tt p) c -> p tt c", p=128),
                          OBT[ob][:, :].rearrange("p (tt c) -> p tt c", c=512), r=[("OBT", ob)])

        P.dma("sp", T["gall_d"][:, :], gall[:, :], r=["gall"])
        P.flush()


def build_rope_tables(P, posi, invf, cosT, sinT, WK, next_wk):
    TWO_PI = 2.0 * math.pi
    C1 = 6.28125
    C2 = TWO_PI - C1
    a = next_wk()
    b = next_wk()
    c = next_wk()
    A, Bk, R = WK[a], WK[b], WK[c]
    N = 2048
    P.dve(lambda e: e.tensor_copy(out=A[:, 0:N], in_=posi[:, :]), r=["posi"], w=[("WK", a)])
    P.dve(lambda e: e.tensor_scalar(out=A[:, 0:N], in0=A[:, 0:N], scalar1=invf[:, 0:1], scalar2=None, op0=ALU.mult),
          r=[("WK", a), "invf"], w=[("WK", a)])
    ki = posi
    P.dve(lambda e: e.tensor_scalar(out=Bk[:, 0:N], in0=A[:, 0:N], scalar1=1.0 / TWO_PI, scalar2=None, op0=ALU.mult),
          r=[("WK", a)], w=[("WK", b)])
    P.dve(lambda e: e.tensor_copy(out=ki[:, :], in_=Bk[:, 0:N]), r=[("WK", b)], w=["posi"])
    P.dve(lambda e: e.tensor_copy(out=Bk[:, 0:N], in_=ki[:, :]), r=["posi"], w=[("WK", b)])
    P.dve(lambda e: e.scalar_tensor_tensor(out=R[:, 0:N], in0=Bk[:, 0:N], scalar=-C1, in1=A[:, 0:N],
                                           op0=ALU.mult, op1=ALU.add), r=[("WK", a), ("WK", b)], w=[("WK", c)])
    P.dve(lambda e: e.scalar_tensor_tensor(out=R[:, 0:N], in0=Bk[:, 0:N], scalar=-C2, in1=R[:, 0:N],
                                           op0=ALU.mult, op1=ALU.add), r=[("WK", b), ("WK", c)], w=[("WK", c)])

    def wrap(X, key):
        P.dve(lambda e: e.tensor_scalar(out=Bk[:, 0:N], in0=X[:, 0:N], scalar1=math.pi, scalar2=-TWO_PI,
                                        op0=ALU.is_gt, op1=ALU.mult), r=[key], w=[("WK", b)])
        P.dve(lambda e: e.tensor_tensor(out=X[:, 0:N], in0=X[:, 0:N], in1=Bk[:, 0:N], op=ALU.add),
              r=[key, ("WK", b)], w=[key])
        P.dve(lambda e: e.tensor_scalar(out=Bk[:, 0:N], in0=X[:, 0:N], scalar1=-math.pi, scalar2=TWO_PI,
                                        op0=ALU.is_lt, op1=ALU.mult), r=[key], w=[("WK", b)])
        P.dve(lambda e: e.tensor_tensor(out=X[:, 0:N], in0=X[:, 0:N], in1=Bk[:, 0:N], op=ALU.add),
              r=[key, ("WK", b)], w=[key])

    wrap(R, ("WK", c))
    P.act(lambda e: e.activation(out=sinT[:, :], in_=R[:, 0:N], func=AF.Sin), r=[("WK", c)], w=["sinT"])
    P.dve(lambda e: e.tensor_scalar(out=R[:, 0:N], in0=R[:, 0:N], scalar1=math.pi / 2, scalar2=None, op0=ALU.add),
          r=[("WK", c)], w=[("WK", c)])
    wrap(R, ("WK", c))
    P.act(lambda e: e.activation(out=cosT[:, :], in_=R[:, 0:N], func=AF.Sin), r=[("WK", c)], w=["cosT"])


def phase_M(nc, P, T):
    mqT, mkT, mv, gA, yaT = T["mqT"], T["mkT"], T["mv"], T["gA"], T["yaT"]
    with contextlib.ExitStack() as st:
        sb = lambda name, shape, dt=F32: st.enter_context(nc.sbuf_tensor("M_" + name, list(shape), dt))
        ps = lambda name, shape, dt=F32: st.enter_context(nc.psum_tensor("M_" + name, list(shape), dt))
        qT = sb("qT", [128, 2, OWN], BF16)
        kT = sb("kT", [128, 2, S], BF16)
        va = sb("va", [128, 64, 257], BF16)
        ktok = sb("ktok", [128, 64, 256], BF16)
        hacc = sb("hacc", [128, 32, 256])
        gat = [sb("gat%d" % i, [128, 256], BF16) for i in range(2)]
        gall = sb("gall", [128, 1024])
        mlng = sb("mlng", [128, D])
        tri = sb("tri", [128, 4, 128])
        identf = sb("identf", [128, 128])
        identb = sb("identb", [128, 128], BF16)
        onesf = sb("onesf", [128, 128])
        LFt = [sb("LFt%d" % d, [128, 256]) for d in range(2)]
        Bc = [sb("Bc%d" % d, [128, 256]) for d in range(2)]
        Aa = [sb("Aa%d" % d, [128, 256]) for d in range(2)]
        EB = [sb("EB%d" % d, [128, 256]) for d in range(2)]
        WST = [sb("WST%d" % d, [128, 256]) for d in range(2)]
        DEC = [sb("DEC%d" % d, [128, 256]) for d in range(2)]
        Cst = [sb("Cst%d" % d, [128, 2, 257]) for d in range(2)]
        Cb = [[sb("Cb%d_%d" % (d, v), [128, 2, 257], BF16) for v in range(2)] for d in range(2)]
        cbver = [0, 0]
        dg = [sb("dg%d" % i, [128, 128]) for i in range(2)]
        Dm = [sb("Dm%d" % i, [128, 128]) for i in range(2)]
        Wm = [sb("Wm%d" % i, [128, 128], BF16) for i in range(2)]
        vw = [sb("vw%d" % i, [128, 257], BF16) for i in range(2)]
        tmpc = [sb("tmpc%d" % i, [128, 257]) for i in range(2)]
        tot = [sb("tot%d" % i, [128, 257]) for i in range(2)]
        sm = [sb("sm%d" % i, [128, 4]) for i in range(2)]
        hs = [sb("hs%d" % i, [128, 256]) for i in range(2)]
        sqv = [sb("sqv%d" % i, [128, 256]) for i in range(2)]
        yab = [sb("yab%d" % i, [128, 256], BF16) for i in range(2)]
        yas = [sb("yas%d" % i, [128, 2, 128], BF16) for i in range(2)]
        pD = ps("pD", [128, 512])
        pST = ps("pST", [128, 512])
        pIs = [ps("pI%d" % i, [128, 512]) for i in range(2)]
        pI = pIs[0]
        pC = ps("pC", [128, 512])
        pU = [ps("pU%d" % i, [128, 512]) for i in range(2)]
        ptrk = ps("ptrk", [128, 1024], BF16)
        ptry = ptrk

        P.dma("sp", gall[:], T["gall_d"], w=["gall"])
        P.dma("sp", mlng[:], T["mlng_rep"], w=["mlng"])
        P.dma("sp", tri[:], T["c_tri"], w=["tri"])
        P.dma("sp", identf[:], T["c_ident"], w=["identf"])
        P.dve(lambda e: e.tensor_copy(out=identb[:], in_=identf[:]), r=["identf"], w=["identb"])
        P.dve(lambda e: e.memset(onesf[:], 1.0), w=["onesf"])
        P.dve(lambda e: e.memset(va[:, :, 256:257], 1.0), w=["va1"])

        g4 = gall[:, :].rearrange("p (c g h) -> p c g h", g=4, h=4)
        v3 = lambda t: t[:, :].rearrange("p (c h) -> p c h", h=4)
        for d in range(2):
            i_d = g4[:, :, 2 * d, :]
            f_d = g4[:, :, 2 * d + 1, :]
            P.act(lambda e, d=d, f_d=f_d: e.activation(out=v3(LFt[d]), in_=f_d, func=AF.Exp, scale=-1.0),
                  r=["gall"], w=[("LFt", d)])
            P.act(lambda e, d=d: e.activation(out=LFt[d][:, :], in_=LFt[d][:, :], func=AF.Ln, bias=1.0),
                  r=[("LFt", d)], w=[("LFt", d)])
            P.pe(lambda e, d=d: e.matmul(pI[:, 0:256], tri[:, d, :], LFt[d][:, :], start=True, stop=True),
                 r=["tri", ("LFt", d)], w=[("pI", 0)])
            P.pe(lambda e, d=d: e.matmul(pC[:, 0:256], onesf[:, :], LFt[d][:, :], start=True, stop=True),
                 r=["onesf", ("LFt", d)], w=["pC"])
            P.dve(lambda e, d=d: e.tensor_scalar(out=Bc[d][:, :], in0=pI[:, 0:256], scalar1=-1.0, scalar2=None, op0=ALU.mult),
                  r=[("pI", 0)], w=[("Bc", d)])
            P.dve(lambda e, d=d, i_d=i_d: e.tensor_tensor(out=v3(Aa[d]), in0=pI[:, 0:256].rearrange("p (c h) -> p c h", h=4),
                                                         in1=i_d, op=ALU.add),
                  r=[("pI", 0), "gall"], w=[("Aa", d)])
            P.act(lambda e, d=d: e.activation(out=EB[d][:, :], in_=Bc[d][:, :], func=AF.Exp),
                  r=[("Bc", d)], w=[("EB", d)])
            P.dve(lambda e, d=d: e.tensor_copy(out=DEC[d][:, :], in_=pC[:, 0:256]), r=["pC"], w=[("DEC", d)])
            P.dve(lambda e, d=d: e.tensor_tensor(out=WST[d][:, :], in0=Aa[d][:, :], in1=DEC[d][:, :], op=ALU.subtract),
                  r=[("Aa", d), ("DEC", d)], w=[("WST", d)])
            P.act(lambda e, d=d: e.activation(out=WST[d][:, :], in_=WST[d][:, :], func=AF.Exp),
                  r=[("WST", d)], w=[("WST", d)])
            P.act(lambda e, d=d: e.activation(out=DEC[d][:, :], in_=DEC[d][:, :], func=AF.Exp, scale=-1.0),
                  r=[("DEC", d), ("WST", d)], w=[("DEC", d)])

        cnt = {"o": 0, "u": 0, "g": 0, "y": 0}

        def output_A1(h, d, c):
            i = cnt["o"] % 2
            cnt["o"] += 1
            col = c * 4 + h
            cs = slice(c * 128, (c + 1) * 128)
            pIx = pIs[i]
            P.dve(lambda e: e.tensor_scalar(out=dg[i][:], in0=identf[:], scalar1=Bc[d][:, col:col + 1], scalar2=None, op0=ALU.mult),
                  r=["identf", ("Bc", d)], w=[("dg", i)])
            P.pe(lambda e: e.matmul(pD[:, 0:128], onesf[:, :], dg[i][:, :], start=True, stop=False), r=["onesf", ("dg", i)], w=["pD"])
            P.pe(lambda e: e.matmul(pD[:, 0:128], identf[:, :], tri[:, 2 + d, :], start=False, stop=True), r=["identf", "tri"], w=["pD"])
            P.act(lambda e: e.activation(out=Dm[i][:], in_=pD[:, 0:128], func=AF.Exp, bias=Aa[d][:, col:col + 1]),
                  r=["pD", ("Aa", d)], w=[("Dm", i)])
            for dkc in range(2):
                P.pe(lambda e, dkc=dkc: e.matmul(pST[:, 0:128], kT[:, dkc, cs], qT[:, dkc, cs], start=(dkc == 0), stop=(dkc == 1)),
                     r=["kT", "qT"], w=["pST"])
            P.dve(lambda e: e.tensor_tensor(out=Wm[i][:], in0=pST[:, 0:128], in1=Dm[i][:], op=ALU.mult),
                  r=["pST", ("Dm", i)], w=[("Wm", i)])
            P.pe(lambda e: e.matmul(pIx[:, 0:257], Wm[i][:, :], va[:, c, :], start=True, stop=True),
                 r=[("Wm", i), "va", "va1"], w=[("pI", i)])
            return (h, d, c, i)

        def output_A2(ctx):
            h, d, c, i = ctx
            col = c * 4 + h
            cs = slice(c * 128, (c + 1) * 128)
            pIx = pIs[i]
            cbv = Cb[d][cbver[d]]
            cbk = ("Cb", d, cbver[d])
            for dkc in range(2):
                P.pe(lambda e, dkc=dkc: e.matmul(pC[:, 0:257], qT[:, dkc, cs], cbv[:, dkc, :], start=(dkc == 0), stop=(dkc == 1)),
                     r=["qT", cbk], w=["pC"])
            P.act(lambda e: e.activation(out=tmpc[i][:], in_=pC[:, 0:257], func=AF.Copy, scale=EB[d][:, col:col + 1]),
                  r=["pC", ("EB", d)], w=[("tmpc", i)])
            P.dve(lambda e: e.tensor_tensor(out=tot[i][:], in0=tmpc[i][:], in1=pIx[:, 0:257], op=ALU.add),
                  r=[("tmpc", i), ("pI", i)], w=[("tot", i)])

        def output_B(ctx):
            h, d, c, i = ctx
            col = c * 4 + h
            P.dve(lambda e: e.tensor_scalar(out=sm[i][:, 3:4], in0=tot[i][:, 256:257], scalar1=-1.0, scalar2=1.0, op0=ALU.mult, op1=ALU.max),
                  r=[("tot", i)], w=[("sm", i)])
            P.dve(lambda e: e.tensor_tensor(out=sm[i][:, 0:1], in0=tot[i][:, 256:257], in1=sm[i][:, 3:4], op=ALU.max),
                  r=[("tot", i), ("sm", i)], w=[("sm", i)])
            P.dve(lambda e: e.reciprocal(out=sm[i][:, 1:2], in_=sm[i][:, 0:1]), r=[("sm", i)], w=[("sm", i)])
            if d == 0:
                P.dve(lambda e: e.tensor_scalar(out=hacc[:, c, :], in0=tot[i][:, 0:256], scalar1=sm[i][:, 1:2], scalar2=None, op0=ALU.mult),
                      r=[("tot", i), ("sm", i)], w=[("hacc", c)])
                return
            gi = cnt["g"] % 2
            cnt["g"] += 1
            P.dma("sp", gat[gi][:], gA[c * 128:(c + 1) * 128, h * 256:(h + 1) * 256], w=[("gat", gi)])
            P.dve(lambda e: e.scalar_tensor_tensor(out=hs[i][:], in0=tot[i][:, 0:256], scalar=sm[i][:, 1:2], in1=hacc[:, c, :],
                                                   op0=ALU.mult, op1=ALU.add),
                  r=[("tot", i), ("sm", i), ("hacc", c)], w=[("hs", i)])
            P.dve(lambda e: e.tensor_tensor(out=sqv[i][:], in0=hs[i][:], in1=hs[i][:], op=ALU.mult), r=[("hs", i)], w=[("sqv", i)])
            P.dve(lambda e: e.reduce_sum(out=sm[i][:, 2:3], in_=sqv[i][:], axis=AX.X), r=[("sqv", i)], w=[("smb", i)])
            P.act(lambda e: e.activation(out=sm[i][:, 2:3], in_=sm[i][:, 2:3], func=AF.Ln, scale=1.0 / 256.0, bias=NORM_EPS),
                  r=[("smb", i)], w=[("smb", i)])
            P.act(lambda e: e.activation(out=sm[i][:, 2:3], in_=sm[i][:, 2:3], func=AF.Exp, scale=-0.5),
                  r=[("smb", i)], w=[("smb", i)])
            P.dve(lambda e: e.scalar_tensor_tensor(out=sqv[i][:], in0=hs[i][:], scalar=sm[i][:, 2:3],
                                                   in1=mlng[:, h * 256:(h + 1) * 256], op0=ALU.mult, op1=ALU.mult),
                  r=[("hs", i), ("smb", i), "mlng"], w=[("sqv", i)])
            P.dve(lambda e: e.tensor_tensor(out=yab[i][:], in0=sqv[i][:], in1=gat[gi][:], op=ALU.mult),
                  r=[("sqv", i), ("gat", gi)], w=[("yab", i)])
            def b2():
                for k in range(2):
                    P.pe(lambda e, k=k: e.transpose(out=ptry[:, k * 128:(k + 1) * 128], in_=yab[i][:, k * 128:(k + 1) * 128],
                                                    identity=identb[:]), r=[("yab", i), "identb"], w=["ptrk"])
                P.act(lambda e: e.activation(out=yas[i][:, :, :], in_=ptry[:, 0:256].rearrange("p (k t) -> p k t", k=2), func=AF.Copy),
                      r=["ptrk"], w=[("yas", i)])
                P.dma("sp", yaT[h * 256:(h + 1) * 256, c * 128:(c + 1) * 128].rearrange("(k p) t -> p k t", p=128),
                      yas[i][:, :, :], r=[("yas", i)])
            return b2

        def update_step(h, d, c):
            i = cnt["u"] % 2
            cnt["u"] += 1
            col = c * 4 + h
            nv = 1 - cbver[d]
            nb_ = Cb[d][nv]
            nbk = ("Cb", d, nv)
            P.act(lambda e: e.activation(out=vw[i][:], in_=va[:, c, :], func=AF.Copy, scale=WST[d][:, col:col + 1]),
                  r=["va", "va1", ("WST", d)], w=[("vw", i)])
            for dkc in range(2):
                P.pe(lambda e, dkc=dkc: e.matmul(pU[dkc][:, 0:257], ktok[:, c, dkc * 128:(dkc + 1) * 128], vw[i][:, :], start=True, stop=True),
                     r=[("ktok", c), ("vw", i)], w=[("pU", dkc)])
                P.dve(lambda e, dkc=dkc: e.scalar_tensor_tensor(out=Cst[d][:, dkc, :], in0=Cst[d][:, dkc, :], scalar=DEC[d][:, col:col + 1],
                                                                in1=pU[dkc][:, 0:257], op0=ALU.mult, op1=ALU.add),
                      r=[("Cst", d, dkc), ("DEC", d), ("pU", dkc)], w=[("Cst", d, dkc)])
                P.act(lambda e, dkc=dkc, nb_=nb_: e.activation(out=nb_[:, dkc, :], in_=Cst[d][:, dkc, :], func=AF.Copy),
                      r=[("Cst", d, dkc)], w=[nbk])
            cbver[d] = nv

        for h in range(4):
            for dkc in range(2):
                r0 = h * 256 + dkc * 128
                P.dma("sp", qT[:, dkc, :], mqT[r0:r0 + 128, 0:OWN], w=["qT"])
                P.dma("sp", kT[:, dkc, :], mkT[r0:r0 + 128, :], w=["kT"])
            P.dma("sp", va[:, :, 0:256], mv[:, h * 256:(h + 1) * 256].rearrange("(c p) f -> p c f", p=128), w=["va"])
            for d in range(2):
                P.dve(lambda e, d=d: e.memset(Cst[d][:], 0.0), w=[("Cst", d, 0), ("Cst", d, 1)])
                P.dve(lambda e, d=d: e.memset(Cb[d][0][:], 0.0), w=[("Cb", d, 0)])
                cbver[d] = 0
            for c in range(64):
                for dkc in range(2):
                    P.pe(lambda e, c=c, dkc=dkc: e.transpose(out=ptrk[:, dkc * 128:(dkc + 1) * 128],
                                                             in_=kT[:, dkc, c * 128:(c + 1) * 128], identity=identb[:]),
                         r=["kT", "identb"], w=["ptrk"])
                if c % 2 == 0:
                    P.act(lambda e, c=c: e.activation(out=ktok[:, c, :], in_=ptrk[:, 0:256], func=AF.Copy), r=["ptrk"], w=[("ktok", c)])
                else:
                    P.dve(lambda e, c=c: e.tensor_copy(out=ktok[:, c, :], in_=ptrk[:, 0:256]), r=["ptrk"], w=[("ktok", c)])
            ctx = output_A1(h, 0, 0)
            for i in range(32):
                output_A2(ctx)
                if i < 31:
                    update_step(h, 0, i)
                update_step(h, 1, 63 - i)
                nctx = output_A1(h, 0, i + 1) if i < 31 else output_A1(h, 1, 31)
                output_B(ctx)
                ctx = nctx
            pb2 = None
            for i in range(32, 64):
                c = 63 - i
                output_A2(ctx)
                if c > 0:
                    update_step(h, 1, c)
                    nctx = output_A1(h, 1, c - 1)
                if pb2 is not None:
                    pb2()
                pb2 = output_B(ctx)
                ctx = nctx
            pb2()
        P.flush()


def phase_M2(nc, P, T):
    mqT, mkT, mv, gA, yaT, hfwd = T["mqT"], T["mkT"], T["mv"], T["gA"], T["yaT"], T["hfwd"]
    with contextlib.ExitStack() as st:
        sb = lambda name, shape, dt=F32: st.enter_context(nc.sbuf_tensor("M2_" + name, list(shape), dt))
        ps = lambda name, shape, dt=F32: st.enter_context(nc.psum_tensor("M2_" + name, list(shape), dt))
        gall = sb("gall", [128, 1024])
        mlng = sb("mlng", [128, D])
        tri = sb("tri", [128, 4, 128])
        identf = sb("identf", [128, 128])
        identb = sb("identb", [128, 128], BF16)
        onesf = sb("onesf", [128, 128])
        LFt = [sb("LFt%d" % d, [128, 256]) for d in range(2)]
        Bc = [sb("Bc%d" % d, [128, 256]) for d in range(2)]
        Aa = [sb("Aa%d" % d, [128, 256]) for d in range(2)]
        EB = [sb("EB%d" % d, [128, 256]) for d in range(2)]
        WST = [sb("WST%d" % d, [128, 256]) for d in range(2)]
        DEC = [sb("DEC%d" % d, [128, 256]) for d in range(2)]
        Cst = [sb("Cst%d" % d, [128, 4, 2, 257]) for d in range(2)]
        Cb = [[sb("Cb%d_%d" % (d, v), [128, 4, 2, 257], BF16) for v in range(2)] for d in range(2)]
        cbver = [0, 0]
        NQ, NK = 2, 4
        qc = [sb("qc%d" % i, [128, 8, 128], BF16) for i in range(NQ)]
        kc = [sb("kc%d" % i, [128, 8, 128], BF16) for i in range(NK)]
        vc = [sb("vc%d" % i, [128, 4, 257], BF16) for i in range(NK)]
        ktc = [sb("ktc%d" % i, [128, 1024], BF16) for i in range(NK)]
        gac = [sb("gac%d" % i, [128, 1024], BF16) for i in range(2)]
        hfr = [sb("hfr%d" % i, [128, 1024]) for i in range(2)]
        hfw = [sb("hfw%d" % i, [128, 1024]) for i in range(2)]
        yasc = [sb("yasc%d" % i, [128, 8, 128], BF16) for i in range(2)]
        NWB = 4
        dg = [sb("dg%d" % i, [128, 128]) for i in range(NWB)]
        Dm = [sb("Dm%d" % i, [128, 128]) for i in range(NWB)]
        Wm = [sb("Wm%d" % i, [128, 128], BF16) for i in range(NWB)]
        vw = [sb("vw%d" % i, [128, 257], BF16) for i in range(NWB)]
        tmpc = [sb("tmpc%d" % i, [128, 257]) for i in range(NWB)]
        tot = [sb("tot%d" % i, [128, 257]) for i in range(NWB)]
        sm = [sb("sm%d" % i, [128, 4]) for i in range(NWB)]
        hs = [sb("hs%d" % i, [128, 256]) for i in range(NWB)]
        sqv = [sb("sqv%d" % i, [128, 256]) for i in range(NWB)]
        yab = [sb("yab%d" % i, [128, 256], BF16) for i in range(NWB)]
        pD = ps("pD", [128, 512])
        pST = ps("pST", [128, 512])
        pI = ps("pI", [128, 512])
        pC = ps("pC", [128, 512])
        pU = [ps("pU%d" % i, [128, 512]) for i in range(2)]
        ptrk = ps("ptrk", [128, 1024], BF16)
        ptry = ps("ptry", [128, 1024], BF16)

        P.dma("sp", gall[:], T["gall_d"], w=["gall"])
        P.dma("sp", mlng[:], T["mlng_rep"], w=["mlng"])
        P.dma("sp", tri[:], T["c_tri"], w=["tri"])
        P.dma("sp", identf[:], T["c_ident"], w=["identf"])
        P.dve(lambda e: e.tensor_copy(out=identb[:], in_=identf[:]), r=["identf"], w=["identb"])
        P.dve(lambda e: e.memset(onesf[:], 1.0), w=["onesf"])
        for i in range(NK):
            P.dve(lambda e, i=i: e.memset(vc[i][:, :, 256:257], 1.0), w=[("vc1", i)])
        for d in range(2):
            P.dve(lambda e, d=d: e.memset(Cst[d][:], 0.0), w=[("Cst", d, h, k) for h in range(4) for k in range(2)])
            P.dve(lambda e, d=d: e.memset(Cb[d][0][:], 0.0), w=[("Cb", d, 0, h) for h in range(4)])

        g4 = gall[:, :].rearrange("p (c g h) -> p c g h", g=4, h=4)
        v3 = lambda t: t[:, :].rearrange("p (c h) -> p c h", h=4)
        for d in range(2):
            i_d = g4[:, :, 2 * d, :]
            f_d = g4[:, :, 2 * d + 1, :]
            P.act(lambda e, d=d, f_d=f_d: e.activation(out=v3(LFt[d]), in_=f_d, func=AF.Exp, scale=-1.0),
                  r=["gall"], w=[("LFt", d)])
            P.act(lambda e, d=d: e.activation(out=LFt[d][:, :], in_=LFt[d][:, :], func=AF.Ln, bias=1.0),
                  r=[("LFt", d)], w=[("LFt", d)])
            P.pe(lambda e, d=d: e.matmul(pI[:, 0:256], tri[:, d, :], LFt[d][:, :], start=True, stop=True),
                 r=["tri", ("LFt", d)], w=["pI"])
            P.pe(lambda e, d=d: e.matmul(pC[:, 0:256], onesf[:, :], LFt[d][:, :], start=True, stop=True),
                 r=["onesf", ("LFt", d)], w=["pC"])
            P.dve(lambda e, d=d: e.tensor_scalar(out=Bc[d][:, :], in0=pI[:, 0:256], scalar1=-1.0, scalar2=None, op0=ALU.mult),
                  r=["pI"], w=[("Bc", d)])
            P.dve(lambda e, d=d, i_d=i_d: e.tensor_tensor(out=v3(Aa[d]), in0=pI[:, 0:256].rearrange("p (c h) -> p c h", h=4),
                                                         in1=i_d, op=ALU.add),
                  r=["pI", "gall"], w=[("Aa", d)])
            P.act(lambda e, d=d: e.activation(out=EB[d][:, :], in_=Bc[d][:, :], func=AF.Exp),
                  r=[("Bc", d)], w=[("EB", d)])
            P.dve(lambda e, d=d: e.tensor_copy(out=DEC[d][:, :], in_=pC[:, 0:256]), r=["pC"], w=[("DEC", d)])
            P.dve(lambda e, d=d: e.tensor_tensor(out=WST[d][:, :], in0=Aa[d][:, :], in1=DEC[d][:, :], op=ALU.subtract),
                  r=[("Aa", d), ("DEC", d)], w=[("WST", d)])
            P.act(lambda e, d=d: e.activation(out=WST[d][:, :], in_=WST[d][:, :], func=AF.Exp),
                  r=[("WST", d)], w=[("WST", d)])
            P.act(lambda e, d=d: e.activation(out=DEC[d][:, :], in_=DEC[d][:, :], func=AF.Exp, scale=-1.0),
                  r=[("DEC", d), ("WST", d)], w=[("DEC", d)])

        cnt = {"w": 0, "k": 0, "q": 0, "g": 0, "cp": 0}

        def load_chunk(c, need_q, need_bw_out):
            ks = cnt["k"] % NK
            cnt["k"] += 1
            cs = slice(c * 128, (c + 1) * 128)
            P.dma("sp", kc[ks][:], mkT[:, cs].rearrange("(f p) t -> p f t", p=128), w=[("kc", ks)])
            P.dma("sp", vc[ks][:, :, 0:256], mv[cs, :].rearrange("p (h f) -> p h f", h=4), w=[("vc", ks)])
            for f in range(8):
                P.pe(lambda e, f=f: e.transpose(out=ptrk[:, f * 128:(f + 1) * 128], in_=kc[ks][:, f, :], identity=identb[:]),
                     r=[("kc", ks), "identb"], w=["ptrk"])
            if cnt["cp"] % 2 == 0:
                P.act(lambda e: e.activation(out=ktc[ks][:, :], in_=ptrk[:, :], func=AF.Copy), r=["ptrk"], w=[("ktc", ks)])
            else:
                P.dve(lambda e: e.tensor_copy(out=ktc[ks][:, :], in_=ptrk[:, :]), r=["ptrk"], w=[("ktc", ks)])
            cnt["cp"] += 1
            ctx = {"c": c, "ks": ks}
            if need_q:
                qs = cnt["q"] % NQ
                cnt["q"] += 1
                P.dma("sp", qc[qs][:], mqT[:, cs].rearrange("(f p) t -> p f t", p=128), w=[("qc", qs)])
                ctx["qs"] = qs
            return ctx

        def load_bw_extra(ctx):
            c = ctx["c"]
            cs = slice(c * 128, (c + 1) * 128)
            gs = cnt["g"] % 2
            cnt["g"] += 1
            P.dma("sp", gac[gs][:], gA[cs, :], w=[("gac", gs)])
            P.dma("sp", hfr[gs][:], hfwd[cs, :], r=[("hfwd", c)], w=[("hfr", gs)])
            ctx["gs"] = gs

        def output_A(ck, h, d):
            i = cnt["w"] % NWB
            cnt["w"] += 1
            c, ks, qs = ck["c"], ck["ks"], ck["qs"]
            col = c * 4 + h
            cbv = Cb[d][cbver[d]]
            cbk = ("Cb", d, cbver[d], h)
            P.dve(lambda e: e.tensor_scalar(out=dg[i][:], in0=identf[:], scalar1=Bc[d][:, col:col + 1], scalar2=None, op0=ALU.mult),
                  r=["identf", ("Bc", d)], w=[("dg", i)])
            P.pe(lambda e: e.matmul(pD[:, 0:128], onesf[:, :], dg[i][:, :], start=True, stop=False), r=["onesf", ("dg", i)], w=["pD"])
            P.pe(lambda e: e.matmul(pD[:, 0:128], identf[:, :], tri[:, 2 + d, :], start=False, stop=True), r=["identf", "tri"], w=["pD"])
            P.act(lambda e: e.activation(out=Dm[i][:], in_=pD[:, 0:128], func=AF.Exp, bias=Aa[d][:, col:col + 1]),
                  r=["pD", ("Aa", d)], w=[("Dm", i)])
            for dkc in range(2):
                P.pe(lambda e, dkc=dkc: e.matmul(pST[:, 0:128], kc[ks][:, h * 2 + dkc, :], qc[qs][:, h * 2 + dkc, :],
                                                 start=(dkc == 0), stop=(dkc == 1)),
                     r=[("kc", ks), ("qc", qs)], w=["pST"])
            for dkc in range(2):
                P.pe(lambda e, dkc=dkc: e.matmul(pC[:, 0:257], qc[qs][:, h * 2 + dkc, :], cbv[:, h, dkc, :],
                                                 start=(dkc == 0), stop=(dkc == 1)),
                     r=[("qc", qs), cbk], w=["pC"])
            P.act(lambda e: e.activation(out=tmpc[i][:], in_=pC[:, 0:257], func=AF.Copy, scale=EB[d][:, col:col + 1]),
                  r=["pC", ("EB", d)], w=[("tmpc", i)])
            P.dve(lambda e: e.tensor_tensor(out=Wm[i][:], in0=pST[:, 0:128], in1=Dm[i][:], op=ALU.mult),
                  r=["pST", ("Dm", i)], w=[("Wm", i)])
            P.pe(lambda e: e.matmul(pI[:, 0:257], Wm[i][:, :], vc[ks][:, h, :], start=True, stop=True),
                 r=[("Wm", i), ("vc", ks), ("vc1", ks)], w=["pI"])
            P.dve(lambda e: e.tensor_tensor(out=tot[i][:], in0=tmpc[i][:], in1=pI[:, 0:257], op=ALU.add),
                  r=[("tmpc", i), "pI"], w=[("tot", i)])
            return (ck, h, d, i)

        def output_B(ctx, ws):
            ck, h, d, i = ctx
            c = ck["c"]
            hsl = slice(h * 256, (h + 1) * 256)
            P.dve(lambda e: e.tensor_scalar(out=sm[i][:, 3:4], in0=tot[i][:, 256:257], scalar1=-1.0, scalar2=1.0, op0=ALU.mult, op1=ALU.max),
                  r=[("tot", i)], w=[("sm", i)])
            P.dve(lambda e: e.tensor_tensor(out=sm[i][:, 0:1], in0=tot[i][:, 256:257], in1=sm[i][:, 3:4], op=ALU.max),
                  r=[("tot", i), ("sm", i)], w=[("sm", i)])
            P.dve(lambda e: e.reciprocal(out=sm[i][:, 1:2], in_=sm[i][:, 0:1]), r=[("sm", i)], w=[("sm", i)])
            if d == 0:
                P.act(lambda e: e.activation(out=hfw[ws][:, hsl], in_=tot[i][:, 0:256], func=AF.Copy, scale=sm[i][:, 1:2]),
                      r=[("tot", i), ("sm", i)], w=[("hfw", ws, h)])
                return
            gs = ck["gs"]
            P.dve(lambda e: e.scalar_tensor_tensor(out=hs[i][:], in0=tot[i][:, 0:256], scalar=sm[i][:, 1:2], in1=hfr[gs][:, hsl],
                                                   op0=ALU.mult, op1=ALU.add),
                  r=[("tot", i), ("sm", i), ("hfr", gs)], w=[("hs", i)])
            P.dve(lambda e: e.tensor_tensor(out=sqv[i][:], in0=hs[i][:], in1=hs[i][:], op=ALU.mult), r=[("hs", i)], w=[("sqv", i)])
            P.dve(lambda e: e.reduce_sum(out=sm[i][:, 2:3], in_=sqv[i][:], axis=AX.X), r=[("sqv", i)], w=[("smb", i)])
            P.act(lambda e: e.activation(out=sm[i][:, 2:3], in_=sm[i][:, 2:3], func=AF.Ln, scale=1.0 / 256.0, bias=NORM_EPS),
                  r=[("smb", i)], w=[("smb", i)])
            P.act(lambda e: e.activation(out=sm[i][:, 2:3], in_=sm[i][:, 2:3], func=AF.Exp, scale=-0.5),
                  r=[("smb", i)], w=[("smb", i)])
            P.dve(lambda e: e.scalar_tensor_tensor(out=sqv[i][:], in0=hs[i][:], scalar=sm[i][:, 2:3],
                                                   in1=mlng[:, hsl], op0=ALU.mult, op1=ALU.mult),
                  r=[("hs", i), ("smb", i), "mlng"], w=[("sqv", i)])
            P.dve(lambda e: e.tensor_tensor(out=yab[i][:], in0=sqv[i][:], in1=gac[gs][:, hsl], op=ALU.mult),
                  r=[("sqv", i), ("gac", gs)], w=[("yab", i)])
            for k in range(2):
                f = h * 2 + k
                P.pe(lambda e, k=k, f=f: e.transpose(out=ptry[:, f * 128:(f + 1) * 128], in_=yab[i][:, k * 128:(k + 1) * 128],
                                                     identity=identb[:]), r=[("yab", i), "identb"], w=["ptry"])

        def update_step(ck, h, d):
            i = cnt["w"] % NWB
            cnt["w"] += 1
            c, ks = ck["c"], ck["ks"]
            col = c * 4 + h
            nv = 1 - cbver[d]
            nb_ = Cb[d][nv]
            nbk = ("Cb", d, nv, h)
            P.act(lambda e: e.activation(out=vw[i][:], in_=vc[ks][:, h, :], func=AF.Copy, scale=WST[d][:, col:col + 1]),
                  r=[("vc", ks), ("vc1", ks), ("WST", d)], w=[("vw", i)])
            for dkc in range(2):
                f = h * 2 + dkc
                P.pe(lambda e, dkc=dkc, f=f: e.matmul(pU[dkc][:, 0:257], ktc[ks][:, f * 128:(f + 1) * 128], vw[i][:, :], start=True, stop=True),
                     r=[("ktc", ks), ("vw", i)], w=[("pU", dkc)])
                P.dve(lambda e, dkc=dkc: e.scalar_tensor_tensor(out=Cst[d][:, h, dkc, :], in0=Cst[d][:, h, dkc, :], scalar=DEC[d][:, col:col + 1],
                                                                in1=pU[dkc][:, 0:257], op0=ALU.mult, op1=ALU.add),
                      r=[("Cst", d, h, dkc), ("DEC", d), ("pU", dkc)], w=[("Cst", d, h, dkc)])
                P.act(lambda e, dkc=dkc: e.activation(out=nb_[:, h, dkc, :], in_=Cst[d][:, h, dkc, :], func=AF.Copy),
                      r=[("Cst", d, h, dkc)], w=[nbk])

        def loads_for(step):
            if step < 32:
                return [load_chunk(step, True, False), load_chunk(63 - step, False, False)]
            return [load_chunk(63 - step, True, True)]

        nxt = loads_for(0)
        for step in range(64):
            cur = nxt
            if step + 1 < 64:
                nxt = loads_for(step + 1)
            ws = step % 2
            if step < 32:
                ca, cbk_ = cur
                ctxs = [output_A(ca, h, 0) for h in range(4)]
                if step < 31:
                    for h in range(4):
                        update_step(ca, h, 0)
                    cbver[0] = 1 - cbver[0]
                for h in range(4):
                    update_step(cbk_, h, 1)
                cbver[1] = 1 - cbver[1]
                for cx in ctxs:
                    output_B(cx, ws)
                c = ca["c"]
                P.dma("sp", hfwd[c * 128:(c + 1) * 128, :], hfw[ws][:, :], r=[("hfw", ws, h) for h in range(4)], w=[("hfwd", c)])
                if step == 31:
                    load_bw_extra(nxt[0])
            else:
                ca = cur[0]
                c = ca["c"]
                ctxs = [output_A(ca, h, 1) for h in range(4)]
                if c > 0:
                    for h in range(4):
                        update_step(ca, h, 1)
                    cbver[1] = 1 - cbver[1]
                for cx in ctxs:
                    output_B(cx, ws)
                P.act(lambda e, ws=ws: e.activation(out=yasc[ws][:, :, :], in_=ptry[:, :].rearrange("p (f t) -> p f t", f=8), func=AF.Copy),
                      r=["ptry"], w=[("yasc", ws)])
                P.dma("sp", yaT[:, c * 128:(c + 1) * 128].rearrange("(f p) t -> p f t", p=128), yasc[ws][:, :, :], r=[("yasc", ws)])
                if step + 1 < 64:
                    load_bw_extra(nxt[0])
        P.flush()


def phase_D(nc, P, T):
    dqT, dkT, dv, gB, ybT = T["dqT"], T["dkT"], T["dv"], T["gB"], T["ybT"]
    with contextlib.ExitStack() as st:
        sb = lambda name, shape, dt=F32: st.enter_context(nc.sbuf_tensor("D_" + name, list(shape), dt))
        ps = lambda name, shape, dt=F32: st.enter_context(nc.psum_tensor("D_" + name, list(shape), dt))
        kT = [sb("dkT%d" % i, [128, S], BF16) for i in range(2)]
        va = [sb("dva%d" % i, [128, 64, 129], BF16) for i in range(2)]
        qT = [sb("dqT%d" % i, [128, OWN], BF16) for i in range(2)]
        NE = 3
        E = [sb("dE%d" % i, [128, 1024], BF16) for i in range(NE)]
        gbt = [sb("dgb%d" % i, [128, 4, 128], BF16) for i in range(2)]
        lamr = sb("lamr", [128, 256])
        ltmp = sb("ltmp", [128, 128])
        lam = sb("lam", [128, 4])
        subg = sb("subg", [128, 128])
        identf = sb("identf", [128, 128])
        identb = sb("identb", [128, 128], BF16)
        r12 = [sb("r12_%d" % i, [128, 4]) for i in range(2)]
        o1 = [sb("o1_%d" % i, [128, 128]) for i in range(2)]
        o2 = [sb("o2_%d" % i, [128, 128]) for i in range(2)]
        sq = [sb("sq_%d" % i, [128, 128]) for i in range(2)]
        ybq = [sb("ybq%d" % i, [128, 128], BF16) for i in range(8)]
        ybs = [sb("ybs%d" % i, [128, 512], BF16) for i in range(2)]
        pS = [ps("pS%d" % i, [128, 1024]) for i in range(2)]
        pacc = [ps("pacc%d" % i, [128, 512]) for i in range(3)]
        ptr = ps("dptr", [128, 512], BF16)

        P.dma("sp", lamr[:], T["lam_rep"], w=["lamr"])
        P.dma("sp", subg[:], T["subln_rep"], w=["subg"])
        P.dma("sp", identf[:], T["c_ident"], w=["identf"])
        P.dve(lambda e: e.tensor_copy(out=identb[:], in_=identf[:]), r=["identf"], w=["identb"])
        P.dve(lambda e: e.tensor_scalar(out=subg[:], in0=subg[:], scalar1=(1.0 - LAMBDA_INIT), scalar2=None, op0=ALU.mult),
              r=["subg"], w=["subg"])
        P.dve(lambda e: e.tensor_tensor(out=ltmp[:, 0:64], in0=lamr[:, 0:64], in1=lamr[:, 64:128], op=ALU.mult),
              r=["lamr"], w=["ltmp"])
        P.dve(lambda e: e.tensor_tensor(out=ltmp[:, 64:128], in0=lamr[:, 128:192], in1=lamr[:, 192:256], op=ALU.mult),
              r=["lamr"], w=["ltmp"])
        P.dve(lambda e: e.reduce_sum(out=lam[:, 0:2], in_=ltmp[:, :].rearrange("p (a b) -> p a b", a=2), axis=AX.X),
              r=["ltmp"], w=["lam"])
        P.act(lambda e: e.activation(out=lam[:, 0:2], in_=lam[:, 0:2], func=AF.Exp), r=["lam"], w=["lam"])
        P.dve(lambda e: e.tensor_tensor(out=lam[:, 2:3], in0=lam[:, 0:1], in1=lam[:, 1:2], op=ALU.subtract),
              r=["lam"], w=["lam"])
        P.dve(lambda e: e.tensor_scalar(out=lam[:, 3:4], in0=lam[:, 2:3], scalar1=LAMBDA_INIT, scalar2=-1.0,
                                        op0=ALU.add, op1=ALU.mult), r=["lam"], w=["lam"])
        for i in range(2):
            P.dve(lambda e, i=i: e.memset(va[i][:, :, 128:129], 1.0), w=[("va1", i)])

        def acc_ap(a, lo, hi):
            return pacc[a // 3][:, (a % 3) * 129 + lo:(a % 3) * 129 + hi]

        accS = [sb("accS%d" % i, [128, 8 * 129]) for i in range(2)]
        mhalf = sb("mhalf", [128, 1])
        P.dve(lambda e: e.memset(mhalf[:], -0.5), w=["mhalf"])

        def load_head(h):
            hs = h % 2
            P.dma("sp", kT[hs][:], dkT[h * 128:(h + 1) * 128, :], w=[("kT", hs)])
            P.dma("sp", qT[hs][:], dqT[h * 128:(h + 1) * 128, :], w=[("qT", hs)])
            P.dma("sp", va[hs][:, :, 0:128], dv[:, h * 128:(h + 1) * 128].rearrange("(kb p) c -> p kb c", p=128),
                  w=[("va", hs)])

        def load_gb(h, qt):
            gs = (h * 8 + qt) % 2
            P.dma("sp", gbt[gs][:], gB[qt * 512:(qt + 1) * 512, h * 128:(h + 1) * 128].rearrange("(u p) c -> p u c", p=128),
                  w=[("gbt", gs)])

        def qk(i, h, qt, kb):
            sl, hs = i % 2, h % 2
            for j in range(2):
                P.pe(lambda e, j=j: e.matmul(
                    pS[sl][:, j * 512:(j + 1) * 512], kT[hs][j * 64:(j + 1) * 64, kb * 128:(kb + 1) * 128],
                    qT[hs][j * 64:(j + 1) * 64, qt * 512:(qt + 1) * 512], start=True, stop=True),
                    r=[("kT", hs), ("qT", hs)], w=[("pS", sl)])

        def ex(i):
            sl, es = i % 2, i % NE
            P.act(lambda e: e.activation(out=E[es][:, :], in_=pS[sl][:, :], func=AF.Exp), r=[("pS", sl)], w=[("E", es)])

        def pv(i, h, kb):
            es, hs = i % NE, h % 2
            for a in range(8):
                j, u = a // 4, a % 4
                P.pe(lambda e, a=a, j=j, u=u: e.matmul(
                    acc_ap(a, 0, 129), E[es][:, j * 512 + u * 128:j * 512 + (u + 1) * 128], va[hs][:, kb, :],
                    start=(kb == 0 and a % 3 == 0), stop=(kb == 63), skip_group_check=True),
                    r=[("E", es), ("va", hs), ("va1", hs)], w=[("accb", a // 3)])

        epi = [0]

        def epilogue(h, qt):
            gs = (h * 8 + qt) % 2
            ys = gs
            ai = gs
            A = accS[ai]
            for b in range(3):
                n = 387 if b < 2 else 258
                P.dve(lambda e, b=b, n=n: e.tensor_copy(out=A[:, b * 387:b * 387 + n], in_=pacc[b][:, 0:n]),
                      r=[("accb", b)], w=[("accS", ai)])
            sa = lambda a, lo, hi: A[:, a * 129 + lo:a * 129 + hi]
            for u in range(4):
                ep = epi[0] % 2
                epi[0] += 1
                a0, a1 = u, 4 + u
                P.dve(lambda e, ep=ep, a0=a0: e.reciprocal(out=r12[ep][:, 0:1], in_=sa(a0, 128, 129)),
                      r=[("accS", ai)], w=[("r12", ep)])
                P.dve(lambda e, ep=ep, a1=a1: e.reciprocal(out=r12[ep][:, 1:2], in_=sa(a1, 128, 129)),
                      r=[("accS", ai)], w=[("r12", ep)])
                P.dve(lambda e, ep=ep: e.tensor_tensor(out=r12[ep][:, 2:3], in0=r12[ep][:, 1:2], in1=lam[:, 3:4], op=ALU.mult),
                      r=[("r12", ep), "lam"], w=[("r12", ep)])
                P.dve(lambda e, ep=ep, a0=a0: e.tensor_scalar(out=o1[ep][:], in0=sa(a0, 0, 128), scalar1=r12[ep][:, 0:1],
                                                              scalar2=None, op0=ALU.mult),
                      r=[("accS", ai), ("r12", ep)], w=[("o1", ep)])
                P.dve(lambda e, ep=ep, a1=a1: e.scalar_tensor_tensor(out=o2[ep][:], in0=sa(a1, 0, 128), scalar=r12[ep][:, 2:3],
                                                                     in1=o1[ep][:], op0=ALU.mult, op1=ALU.add),
                      r=[("accS", ai), ("r12", ep), ("o1", ep)], w=[("o2", ep)])
                P.dve(lambda e, ep=ep: e.tensor_tensor(out=sq[ep][:], in0=o2[ep][:], in1=o2[ep][:], op=ALU.mult),
                      r=[("o2", ep)], w=[("sq", ep)])
                P.dve(lambda e, ep=ep: e.reduce_sum(out=r12[ep][:, 3:4], in_=sq[ep][:], axis=AX.X),
                      r=[("sq", ep)], w=[("r12b", ep)])
                P.dve(lambda e, ep=ep: e.tensor_scalar(out=r12[ep][:, 3:4], in0=r12[ep][:, 3:4], scalar1=1.0 / 128.0, scalar2=NORM_EPS,
                                                       op0=ALU.mult, op1=ALU.add), r=[("r12b", ep)], w=[("r12b", ep)])
                P.pool(lambda e, ep=ep: e.tensor_tensor(out=r12[ep][:, 3:4], in0=r12[ep][:, 3:4], in1=mhalf[:, 0:1], op=ALU.pow),
                       r=[("r12b", ep), "mhalf"], w=[("r12b", ep)])
                P.dve(lambda e, ep=ep: e.scalar_tensor_tensor(out=o1[ep][:], in0=o2[ep][:], scalar=r12[ep][:, 3:4],
                                                              in1=subg[:], op0=ALU.mult, op1=ALU.mult),
                      r=[("o2", ep), ("r12b", ep), "subg"], w=[("o1", ep)])
                yq = gs * 4 + u
                P.dve(lambda e, ep=ep, u=u, yq=yq: e.tensor_tensor(out=ybq[yq][:], in0=o1[ep][:], in1=gbt[gs][:, u, :], op=ALU.mult),
                      r=[("o1", ep), ("gbt", gs)], w=[("ybq", yq)])

        def epilogue2(h, qt):
            gs = (h * 8 + qt) % 2
            ys = gs
            for u in range(4):
                yq = gs * 4 + u
                P.pe(lambda e, u=u, yq=yq: e.transpose(out=ptr[:, u * 128:(u + 1) * 128], in_=ybq[yq][:], identity=identb[:]),
                     r=[("ybq", yq), "identb"], w=["ptr"])
            P.dve(lambda e: e.tensor_copy(out=ybs[ys][:, :], in_=ptr[:, :]), r=["ptr"], w=[("ybs", ys)])
            P.dma("sp", ybT[h * 128:(h + 1) * 128, qt * 512:(qt + 1) * 512], ybs[ys][:], r=[("ybs", ys)])

        blocks = [(h, qt, kb) for h in range(8) for qt in range(8) for kb in range(64)]
        nb = len(blocks)
        load_head(0)
        load_gb(0, 0)
        qk(0, *blocks[0])
        qk(1, *blocks[1])
        for i, (h, qt, kb) in enumerate(blocks):
            if kb == 0 and qt == 0 and h + 1 < 8:
                load_head(h + 1)
            if kb == 0:
                nxt = h * 8 + qt + 1
                if nxt < 64:
                    load_gb(nxt // 8, nxt % 8)
            ex(i)
            pv(i, h, kb)
            if i + 2 < nb:
                qk(i + 2, *blocks[i + 2])
            if kb == 63:
                epilogue(h, qt)
            if kb == 40 and (h, qt) != (0, 0):
                pq = h * 8 + qt - 1
                epilogue2(pq // 8, pq % 8)
        epilogue2(7, 7)
        P.flush()


def phase_O(nc, P, T):
    x, out, yaT, ybT, sgT = T["x"], T["out"], T["yaT"], T["ybT"], T["sgT"]
    with contextlib.ExitStack() as st:
        sb = lambda name, shape, dt=F32: st.enter_context(nc.sbuf_tensor("O_" + name, list(shape), dt))
        ps = lambda name, shape, dt=F32: st.enter_context(nc.psum_tensor("O_" + name, list(shape), dt))
        Wa, Wb, Wo = T["ow"]["Wa"], T["ow"]["Wb"], T["ow"]["Wo"]
        fing = sb("fing", [128, D])
        ya = [sb("ya%d" % i, [128, 8, 512], BF16) for i in range(2)]
        yb = [sb("yb%d" % i, [128, 8, 512], BF16) for i in range(2)]
        sa = [sb("sa%d" % i, [128, 8, 512], BF16) for i in range(2)]
        sbb = [sb("sb%d" % i, [128, 8, 512], BF16) for i in range(2)]
        mixT = [sb("mixT%d" % i, [128, 8, 512], BF16) for i in range(2)]
        t1 = [sb("t1_%d" % i, [128, 512]) for i in range(2)]
        t2 = [sb("t2_%d" % i, [128, 512]) for i in range(2)]
        xt = [sb("xt%d" % i, [128, D]) for i in range(2)]
        xo = [sb("xo%d" % i, [128, D]) for i in range(2)]
        junk = sb("junk", [128, D], BF16)
        ssq = [sb("ssq%d" % i, [128, 2]) for i in range(2)]
        pa = [ps("pa%d" % i, [128, 512]) for i in range(2)]
        pb = [ps("pb%d" % i, [128, 512]) for i in range(2)]
        po = [ps("po%d" % i, [128, 512]) for i in range(4)]

        P.dma("sp", fing[:], T["fing_rep"], w=["fing"])
        wkeys = lambda k: [(k, 0), (k, 1)]
        cnt = {"p": 0, "o": 0, "x": 0}
        for tt in range(8):
            sl = tt % 2
            ts_ = slice(tt * 512, (tt + 1) * 512)
            P.dma("sp", ya[sl][:], yaT[:, ts_].rearrange("(cc p) t -> p cc t", p=128), w=[("ya", sl)])
            P.dma("sp", yb[sl][:], ybT[:, ts_].rearrange("(cc p) t -> p cc t", p=128), w=[("yb", sl)])
            P.dma("sp", sa[sl][:], sgT[0, :, ts_].rearrange("(cc p) t -> p cc t", p=128), w=[("sa", sl)])
            P.dma("sp", sbb[sl][:], sgT[1, :, ts_].rearrange("(cc p) t -> p cc t", p=128), w=[("sb", sl)])
            for dd in range(8):
                i = cnt["p"] % 2
                cnt["p"] += 1
                for cc in range(8):
                    P.pe(lambda e, i=i, cc=cc, dd=dd, sl=sl: e.matmul(pa[i][:, :], Wa[:, cc, dd * 128:(dd + 1) * 128], ya[sl][:, cc, :],
                                                                      start=(cc == 0), stop=(cc == 7)),
                         r=wkeys("Wa") + [("ya", sl)], w=[("pa", i)])
                for cc in range(8):
                    P.pe(lambda e, i=i, cc=cc, dd=dd, sl=sl: e.matmul(pb[i][:, :], Wb[:, cc, dd * 128:(dd + 1) * 128], yb[sl][:, cc, :],
                                                                      start=(cc == 0), stop=(cc == 7)),
                         r=wkeys("Wb") + [("yb", sl)], w=[("pb", i)])
                P.dve(lambda e, i=i, dd=dd, sl=sl: e.tensor_tensor(out=t1[i][:], in0=pa[i][:, :], in1=sa[sl][:, dd, :], op=ALU.mult),
                      r=[("pa", i), ("sa", sl)], w=[("t1", i)])
                P.dve(lambda e, i=i, dd=dd, sl=sl: e.tensor_tensor(out=t2[i][:], in0=pb[i][:, :], in1=sbb[sl][:, dd, :], op=ALU.mult),
                      r=[("pb", i), ("sb", sl)], w=[("t2", i)])
                P.dve(lambda e, i=i, dd=dd, sl=sl: e.tensor_tensor(out=mixT[sl][:, dd, :], in0=t1[i][:], in1=t2[i][:], op=ALU.add),
                      r=[("t1", i), ("t2", i)], w=[("mixT", sl)])
            for u in range(4):
                xs = cnt["x"] % 2
                cnt["x"] += 1
                r0 = tt * 512 + u * 128
                P.dma("sp", xt[xs][:], x[r0:r0 + 128, :], w=[("xt", xs)])
                for eg in range(2):
                    o = cnt["o"] % 4
                    cnt["o"] += 1
                    for dd in range(8):
                        P.pe(lambda e, o=o, dd=dd, sl=sl, u=u, eg=eg: e.matmul(
                            po[o][:, :], mixT[sl][:, dd, u * 128:(u + 1) * 128], Wo[:, dd, eg * 512:(eg + 1) * 512],
                            start=(dd == 0), stop=(dd == 7)), r=[("mixT", sl), ("Wo", eg)], w=[("po", o)])
                    P.dve(lambda e, o=o, xs=xs, eg=eg: e.tensor_tensor(out=xo[xs][:, eg * 512:(eg + 1) * 512], in0=po[o][:, :],
                                                                       in1=xt[xs][:, eg * 512:(eg + 1) * 512], op=ALU.add),
                          r=[("po", o), ("xt", xs)], w=[("xo", xs, eg)])
                P.act(lambda e, xs=xs: e.activation(out=junk[:], in_=xo[xs][:], func=AF.Square, accum_out=ssq[xs][:, 0:1]),
                      r=[("xo", xs, 0), ("xo", xs, 1)], w=[("ssq", xs)])
                P.act(lambda e, xs=xs: e.activation(out=ssq[xs][:, 0:1], in_=ssq[xs][:, 0:1], func=AF.Sqrt, scale=1.0 / D, bias=NORM_EPS),
                      r=[("ssq", xs)], w=[("ssq", xs)])
                P.dve(lambda e, xs=xs: e.reciprocal(out=ssq[xs][:, 1:2], in_=ssq[xs][:, 0:1]), r=[("ssq", xs)], w=[("ssq", xs)])
                P.dve(lambda e, xs=xs: e.scalar_tensor_tensor(out=xo[xs][:], in0=xo[xs][:], scalar=ssq[xs][:, 1:2], in1=fing[:],
                                                              op0=ALU.mult, op1=ALU.mult),
                      r=[("xo", xs, 0), ("xo", xs, 1), ("ssq", xs), "fing"], w=[("xo", xs, 0), ("xo", xs, 1)])
                P.dma("sp", out[r0:r0 + 128, :], xo[xs][:], r=[("xo", xs, 0), ("xo", xs, 1)])
        P.flush()


def make_in_maps(x, positions, norm_g, w_in, ml_gate_b, ml_conv_w, ml_norm_g, da_lambda,
                 da_subln_g, gate_b, w_branch_a, w_branch_b, w_out, final_g):
    f32 = np.float32
    x = np.asarray(x, f32)
    positions = np.asarray(positions, np.int32)
    w_in0 = np.ascontiguousarray(np.asarray(w_in, f32)[0])
    gb0 = np.asarray(ml_gate_b, f32)[0]
    cw0 = np.asarray(ml_conv_w, f32)[0]
    w_in1 = w_in0.copy()
    ag = w_in0[:, C_AG:C_AG + 16].reshape(D, 4, 4)
    w_in1[:, C_AG:C_AG + 16] = ag[:, [2, 3, 0, 1], :].reshape(D, 16)
    gb1 = gb0[[2, 3, 0, 1], :]
    cw1 = cw0[::-1, :]
    rep = lambda v, n=128: np.ascontiguousarray(np.broadcast_to(np.asarray(v, f32).reshape(1, -1), (n, np.asarray(v).size)))
    ident = np.eye(128, dtype=f32)
    rsw = np.zeros((128, 128), f32)
    for r in range(128):
        m = r % 64
        if m < 32:
            rsw[r + 32, r] = -1.0
        else:
            rsw[r - 32, r] = 1.0
    invf = (10000.0 ** (-np.arange(0, 64, 2, dtype=f32) / f32(64))).astype(f32)
    invf_p = np.array([invf[(p % 64) % 32] for p in range(128)], f32).reshape(128, 1)
    ii = np.arange(128)
    U = (ii[:, None] <= ii[None, :]).astype(f32)
    L = (ii[:, None] >= ii[None, :]).astype(f32)
    NEG = -30000.0
    tri = np.stack([U, L, (1 - U) * NEG, (1 - L) * NEG], axis=1).astype(f32)
    common = {
        "normg_rep": rep(np.asarray(norm_g, f32)[0]),
        "mlng_rep": rep(np.asarray(ml_norm_g, f32)[0].reshape(-1)),
        "lam_rep": rep(np.asarray(da_lambda, f32)[0].reshape(-1)),
        "subln_rep": rep(np.asarray(da_subln_g, f32)[0]),
        "gb_fm": np.ascontiguousarray(np.asarray(gate_b, f32)[0].reshape(2, 8, 128).transpose(2, 0, 1)),
        "w_a": np.ascontiguousarray(np.asarray(w_branch_a, f32)[0]),
        "w_b": np.ascontiguousarray(np.asarray(w_branch_b, f32)[0]),
        "w_o": np.ascontiguousarray(np.asarray(w_out, f32)[0]),
        "fing_rep": rep(np.asarray(final_g, f32)),
        "c_ident": ident, "c_rswap": rsw, "c_invf": invf_p, "c_tri": tri,
    }
    in_maps = []
    for core in range(NCORES):
        b, half = core // 2, core % 2
        xb = x[b]
        pb = positions[b]
        if half == 1:
            xb = xb[::-1]
            pb = pb[::-1]
        gbx = gb1 if half else gb0
        cwx = cw1 if half else cw0
        m = dict(common)
        m["x"] = np.ascontiguousarray(xb)
        m["posr"] = np.ascontiguousarray(np.broadcast_to(pb.reshape(1, S), (128, S))).astype(np.int32)
        m["w_in"] = w_in1 if half else w_in0
        m["gateb_rep"] = rep(gbx.reshape(-1))
        m["convw"] = np.ascontiguousarray(cwx.reshape(5, 16, 128).transpose(2, 1, 0))
        in_maps.append(m)
    return in_maps


_NC_CACHE = {}


def kernel(**inputs):
    in_maps = make_in_maps(**inputs)
    if "nc" not in _NC_CACHE:
        _NC_CACHE["nc"] = build_nc()
    nc = _NC_CACHE["nc"]
    res = run_bass_kernel_spmd(nc, in_maps, core_ids=list(range(NCORES)))
    B = 4
    outp = np.empty((B, S, D), np.float32)
    for core in range(NCORES):
        b, half = core // 2, core % 2
        o = np.asarray(res.results[core]["out"], np.float32)
        if half == 0:
            outp[b, :OWN] = o
        else:
            outp[b, OWN:] = o[::-1]
    return outp
```

```python
import contextlib
import math
import numpy as np
import concourse.bass as bass
import concourse.mybir as mybir
from concourse.bass_utils import run_bass_kernel_spmd

F32, BF16, I32 = mybir.dt.float32, mybir.dt.bfloat16, mybir.dt.int32
AF = mybir.ActivationFunctionType
ALU = mybir.AluOpType
AX = mybir.AxisListType

D = 1024
S = 8192
OWN = 4096
NCORES = 8
PROJ = 11280
C_AQ, C_AK, C_AV, C_AO, C_AZ, C_AG = 0, 1024, 2048, 3072, 4096, 5120
C_BQ, C_BK, C_BV, C_BZ, C_GA, C_GB = 5136, 6160, 7184, 8208, 9232, 10256
NORM_EPS = 1e-6
LAMBDA_INIT = 0.8 - 0.6 * math.exp(-0.3 * 0)
SAME_ENGINE_SYNC = True
SAME_ENGINE_RAW_ONLY = False


class _Op:
    __slots__ = ("eng", "fn", "dma", "clock", "idx", "waits", "signal", "semval", "sem", "know")


class Prog:
    ENGS = ("sp", "act", "dve", "pool", "pe")

    def __init__(self, nc, stack, n_dma=8):
        self.nc = nc
        self.sem = {e: stack.enter_context(nc.semaphore("cs_" + e)) for e in self.ENGS}
        self.dsem = {q: [stack.enter_context(nc.semaphore("ds_%s_%d" % (q, i))) for i in range(n_dma)]
                     for q in ("sp", "pool", "act")}
        self.n_dma = n_dma
        self.dcount = {q: 0 for q in self.dsem}
        self.dlast = {}
        self.cnt = {e: 0 for e in self.ENGS}
        self.sigcnt = {e: 0 for e in self.ENGS}
        self.know = {e: {} for e in self.ENGS}
        self.pending = {e: [] for e in self.ENGS}
        self.last_w = {}
        self.readers = {}
        self.nops = 0

    def add(self, eng, fn, reads=(), writes=(), dma=False, extra_deps=()):
        op = _Op()
        op.eng, op.fn, op.dma, op.signal, op.semval = eng, fn, dma, False, None
        self.nops += 1
        deps = []
        seen = set()

        raw = set()

        def push(d, is_raw=False):
            if d is None:
                return
            if is_raw:
                raw.add(id(d))
            if id(d) not in seen:
                seen.add(id(d))
                deps.append(d)

        for k in reads:
            push(self.last_w.get(k), True)
        for k in writes:
            push(self.last_w.get(k))
            for r in self.readers.get(k, ()):
                push(r)
        for d in extra_deps:
            push(d, True)
        if dma:
            slot = self.dcount[eng] % self.n_dma
            self.dcount[eng] += 1
            op.clock = (eng, slot)
            prev = self.dlast.get(op.clock)
            op.idx = (prev.idx + 1) if prev is not None else 1
            op.sem = self.dsem[eng][slot]
            op.semval = 16 * op.idx
            push(prev)
            self.dlast[op.clock] = op
        else:
            self.cnt[eng] += 1
            op.clock = eng
            op.idx = self.cnt[eng]
            op.sem = self.sem[eng]
        know = self.know[eng]
        waits = []
        for d in deps:
            if (not d.dma) and (not dma) and d.eng == eng:
                if eng == "pe" or not SAME_ENGINE_SYNC:
                    continue
                if SAME_ENGINE_RAW_ONLY and id(d) not in raw:
                    continue
            if know.get(d.clock, 0) >= d.idx:
                continue
            waits.append(d)
            d.signal = True
            for c, v in d.know.items():
                if know.get(c, 0) < v:
                    know[c] = v
            if know.get(d.clock, 0) < d.idx:
                know[d.clock] = d.idx
        op.waits = waits
        op.know = dict(know)
        for k in writes:
            self.last_w[k] = op
            self.readers[k] = []
        for k in reads:
            self.readers.setdefault(k, []).append(op)
        self.pending[eng].append(op)
        return op

    def pe(self, fn, r=(), w=()):
        return self.add("pe", fn, r, w)

    def act(self, fn, r=(), w=()):
        return self.add("act", fn, r, w)

    def dve(self, fn, r=(), w=()):
        return self.add("dve", fn, r, w)

    def pool(self, fn, r=(), w=()):
        return self.add("pool", fn, r, w)

    def dma(self, q, out, in_, r=(), w=()):
        return self.add(q, lambda e: e.dma_start(out=out, in_=in_), r, w, dma=True)

    def flush(self, final=False):
        outstanding = [d for d in self.dlast.values()]
        self.add("sp", None, extra_deps=outstanding)
        for e in self.ENGS:
            for op in self.pending[e]:
                if not op.dma and op.signal:
                    self.sigcnt[e] += 1
                    op.semval = self.sigcnt[e]
                elif not op.dma:
                    op.semval = None
        pend = self.pending
        sems = self.sem

        def make_body(eng):
            ops = pend[eng]

            def body(e):
                for op in ops:
                    for d in op.waits:
                        assert d.semval is not None
                        e.wait_ge(d.sem, d.semval)
                    if op.fn is None:
                        continue
                    ins = op.fn(e)
                    if op.dma:
                        ins.then_inc(op.sem, 16)
                    elif op.signal:
                        ins.then_inc(sems[eng], 1)
            return body

        with self.nc.Block() as block:
            block.sync(make_body("sp"))
            block.scalar(make_body("act"))
            block.vector(make_body("dve"))
            block.gpsimd(make_body("pool"))
            block.tensor(make_body("pe"))
        self.pending = {e: [] for e in self.ENGS}
        self.last_w = {}
        self.readers = {}
        full = {}
        for e in self.ENGS:
            full[e] = self.cnt[e]
        for c, d in self.dlast.items():
            full[c] = d.idx
        self.know = {e: dict(full) for e in self.ENGS}


def build_nc(debug=False, phases=("P", "M", "D", "O")):
    nc = bass.Bass("TRN2", target_bir_lowering=False)
    IN = lambda name, shape, dt=F32: nc.dram_tensor(name, list(shape), dt, kind="ExternalInput").ap()
    skind = "ExternalOutput" if debug else "Internal"
    SCR = lambda name, shape, dt=BF16: nc.dram_tensor(name, list(shape), dt, kind=skind).ap()

    x = IN("x", [S, D])
    posr = IN("posr", [128, S], I32)
    w_in = IN("w_in", [D, PROJ])
    normg_rep = IN("normg_rep", [128, D])
    gateb_rep = IN("gateb_rep", [128, 16])
    convw = IN("convw", [128, 16, 5])
    mlng_rep = IN("mlng_rep", [128, D])
    lam_rep = IN("lam_rep", [128, 256])
    subln_rep = IN("subln_rep", [128, 128])
    gb_fm = IN("gb_fm", [128, 2, 8])
    w_a = IN("w_a", [D, D])
    w_b = IN("w_b", [D, D])
    w_o = IN("w_o", [D, D])
    fing_rep = IN("fing_rep", [128, D])
    c_ident = IN("c_ident", [128, 128])
    c_rswap = IN("c_rswap", [128, 128])
    c_invf = IN("c_invf", [128, 1])
    c_tri = IN("c_tri", [128, 4, 128])
    out = nc.dram_tensor("out", [OWN, D], F32, kind="ExternalOutput").ap()

    mqT = SCR("mqT", [D, S])
    mkT = SCR("mkT", [D, S])
    mv = SCR("mv", [S, D])
    dqT = SCR("dqT", [D, OWN])
    dkT = SCR("dkT", [D, S])
    dv = SCR("dv", [S, D])
    gA = SCR("gA", [OWN, D])
    gB = SCR("gB", [OWN, D])
    sgT = SCR("sgT", [2, D, OWN])
    gall_d = SCR("gall_d", [128, 64 * 16], F32)
    yaT = SCR("yaT", [D, OWN])
    ybT = SCR("ybT", [D, OWN])
    hfwd = SCR("hfwd", [OWN, D], F32)

    with contextlib.ExitStack() as top:
        P = Prog(nc, top)
        if "P" in phases:
            phase_P(nc, P, locals())
        if "M" in phases:
            phase_M(nc, P, locals())
        if "M2" in phases:
            phase_M2(nc, P, locals())
        ow = None
        if "O" in phases:
            ow = {k: top.enter_context(nc.sbuf_tensor("OW_" + k, [128, 8, D], BF16)) for k in ("Wa", "Wb", "Wo")}
            for (k, srcw) in (("Wa", w_a), ("Wb", w_b), ("Wo", w_o)):
                for hh in range(2):
                    P.dma("pool", ow[k][:, :, hh * 512:(hh + 1) * 512],
                          srcw[:, hh * 512:(hh + 1) * 512].rearrange("(kc p) c -> p kc c", p=128), w=[(k, hh)])
        if "D" in phases:
            phase_D(nc, P, locals())
        if "O" in phases:
            phase_O(nc, P, locals())
    return nc


def phase_P(nc, P, T):
    x, posr, w_in = T["x"], T["posr"], T["w_in"]
    with contextlib.ExitStack() as st:
        sb = lambda name, shape, dt=F32: st.enter_context(nc.sbuf_tensor("P_" + name, list(shape), dt))
        ps = lambda name, shape, dt=F32: st.enter_context(nc.psum_tensor("P_" + name, list(shape), dt))
        hT = sb("hT", [128, 8, 2048], BF16)
        NX = 4
        xt = [sb("xt%d" % i, [128, D]) for i in range(NX)]
        xn = [sb("xn%d" % i, [128, D], BF16) for i in range(NX)]
        junk = sb("junk", [128, D], BF16)
        ss = [sb("ss%d" % i, [128, 1]) for i in range(NX)]
        rstd = [sb("rstd%d" % i, [128, 1]) for i in range(NX)]
        grep = sb("grep", [128, D])
        NW = 3
        wB = [sb("wB%d" % i, [128, 8, 512], BF16) for i in range(NW)]
        wG = sb("wG", [128, 8, 16], BF16)
        NWK = 5
        WK = [sb("WK%d" % i, [128, 2054]) for i in range(NWK)]
        OB = [sb("OB%d" % i, [128, 2050], BF16) for i in range(2)]
        OBT = [sb("OBT%d" % i, [128, 8192], BF16) for i in range(2)]
        obt_slot = [0]
        cosT = sb("cosT", [128, 2048])
        sinT = sb("sinT", [128, 2048])
        gall = sb("gall", [128, 64 * 16])
        gbrep = sb("gbrep", [128, 16])
        cw = sb("cw", [128, 16, 5])
        carry = sb("carry", [128, 16, 4])
        identf = sb("identf", [128, 128])
        identb = sb("identb", [128, 128], BF16)
        rswap = sb("rswap", [128, 128])
        invf = sb("invf", [128, 1])
        gbfm = sb("gbfm", [128, 2, 8])
        posi = sb("posi", [128, 2048], I32)
        ptr = [ps("ptr%d" % i, [128, D], BF16) for i in range(2)]
        pp = [ps("pp%d" % i, [128, 512]) for i in range(4)]
        prot = [ps("prot%d" % i, [128, 512]) for i in range(2)]

        P.dma("sp", grep[:], T["normg_rep"], w=["grep"])
        P.dma("sp", gbrep[:], T["gateb_rep"], w=["gbrep"])
        P.dma("sp", cw[:], T["convw"], w=["cw"])
        P.dma("sp", identf[:], T["c_ident"], w=["identf"])
        P.dma("sp", rswap[:], T["c_rswap"], w=["rswap"])
        P.dma("sp", invf[:], T["c_invf"], w=["invf"])
        P.dma("sp", gbfm[:], T["gb_fm"], w=["gbfm"])
        P.dma("pool", wG[:], w_in[:, C_AG:C_AG + 16].rearrange("(kc p) c -> p kc c", p=128), w=["wG"])
        P.dve(lambda e: e.tensor_copy(out=identb[:], in_=identf[:]), r=["identf"], w=["identb"])
        P.dve(lambda e: e.memset(carry[:], 0.0), w=["carry"])
        for i in range(NWK):
            P.dve(lambda e, i=i: e.memset(WK[i][:, 2052:2054], 0.0), w=[("WKz", i), ("WK", i)])

        wslot = [0]

        def load_w(col0, ncols):
            s = wslot[0] % NW
            wslot[0] += 1
            P.dma("pool", wB[s][:, :, 0:ncols],
                  w_in[:, col0:col0 + ncols].rearrange("(kc p) c -> p kc c", p=128), w=[("wB", s)])
            return s

        ppslot = [0]

        def next_pp():
            s = ppslot[0] % 4
            ppslot[0] += 1
            return s

        wkslot = [0]

        def next_wk():
            s = wkslot[0] % NWK
            wkslot[0] += 1
            return s

        obslot = [0]

        def next_ob():
            s = obslot[0] % 2
            obslot[0] += 1
            return s

        def fm_mm(ws, wc, tq):
            s = next_pp()
            for kc in range(8):
                P.pe(lambda e, s=s, ws=ws, wc=wc, kc=kc, tq=tq: e.matmul(
                    pp[s][:, :], wB[ws][:, kc, wc * 128:(wc + 1) * 128], hT[:, kc, tq * 512:(tq + 1) * 512],
                    start=(kc == 0), stop=(kc == 7)),
                    r=[("wB", ws), "hT"], w=[("pp", s)])
            return s

        def tm_mm(ws, ncols, tt):
            s = next_pp()
            for kc in range(8):
                P.pe(lambda e, s=s, ws=ws, kc=kc, tt=tt, ncols=ncols: e.matmul(
                    pp[s][:, 0:ncols], hT[:, kc, tt * 128:(tt + 1) * 128], wB[ws][:, kc, 0:ncols],
                    start=(kc == 0), stop=(kc == 7)),
                    r=[("wB", ws), "hT"], w=[("pp", s)])
            return s

        def load_x(stile_, tt_):
            r0_ = stile_ * 2048 + tt_ * 128
            P.dma("sp", xt[tt_ % NX][:], x[r0_:r0_ + 128, :], w=[("xt", tt_ % NX)])

        for stile in range(4):
            own = stile < 2
            t0 = stile * 2048
            last = stile == 3
            if stile == 0:
                for tt in range(NX):
                    load_x(0, tt)
            def stage_a(tt):
                sl = tt % NX
                pl = tt % 2
                P.act(lambda e, sl=sl: e.activation(out=junk[:], in_=xt[sl][:], func=AF.Square, accum_out=ss[sl][:]),
                      r=[("xt", sl)], w=[("ss", sl)])
                P.act(lambda e, sl=sl: e.activation(out=ss[sl][:], in_=ss[sl][:], func=AF.Sqrt,
                                                    scale=1.0 / D, bias=NORM_EPS),
                      r=[("ss", sl)], w=[("ss", sl)])
                P.dve(lambda e, sl=sl: e.reciprocal(out=rstd[sl][:], in_=ss[sl][:]), r=[("ss", sl)], w=[("rstd", sl)])
                P.dve(lambda e, sl=sl: e.scalar_tensor_tensor(out=xn[sl][:], in0=xt[sl][:], scalar=rstd[sl][:],
                                                              in1=grep[:], op0=ALU.mult, op1=ALU.mult),
                      r=[("xt", sl), ("rstd", sl), "grep"], w=[("xn", sl)])
                for kc in range(8):
                    P.pe(lambda e, sl=sl, kc=kc, pl=pl: e.transpose(out=ptr[pl][:, kc * 128:(kc + 1) * 128],
                                                                    in_=xn[sl][:, kc * 128:(kc + 1) * 128],
                                                                    identity=identb[:]),
                         r=[("xn", sl), "identb"], w=[("ptr", pl)])
                if tt + NX < 16:
                    load_x(stile, tt + NX)

            def stage_b(tt):
                pl = tt % 2
                if tt % 2 == 0:
                    P.act(lambda e, pl=pl, tt=tt: e.activation(
                        out=hT[:, :, tt * 128:(tt + 1) * 128],
                        in_=ptr[pl][:, :].rearrange("p (k t) -> p k t", k=8), func=AF.Copy),
                        r=[("ptr", pl)], w=["hT"])
                else:
                    P.dve(lambda e, pl=pl, tt=tt: e.tensor_copy(
                        out=hT[:, :, tt * 128:(tt + 1) * 128],
                        in_=ptr[pl][:, :].rearrange("p (k t) -> p k t", k=8)),
                        r=[("ptr", pl)], w=["hT"])

            stage_a(0)
            for tt in range(16):
                if tt + 1 < 16:
                    stage_a(tt + 1)
                stage_b(tt)
            if stile < 3:
                for tt in range(NX):
                    load_x(stile + 1, tt)

            P.dma("sp", posi[:], posr[:, t0:t0 + 2048], w=["posi"])
            build_rope_tables(P, posi, invf, cosT, sinT, WK, next_wk)

            for tt in range(16):
                s = next_pp()
                for kc in range(8):
                    P.pe(lambda e, s=s, kc=kc, tt=tt: e.matmul(
                        pp[s][:, 0:16], hT[:, kc, tt * 128:(tt + 1) * 128], wG[:, kc, :],
                        start=(kc == 0), stop=(kc == 7)), r=["wG", "hT"], w=[("pp", s)])
                gt = stile * 16 + tt
                P.dve(lambda e, s=s, gt=gt: e.tensor_tensor(out=gall[:, gt * 16:(gt + 1) * 16], in0=pp[s][:, 0:16],
                                                            in1=gbrep[:], op=ALU.add),
                      r=[("pp", s), "gbrep"], w=["gall"])

            def gen_tm_v():
                for (c0, dst) in ((C_AV, T["mv"]), (C_BV, T["dv"])):
                    for cg in range(2):
                        ws = load_w(c0 + cg * 512, 512)
                        ob = obt_slot[0] % 2
                        obt_slot[0] += 1
                        for tt in range(16):
                            s = tm_mm(ws, 512, tt)
                            if True:
                                P.act(lambda e, s=s, ob=ob, tt=tt: e.activation(
                                    out=OBT[ob][:, tt * 512:(tt + 1) * 512], in_=pp[s][:, :], func=AF.Copy),
                                    r=[("pp", s)], w=[("OBT", ob)])
                            else:
                                P.dve(lambda e, s=s, ob=ob, tt=tt: e.tensor_copy(
                                    out=OBT[ob][:, tt * 512:(tt + 1) * 512], in_=pp[s][:, :]),
                                    r=[("pp", s)], w=[("OBT", ob)])
                            if tt == 15:
                                P.dma("sp", dst[t0:t0 + 2048, cg * 512:(cg + 1) * 512].rearrange("(tt p) c -> p tt c", p=128),
                                      OBT[ob][:, :].rearrange("p (tt c) -> p tt c", c=512), r=[("OBT", ob)])
                            yield

            def gen_conv():
                prev2 = [None]
                for fam, (c0, dst, ksc) in enumerate(((C_AQ, T["mqT"], 1.0), (C_AK, T["mkT"], 1.0 / 16.0))):
                    for g4 in range(2):
                        ws = load_w(c0 + g4 * 512, 512)
                        for wc in range(4):
                            fc = fam * 8 + g4 * 4 + wc
                            row0 = (g4 * 4 + wc) * 128
                            stg = next_wk()
                            acc = next_wk()
                            sg = next_wk()
                            P.dve(lambda e, stg=stg, fc=fc: e.tensor_copy(out=WK[stg][:, 0:4], in_=carry[:, fc, :]),
                                  r=["carry"], w=[("WK", stg)])
                            for tq in range(4):
                                s = fm_mm(ws, wc, tq)
                                P.act(lambda e, s=s, stg=stg, tq=tq: e.activation(
                                    out=WK[stg][:, 4 + tq * 512:4 + (tq + 1) * 512], in_=pp[s][:, :], func=AF.Copy),
                                    r=[("pp", s)], w=[("WK", stg)])
                            P.dve(lambda e, stg=stg, fc=fc: e.tensor_copy(out=carry[:, fc, :], in_=WK[stg][:, 2048:2052]),
                                  r=[("WK", stg)], w=["carry"])
                            nout = 2050 if last else 2048
                            P.act(lambda e, stg=stg, acc=acc, fc=fc, nout=nout: e.activation(
                                out=WK[acc][:, 0:nout], in_=WK[stg][:, 0:nout], func=AF.Copy, scale=cw[:, fc, 0:1]),
                                r=[("WK", stg), "cw", ("WKz", stg)], w=[("WK", acc)])
                            for j in range(1, 5):
                                P.dve(lambda e, stg=stg, acc=acc, fc=fc, nout=nout, j=j: e.scalar_tensor_tensor(
                                    out=WK[acc][:, 0:nout], in0=WK[stg][:, j:j + nout], scalar=cw[:, fc, j:j + 1],
                                    in1=WK[acc][:, 0:nout], op0=ALU.mult, op1=ALU.add),
                                    r=[("WK", stg), "cw", ("WK", acc)], w=[("WK", acc)])

                            def part2(acc=acc, sg=sg, nout=nout, ksc=ksc, dst=dst, row0=row0):
                                ob = next_ob()
                                P.act(lambda e: e.activation(
                                    out=WK[sg][:, 0:nout], in_=WK[acc][:, 0:nout], func=AF.Sigmoid),
                                    r=[("WK", acc)], w=[("WK", sg)])
                                P.dve(lambda e: e.scalar_tensor_tensor(
                                    out=OB[ob][:, 0:nout], in0=WK[acc][:, 0:nout], scalar=ksc, in1=WK[sg][:, 0:nout],
                                    op0=ALU.mult, op1=ALU.mult), r=[("WK", acc), ("WK", sg)], w=[("OB", ob)])
                                if stile == 0:
                                    P.dma("sp", dst[row0:row0 + 128, 0:nout - 2], OB[ob][:, 2:nout], r=[("OB", ob)])
                                else:
                                    P.dma("sp", dst[row0:row0 + 128, t0 - 2:t0 - 2 + nout], OB[ob][:, 0:nout], r=[("OB", ob)])

                            if prev2[0] is not None:
                                prev2[0]()
                            prev2[0] = part2
                            yield
                if prev2[0] is not None:
                    prev2[0]()

            g_tm = gen_tm_v()
            for _ in gen_conv():
                for _k in range(4):
                    next(g_tm, None)
            for _ in g_tm:
                pass

            fams = [(C_BK, T["dkT"], 1.0, t0)]
            if own:
                fams.append((C_BQ, T["dqT"], 0.125, t0))
            pend = [None]
            for (c0, dst, sc, tcol) in fams:
                for g4 in range(2):
                    ws = load_w(c0 + g4 * 512, 512)
                    for wc in range(4):
                        row0 = (g4 * 4 + wc) * 128
                        ob = next_ob()
                        for tq in range(4):
                            s = fm_mm(ws, wc, tq)
                            wk = next_wk()
                            P.act(lambda e, s=s, wk=wk: e.activation(out=WK[wk][:, 0:512], in_=pp[s][:, :], func=AF.Copy),
                                  r=[("pp", s)], w=[("WK", wk)])

                            def tail(wk=wk, tq=tq, ob=ob, sc=sc, dst=dst, row0=row0, tcol=tcol):
                                pr = tq % 2
                                cs = slice(tq * 512, (tq + 1) * 512)
                                P.pe(lambda e: e.matmul(prot[pr][:, :], rswap[:, :], WK[wk][:, 0:512], start=True, stop=True),
                                     r=[("WK", wk), "rswap"], w=[("prot", pr)])
                                P.dve(lambda e: e.scalar_tensor_tensor(
                                    out=WK[wk][:, 512:1024], in0=WK[wk][:, 0:512], scalar=sc, in1=cosT[:, cs],
                                    op0=ALU.mult, op1=ALU.mult), r=[("WK", wk), "cosT"], w=[("WK", wk)])
                                P.dve(lambda e: e.scalar_tensor_tensor(
                                    out=WK[wk][:, 1024:1536], in0=prot[pr][:, :], scalar=sc, in1=sinT[:, cs],
                                    op0=ALU.mult, op1=ALU.mult), r=[("prot", pr), "sinT"], w=[("WK", wk)])
                                P.dve(lambda e: e.tensor_tensor(
                                    out=OB[ob][:, cs], in0=WK[wk][:, 512:1024], in1=WK[wk][:, 1024:1536], op=ALU.add),
                                    r=[("WK", wk)], w=[("OB", ob)])
                                if tq == 3:
                                    P.dma("sp", dst[row0:row0 + 128, tcol:tcol + 2048], OB[ob][:, 0:2048], r=[("OB", ob)])

                            if pend[0] is not None:
                                pend[0]()
                            pend[0] = tail
            if pend[0] is not None:
                pend[0]()

            if own:
                for gi, c0 in enumerate((C_GA, C_GB)):
                    for g4 in range(2):
                        ws = load_w(c0 + g4 * 512, 512)
                        for wc in range(4):
                            cc = g4 * 4 + wc
                            ob = next_ob()
                            for tq in range(4):
                                s = fm_mm(ws, wc, tq)
                                P.act(lambda e, s=s, ob=ob, tq=tq, gi=gi, cc=cc: e.activation(
                                    out=OB[ob][:, tq * 512:(tq + 1) * 512], in_=pp[s][:, :], func=AF.Sigmoid,
                                    bias=gbfm[:, gi, cc:cc + 1]), r=[("pp", s), "gbfm"], w=[("OB", ob)])
                            P.dma("sp", T["sgT"][gi, cc * 128:(cc + 1) * 128, t0:t0 + 2048], OB[ob][:, 0:2048],
                                  r=[("OB", ob)])
                for cg in range(2):
                    wso = load_w(C_AO + cg * 512, 512)
                    wsz = load_w(C_AZ + cg * 512, 512)
                    ob = obt_slot[0] % 2
                    obt_slot[0] += 1
                    for tt in range(16):
                        so = tm_mm(wso, 512, tt)
                        sz = tm_mm(wsz, 512, tt)
                        wk = next_wk()
                        P.act(lambda e, so=so, wk=wk: e.activation(out=WK[wk][:, 0:512], in_=pp[so][:, :], func=AF.Sigmoid),
                              r=[("pp", so)], w=[("WK", wk)])
                        P.act(lambda e, sz=sz, wk=wk: e.activation(out=WK[wk][:, 512:1024], in_=pp[sz][:, :], func=AF.Sigmoid),
                              r=[("pp", sz)], w=[("WK", wk)])
                        P.dve(lambda e, sz=sz, wk=wk: e.tensor_tensor(out=WK[wk][:, 512:1024], in0=pp[sz][:, :],
                                                                      in1=WK[wk][:, 512:1024], op=ALU.mult),
                              r=[("pp", sz), ("WK", wk)], w=[("WK", wk)])
                        P.dve(lambda e, wk=wk, ob=ob, tt=tt: e.tensor_tensor(
                            out=OBT[ob][:, tt * 512:(tt + 1) * 512], in0=WK[wk][:, 0:512], in1=WK[wk][:, 512:1024],
                            op=ALU.mult), r=[("WK", wk)], w=[("OBT", ob)])
                    P.dma("sp", T["gA"][t0:t0 + 2048, cg * 512:(cg + 1) * 512].rearrange("(tt p) c -> p tt c", p=128),
                          OBT[ob][:, :].rearrange("p (tt c) -> p tt c", c=512), r=[("OBT", ob)])
                for cg in range(2):
                    wsz = load_w(C_BZ + cg * 512, 512)
                    ob = obt_slot[0] % 2
                    obt_slot[0] += 1
                    for tt in range(16):
                        sz = tm_mm(wsz, 512, tt)
                        wk = next_wk()
                        P.act(lambda e, sz=sz, wk=wk: e.activation(out=WK[wk][:, 0:512], in_=pp[sz][:, :], func=AF.Sigmoid),
                              r=[("pp", sz)], w=[("WK", wk)])
                        P.dve(lambda e, sz=sz, wk=wk, ob=ob, tt=tt: e.tensor_tensor(
                            out=OBT[ob][:, tt * 512:(tt + 1) * 512], in0=pp[sz][:, :], in1=WK[wk][:, 0:512],
                            op=ALU.mult), r=[("pp", sz), ("WK", wk)], w=[("OBT", ob)])
                    P.dma("sp", T["gB"][t0:t0 + 2048, cg * 512:(cg + 1) * 512].rearrange("(tt p) c -> p tt c", p=128),
                          OBT[ob][:, :].rearrange("p (tt c) -> p tt c", c=512), r=[("OBT", ob)])

        P.dma("sp", T["gall_d"][:, :], gall[:, :], r=["gall"])
        P.flush()


def build_rope_tables(P, posi, invf, cosT, sinT, WK, next_wk):
    TWO_PI = 2.0 * math.pi
    C1 = 6.28125
    C2 = TWO_PI - C1
    a = next_wk()
    b = next_wk()
    c = next_wk()
    A, Bk, R = WK[a], WK[b], WK[c]
    N = 2048
    P.dve(lambda e: e.tensor_copy(out=A[:, 0:N], in_=posi[:, :]), r=["posi"], w=[("WK", a)])
    P.dve(lambda e: e.tensor_scalar(out=A[:, 0:N], in0=A[:, 0:N], scalar1=invf[:, 0:1], scalar2=None, op0=ALU.mult),
          r=[("WK", a), "invf"], w=[("WK", a)])
    ki = posi
    P.dve(lambda e: e.tensor_scalar(out=Bk[:, 0:N], in0=A[:, 0:N], scalar1=1.0 / TWO_PI, scalar2=None, op0=ALU.mult),
          r=[("WK", a)], w=[("WK", b)])
    P.dve(lambda e: e.tensor_copy(out=ki[:, :], in_=Bk[:, 0:N]), r=[("WK", b)], w=["posi"])
    P.dve(lambda e: e.tensor_copy(out=Bk[:, 0:N], in_=ki[:, :]), r=["posi"], w=[("WK", b)])
    P.dve(lambda e: e.scalar_tensor_tensor(out=R[:, 0:N], in0=Bk[:, 0:N], scalar=-C1, in1=A[:, 0:N],
                                           op0=ALU.mult, op1=ALU.add), r=[("WK", a), ("WK", b)], w=[("WK", c)])
    P.dve(lambda e: e.scalar_tensor_tensor(out=R[:, 0:N], in0=Bk[:, 0:N], scalar=-C2, in1=R[:, 0:N],
                                           op0=ALU.mult, op1=ALU.add), r=[("WK", b), ("WK", c)], w=[("WK", c)])

    def wrap(X, key):
        P.dve(lambda e: e.tensor_scalar(out=Bk[:, 0:N], in0=X[:, 0:N], scalar1=math.pi, scalar2=-TWO_PI,
                                        op0=ALU.is_gt, op1=ALU.mult), r=[key], w=[("WK", b)])
        P.dve(lambda e: e.tensor_tensor(out=X[:, 0:N], in0=X[:, 0:N], in1=Bk[:, 0:N], op=ALU.add),
              r=[key, ("WK", b)], w=[key])
        P.dve(lambda e: e.tensor_scalar(out=Bk[:, 0:N], in0=X[:, 0:N], scalar1=-math.pi, scalar2=TWO_PI,
                                        op0=ALU.is_lt, op1=ALU.mult), r=[key], w=[("WK", b)])
        P.dve(lambda e: e.tensor_tensor(out=X[:, 0:N], in0=X[:, 0:N], in1=Bk[:, 0:N], op=ALU.add),
              r=[key, ("WK", b)], w=[key])

    wrap(R, ("WK", c))
    P.act(lambda e: e.activation(out=sinT[:, :], in_=R[:, 0:N], func=AF.Sin), r=[("WK", c)], w=["sinT"])
    P.dve(lambda e: e.tensor_scalar(out=R[:, 0:N], in0=R[:, 0:N], scalar1=math.pi / 2, scalar2=None, op0=ALU.add),
          r=[("WK", c)], w=[("WK", c)])
    wrap(R, ("WK", c))
    P.act(lambda e: e.activation(out=cosT[:, :], in_=R[:, 0:N], func=AF.Sin), r=[("WK", c)], w=["cosT"])


def phase_M(nc, P, T):
    mqT, mkT, mv, gA, yaT = T["mqT"], T["mkT"], T["mv"], T["gA"], T["yaT"]
    with contextlib.ExitStack() as st:
        sb = lambda name, shape, dt=F32: st.enter_context(nc.sbuf_tensor("M_" + name, list(shape), dt))
        ps = lambda name, shape, dt=F32: st.enter_context(nc.psum_tensor("M_" + name, list(shape), dt))
        qT = sb("qT", [128, 2, OWN], BF16)
        kT = sb("kT", [128, 2, S], BF16)
        va = sb("va", [128, 64, 257], BF16)
        ktok = sb("ktok", [128, 64, 256], BF16)
        hacc = sb("hacc", [128, 32, 256])
        gat = [sb("gat%d" % i, [128, 256], BF16) for i in range(2)]
        gall = sb("gall", [128, 1024])
        mlng = sb("mlng", [128, D])
        tri = sb("tri", [128, 4, 128])
        identf = sb("identf", [128, 128])
        identb = sb("identb", [128, 128], BF16)
        onesf = sb("onesf", [128, 128])
        LFt = [sb("LFt%d" % d, [128, 256]) for d in range(2)]
        Bc = [sb("Bc%d" % d, [128, 256]) for d in range(2)]
        Aa = [sb("Aa%d" % d, [128, 256]) for d in range(2)]
        EB = [sb("EB%d" % d, [128, 256]) for d in range(2)]
        WST = [sb("WST%d" % d, [128, 256]) for d in range(2)]
        DEC = [sb("DEC%d" % d, [128, 256]) for d in range(2)]
        Cst = [sb("Cst%d" % d, [128, 2, 257]) for d in range(2)]
        Cb = [[sb("Cb%d_%d" % (d, v), [128, 2, 257], BF16) for v in range(2)] for d in range(2)]
        cbver = [0, 0]
        dg = [sb("dg%d" % i, [128, 128]) for i in range(2)]
        Dm = [sb("Dm%d" % i, [128, 128]) for i in range(2)]
        Wm = [sb("Wm%d" % i, [128, 128], BF16) for i in range(2)]
        vw = [sb("vw%d" % i, [128, 257], BF16) for i in range(2)]
        tmpc = [sb("tmpc%d" % i, [128, 257]) for i in range(2)]
        tot = [sb("tot%d" % i, [128, 257]) for i in range(2)]
        sm = [sb("sm%d" % i, [128, 4]) for i in range(2)]
        hs = [sb("hs%d" % i, [128, 256]) for i in range(2)]
        sqv = [sb("sqv%d" % i, [128, 256]) for i in range(2)]
        yab = [sb("yab%d" % i, [128, 256], BF16) for i in range(2)]
        yas = [sb("yas%d" % i, [128, 2, 128], BF16) for i in range(2)]
        pD = ps("pD", [128, 512])
        pST = ps("pST", [128, 512])
        pIs = [ps("pI%d" % i, [128, 512]) for i in range(2)]
        pI = pIs[0]
        pC = ps("pC", [128, 512])
        pU = [ps("pU%d" % i, [128, 512]) for i in range(2)]
        ptrk = ps("ptrk", [128, 1024], BF16)
        ptry = ptrk

        P.dma("sp", gall[:], T["gall_d"], w=["gall"])
        P.dma("sp", mlng[:], T["mlng_rep"], w=["mlng"])
        P.dma("sp", tri[:], T["c_tri"], w=["tri"])
        P.dma("sp", identf[:], T["c_ident"], w=["identf"])
        P.dve(lambda e: e.tensor_copy(out=identb[:], in_=identf[:]), r=["identf"], w=["identb"])
        P.dve(lambda e: e.memset(onesf[:], 1.0), w=["onesf"])
        P.dve(lambda e: e.memset(va[:, :, 256:257], 1.0), w=["va1"])

        g4 = gall[:, :].rearrange("p (c g h) -> p c g h", g=4, h=4)
        v3 = lambda t: t[:, :].rearrange("p (c h) -> p c h", h=4)
        for d in range(2):
            i_d = g4[:, :, 2 * d, :]
            f_d = g4[:, :, 2 * d + 1, :]
            P.act(lambda e, d=d, f_d=f_d: e.activation(out=v3(LFt[d]), in_=f_d, func=AF.Exp, scale=-1.0),
                  r=["gall"], w=[("LFt", d)])
            P.act(lambda e, d=d: e.activation(out=LFt[d][:, :], in_=LFt[d][:, :], func=AF.Ln, bias=1.0),
                  r=[("LFt", d)], w=[("LFt", d)])
            P.pe(lambda e, d=d: e.matmul(pI[:, 0:256], tri[:, d, :], LFt[d][:, :], start=True, stop=True),
                 r=["tri", ("LFt", d)], w=[("pI", 0)])
            P.pe(lambda e, d=d: e.matmul(pC[:, 0:256], onesf[:, :], LFt[d][:, :], start=True, stop=True),
                 r=["onesf", ("LFt", d)], w=["pC"])
            P.dve(lambda e, d=d: e.tensor_scalar(out=Bc[d][:, :], in0=pI[:, 0:256], scalar1=-1.0, scalar2=None, op0=ALU.mult),
                  r=[("pI", 0)], w=[("Bc", d)])
            P.dve(lambda e, d=d, i_d=i_d: e.tensor_tensor(out=v3(Aa[d]), in0=pI[:, 0:256].rearrange("p (c h) -> p c h", h=4),
                                                         in1=i_d, op=ALU.add),
                  r=[("pI", 0), "gall"], w=[("Aa", d)])
            P.act(lambda e, d=d: e.activation(out=EB[d][:, :], in_=Bc[d][:, :], func=AF.Exp),
                  r=[("Bc", d)], w=[("EB", d)])
            P.dve(lambda e, d=d: e.tensor_copy(out=DEC[d][:, :], in_=pC[:, 0:256]), r=["pC"], w=[("DEC", d)])
            P.dve(lambda e, d=d: e.tensor_tensor(out=WST[d][:, :], in0=Aa[d][:, :], in1=DEC[d][:, :], op=ALU.subtract),
                  r=[("Aa", d), ("DEC", d)], w=[("WST", d)])
            P.act(lambda e, d=d: e.activation(out=WST[d][:, :], in_=WST[d][:, :], func=AF.Exp),
                  r=[("WST", d)], w=[("WST", d)])
            P.act(lambda e, d=d: e.activation(out=DEC[d][:, :], in_=DEC[d][:, :], func=AF.Exp, scale=-1.0),
                  r=[("DEC", d), ("WST", d)], w=[("DEC", d)])

        cnt = {"o": 0, "u": 0, "g": 0, "y": 0}

        def output_A1(h, d, c):
            i = cnt["o"] % 2
            cnt["o"] += 1
            col = c * 4 + h
            cs = slice(c * 128, (c + 1) * 128)
            pIx = pIs[i]
            P.dve(lambda e: e.tensor_scalar(out=dg[i][:], in0=identf[:], scalar1=Bc[d][:, col:col + 1], scalar2=None, op0=ALU.mult),
                  r=["identf", ("Bc", d)], w=[("dg", i)])
            P.pe(lambda e: e.matmul(pD[:, 0:128], onesf[:, :], dg[i][:, :], start=True, stop=False), r=["onesf", ("dg", i)], w=["pD"])
            P.pe(lambda e: e.matmul(pD[:, 0:128], identf[:, :], tri[:, 2 + d, :], start=False, stop=True), r=["identf", "tri"], w=["pD"])
            P.act(lambda e: e.activation(out=Dm[i][:], in_=pD[:, 0:128], func=AF.Exp, bias=Aa[d][:, col:col + 1]),
                  r=["pD", ("Aa", d)], w=[("Dm", i)])
            for dkc in range(2):
                P.pe(lambda e, dkc=dkc: e.matmul(pST[:, 0:128], kT[:, dkc, cs], qT[:, dkc, cs], start=(dkc == 0), stop=(dkc == 1)),
                     r=["kT", "qT"], w=["pST"])
            P.dve(lambda e: e.tensor_tensor(out=Wm[i][:], in0=pST[:, 0:128], in1=Dm[i][:], op=ALU.mult),
                  r=["pST", ("Dm", i)], w=[("Wm", i)])
            P.pe(lambda e: e.matmul(pIx[:, 0:257], Wm[i][:, :], va[:, c, :], start=True, stop=True),
                 r=[("Wm", i), "va", "va1"], w=[("pI", i)])
            return (h, d, c, i)

        def output_A2(ctx):
            h, d, c, i = ctx
            col = c * 4 + h
            cs = slice(c * 128, (c + 1) * 128)
            pIx = pIs[i]
            cbv = Cb[d][cbver[d]]
            cbk = ("Cb", d, cbver[d])
            for dkc in range(2):
                P.pe(lambda e, dkc=dkc: e.matmul(pC[:, 0:257], qT[:, dkc, cs], cbv[:, dkc, :], start=(dkc == 0), stop=(dkc == 1)),
                     r=["qT", cbk], w=["pC"])
            P.act(lambda e: e.activation(out=tmpc[i][:], in_=pC[:, 0:257], func=AF.Copy, scale=EB[d][:, col:col + 1]),
                  r=["pC", ("EB", d)], w=[("tmpc", i)])
            P.dve(lambda e: e.tensor_tensor(out=tot[i][:], in0=tmpc[i][:], in1=pIx[:, 0:257], op=ALU.add),
                  r=[("tmpc", i), ("pI", i)], w=[("tot", i)])

        def output_B(ctx):
            h, d, c, i = ctx
            col = c * 4 + h
            P.dve(lambda e: e.tensor_scalar(out=sm[i][:, 3:4], in0=tot[i][:, 256:257], scalar1=-1.0, scalar2=1.0, op0=ALU.mult, op1=ALU.max),
                  r=[("tot", i)], w=[("sm", i)])
            P.dve(lambda e: e.tensor_tensor(out=sm[i][:, 0:1], in0=tot[i][:, 256:257], in1=sm[i][:, 3:4], op=ALU.max),
                  r=[("tot", i), ("sm", i)], w=[("sm", i)])
            P.dve(lambda e: e.reciprocal(out=sm[i][:, 1:2], in_=sm[i][:, 0:1]), r=[("sm", i)], w=[("sm", i)])
            if d == 0:
                P.dve(lambda e: e.tensor_scalar(out=hacc[:, c, :], in0=tot[i][:, 0:256], scalar1=sm[i][:, 1:2], scalar2=None, op0=ALU.mult),
                      r=[("tot", i), ("sm", i)], w=[("hacc", c)])
                return
            gi = cnt["g"] % 2
            cnt["g"] += 1
            P.dma("sp", gat[gi][:], gA[c * 128:(c + 1) * 128, h * 256:(h + 1) * 256], w=[("gat", gi)])
            P.dve(lambda e: e.scalar_tensor_tensor(out=hs[i][:], in0=tot[i][:, 0:256], scalar=sm[i][:, 1:2], in1=hacc[:, c, :],
                                                   op0=ALU.mult, op1=ALU.add),
                  r=[("tot", i), ("sm", i), ("hacc", c)], w=[("hs", i)])
            P.dve(lambda e: e.tensor_tensor(out=sqv[i][:], in0=hs[i][:], in1=hs[i][:], op=ALU.mult), r=[("hs", i)], w=[("sqv", i)])
            P.dve(lambda e: e.reduce_sum(out=sm[i][:, 2:3], in_=sqv[i][:], axis=AX.X), r=[("sqv", i)], w=[("smb", i)])
            P.act(lambda e: e.activation(out=sm[i][:, 2:3], in_=sm[i][:, 2:3], func=AF.Ln, scale=1.0 / 256.0, bias=NORM_EPS),
                  r=[("smb", i)], w=[("smb", i)])
            P.act(lambda e: e.activation(out=sm[i][:, 2:3], in_=sm[i][:, 2:3], func=AF.Exp, scale=-0.5),
                  r=[("smb", i)], w=[("smb", i)])
            P.dve(lambda e: e.scalar_tensor_tensor(out=sqv[i][:], in0=hs[i][:], scalar=sm[i][:, 2:3],
                                                   in1=mlng[:, h * 256:(h + 1) * 256], op0=ALU.mult, op1=ALU.mult),
                  r=[("hs", i), ("smb", i), "mlng"], w=[("sqv", i)])
            P.dve(lambda e: e.tensor_tensor(out=yab[i][:], in0=sqv[i][:], in1=gat[gi][:], op=ALU.mult),
                  r=[("sqv", i), ("gat", gi)], w=[("yab", i)])
            def b2():
                for k in range(2):
                    P.pe(lambda e, k=k: e.transpose(out=ptry[:, k * 128:(k + 1) * 128], in_=yab[i][:, k * 128:(k + 1) * 128],
                                                    identity=identb[:]), r=[("yab", i), "identb"], w=["ptrk"])
                P.act(lambda e: e.activation(out=yas[i][:, :, :], in_=ptry[:, 0:256].rearrange("p (k t) -> p k t", k=2), func=AF.Copy),
                      r=["ptrk"], w=[("yas", i)])
                P.dma("sp", yaT[h * 256:(h + 1) * 256, c * 128:(c + 1) * 128].rearrange("(k p) t -> p k t", p=128),
                      yas[i][:, :, :], r=[("yas", i)])
            return b2

        def update_step(h, d, c):
            i = cnt["u"] % 2
            cnt["u"] += 1
            col = c * 4 + h
            nv = 1 - cbver[d]
            nb_ = Cb[d][nv]
            nbk = ("Cb", d, nv)
            P.act(lambda e: e.activation(out=vw[i][:], in_=va[:, c, :], func=AF.Copy, scale=WST[d][:, col:col + 1]),
                  r=["va", "va1", ("WST", d)], w=[("vw", i)])
            for dkc in range(2):
                P.pe(lambda e, dkc=dkc: e.matmul(pU[dkc][:, 0:257], ktok[:, c, dkc * 128:(dkc + 1) * 128], vw[i][:, :], start=True, stop=True),
                     r=[("ktok", c), ("vw", i)], w=[("pU", dkc)])
                P.dve(lambda e, dkc=dkc: e.scalar_tensor_tensor(out=Cst[d][:, dkc, :], in0=Cst[d][:, dkc, :], scalar=DEC[d][:, col:col + 1],
                                                                in1=pU[dkc][:, 0:257], op0=ALU.mult, op1=ALU.add),
                      r=[("Cst", d, dkc), ("DEC", d), ("pU", dkc)], w=[("Cst", d, dkc)])
                P.act(lambda e, dkc=dkc, nb_=nb_: e.activation(out=nb_[:, dkc, :], in_=Cst[d][:, dkc, :], func=AF.Copy),
                      r=[("Cst", d, dkc)], w=[nbk])
            cbver[d] = nv

        for h in range(4):
            for dkc in range(2):
                r0 = h * 256 + dkc * 128
                P.dma("sp", qT[:, dkc, :], mqT[r0:r0 + 128, 0:OWN], w=["qT"])
                P.dma("sp", kT[:, dkc, :], mkT[r0:r0 + 128, :], w=["kT"])
            P.dma("sp", va[:, :, 0:256], mv[:, h * 256:(h + 1) * 256].rearrange("(c p) f -> p c f", p=128), w=["va"])
            for d in range(2):
                P.dve(lambda e, d=d: e.memset(Cst[d][:], 0.0), w=[("Cst", d, 0), ("Cst", d, 1)])
                P.dve(lambda e, d=d: e.memset(Cb[d][0][:], 0.0), w=[("Cb", d, 0)])
                cbver[d] = 0
            for c in range(64):
                for dkc in range(2):
                    P.pe(lambda e, c=c, dkc=dkc: e.transpose(out=ptrk[:, dkc * 128:(dkc + 1) * 128],
                                                             in_=kT[:, dkc, c * 128:(c + 1) * 128], identity=identb[:]),
                         r=["kT", "identb"], w=["ptrk"])
                if c % 2 == 0:
                    P.act(lambda e, c=c: e.activation(out=ktok[:, c, :], in_=ptrk[:, 0:256], func=AF.Copy), r=["ptrk"], w=[("ktok", c)])
                else:
                    P.dve(lambda e, c=c: e.tensor_copy(out=ktok[:, c, :], in_=ptrk[:, 0:256]), r=["ptrk"], w=[("ktok", c)])
            ctx = output_A1(h, 0, 0)
            for i in range(32):
                output_A2(ctx)
                if i < 31:
                    update_step(h, 0, i)
                update_step(h, 1, 63 - i)
                nctx = output_A1(h, 0, i + 1) if i < 31 else output_A1(h, 1, 31)
                output_B(ctx)
                ctx = nctx
            pb2 = None
            for i in range(32, 64):
                c = 63 - i
                output_A2(ctx)
                if c > 0:
                    update_step(h, 1, c)
                    nctx = output_A1(h, 1, c - 1)
                if pb2 is not None:
                    pb2()
                pb2 = output_B(ctx)
                ctx = nctx
            pb2()
        P.flush()


def phase_M2(nc, P, T):
    mqT, mkT, mv, gA, yaT, hfwd = T["mqT"], T["mkT"], T["mv"], T["gA"], T["yaT"], T["hfwd"]
    with contextlib.ExitStack() as st:
        sb = lambda name, shape, dt=F32: st.enter_context(nc.sbuf_tensor("M2_" + name, list(shape), dt))
        ps = lambda name, shape, dt=F32: st.enter_context(nc.psum_tensor("M2_" + name, list(shape), dt))
        gall = sb("gall", [128, 1024])
        mlng = sb("mlng", [128, D])
        tri = sb("tri", [128, 4, 128])
        identf = sb("identf", [128, 128])
        identb = sb("identb", [128, 128], BF16)
        onesf = sb("onesf", [128, 128])
        LFt = [sb("LFt%d" % d, [128, 256]) for d in range(2)]
        Bc = [sb("Bc%d" % d, [128, 256]) for d in range(2)]
        Aa = [sb("Aa%d" % d, [128, 256]) for d in range(2)]
        EB = [sb("EB%d" % d, [128, 256]) for d in range(2)]
        WST = [sb("WST%d" % d, [128, 256]) for d in range(2)]
        DEC = [sb("DEC%d" % d, [128, 256]) for d in range(2)]
        Cst = [sb("Cst%d" % d, [128, 4, 2, 257]) for d in range(2)]
        Cb = [[sb("Cb%d_%d" % (d, v), [128, 4, 2, 257], BF16) for v in range(2)] for d in range(2)]
        cbver = [0, 0]
        NQ, NK = 2, 4
        qc = [sb("qc%d" % i, [128, 8, 128], BF16) for i in range(NQ)]
        kc = [sb("kc%d" % i, [128, 8, 128], BF16) for i in range(NK)]
        vc = [sb("vc%d" % i, [128, 4, 257], BF16) for i in range(NK)]
        ktc = [sb("ktc%d" % i, [128, 1024], BF16) for i in range(NK)]
        gac = [sb("gac%d" % i, [128, 1024], BF16) for i in range(2)]
        hfr = [sb("hfr%d" % i, [128, 1024]) for i in range(2)]
        hfw = [sb("hfw%d" % i, [128, 1024]) for i in range(2)]
        yasc = [sb("yasc%d" % i, [128, 8, 128], BF16) for i in range(2)]
        NWB = 4
        dg = [sb("dg%d" % i, [128, 128]) for i in range(NWB)]
        Dm = [sb("Dm%d" % i, [128, 128]) for i in range(NWB)]
        Wm = [sb("Wm%d" % i, [128, 128], BF16) for i in range(NWB)]
        vw = [sb("vw%d" % i, [128, 257], BF16) for i in range(NWB)]
        tmpc = [sb("tmpc%d" % i, [128, 257]) for i in range(NWB)]
        tot = [sb("tot%d" % i, [128, 257]) for i in range(NWB)]
        sm = [sb("sm%d" % i, [128, 4]) for i in range(NWB)]
        hs = [sb("hs%d" % i, [128, 256]) for i in range(NWB)]
        sqv = [sb("sqv%d" % i, [128, 256]) for i in range(NWB)]
        yab = [sb("yab%d" % i, [128, 256], BF16) for i in range(NWB)]
        pD = ps("pD", [128, 512])
        pST = ps("pST", [128, 512])
        pI = ps("pI", [128, 512])
        pC = ps("pC", [128, 512])
        pU = [ps("pU%d" % i, [128, 512]) for i in range(2)]
        ptrk = ps("ptrk", [128, 1024], BF16)
        ptry = ps("ptry", [128, 1024], BF16)

        P.dma("sp", gall[:], T["gall_d"], w=["gall"])
        P.dma("sp", mlng[:], T["mlng_rep"], w=["mlng"])
        P.dma("sp", tri[:], T["c_tri"], w=["tri"])
        P.dma("sp", identf[:], T["c_ident"], w=["identf"])
        P.dve(lambda e: e.tensor_copy(out=identb[:], in_=identf[:]), r=["identf"], w=["identb"])
        P.dve(lambda e: e.memset(onesf[:], 1.0), w=["onesf"])
        for i in range(NK):
            P.dve(lambda e, i=i: e.memset(vc[i][:, :, 256:257], 1.0), w=[("vc1", i)])
        for d in range(2):
            P.dve(lambda e, d=d: e.memset(Cst[d][:], 0.0), w=[("Cst", d, h, k) for h in range(4) for k in range(2)])
            P.dve(lambda e, d=d: e.memset(Cb[d][0][:], 0.0), w=[("Cb", d, 0, h) for h in range(4)])

        g4 = gall[:, :].rearrange("p (c g h) -> p c g h", g=4, h=4)
        v3 = lambda t: t[:, :].rearrange("p (c h) -> p c h", h=4)
        for d in range(2):
            i_d = g4[:, :, 2 * d, :]
            f_d = g4[:, :, 2 * d + 1, :]
            P.act(lambda e, d=d, f_d=f_d: e.activation(out=v3(LFt[d]), in_=f_d, func=AF.Exp, scale=-1.0),
                  r=["gall"], w=[("LFt", d)])
            P.act(lambda e, d=d: e.activation(out=LFt[d][:, :], in_=LFt[d][:, :], func=AF.Ln, bias=1.0),
                  r=[("LFt", d)], w=[("LFt", d)])
            P.pe(lambda e, d=d: e.matmul(pI[:, 0:256], tri[:, d, :], LFt[d][:, :], start=True, stop=True),
                 r=["tri", ("LFt", d)], w=["pI"])
            P.pe(lambda e, d=d: e.matmul(pC[:, 0:256], onesf[:, :], LFt[d][:, :], start=True, stop=True),
                 r=["onesf", ("LFt", d)], w=["pC"])
            P.dve(lambda e, d=d: e.tensor_scalar(out=Bc[d][:, :], in0=pI[:, 0:256], scalar1=-1.0, scalar2=None, op0=ALU.mult),
                  r=["pI"], w=[("Bc", d)])
            P.dve(lambda e, d=d, i_d=i_d: e.tensor_tensor(out=v3(Aa[d]), in0=pI[:, 0:256].rearrange("p (c h) -> p c h", h=4),
                                                         in1=i_d, op=ALU.add),
                  r=["pI", "gall"], w=[("Aa", d)])
            P.act(lambda e, d=d: e.activation(out=EB[d][:, :], in_=Bc[d][:, :], func=AF.Exp),
                  r=[("Bc", d)], w=[("EB", d)])
            P.dve(lambda e, d=d: e.tensor_copy(out=DEC[d][:, :], in_=pC[:, 0:256]), r=["pC"], w=[("DEC", d)])
            P.dve(lambda e, d=d: e.tensor_tensor(out=WST[d][:, :], in0=Aa[d][:, :], in1=DEC[d][:, :], op=ALU.subtract),
                  r=[("Aa", d), ("DEC", d)], w=[("WST", d)])
            P.act(lambda e, d=d: e.activation(out=WST[d][:, :], in_=WST[d][:, :], func=AF.Exp),
                  r=[("WST", d)], w=[("WST", d)])
            P.act(lambda e, d=d: e.activation(out=DEC[d][:, :], in_=DEC[d][:, :], func=AF.Exp, scale=-1.0),
                  r=[("DEC", d), ("WST", d)], w=[("DEC", d)])

        cnt = {"w": 0, "k": 0, "q": 0, "g": 0, "cp": 0}

        def load_chunk(c, need_q, need_bw_out):
            ks = cnt["k"] % NK
            cnt["k"] += 1
            cs = slice(c * 128, (c + 1) * 128)
            P.dma("sp", kc[ks][:], mkT[:, cs].rearrange("(f p) t -> p f t", p=128), w=[("kc", ks)])
            P.dma("sp", vc[ks][:, :, 0:256], mv[cs, :].rearrange("p (h f) -> p h f", h=4), w=[("vc", ks)])
            for f in range(8):
                P.pe(lambda e, f=f: e.transpose(out=ptrk[:, f * 128:(f + 1) * 128], in_=kc[ks][:, f, :], identity=identb[:]),
                     r=[("kc", ks), "identb"], w=["ptrk"])
            if cnt["cp"] % 2 == 0:
                P.act(lambda e: e.activation(out=ktc[ks][:, :], in_=ptrk[:, :], func=AF.Copy), r=["ptrk"], w=[("ktc", ks)])
            else:
                P.dve(lambda e: e.tensor_copy(out=ktc[ks][:, :], in_=ptrk[:, :]), r=["ptrk"], w=[("ktc", ks)])
            cnt["cp"] += 1
            ctx = {"c": c, "ks": ks}
            if need_q:
                qs = cnt["q"] % NQ
                cnt["q"] += 1
                P.dma("sp", qc[qs][:], mqT[:, cs].rearrange("(f p) t -> p f t", p=128), w=[("qc", qs)])
                ctx["qs"] = qs
            return ctx

        def load_bw_extra(ctx):
            c = ctx["c"]
            cs = slice(c * 128, (c + 1) * 128)
            gs = cnt["g"] % 2
            cnt["g"] += 1
            P.dma("sp", gac[gs][:], gA[cs, :], w=[("gac", gs)])
            P.dma("sp", hfr[gs][:], hfwd[cs, :], r=[("hfwd", c)], w=[("hfr", gs)])
            ctx["gs"] = gs

        def output_A(ck, h, d):
            i = cnt["w"] % NWB
            cnt["w"] += 1
            c, ks, qs = ck["c"], ck["ks"], ck["qs"]
            col = c * 4 + h
            cbv = Cb[d][cbver[d]]
            cbk = ("Cb", d, cbver[d], h)
            P.dve(lambda e: e.tensor_scalar(out=dg[i][:], in0=identf[:], scalar1=Bc[d][:, col:col + 1], scalar2=None, op0=ALU.mult),
                  r=["identf", ("Bc", d)], w=[("dg", i)])
            P.pe(lambda e: e.matmul(pD[:, 0:128], onesf[:, :], dg[i][:, :], start=True, stop=False), r=["onesf", ("dg", i)], w=["pD"])
            P.pe(lambda e: e.matmul(pD[:, 0:128], identf[:, :], tri[:, 2 + d, :], start=False, stop=True), r=["identf", "tri"], w=["pD"])
            P.act(lambda e: e.activation(out=Dm[i][:], in_=pD[:, 0:128], func=AF.Exp, bias=Aa[d][:, col:col + 1]),
                  r=["pD", ("Aa", d)], w=[("Dm", i)])
            for dkc in range(2):
                P.pe(lambda e, dkc=dkc: e.matmul(pST[:, 0:128], kc[ks][:, h * 2 + dkc, :], qc[qs][:, h * 2 + dkc, :],
                                                 start=(dkc == 0), stop=(dkc == 1)),
                     r=[("kc", ks), ("qc", qs)], w=["pST"])
            for dkc in range(2):
                P.pe(lambda e, dkc=dkc: e.matmul(pC[:, 0:257], qc[qs][:, h * 2 + dkc, :], cbv[:, h, dkc, :],
                                                 start=(dkc == 0), stop=(dkc == 1)),
                     r=[("qc", qs), cbk], w=["pC"])
            P.act(lambda e: e.activation(out=tmpc[i][:], in_=pC[:, 0:257], func=AF.Copy, scale=EB[d][:, col:col + 1]),
                  r=["pC", ("EB", d)], w=[("tmpc", i)])
            P.dve(lambda e: e.tensor_tensor(out=Wm[i][:], in0=pST[:, 0:128], in1=Dm[i][:], op=ALU.mult),
                  r=["pST", ("Dm", i)], w=[("Wm", i)])
            P.pe(lambda e: e.matmul(pI[:, 0:257], Wm[i][:, :], vc[ks][:, h, :], start=True, stop=True),
                 r=[("Wm", i), ("vc", ks), ("vc1", ks)], w=["pI"])
            P.dve(lambda e: e.tensor_tensor(out=tot[i][:], in0=tmpc[i][:], in1=pI[:, 0:257], op=ALU.add),
                  r=[("tmpc", i), "pI"], w=[("tot", i)])
            return (ck, h, d, i)

        def output_B(ctx, ws):
            ck, h, d, i = ctx
            c = ck["c"]
            hsl = slice(h * 256, (h + 1) * 256)
            P.dve(lambda e: e.tensor_scalar(out=sm[i][:, 3:4], in0=tot[i][:, 256:257], scalar1=-1.0, scalar2=1.0, op0=ALU.mult, op1=ALU.max),
                  r=[("tot", i)], w=[("sm", i)])
            P.dve(lambda e: e.tensor_tensor(out=sm[i][:, 0:1], in0=tot[i][:, 256:257], in1=sm[i][:, 3:4], op=ALU.max),
                  r=[("tot", i), ("sm", i)], w=[("sm", i)])
            P.dve(lambda e: e.reciprocal(out=sm[i][:, 1:2], in_=sm[i][:, 0:1]), r=[("sm", i)], w=[("sm", i)])
            if d == 0:
                P.act(lambda e: e.activation(out=hfw[ws][:, hsl], in_=tot[i][:, 0:256], func=AF.Copy, scale=sm[i][:, 1:2]),
                      r=[("tot", i), ("sm", i)], w=[("hfw", ws, h)])
                return
            gs = ck["gs"]
            P.dve(lambda e: e.scalar_tensor_tensor(out=hs[i][:], in0=tot[i][:, 0:256], scalar=sm[i][:, 1:2], in1=hfr[gs][:, hsl],
                                                   op0=ALU.mult, op1=ALU.add),
                  r=[("tot", i), ("sm", i), ("hfr", gs)], w=[("hs", i)])
            P.dve(lambda e: e.tensor_tensor(out=sqv[i][:], in0=hs[i][:], in1=hs[i][:], op=ALU.mult), r=[("hs", i)], w=[("sqv", i)])
            P.dve(lambda e: e.reduce_sum(out=sm[i][:, 2:3], in_=sqv[i][:], axis=AX.X), r=[("sqv", i)], w=[("smb", i)])
            P.act(lambda e: e.activation(out=sm[i][:, 2:3], in_=sm[i][:, 2:3], func=AF.Ln, scale=1.0 / 256.0, bias=NORM_EPS),
                  r=[("smb", i)], w=[("smb", i)])
            P.act(lambda e: e.activation(out=sm[i][:, 2:3], in_=sm[i][:, 2:3], func=AF.Exp, scale=-0.5),
                  r=[("smb", i)], w=[("smb", i)])
            P.dve(lambda e: e.scalar_tensor_tensor(out=sqv[i][:], in0=hs[i][:], scalar=sm[i][:, 2:3],
                                                   in1=mlng[:, hsl], op0=ALU.mult, op1=ALU.mult),
                  r=[("hs", i), ("smb", i), "mlng"], w=[("sqv", i)])
            P.dve(lambda e: e.tensor_tensor(out=yab[i][:], in0=sqv[i][:], in1=gac[gs][:, hsl], op=ALU.mult),
                  r=[("sqv", i), ("gac", gs)], w=[("yab", i)])
            for k in range(2):
                f = h * 2 + k
                P.pe(lambda e, k=k, f=f: e.transpose(out=ptry[:, f * 128:(f + 1) * 128], in_=yab[i][:, k * 128:(k + 1) * 128],
                                                     identity=identb[:]), r=[("yab", i), "identb"], w=["ptry"])

        def update_step(ck, h, d):
            i = cnt["w"] % NWB
            cnt["w"] += 1
            c, ks = ck["c"], ck["ks"]
            col = c * 4 + h
            nv = 1 - cbver[d]
            nb_ = Cb[d][nv]
            nbk = ("Cb", d, nv, h)
            P.act(lambda e: e.activation(out=vw[i][:], in_=vc[ks][:, h, :], func=AF.Copy, scale=WST[d][:, col:col + 1]),
                  r=[("vc", ks), ("vc1", ks), ("WST", d)], w=[("vw", i)])
            for dkc in range(2):
                f = h * 2 + dkc
                P.pe(lambda e, dkc=dkc, f=f: e.matmul(pU[dkc][:, 0:257], ktc[ks][:, f * 128:(f + 1) * 128], vw[i][:, :], start=True, stop=True),
                     r=[("ktc", ks), ("vw", i)], w=[("pU", dkc)])
                P.dve(lambda e, dkc=dkc: e.scalar_tensor_tensor(out=Cst[d][:, h, dkc, :], in0=Cst[d][:, h, dkc, :], scalar=DEC[d][:, col:col + 1],
                                                                in1=pU[dkc][:, 0:257], op0=ALU.mult, op1=ALU.add),
                      r=[("Cst", d, h, dkc), ("DEC", d), ("pU", dkc)], w=[("Cst", d, h, dkc)])
                P.act(lambda e, dkc=dkc: e.activation(out=nb_[:, h, dkc, :], in_=Cst[d][:, h, dkc, :], func=AF.Copy),
                      r=[("Cst", d, h, dkc)], w=[nbk])

        def loads_for(step):
            if step < 32:
                return [load_chunk(step, True, False), load_chunk(63 - step, False, False)]
            return [load_chunk(63 - step, True, True)]

        nxt = loads_for(0)
        for step in range(64):
            cur = nxt
            if step + 1 < 64:
                nxt = loads_for(step + 1)
            ws = step % 2
            if step < 32:
                ca, cbk_ = cur
                ctxs = [output_A(ca, h, 0) for h in range(4)]
                if step < 31:
                    for h in range(4):
                        update_step(ca, h, 0)
                    cbver[0] = 1 - cbver[0]
                for h in range(4):
                    update_step(cbk_, h, 1)
                cbver[1] = 1 - cbver[1]
                for cx in ctxs:
                    output_B(cx, ws)
                c = ca["c"]
                P.dma("sp", hfwd[c * 128:(c + 1) * 128, :], hfw[ws][:, :], r=[("hfw", ws, h) for h in range(4)], w=[("hfwd", c)])
                if step == 31:
                    load_bw_extra(nxt[0])
            else:
                ca = cur[0]
                c = ca["c"]
                ctxs = [output_A(ca, h, 1) for h in range(4)]
                if c > 0:
                    for h in range(4):
                        update_step(ca, h, 1)
                    cbver[1] = 1 - cbver[1]
                for cx in ctxs:
                    output_B(cx, ws)
                P.act(lambda e, ws=ws: e.activation(out=yasc[ws][:, :, :], in_=ptry[:, :].rearrange("p (f t) -> p f t", f=8), func=AF.Copy),
                      r=["ptry"], w=[("yasc", ws)])
                P.dma("sp", yaT[:, c * 128:(c + 1) * 128].rearrange("(f p) t -> p f t", p=128), yasc[ws][:, :, :], r=[("yasc", ws)])
                if step + 1 < 64:
                    load_bw_extra(nxt[0])
        P.flush()


def phase_D(nc, P, T):
    dqT, dkT, dv, gB, ybT = T["dqT"], T["dkT"], T["dv"], T["gB"], T["ybT"]
    with contextlib.ExitStack() as st:
        sb = lambda name, shape, dt=F32: st.enter_context(nc.sbuf_tensor("D_" + name, list(shape), dt))
        ps = lambda name, shape, dt=F32: st.enter_context(nc.psum_tensor("D_" + name, list(shape), dt))
        kT = [sb("dkT%d" % i, [128, S], BF16) for i in range(2)]
        va = [sb("dva%d" % i, [128, 64, 129], BF16) for i in range(2)]
        qT = [sb("dqT%d" % i, [128, OWN], BF16) for i in range(2)]
        NE = 3
        E = [sb("dE%d" % i, [128, 1024], BF16) for i in range(NE)]
        gbt = [sb("dgb%d" % i, [128, 4, 128], BF16) for i in range(2)]
        lamr = sb("lamr", [128, 256])
        ltmp = sb("ltmp", [128, 128])
        lam = sb("lam", [128, 4])
        subg = sb("subg", [128, 128])
        identf = sb("identf", [128, 128])
        identb = sb("identb", [128, 128], BF16)
        r12 = [sb("r12_%d" % i, [128, 4]) for i in range(2)]
        o1 = [sb("o1_%d" % i, [128, 128]) for i in range(2)]
        o2 = [sb("o2_%d" % i, [128, 128]) for i in range(2)]
        sq = [sb("sq_%d" % i, [128, 128]) for i in range(2)]
        ybq = [sb("ybq%d" % i, [128, 128], BF16) for i in range(8)]
        ybs = [sb("ybs%d" % i, [128, 512], BF16) for i in range(2)]
        pS = [ps("pS%d" % i, [128, 1024]) for i in range(2)]
        pacc = [ps("pacc%d" % i, [128, 512]) for i in range(3)]
        ptr = ps("dptr", [128, 512], BF16)

        P.dma("sp", lamr[:], T["lam_rep"], w=["lamr"])
        P.dma("sp", subg[:], T["subln_rep"], w=["subg"])
        P.dma("sp", identf[:], T["c_ident"], w=["identf"])
        P.dve(lambda e: e.tensor_copy(out=identb[:], in_=identf[:]), r=["identf"], w=["identb"])
        P.dve(lambda e: e.tensor_scalar(out=subg[:], in0=subg[:], scalar1=(1.0 - LAMBDA_INIT), scalar2=None, op0=ALU.mult),
              r=["subg"], w=["subg"])
        P.dve(lambda e: e.tensor_tensor(out=ltmp[:, 0:64], in0=lamr[:, 0:64], in1=lamr[:, 64:128], op=ALU.mult),
              r=["lamr"], w=["ltmp"])
        P.dve(lambda e: e.tensor_tensor(out=ltmp[:, 64:128], in0=lamr[:, 128:192], in1=lamr[:, 192:256], op=ALU.mult),
              r=["lamr"], w=["ltmp"])
        P.dve(lambda e: e.reduce_sum(out=lam[:, 0:2], in_=ltmp[:, :].rearrange("p (a b) -> p a b", a=2), axis=AX.X),
              r=["ltmp"], w=["lam"])
        P.act(lambda e: e.activation(out=lam[:, 0:2], in_=lam[:, 0:2], func=AF.Exp), r=["lam"], w=["lam"])
        P.dve(lambda e: e.tensor_tensor(out=lam[:, 2:3], in0=lam[:, 0:1], in1=lam[:, 1:2], op=ALU.subtract),
              r=["lam"], w=["lam"])
        P.dve(lambda e: e.tensor_scalar(out=lam[:, 3:4], in0=lam[:, 2:3], scalar1=LAMBDA_INIT, scalar2=-1.0,
                                        op0=ALU.add, op1=ALU.mult), r=["lam"], w=["lam"])
        for i in range(2):
            P.dve(lambda e, i=i: e.memset(va[i][:, :, 128:129], 1.0), w=[("va1", i)])

        def acc_ap(a, lo, hi):
            return pacc[a // 3][:, (a % 3) * 129 + lo:(a % 3) * 129 + hi]

        accS = [sb("accS%d" % i, [128, 8 * 129]) for i in range(2)]
        mhalf = sb("mhalf", [128, 1])
        P.dve(lambda e: e.memset(mhalf[:], -0.5), w=["mhalf"])

        def load_head(h):
            hs = h % 2
            P.dma("sp", kT[hs][:], dkT[h * 128:(h + 1) * 128, :], w=[("kT", hs)])
            P.dma("sp", qT[hs][:], dqT[h * 128:(h + 1) * 128, :], w=[("qT", hs)])
            P.dma("sp", va[hs][:, :, 0:128], dv[:, h * 128:(h + 1) * 128].rearrange("(kb p) c -> p kb c", p=128),
                  w=[("va", hs)])

        def load_gb(h, qt):
            gs = (h * 8 + qt) % 2
            P.dma("sp", gbt[gs][:], gB[qt * 512:(qt + 1) * 512, h * 128:(h + 1) * 128].rearrange("(u p) c -> p u c", p=128),
                  w=[("gbt", gs)])

        def qk(i, h, qt, kb):
            sl, hs = i % 2, h % 2
            for j in range(2):
                P.pe(lambda e, j=j: e.matmul(
                    pS[sl][:, j * 512:(j + 1) * 512], kT[hs][j * 64:(j + 1) * 64, kb * 128:(kb + 1) * 128],
                    qT[hs][j * 64:(j + 1) * 64, qt * 512:(qt + 1) * 512], start=True, stop=True),
                    r=[("kT", hs), ("qT", hs)], w=[("pS", sl)])

        def ex(i):
            sl, es = i % 2, i % NE
            P.act(lambda e: e.activation(out=E[es][:, :], in_=pS[sl][:, :], func=AF.Exp), r=[("pS", sl)], w=[("E", es)])

        def pv(i, h, kb):
            es, hs = i % NE, h % 2
            for a in range(8):
                j, u = a // 4, a % 4
                P.pe(lambda e, a=a, j=j, u=u: e.matmul(
                    acc_ap(a, 0, 129), E[es][:, j * 512 + u * 128:j * 512 + (u + 1) * 128], va[hs][:, kb, :],
                    start=(kb == 0 and a % 3 == 0), stop=(kb == 63), skip_group_check=True),
                    r=[("E", es), ("va", hs), ("va1", hs)], w=[("accb", a // 3)])

        epi = [0]

        def epilogue(h, qt):
            gs = (h * 8 + qt) % 2
            ys = gs
            ai = gs
            A = accS[ai]
            for b in range(3):
                n = 387 if b < 2 else 258
                P.dve(lambda e, b=b, n=n: e.tensor_copy(out=A[:, b * 387:b * 387 + n], in_=pacc[b][:, 0:n]),
                      r=[("accb", b)], w=[("accS", ai)])
            sa = lambda a, lo, hi: A[:, a * 129 + lo:a * 129 + hi]
            for u in range(4):
                ep = epi[0] % 2
                epi[0] += 1
                a0, a1 = u, 4 + u
                P.dve(lambda e, ep=ep, a0=a0: e.reciprocal(out=r12[ep][:, 0:1], in_=sa(a0, 128, 129)),
                      r=[("accS", ai)], w=[("r12", ep)])
                P.dve(lambda e, ep=ep, a1=a1: e.reciprocal(out=r12[ep][:, 1:2], in_=sa(a1, 128, 129)),
                      r=[("accS", ai)], w=[("r12", ep)])
                P.dve(lambda e, ep=ep: e.tensor_tensor(out=r12[ep][:, 2:3], in0=r12[ep][:, 1:2], in1=lam[:, 3:4], op=ALU.mult),
                      r=[("r12", ep), "lam"], w=[("r12", ep)])
                P.dve(lambda e, ep=ep, a0=a0: e.tensor_scalar(out=o1[ep][:], in0=sa(a0, 0, 128), scalar1=r12[ep][:, 0:1],
                                                              scalar2=None, op0=ALU.mult),
                      r=[("accS", ai), ("r12", ep)], w=[("o1", ep)])
                P.dve(lambda e, ep=ep, a1=a1: e.scalar_tensor_tensor(out=o2[ep][:], in0=sa(a1, 0, 128), scalar=r12[ep][:, 2:3],
                                                                     in1=o1[ep][:], op0=ALU.mult, op1=ALU.add),
                      r=[("accS", ai), ("r12", ep), ("o1", ep)], w=[("o2", ep)])
                P.dve(lambda e, ep=ep: e.tensor_tensor(out=sq[ep][:], in0=o2[ep][:], in1=o2[ep][:], op=ALU.mult),
                      r=[("o2", ep)], w=[("sq", ep)])
                P.dve(lambda e, ep=ep: e.reduce_sum(out=r12[ep][:, 3:4], in_=sq[ep][:], axis=AX.X),
                      r=[("sq", ep)], w=[("r12b", ep)])
                P.dve(lambda e, ep=ep: e.tensor_scalar(out=r12[ep][:, 3:4], in0=r12[ep][:, 3:4], scalar1=1.0 / 128.0, scalar2=NORM_EPS,
                                                       op0=ALU.mult, op1=ALU.add), r=[("r12b", ep)], w=[("r12b", ep)])
                P.pool(lambda e, ep=ep: e.tensor_tensor(out=r12[ep][:, 3:4], in0=r12[ep][:, 3:4], in1=mhalf[:, 0:1], op=ALU.pow),
                       r=[("r12b", ep), "mhalf"], w=[("r12b", ep)])
                P.dve(lambda e, ep=ep: e.scalar_tensor_tensor(out=o1[ep][:], in0=o2[ep][:], scalar=r12[ep][:, 3:4],
                                                              in1=subg[:], op0=ALU.mult, op1=ALU.mult),
                      r=[("o2", ep), ("r12b", ep), "subg"], w=[("o1", ep)])
                yq = gs * 4 + u
                P.dve(lambda e, ep=ep, u=u, yq=yq: e.tensor_tensor(out=ybq[yq][:], in0=o1[ep][:], in1=gbt[gs][:, u, :], op=ALU.mult),
                      r=[("o1", ep), ("gbt", gs)], w=[("ybq", yq)])

        def epilogue2(h, qt):
            gs = (h * 8 + qt) % 2
            ys = gs
            for u in range(4):
                yq = gs * 4 + u
                P.pe(lambda e, u=u, yq=yq: e.transpose(out=ptr[:, u * 128:(u + 1) * 128], in_=ybq[yq][:], identity=identb[:]),
                     r=[("ybq", yq), "identb"], w=["ptr"])
            P.dve(lambda e: e.tensor_copy(out=ybs[ys][:, :], in_=ptr[:, :]), r=["ptr"], w=[("ybs", ys)])
            P.dma("sp", ybT[h * 128:(h + 1) * 128, qt * 512:(qt + 1) * 512], ybs[ys][:], r=[("ybs", ys)])

        blocks = [(h, qt, kb) for h in range(8) for qt in range(8) for kb in range(64)]
        nb = len(blocks)
        load_head(0)
        load_gb(0, 0)
        qk(0, *blocks[0])
        qk(1, *blocks[1])
        for i, (h, qt, kb) in enumerate(blocks):
            if kb == 0 and qt == 0 and h + 1 < 8:
                load_head(h + 1)
            if kb == 0:
                nxt = h * 8 + qt + 1
                if nxt < 64:
                    load_gb(nxt // 8, nxt % 8)
            ex(i)
            if i + 2 < nb:
                qk(i + 2, *blocks[i + 2])
            pv(i, h, kb)
            if kb == 63:
                epilogue(h, qt)
            if kb == 40 and (h, qt) != (0, 0):
                pq = h * 8 + qt - 1
                epilogue2(pq // 8, pq % 8)
        epilogue2(7, 7)
        P.flush()


def phase_O(nc, P, T):
    x, out, yaT, ybT, sgT = T["x"], T["out"], T["yaT"], T["ybT"], T["sgT"]
    with contextlib.ExitStack() as st:
        sb = lambda name, shape, dt=F32: st.enter_context(nc.sbuf_tensor("O_" + name, list(shape), dt))
        ps = lambda name, shape, dt=F32: st.enter_context(nc.psum_tensor("O_" + name, list(shape), dt))
        Wa, Wb, Wo = T["ow"]["Wa"], T["ow"]["Wb"], T["ow"]["Wo"]
        fing = sb("fing", [128, D])
        ya = [sb("ya%d" % i, [128, 8, 512], BF16) for i in range(2)]
        yb = [sb("yb%d" % i, [128, 8, 512], BF16) for i in range(2)]
        sa = [sb("sa%d" % i, [128, 8, 512], BF16) for i in range(2)]
        sbb = [sb("sb%d" % i, [128, 8, 512], BF16) for i in range(2)]
        mixT = [sb("mixT%d" % i, [128, 8, 512], BF16) for i in range(2)]
        t1 = [sb("t1_%d" % i, [128, 512]) for i in range(2)]
        t2 = [sb("t2_%d" % i, [128, 512]) for i in range(2)]
        xt = [sb("xt%d" % i, [128, D]) for i in range(2)]
        xo = [sb("xo%d" % i, [128, D]) for i in range(2)]
        junk = sb("junk", [128, D], BF16)
        ssq = [sb("ssq%d" % i, [128, 2]) for i in range(2)]
        pa = [ps("pa%d" % i, [128, 512]) for i in range(2)]
        pb = [ps("pb%d" % i, [128, 512]) for i in range(2)]
        po = [ps("po%d" % i, [128, 512]) for i in range(4)]

        P.dma("sp", fing[:], T["fing_rep"], w=["fing"])
        wkeys = lambda k: [(k, 0), (k, 1)]
        cnt = {"p": 0, "o": 0, "x": 0}
        for tt in range(8):
            sl = tt % 2
            ts_ = slice(tt * 512, (tt + 1) * 512)
            P.dma("sp", ya[sl][:], yaT[:, ts_].rearrange("(cc p) t -> p cc t", p=128), w=[("ya", sl)])
            P.dma("sp", yb[sl][:], ybT[:, ts_].rearrange("(cc p) t -> p cc t", p=128), w=[("yb", sl)])
            P.dma("sp", sa[sl][:], sgT[0, :, ts_].rearrange("(cc p) t -> p cc t", p=128), w=[("sa", sl)])
            P.dma("sp", sbb[sl][:], sgT[1, :, ts_].rearrange("(cc p) t -> p cc t", p=128), w=[("sb", sl)])
            for dd in range(8):
                i = cnt["p"] % 2
                cnt["p"] += 1
                for cc in range(8):
                    P.pe(lambda e, i=i, cc=cc, dd=dd, sl=sl: e.matmul(pa[i][:, :], Wa[:, cc, dd * 128:(dd + 1) * 128], ya[sl][:, cc, :],
                                                                      start=(cc == 0), stop=(cc == 7)),
                         r=wkeys("Wa") + [("ya", sl)], w=[("pa", i)])
                for cc in range(8):
                    P.pe(lambda e, i=i, cc=cc, dd=dd, sl=sl: e.matmul(pb[i][:, :], Wb[:, cc, dd * 128:(dd + 1) * 128], yb[sl][:, cc, :],
                                                                      start=(cc == 0), stop=(cc == 7)),
                         r=wkeys("Wb") + [("yb", sl)], w=[("pb", i)])
                P.dve(lambda e, i=i, dd=dd, sl=sl: e.tensor_tensor(out=t1[i][:], in0=pa[i][:, :], in1=sa[sl][:, dd, :], op=ALU.mult),
                      r=[("pa", i), ("sa", sl)], w=[("t1", i)])
                P.dve(lambda e, i=i, dd=dd, sl=sl: e.tensor_tensor(out=t2[i][:], in0=pb[i][:, :], in1=sbb[sl][:, dd, :], op=ALU.mult),
                      r=[("pb", i), ("sb", sl)], w=[("t2", i)])
                P.dve(lambda e, i=i, dd=dd, sl=sl: e.tensor_tensor(out=mixT[sl][:, dd, :], in0=t1[i][:], in1=t2[i][:], op=ALU.add),
                      r=[("t1", i), ("t2", i)], w=[("mixT", sl)])
            for u in range(4):
                xs = cnt["x"] % 2
                cnt["x"] += 1
                r0 = tt * 512 + u * 128
                P.dma("sp", xt[xs][:], x[r0:r0 + 128, :], w=[("xt", xs)])
                for eg in range(2):
                    o = cnt["o"] % 4
                    cnt["o"] += 1
                    for dd in range(8):
                        P.pe(lambda e, o=o, dd=dd, sl=sl, u=u, eg=eg: e.matmul(
                            po[o][:, :], mixT[sl][:, dd, u * 128:(u + 1) * 128], Wo[:, dd, eg * 512:(eg + 1) * 512],
                            start=(dd == 0), stop=(dd == 7)), r=[("mixT", sl), ("Wo", eg)], w=[("po", o)])
                    P.dve(lambda e, o=o, xs=xs, eg=eg: e.tensor_tensor(out=xo[xs][:, eg * 512:(eg + 1) * 512], in0=po[o][:, :],
                                                                       in1=xt[xs][:, eg * 512:(eg + 1) * 512], op=ALU.add),
                          r=[("po", o), ("xt", xs)], w=[("xo", xs, eg)])
                P.act(lambda e, xs=xs: e.activation(out=junk[:], in_=xo[xs][:], func=AF.Square, accum_out=ssq[xs][:, 0:1]),
                      r=[("xo", xs, 0), ("xo", xs, 1)], w=[("ssq", xs)])
                P.act(lambda e, xs=xs: e.activation(out=ssq[xs][:, 0:1], in_=ssq[xs][:, 0:1], func=AF.Sqrt, scale=1.0 / D, bias=NORM_EPS),
                      r=[("ssq", xs)], w=[("ssq", xs)])
                P.dve(lambda e, xs=xs: e.reciprocal(out=ssq[xs][:, 1:2], in_=ssq[xs][:, 0:1]), r=[("ssq", xs)], w=[("ssq", xs)])
                P.dve(lambda e, xs=xs: e.scalar_tensor_tensor(out=xo[xs][:], in0=xo[xs][:], scalar=ssq[xs][:, 1:2], in1=fing[:],
                                                              op0=ALU.mult, op1=ALU.mult),
                      r=[("xo", xs, 0), ("xo", xs, 1), ("ssq", xs), "fing"], w=[("xo", xs, 0), ("xo", xs, 1)])
                P.dma("sp", out[r0:r0 + 128, :], xo[xs][:], r=[("xo", xs, 0), ("xo", xs, 1)])
        P.flush()


def make_in_maps(x, positions, norm_g, w_in, ml_gate_b, ml_conv_w, ml_norm_g, da_lambda,
                 da_subln_g, gate_b, w_branch_a, w_branch_b, w_out, final_g):
    f32 = np.float32
    x = np.asarray(x, f32)
    positions = np.asarray(positions, np.int32)
    w_in0 = np.ascontiguousarray(np.asarray(w_in, f32)[0])
    gb0 = np.asarray(ml_gate_b, f32)[0]
    cw0 = np.asarray(ml_conv_w, f32)[0]
    w_in1 = w_in0.copy()
    ag = w_in0[:, C_AG:C_AG + 16].reshape(D, 4, 4)
    w_in1[:, C_AG:C_AG + 16] = ag[:, [2, 3, 0, 1], :].reshape(D, 16)
    gb1 = gb0[[2, 3, 0, 1], :]
    cw1 = cw0[::-1, :]
    rep = lambda v, n=128: np.ascontiguousarray(np.broadcast_to(np.asarray(v, f32).reshape(1, -1), (n, np.asarray(v).size)))
    ident = np.eye(128, dtype=f32)
    rsw = np.zeros((128, 128), f32)
    for r in range(128):
        m = r % 64
        if m < 32:
            rsw[r + 32, r] = -1.0
        else:
            rsw[r - 32, r] = 1.0
    invf = (10000.0 ** (-np.arange(0, 64, 2, dtype=f32) / f32(64))).astype(f32)
    invf_p = np.array([invf[(p % 64) % 32] for p in range(128)], f32).reshape(128, 1)
    ii = np.arange(128)
    U = (ii[:, None] <= ii[None, :]).astype(f32)
    L = (ii[:, None] >= ii[None, :]).astype(f32)
    NEG = -30000.0
    tri = np.stack([U, L, (1 - U) * NEG, (1 - L) * NEG], axis=1).astype(f32)
    common = {
        "normg_rep": rep(np.asarray(norm_g, f32)[0]),
        "mlng_rep": rep(np.asarray(ml_norm_g, f32)[0].reshape(-1)),
        "lam_rep": rep(np.asarray(da_lambda, f32)[0].reshape(-1)),
        "subln_rep": rep(np.asarray(da_subln_g, f32)[0]),
        "gb_fm": np.ascontiguousarray(np.asarray(gate_b, f32)[0].reshape(2, 8, 128).transpose(2, 0, 1)),
        "w_a": np.ascontiguousarray(np.asarray(w_branch_a, f32)[0]),
        "w_b": np.ascontiguousarray(np.asarray(w_branch_b, f32)[0]),
        "w_o": np.ascontiguousarray(np.asarray(w_out, f32)[0]),
        "fing_rep": rep(np.asarray(final_g, f32)),
        "c_ident": ident, "c_rswap": rsw, "c_invf": invf_p, "c_tri": tri,
    }
    in_maps = []
    for core in range(NCORES):
        b, half = core // 2, core % 2
        xb = x[b]
        pb = positions[b]
        if half == 1:
            xb = xb[::-1]
            pb = pb[::-1]
        gbx = gb1 if half else gb0
        cwx = cw1 if half else cw0
        m = dict(common)
        m["x"] = np.ascontiguousarray(xb)
        m["posr"] = np.ascontiguousarray(np.broadcast_to(pb.reshape(1, S), (128, S))).astype(np.int32)
        m["w_in"] = w_in1 if half else w_in0
        m["gateb_rep"] = rep(gbx.reshape(-1))
        m["convw"] = np.ascontiguousarray(cwx.reshape(5, 16, 128).transpose(2, 1, 0))
        in_maps.append(m)
    return in_maps


_NC_CACHE = {}


def kernel(**inputs):
    in_maps = make_in_maps(**inputs)
    if "nc" not in _NC_CACHE:
        _NC_CACHE["nc"] = build_nc()
    nc = _NC_CACHE["nc"]
    res = run_bass_kernel_spmd(nc, in_maps, core_ids=list(range(NCORES)))
    B = 4
    outp = np.empty((B, S, D), np.float32)
    for core in range(NCORES):
        b, half = core // 2, core % 2
        o = np.asarray(res.results[core]["out"], np.float32)
        if half == 0:
            outp[b, :OWN] = o
        else:
            outp[b, OWN:] = o[::-1]
    return outp
```
